# Optimizing a Trainium2 kernel written in Bass

```python
import math
import jax, jax.numpy as jnp
from jax import lax
import numpy as np

D_MODEL = 4096
BATCH = 8
SEQ = 2048
DEPTH = 1

H_A = 16
QK_NOPE = 128
QK_ROPE = 64
QK_DIM_A = QK_NOPE + QK_ROPE
V_DIM_A = 128
Q_LORA = 1024
KV_LORA = 512
ROPE_THETA = 10000.0
H_B = 16
HEAD_DIM_B = 128
IDX_HEADS = 32
IDX_DIM = 64
TOPK_MAX = 256
N_BUCKETS = 32
MAX_DISTANCE = 128
Q_BLOCK = 128
EPS = 1e-6

WIDTH_A = H_A * V_DIM_A
WIDTH_B = H_B * HEAD_DIM_B

COL_SIZES = [
    Q_LORA,
    KV_LORA,
    QK_ROPE,
    H_B * HEAD_DIM_B,
    HEAD_DIM_B,
    HEAD_DIM_B,
    IDX_HEADS * IDX_DIM,
    IDX_DIM,
    IDX_HEADS,
    WIDTH_A,
    WIDTH_B,
    D_MODEL,
    D_MODEL,
]
IN_COLS = int(sum(COL_SIZES))
SPLIT_POINTS = [int(v) for v in np.cumsum(COL_SIZES)[:-1]]

kernel_name = "hybrid_mla_dsa_gated_merge"


def rmsnorm(x, g):
    xf = x.astype(jnp.float32)
    y = xf * lax.rsqrt(jnp.mean(xf * xf, axis=-1, keepdims=True) + EPS)
    return (y * g.astype(jnp.float32)).astype(x.dtype)


def apply_rope(x, pos):
    half = x.shape[-1] // 2
    inv = ROPE_THETA ** (-jnp.arange(half, dtype=jnp.float32) / half)
    ang = pos.astype(jnp.float32)[:, :, None, None] * inv
    cos, sin = jnp.cos(ang), jnp.sin(ang)
    x1 = x[..., :half].astype(jnp.float32)
    x2 = x[..., half:].astype(jnp.float32)
    return jnp.concatenate([x1 * cos - x2 * sin, x1 * sin + x2 * cos], axis=-1).astype(x.dtype)


def t5_bucket(dist):
    max_exact = N_BUCKETS // 2
    n = jnp.maximum(dist, 0)
    nf = jnp.maximum(n, 1).astype(jnp.float32)
    large = max_exact + (jnp.log(nf / max_exact) / math.log(MAX_DISTANCE / max_exact)
                         * (N_BUCKETS - max_exact)).astype(jnp.int32)
    large = jnp.minimum(large, N_BUCKETS - 1)
    return jnp.where(n < max_exact, n, large)


def to_blocks(a):
    b, l = a.shape[0], a.shape[1]
    a = a.reshape((b, l // Q_BLOCK, Q_BLOCK) + a.shape[2:])
    return jnp.moveaxis(a, 1, 0)


def from_blocks(a):
    a = jnp.moveaxis(a, 0, 1)
    return a.reshape((a.shape[0], a.shape[1] * a.shape[2]) + a.shape[3:])


def causal_dense_attention(q, k, v, pos, scale):
    def one(args):
        qi, pi = args
        s = jnp.einsum('bqhd,bshd->bhqs', qi, k, preferred_element_type=jnp.float32) * scale
        mask = pi[:, None, :, None] >= pos[:, None, None, :]
        s = jnp.where(mask, s, -jnp.inf)
        p = jax.nn.softmax(s, axis=-1).astype(v.dtype)
        return jnp.einsum('bhqs,bshd->bqhd', p, v)
    out = lax.map(one, (to_blocks(q), to_blocks(pos)))
    return from_blocks(out)


def indexed_sparse_attention(q, k, v, q_idx, k_idx, w_idx, pos, t5_bias, scale):
    n_keys = k.shape[1]
    topk = min(TOPK_MAX, n_keys // 4)
    gather = jax.vmap(lambda table, idx: table[idx])

    def one(args):
        qi, qxi, wi, pi = args
        logits = jnp.einsum('bqhd,bsd->bqsh', qxi, k_idx, preferred_element_type=jnp.float32)
        score = jnp.einsum('bqsh,bqh->bqs', jax.nn.relu(logits), wi.astype(jnp.float32))
        admissible = pi[:, :, None] >= pos[:, None, :]
        score = jnp.where(admissible, score, -jnp.inf)
        _, sel = lax.top_k(score, topk)
        kg = gather(k, sel)
        vg = gather(v, sel)
        dist = pi[:, :, None] - gather(pos, sel)
        bias = t5_bias[t5_bucket(dist)].astype(jnp.float32)
        s = jnp.einsum('bqhd,bqkd->bqhk', qi, kg, preferred_element_type=jnp.float32) * scale
        s = s + jnp.swapaxes(bias, -1, -2)
        s = jnp.where((dist >= 0)[:, :, None, :], s, -jnp.inf)
        p = jax.nn.softmax(s, axis=-1).astype(vg.dtype)
        return jnp.einsum('bqhk,bqkd->bqhd', p, vg)

    out = lax.map(one, (to_blocks(q), to_blocks(q_idx), to_blocks(w_idx), to_blocks(pos)))
    return from_blocks(out)


def setup_inputs(seed: int = 0) -> dict:
    key = jax.random.key(seed)
    ks = jax.random.split(key, 20)
    f32 = jnp.float32
    nrm = lambda k, shape, fan: jax.random.normal(k, shape, f32) * (fan ** -0.5)
    gain = lambda k, n: 1.0 + 0.05 * jax.random.normal(k, (n,), f32)
    x = jax.random.normal(ks[0], (BATCH, SEQ, D_MODEL), f32)
    offset = jax.random.randint(ks[1], (BATCH, 1), 0, 1024, dtype=jnp.int32)
    positions = offset + jnp.arange(SEQ, dtype=jnp.int32)[None, :]
    return {
        "x": x,
        "positions": positions,
        "g_pre": gain(ks[2], D_MODEL),
        "w_in": nrm(ks[3], (D_MODEL, IN_COLS), D_MODEL),
        "g_q_lat": gain(ks[4], Q_LORA),
        "g_kv_lat": gain(ks[5], KV_LORA),
        "w_uq": nrm(ks[6], (Q_LORA, H_A * QK_DIM_A), Q_LORA),
        "w_ukv": nrm(ks[7], (KV_LORA, H_A * (QK_NOPE + V_DIM_A)), KV_LORA),
        "g_qn_a": gain(ks[8], QK_DIM_A),
        "g_kn_a": gain(ks[9], QK_DIM_A),
        "g_qn_b": gain(ks[10], HEAD_DIM_B),
        "g_kn_b": gain(ks[11], HEAD_DIM_B),
        "t5_bias": 0.5 * jax.random.normal(ks[12], (N_BUCKETS, H_B), f32),
        "p_a": nrm(ks[13], (WIDTH_A, D_MODEL), WIDTH_A),
        "p_b": nrm(ks[14], (WIDTH_B, D_MODEL), WIDTH_B),
        "w_o": nrm(ks[15], (D_MODEL, D_MODEL), D_MODEL),
    }


def reference(x, positions, g_pre, w_in, g_q_lat, g_kv_lat, w_uq, w_ukv, g_qn_a, g_kn_a,
              g_qn_b, g_kn_b, t5_bias, p_a, p_b, w_o):
    b, l, _ = x.shape
    h = x
    for _layer in range(DEPTH):
        hn = rmsnorm(h, g_pre)
        proj = hn @ w_in
        (cq, ckv, k_rope, q_b, k_b, v_b, q_idx, k_idx, w_idx,
         gate_a, gate_b, merge_a, merge_b) = jnp.split(proj, SPLIT_POINTS, axis=-1)

        q_a = (rmsnorm(cq, g_q_lat) @ w_uq).reshape(b, l, H_A, QK_DIM_A)
        kv = (rmsnorm(ckv, g_kv_lat) @ w_ukv).reshape(b, l, H_A, QK_NOPE + V_DIM_A)
        k_nope, v_a = kv[..., :QK_NOPE], kv[..., QK_NOPE:]
        k_r = jnp.broadcast_to(k_rope[:, :, None, :], (b, l, H_A, QK_ROPE))
        k_a = jnp.concatenate([k_nope, k_r], axis=-1)
        q_a = rmsnorm(q_a, g_qn_a)
        k_a = rmsnorm(k_a, g_kn_a)
        q_a = jnp.concatenate([q_a[..., :QK_NOPE], apply_rope(q_a[..., QK_NOPE:], positions)], -1)
        k_a = jnp.concatenate([k_a[..., :QK_NOPE], apply_rope(k_a[..., QK_NOPE:], positions)], -1)
        o_a = causal_dense_attention(q_a, k_a, v_a, positions, QK_DIM_A ** -0.5)
        o_a = o_a.reshape(b, l, WIDTH_A) * jax.nn.silu(gate_a)

        q_bh = rmsnorm(q_b.reshape(b, l, H_B, HEAD_DIM_B), g_qn_b)
        k_bh = rmsnorm(k_b, g_kn_b)
        q_ix = q_idx.reshape(b, l, IDX_HEADS, IDX_DIM)
        w_ix = w_idx * (IDX_HEADS ** -0.5)
        o_b = indexed_sparse_attention(q_bh, k_bh, v_b, q_ix, k_idx, w_ix, positions,
                                       t5_bias, HEAD_DIM_B ** -0.5)
        o_b = o_b.reshape(b, l, WIDTH_B) * jax.nn.silu(gate_b)

        merged = jax.nn.sigmoid(merge_a) * (o_a @ p_a) + jax.nn.sigmoid(merge_b) * (o_b @ p_b)
        h = h + merged @ w_o
    return h
```

```python
import math
import numpy as np
import concourse.bass as bass
import concourse.mybir as mybir
from concourse.bass_utils import run_bass_kernel_spmd

F32 = mybir.dt.float32
BF16 = mybir.dt.bfloat16
I32 = mybir.dt.int32
U8 = mybir.dt.uint8
AF = mybir.ActivationFunctionType
ALU = mybir.AluOpType
AX = mybir.AxisListType

D = 4096
T = 2048
NT = 16
IN_COLS = 18336
EPS = 1e-6
C_CQ, C_CKV, C_KR, C_QB, C_KB, C_VB, C_QI, C_KI, C_WI, C_GA, C_GB, C_MA, C_MB = (
    0, 1024, 1536, 1600, 3648, 3776, 3904, 5952, 6016, 6048, 8096, 10144, 14240)
NITER = 24
SZ = {F32: 4, BF16: 2, I32: 4, U8: 1}


class Buf:
    __slots__ = ("name", "w", "r")

    def __init__(self, name):
        self.name = name
        self.w = None
        self.r = {}


class Op:
    __slots__ = ("eng", "fn", "deps", "flag", "dma", "tok", "n")

    def __init__(self, eng, fn, dma):
        self.eng = eng
        self.fn = fn
        self.deps = set()
        self.flag = False
        self.dma = dma
        self.tok = None
        self.n = None


class Prog:
    ENG = ("pe", "act", "dve", "pool", "sp")
    NDSEM = 8
    CH = 30000

    def __init__(self):
        self.ops = []
        self.barrier_deps = set()
        self.last = {}
        self.dma_hist = {"sp": [], "pool": [], "act": []}

    def op(self, eng, fn, r=(), w=(), dma=False):
        idx = len(self.ops)
        o = Op(eng, fn, dma)
        deps = set(self.barrier_deps)
        for b in r:
            if b.w is not None:
                deps.add(b.w)
        for b in w:
            if b.w is not None:
                deps.add(b.w)
            deps.update(b.r.values())
        key = ("dma", eng, idx) if dma else eng
        for b in r:
            b.r[key if not dma else ("dma", idx)] = idx
        for b in w:
            b.w = idx
            b.r = {}
        if eng == "pe" and not dma:
            deps = {d for d in deps if not (self.ops[d].eng == "pe" and not self.ops[d].dma)}
        o.deps = deps
        self.ops.append(o)
        if dma:
            self.dma_hist[eng].append(idx)
        else:
            self.last[eng] = idx
        return idx

    def barrier(self):
        deps = set(self.last.values())
        for q, h in self.dma_hist.items():
            deps.update(h[-self.NDSEM:])
        self.barrier_deps = deps

    def finalize(self):
        for o in self.ops:
            for d in o.deps:
                self.ops[d].flag = True
        cnt = {e: 0 for e in self.ENG}
        dcnt = {"sp": 0, "pool": 0, "act": 0}
        for o in self.ops:
            if o.dma:
                o.n = dcnt[o.eng]
                dcnt[o.eng] += 1
                o.tok = (("d", o.eng, o.n % self.NDSEM), 16 * (o.n // self.NDSEM + 1))
            elif o.flag:
                k = cnt[o.eng]
                cnt[o.eng] += 1
                o.tok = (("c", o.eng, k // self.CH), k % self.CH + 1)
        self.cnt = cnt
        self.dcnt = dcnt

    def sem_keys(self):
        keys = []
        for e in self.ENG:
            for i in range(max(1, (self.cnt[e] + self.CH - 1) // self.CH)):
                keys.append(("c", e, i))
        for q in ("sp", "pool", "act"):
            if self.dcnt[q]:
                for i in range(self.NDSEM):
                    keys.append(("d", q, i))
        return keys

    def emit_engine(self, ename, eng, sems, final_wait=False):
        known = {}
        for o in self.ops:
            if o.eng != ename:
                continue
            waits = {}
            for d in o.deps:
                k, v = self.ops[d].tok
                if waits.get(k, 0) < v:
                    waits[k] = v
            if o.dma and o.n >= self.NDSEM:
                k = ("d", o.eng, o.n % self.NDSEM)
                v = 16 * (o.n // self.NDSEM)
                if waits.get(k, 0) < v:
                    waits[k] = v
            for k, v in waits.items():
                if known.get(k, 0) < v:
                    eng.wait_ge(sems[k], v)
                    known[k] = v
            ins = o.fn(eng)
            if o.dma:
                ins.then_inc(sems[o.tok[0]], 16)
            elif o.flag:
                ins.then_inc(sems[o.tok[0]], 1)
        if final_wait:
            for q in ("sp", "pool", "act"):
                n = self.dcnt[q]
                for i in range(min(n, self.NDSEM)):
                    tot = (n - i + self.NDSEM - 1) // self.NDSEM
                    k = ("d", q, i)
                    if known.get(k, 0) < 16 * tot:
                        eng.wait_ge(sems[k], 16 * tot)


def _t5_bucket_np(d):
    n = np.maximum(d, 0)
    nf = np.maximum(n, 1).astype(np.float32)
    large = 16 + (np.log(nf / np.float32(16)) / np.float32(math.log(128 / 16)) * np.float32(16)).astype(np.int32)
    large = np.minimum(large, 31)
    return np.where(n < 16, n, large)


def _consts():
    c = {}
    c["ident"] = np.eye(128, dtype=np.float32)
    m = np.arange(384)
    bucket = _t5_bucket_np(np.where(m < 256, m, 0))
    c["onehot"] = (bucket[None, :] == np.arange(32)[:, None]).astype(np.float32)
    half = 32
    inv = (np.float32(10000.0) ** (-np.arange(half, dtype=np.float32) / np.float32(half))).astype(np.float32)
    c["inv64"] = np.concatenate([inv, inv]).reshape(64, 1).astype(np.float32)
    P = np.zeros((64, 64), np.float32)
    for i in range(32):
        P[i, i + 32] = -1.0
        P[i + 32, i] = 1.0
    c["rotT"] = np.ascontiguousarray(P.T)
    s = np.arange(128)[:, None]
    cc = np.arange(896)[None, :]
    c["cmask"] = (cc - 384 >= s).astype(np.float32)
    c["negmask"] = np.where(np.arange(128)[None, :] <= np.arange(128)[:, None], 0.0, -1e30).astype(np.float32)
    c["pow2"] = np.tile((0.5 ** np.arange(NITER)).astype(np.float32)[None, :], (128, 1))
    return c


CONSTS = _consts()


def build_nc(stop_after=99, debug=False):
    nc = bass.Bass("TRN2", target_bir_lowering=False)
    P = Prog()

    def din(name, shape, dt=F32):
        return nc.dram_tensor(name, list(shape), dt, kind="ExternalInput").ap()

    x_d = din("x", [T, D])
    pos_d = din("positions", [T], I32)
    g_pre_d = din("g_pre", [D])
    w_in_d = din("w_in", [D, IN_COLS])
    g_q_lat_d = din("g_q_lat", [1024])
    g_kv_lat_d = din("g_kv_lat", [512])
    w_uq_d = din("w_uq", [1024, 3072])
    w_ukv_d = din("w_ukv", [512, 4096])
    g_qn_a_d = din("g_qn_a", [192])
    g_kn_a_d = din("g_kn_a", [192])
    g_qn_b_d = din("g_qn_b", [128])
    g_kn_b_d = din("g_kn_b", [128])
    t5_d = din("t5_bias", [32, 16])
    p_a_d = din("p_a", [2048, D])
    p_b_d = din("p_b", [2048, D])
    w_o_d = din("w_o", [D, D])
    c_ident = din("c_ident", [128, 128])
    c_onehot = din("c_onehot", [32, 384])
    c_inv64 = din("c_inv64", [64, 1])
    c_rotT = din("c_rotT", [64, 64])
    c_cmask = din("c_cmask", [128, 896])
    c_negmask = din("c_negmask", [128, 128])
    c_pow2 = din("c_pow2", [128, NITER])

    out_d = nc.dram_tensor("out", [T, D], F32, kind="ExternalOutput").ap()

    def dscr(name, shape, dt):
        kind = "ExternalOutput" if debug else "Internal"
        return nc.dram_tensor(name, list(shape), dt, kind=kind).ap()

    projT = dscr("projT", [IN_COLS, T], BF16)
    kidxT = dscr("kidxT", [128, T], BF16)
    vb_s = dscr("vb_s", [T, 128], BF16)
    widx_s = dscr("widx_s", [T, 32], F32)
    oaT = dscr("oaT", [2048, T], BF16)
    obT = dscr("obT", [2048, T], BF16)
    Xb = dscr("Xb", [16, 129 * 384], F32)

    import contextlib
    st = contextlib.ExitStack()
    with st:
        arena = st.enter_context(nc.sbuf_tensor("arena", [128, 200 * 1024], U8))
        psum = st.enter_context(nc.psum_tensor("psum", [128, 4096], F32))

        def sb(off, shape, dt, parts=128, p0=0):
            n = int(np.prod(shape)) * SZ[dt]
            assert off + n <= 200 * 1024, (off, n)
            ap = arena[p0:p0 + parts, off:off + n].bitcast(dt)
            if len(shape) == 2:
                ap = ap.rearrange("p (a b) -> p a b", a=shape[0])
            elif len(shape) == 3:
                ap = ap.rearrange("p (a b c) -> p a b c", a=shape[0], b=shape[1])
            return ap

        def bank(i, n=1):
            return psum[:, i * 512:(i + n) * 512]

        def bank16(i):
            return psum[:, i * 512:(i + 1) * 512].bitcast(BF16)

        PSB = [Buf(f"ps{i}") for i in range(8)]

        class Alloc:
            def __init__(self, base=0):
                self.cur = base

            def __call__(self, shape, dt, parts=128):
                n = int(np.prod(shape)) * SZ[dt]
                off = self.cur
                self.cur = (off + n + 63) // 64 * 64
                return sb(off, shape, dt, parts)

        def dump(name, ap, reads, parts=128):
            if not debug:
                return
            shp = [parts] + list(ap.shape[1:])
            dt_ = nc.dram_tensor("dbg_" + name, shp, ap.dtype, kind="ExternalOutput").ap()
            P.op("sp", lambda e: e.dma_start(out=dt_, in_=ap), r=reads, dma=True)

        CB = 192 * 1024
        ca = Alloc(CB)
        ident_f = ca([128], F32)
        ident_b = ca([128], BF16)
        ones_b = ca([128], BF16)
        eps_t = ca([4], F32)
        B_const = Buf("consts")
        P.op("sp", lambda e: e.dma_start(out=ident_f, in_=c_ident), w=[B_const], dma=True)
        P.op("dve", lambda e: e.tensor_copy(out=ident_b, in_=ident_f), r=[B_const], w=[B_const])
        P.op("dve", lambda e: e.memset(ones_b, 1.0), w=[B_const])
        P.op("dve", lambda e: e.memset(eps_t, EPS), w=[B_const])
        assert ca.cur <= 200 * 1024

        a = Alloc(0)
        hnT = a([32, T], BF16)
        B_hnT = [Buf(f"hnT{i}") for i in range(4)]
        p1_base = a.cur
        xt = [a([D], F32) for _ in range(2)]
        B_xt = [Buf("xt0"), Buf("xt1")]
        gbc = a([D], F32)
        B_gbc = Buf("gbc")
        hn_tm = a([D], BF16)
        B_hn = Buf("hn_tm")
        ss = a([NT], F32)
        rstd0 = a([NT], F32)
        B_ss = [Buf(f"ss{i}") for i in range(NT)]
        assert a.cur <= CB, a.cur

        P.op("sp", lambda e: e.dma_start(out=gbc, in_=g_pre_d.partition_broadcast(128)), w=[B_gbc], dma=True)
        for tt in range(NT):
            sl = tt % 2
            P.op("sp", lambda e, tt=tt, sl=sl: e.dma_start(out=xt[sl], in_=x_d[tt * 128:(tt + 1) * 128, :]),
                 w=[B_xt[sl]], dma=True)
            P.op("dve", lambda e, tt=tt, sl=sl: e.scalar_tensor_tensor(
                out=hn_tm, in0=xt[sl], scalar=1.0, in1=xt[sl], op0=ALU.mult, op1=ALU.mult,
                accum_out=ss[:, tt:tt + 1]), r=[B_xt[sl]], w=[B_hn, B_ss[tt]])
            P.op("act", lambda e, tt=tt: e.activation(out=rstd0[:, tt:tt + 1], in_=ss[:, tt:tt + 1], func=AF.Sqrt,
                                                      scale=1.0 / D, bias=eps_t[:, 0:1]),
                 r=[B_ss[tt], B_const], w=[B_ss[tt]])
            P.op("dve", lambda e, tt=tt: e.reciprocal(out=rstd0[:, tt:tt + 1], in_=rstd0[:, tt:tt + 1]),
                 r=[B_ss[tt]], w=[B_ss[tt]])
            P.op("dve", lambda e, tt=tt, sl=sl: e.scalar_tensor_tensor(
                out=hn_tm, in0=xt[sl], scalar=rstd0[:, tt:tt + 1], in1=gbc, op0=ALU.mult, op1=ALU.mult),
                 r=[B_xt[sl], B_ss[tt], B_gbc], w=[B_hn])
            for q in range(4):
                bk = (tt * 4 + q) % 8
                for j in range(8):
                    c = q * 8 + j
                    P.op("pe", lambda e, bk=bk, j=j, c=c: e.transpose(
                        out=bank16(bk)[:, j * 128:(j + 1) * 128], in_=hn_tm[:, c * 128:(c + 1) * 128],
                        identity=ident_b), r=[B_hn, B_const], w=[PSB[bk]])
                P.op("act", lambda e, bk=bk, q=q, tt=tt: e.activation(
                    out=hnT[:, q * 8:(q + 1) * 8, tt * 128:(tt + 1) * 128],
                    in_=bank16(bk).rearrange("p (a b) -> p a b", a=8), func=AF.Copy),
                     r=[PSB[bk]], w=[B_hnT[tt // 4]])
        P.barrier()

        a = Alloc(p1_base)
        WB = [a([32, 256], BF16) for _ in range(2)]
        B_WB = [Buf("WB0"), Buf("WB1")]
        SG = [a([T], BF16) for _ in range(2)]
        B_SG = [Buf("SG0"), Buf("SG1")]
        WT = a([32, 160], BF16)
        B_WT = Buf("WT")
        SGv = [a([128], BF16) for _ in range(2)]
        SGw = [a([32], F32) for _ in range(2)]
        B_SGv = [Buf("SGv0"), Buf("SGv1")]
        assert a.cur <= CB, a.cur

        groups = []

        def add_range(c0, c1, act):
            c = c0
            while c < c1:
                wdt = min(256, c1 - c)
                blks = []
                for o in range(0, wdt, 128):
                    blks.append((o, min(128, wdt - o), "proj", c + o, act))
                groups.append((c, wdt, blks))
                c += wdt

        add_range(C_CQ, C_KR, AF.Copy)
        groups.append((C_KR, 64, [(0, 64, "proj", C_KR, AF.Copy)]))
        add_range(C_QB, C_VB, AF.Copy)
        add_range(C_QI, C_KI, AF.Copy)
        groups.append((C_KI, 64, [(0, 128, "kidx", 0, AF.Copy)]))
        add_range(C_GA, C_MA, AF.Silu)
        add_range(C_MA, IN_COLS, AF.Sigmoid)

        def w_src(c0, wdt):
            return w_in_d[:, c0:c0 + wdt].rearrange("(k p) c -> p k c", p=128)

        def issue_load(gi):
            c0, wdt, blks = groups[gi]
            sl = gi % 2
            if blks[0][2] == "kidx":
                P.op("pool", lambda e: e.dma_start(out=WB[sl][:, :, 0:64], in_=w_src(c0, 64)), w=[B_WB[sl]], dma=True)
                P.op("pool", lambda e: e.dma_start(out=WB[sl][:, :, 64:128], in_=w_src(c0, 64)), w=[B_WB[sl]],
                     dma=True)
            else:
                P.op("pool", lambda e: e.dma_start(out=WB[sl][:, :, 0:wdt], in_=w_src(c0, wdt)), w=[B_WB[sl]],
                     dma=True)

        P.op("pool", lambda e: e.dma_start(out=WT[:, :, 0:128], in_=w_src(C_VB, 128)), w=[B_WT], dma=True)
        P.op("pool", lambda e: e.dma_start(out=WT[:, :, 128:160], in_=w_src(C_WI, 32)), w=[B_WT], dma=True)
        issue_load(0)
        issue_load(1)
        nblk = 0
        ngroups = len(groups) if stop_after >= 1 else 0
        for gi in range(ngroups):
            c0, wdt, blks = groups[gi]
            sl = gi % 2
            for (o, M, kind, row, act) in blks:
                bs = (nblk % 2) * 4
                sg = nblk % 2
                nblk += 1
                for k in range(32):
                    for tb in range(4):
                        P.op("pe", lambda e, k=k, tb=tb, bs=bs, o=o, M=M, sl=sl: e.matmul(
                            bank(bs + tb)[0:M, :], lhsT=WB[sl][:, k, o:o + M], rhs=hnT[:, k, tb * 512:(tb + 1) * 512],
                            start=(k == 0), stop=(k == 31)), r=[B_WB[sl], B_hnT[tb]], w=[PSB[bs + tb]])
                for tb in range(4):
                    P.op("act", lambda e, tb=tb, bs=bs, M=M, sg=sg, act=act: e.activation(
                        out=SG[sg][0:M, tb * 512:(tb + 1) * 512], in_=bank(bs + tb)[0:M, :], func=act),
                         r=[PSB[bs + tb]], w=[B_SG[sg]])
                if kind == "proj":
                    P.op("sp", lambda e, sg=sg, M=M, row=row: e.dma_start(out=projT[row:row + M, :], in_=SG[sg][0:M, :]),
                         r=[B_SG[sg]], dma=True)
                else:
                    P.op("sp", lambda e, sg=sg: e.dma_start(out=kidxT, in_=SG[sg]), r=[B_SG[sg]], dma=True)
            if gi + 2 < ngroups:
                issue_load(gi + 2)
        if stop_after >= 1:
            for tt in range(NT):
                bk = tt % 2
                sg = tt % 2
                for k in range(32):
                    P.op("pe", lambda e, k=k, tt=tt, bk=bk: e.matmul(
                        bank(bk)[:, 0:160], lhsT=hnT[:, k, tt * 128:(tt + 1) * 128], rhs=WT[:, k, :],
                        start=(k == 0), stop=(k == 31)), r=[B_WT, B_hnT[tt // 4]], w=[PSB[bk]])
                P.op("act", lambda e, bk=bk, sg=sg: e.activation(out=SGv[sg], in_=bank(bk)[:, 0:128], func=AF.Copy),
                     r=[PSB[bk]], w=[B_SGv[sg]])
                P.op("act", lambda e, bk=bk, sg=sg: e.activation(out=SGw[sg], in_=bank(bk)[:, 128:160], func=AF.Copy),
                     r=[PSB[bk]], w=[B_SGv[sg]])
                P.op("sp", lambda e, sg=sg, tt=tt: e.dma_start(out=vb_s[tt * 128:(tt + 1) * 128, :], in_=SGv[sg]),
                     r=[B_SGv[sg]], dma=True)
                P.op("sp", lambda e, sg=sg, tt=tt: e.dma_start(out=widx_s[tt * 128:(tt + 1) * 128, :], in_=SGw[sg]),
                     r=[B_SGv[sg]], dma=True)
        P.barrier()


        if stop_after >= 2:
            a = Alloc(0)
            cqT = a([8, T], BF16)
            ckvT = a([4, T], BF16)
            krb = a([T], BF16)
            Ct = a([T], F32)
            St = a([T], F32)
            Rk = a([T], F32)
            krsq = a([T], BF16)
            gql = a([8], F32)
            gkl = a([4], F32)
            gq_n = a([1], F32)
            gq_r = a([1], F32)
            gk_n = a([1], F32)
            gk_r = a([1], F32)
            inv64 = a([1], F32)
            rotT = a([64], F32)
            cm_f = a([896], F32)
            cm_b = a([896], BF16)
            B_lat = [Buf(f"lat{i}") for i in range(4)]
            B_kr = Buf("kr")
            B_g = Buf("g2")
            B_tab = Buf("tab")
            ph_base = a.cur
            posi = a([T], I32)
            posf = a([T], F32)
            ang = a([T], F32)
            tq = a([T], F32)
            nn = a([T], F32)
            B_tmp = Buf("tmp2")
            assert a.cur <= CB, a.cur
            H64 = slice(0, 64)

            def gvec(dst, src, n0, n1, eng="sp"):
                P.op(eng, lambda e: e.dma_start(out=dst, in_=src[n0:n1].rearrange("(p o) -> p o", o=1)), w=[B_g],
                     dma=True)

            P.op("sp", lambda e: e.dma_start(out=cqT, in_=projT[0:1024, :].rearrange("(k p) t -> p k t", p=128)),
                 w=B_lat, dma=True)
            P.op("sp", lambda e: e.dma_start(out=ckvT, in_=projT[1024:1536, :].rearrange("(k p) t -> p k t", p=128)),
                 w=B_lat, dma=True)
            P.op("sp", lambda e: e.dma_start(out=krb[H64], in_=projT[C_KR:C_KR + 64, :]), w=[B_kr], dma=True)
            P.op("sp", lambda e: e.dma_start(out=gql, in_=g_q_lat_d.rearrange("(k p) -> p k", p=128),
                                            allow_slow_non_contiguous=True), w=[B_g], dma=True)
            P.op("sp", lambda e: e.dma_start(out=gkl, in_=g_kv_lat_d.rearrange("(k p) -> p k", p=128),
                                            allow_slow_non_contiguous=True), w=[B_g], dma=True)
            gvec(gq_n, g_qn_a_d, 0, 128)
            gvec(gq_r[H64], g_qn_a_d, 128, 192)
            gvec(gk_n, g_kn_a_d, 0, 128)
            gvec(gk_r[H64], g_kn_a_d, 128, 192)
            P.op("sp", lambda e: e.dma_start(out=inv64[H64], in_=c_inv64), w=[B_g], dma=True)
            P.op("sp", lambda e: e.dma_start(out=rotT[H64], in_=c_rotT), w=[B_g], dma=True)
            P.op("sp", lambda e: e.dma_start(out=cm_f, in_=c_cmask), w=[B_g], dma=True)
            P.op("sp", lambda e: e.dma_start(out=posi[H64], in_=pos_d.partition_broadcast(64)), w=[B_tmp], dma=True)
            P.op("dve", lambda e: e.tensor_copy(out=cm_b, in_=cm_f), r=[B_g], w=[B_g])
            qs = 192.0 ** -0.5
            P.op("dve", lambda e: e.tensor_scalar(out=gq_n, in0=gq_n, scalar1=qs, scalar2=None, op0=ALU.mult),
                 r=[B_g], w=[B_g])
            P.op("dve", lambda e: e.tensor_scalar(out=gq_r[H64], in0=gq_r[H64], scalar1=qs, scalar2=None, op0=ALU.mult),
                 r=[B_g], w=[B_g])
            TWO_PI_INV = float(np.float32(1.0 / (2 * math.pi)))
            MAGIC = 12582912.0
            C1 = 6.28125
            C2 = float(2 * math.pi - 6.28125)
            PI_LO = 3.1415925
            P.op("dve", lambda e: e.tensor_copy(out=posf[H64], in_=posi[H64]), r=[B_tmp], w=[B_tmp])
            P.op("dve", lambda e: e.tensor_scalar(out=ang[H64], in0=posf[H64], scalar1=inv64[H64, 0:1], scalar2=None,
                                                  op0=ALU.mult), r=[B_tmp, B_g], w=[B_tmp])
            for which, dst in (("sin", St), ("cos", Ct)):
                if which == "cos":
                    P.op("dve", lambda e: e.tensor_scalar(out=ang[H64], in0=ang[H64], scalar1=float(math.pi / 2),
                                                          scalar2=None, op0=ALU.add), r=[B_tmp], w=[B_tmp])
                P.op("dve", lambda e: e.tensor_scalar(out=tq[H64], in0=ang[H64], scalar1=TWO_PI_INV, scalar2=MAGIC,
                                                      op0=ALU.mult, op1=ALU.add), r=[B_tmp], w=[B_tmp])
                P.op("dve", lambda e: e.tensor_scalar(out=nn[H64], in0=tq[H64], scalar1=-MAGIC, scalar2=None,
                                                      op0=ALU.add), r=[B_tmp], w=[B_tmp])
                P.op("dve", lambda e: e.scalar_tensor_tensor(out=tq[H64], in0=nn[H64], scalar=-C1, in1=ang[H64],
                                                             op0=ALU.mult, op1=ALU.add), r=[B_tmp], w=[B_tmp])
                P.op("dve", lambda e: e.scalar_tensor_tensor(out=tq[H64], in0=nn[H64], scalar=-C2, in1=tq[H64],
                                                             op0=ALU.mult, op1=ALU.add), r=[B_tmp], w=[B_tmp])
                P.op("dve", lambda e: e.tensor_scalar(out=tq[H64], in0=tq[H64], scalar1=-PI_LO, scalar2=PI_LO,
                                                      op0=ALU.max, op1=ALU.min), r=[B_tmp], w=[B_tmp])
                P.op("act", lambda e, dst=dst: e.activation(out=dst[H64], in_=tq[H64], func=AF.Sin),
                     r=[B_tmp], w=[B_tab])
            P.barrier()

            a = Alloc(ph_base)
            wq = [a([8, 192], BF16) for _ in range(2)]
            wkv = [a([4, 256], BF16) for _ in range(2)]
            qn = [a([T], BF16) for _ in range(2)]
            qr = [a([T], BF16) for _ in range(2)]
            kn = [a([T], BF16) for _ in range(2)]
            kr = [a([T], BF16) for _ in range(2)]
            vh = [a([16, 128], BF16) for _ in range(2)]
            gt = [a([T], BF16) for _ in range(2)]
            oh = [a([T], BF16) for _ in range(2)]
            B_wq = [Buf("wq0"), Buf("wq1")]
            B_wkv = [Buf("wkv0"), Buf("wkv1")]
            B_q = [Buf("q0"), Buf("q1")]
            B_k = [Buf("k0"), Buf("k1")]
            B_v = [Buf("v0"), Buf("v1")]
            B_gt = [Buf("gt0"), Buf("gt1")]
            B_oh = [Buf("oh0"), Buf("oh1")]
            NTMP = 2
            sq = [a([512], BF16) for _ in range(NTMP)]
            sqr = [a([512], BF16) for _ in range(NTMP)]
            rs = [a([512], F32) for _ in range(NTMP)]
            yv = [a([512], F32) for _ in range(NTMP)]
            t1 = [a([512], F32) for _ in range(NTMP)]
            t2 = [a([512], F32) for _ in range(NTMP)]
            B_sq = [Buf(f"sq{i}") for i in range(NTMP)]
            B_sqr = [Buf(f"sqr{i}") for i in range(NTMP)]
            B_rs = [Buf(f"rs{i}") for i in range(NTMP)]
            B_y = [Buf(f"y{i}") for i in range(NTMP)]
            B_t1 = [Buf(f"t1{i}") for i in range(NTMP)]
            B_t2 = [Buf(f"t2{i}") for i in range(NTMP)]
            NPT = 4
            pT = [a([512], BF16) for _ in range(NPT)]
            B_pT = [Buf(f"pT{i}") for i in range(NPT)]
            rden = [a([512], F32) for _ in range(2)]
            og = [a([512], F32) for _ in range(2)]
            B_rden = [Buf("rden0"), Buf("rden1")]
            B_og = [Buf("og0"), Buf("og1")]
            assert a.cur <= CB, a.cur
            BA, BB, BC, BD = 0, 1, 2, 3
            BS = (4, 5)
            BO, BDEN = 6, 7
            cnt = {"tmp": 0, "pt": 0, "s": 0, "fin": 0}

            def rstd_from(bankc, tsl, scale, parts=128):
                P.op("act", lambda e: e.activation(out=rs[tsl], in_=bank(bankc), func=AF.Sqrt, scale=scale,
                                                   bias=eps_t[:, 0:1]), r=[PSB[bankc], B_const], w=[B_rs[tsl]])
                P.op("dve", lambda e: e.reciprocal(out=rs[tsl], in_=rs[tsl]), r=[B_rs[tsl]], w=[B_rs[tsl]])

            for (latT, nk, gl, nfeat, li) in ((cqT, 8, gql, 1024.0, 0), (ckvT, 4, gkl, 512.0, 1)):
                for tb in range(4):
                    tsl = cnt["tmp"] % NTMP
                    cnt["tmp"] += 1
                    ts_ = slice(tb * 512, (tb + 1) * 512)
                    for k in range(nk):
                        eng = "dve" if k % 2 == 0 else "pool"
                        sl2 = k % NTMP
                        P.op(eng, lambda e, k=k, sl2=sl2, latT=latT, ts_=ts_: e.tensor_tensor(
                            out=sq[sl2], in0=latT[:, k, ts_], in1=latT[:, k, ts_], op=ALU.mult),
                             r=[B_lat[tb]], w=[B_sq[sl2]])
                        P.op("pe", lambda e, k=k, sl2=sl2, nk=nk: e.matmul(bank(BC), lhsT=ones_b, rhs=sq[sl2],
                                                                         start=(k == 0), stop=(k == nk - 1)),
                             r=[B_sq[sl2], B_const], w=[PSB[BC]])
                    rstd_from(BC, tsl, 1.0 / nfeat)
                    for k in range(nk):
                        P.op("dve", lambda e, k=k, latT=latT, ts_=ts_, gl=gl, tsl=tsl: e.scalar_tensor_tensor(
                            out=latT[:, k, ts_], in0=latT[:, k, ts_], scalar=gl[:, k:k + 1], in1=rs[tsl],
                            op0=ALU.mult, op1=ALU.mult), r=[B_rs[tsl], B_g, B_lat[tb]], w=[B_lat[tb]])

            P.op("pool", lambda e: e.tensor_tensor(out=krsq[H64], in0=krb[H64], in1=krb[H64], op=ALU.mult),
                 r=[B_kr], w=[B_kr])
            for tb in range(4):
                tsl = cnt["tmp"] % NTMP
                cnt["tmp"] += 1
                ts_ = slice(tb * 512, (tb + 1) * 512)
                P.op("dve", lambda e, ts_=ts_, tsl=tsl: e.tensor_scalar(out=yv[tsl][H64], in0=krb[H64, ts_],
                                                                        scalar1=gk_r[H64, 0:1], scalar2=None,
                                                                        op0=ALU.mult), r=[B_kr, B_g], w=[B_y[tsl]])
                P.op("pe", lambda e, tsl=tsl: e.matmul(bank(BD)[H64, :], lhsT=rotT[H64], rhs=yv[tsl][H64],
                                                       start=True, stop=True), r=[B_y[tsl], B_g], w=[PSB[BD]])
                P.op("pool", lambda e, ts_=ts_, tsl=tsl: e.tensor_tensor(out=t1[tsl][H64], in0=yv[tsl][H64],
                                                                         in1=Ct[H64, ts_], op=ALU.mult),
                     r=[B_y[tsl], B_tab], w=[B_t1[tsl]])
                P.op("dve", lambda e, ts_=ts_, tsl=tsl: e.tensor_tensor(out=t2[tsl][H64], in0=bank(BD)[H64, :],
                                                                        in1=St[H64, ts_], op=ALU.mult),
                     r=[PSB[BD], B_tab], w=[B_t2[tsl]])
                P.op("pool", lambda e, ts_=ts_, tsl=tsl: e.tensor_tensor(out=Rk[H64, ts_], in0=t1[tsl][H64],
                                                                         in1=t2[tsl][H64], op=ALU.add),
                     r=[B_t1[tsl], B_t2[tsl]], w=[B_tab])

            def prep(h, sl):
                P.op("pool", lambda e: e.dma_start(out=wq[sl], in_=w_uq_d[:, h * 192:(h + 1) * 192].rearrange(
                    "(k p) c -> p k c", p=128)), w=[B_wq[sl]], dma=True)
                P.op("pool", lambda e: e.dma_start(out=wkv[sl], in_=w_ukv_d[:, h * 256:(h + 1) * 256].rearrange(
                    "(k p) c -> p k c", p=128)), w=[B_wkv[sl]], dma=True)
                P.op("sp", lambda e: e.dma_start(out=gt[sl], in_=projT[C_GA + h * 128:C_GA + (h + 1) * 128, :]),
                     w=[B_gt[sl]], dma=True)
                def body_tb(tb):
                    ts_ = slice(tb * 512, (tb + 1) * 512)
                    tsl = cnt["tmp"] % NTMP
                    cnt["tmp"] += 1
                    body_q(tb, ts_, tsl)
                    tsl = cnt["tmp"] % NTMP
                    cnt["tmp"] += 1
                    body_k(tb, ts_, tsl)

                def body_q(tb, ts_, tsl):
                    for k in range(8):
                        P.op("pe", lambda e, k=k: e.matmul(bank(BA), lhsT=wq[sl][:, k, 0:128], rhs=cqT[:, k, ts_],
                                                           start=(k == 0), stop=(k == 7)),
                             r=[B_wq[sl], B_lat[tb]], w=[PSB[BA]])
                    for k in range(8):
                        P.op("pe", lambda e, k=k: e.matmul(bank(BB)[H64, :], lhsT=wq[sl][:, k, 128:192],
                                                           rhs=cqT[:, k, ts_], start=(k == 0), stop=(k == 7)),
                             r=[B_wq[sl], B_lat[tb]], w=[PSB[BB]])
                    P.op("act", lambda e: e.activation(out=sq[tsl], in_=bank(BA), func=AF.Square),
                         r=[PSB[BA]], w=[B_sq[tsl]])
                    P.op("act", lambda e: e.activation(out=sqr[tsl][H64], in_=bank(BB)[H64, :], func=AF.Square),
                         r=[PSB[BB]], w=[B_sqr[tsl]])
                    P.op("pe", lambda e: e.matmul(bank(BC), lhsT=ones_b, rhs=sq[tsl], start=True, stop=False),
                         r=[B_sq[tsl], B_const], w=[PSB[BC]])
                    P.op("pe", lambda e: e.matmul(bank(BC), lhsT=ones_b[H64, :], rhs=sqr[tsl][H64], start=False,
                                                  stop=True), r=[B_sqr[tsl], B_const], w=[PSB[BC]])
                    rstd_from(BC, tsl, 1.0 / 192.0)
                    P.op("dve", lambda e: e.scalar_tensor_tensor(out=qn[sl][:, ts_], in0=bank(BA), scalar=gq_n[:, 0:1],
                                                                 in1=rs[tsl], op0=ALU.mult, op1=ALU.mult),
                         r=[PSB[BA], B_rs[tsl], B_g], w=[B_q[sl]])
                    P.op("dve", lambda e: e.tensor_scalar(out=yv[tsl][H64], in0=bank(BB)[H64, :], scalar1=gq_r[H64, 0:1],
                                                          scalar2=None, op0=ALU.mult), r=[PSB[BB], B_g], w=[B_y[tsl]])
                    P.op("pe", lambda e: e.matmul(bank(BD)[H64, :], lhsT=rotT[H64], rhs=yv[tsl][H64], start=True,
                                                  stop=True), r=[B_y[tsl], B_g], w=[PSB[BD]])
                    P.op("pool", lambda e: e.tensor_tensor(out=t1[tsl][H64], in0=yv[tsl][H64], in1=Ct[H64, ts_],
                                                           op=ALU.mult), r=[B_y[tsl], B_tab], w=[B_t1[tsl]])
                    P.op("dve", lambda e: e.tensor_tensor(out=t2[tsl][H64], in0=bank(BD)[H64, :], in1=St[H64, ts_],
                                                          op=ALU.mult), r=[PSB[BD], B_tab], w=[B_t2[tsl]])
                    P.op("pool", lambda e: e.tensor_tensor(out=t1[tsl][H64], in0=t1[tsl][H64], in1=t2[tsl][H64],
                                                           op=ALU.add), r=[B_t1[tsl], B_t2[tsl]], w=[B_t1[tsl]])
                    P.op("dve", lambda e: e.tensor_tensor(out=qr[sl][H64, ts_], in0=t1[tsl][H64], in1=rs[tsl][H64],
                                                          op=ALU.mult), r=[B_t1[tsl], B_rs[tsl]], w=[B_q[sl]])
                def body_k(tb, ts_, tsl):
                    for k in range(4):
                        P.op("pe", lambda e, k=k: e.matmul(bank(BA), lhsT=wkv[sl][:, k, 0:128], rhs=ckvT[:, k, ts_],
                                                           start=(k == 0), stop=(k == 3)),
                             r=[B_wkv[sl], B_lat[tb]], w=[PSB[BA]])
                    P.op("act", lambda e: e.activation(out=sq[tsl], in_=bank(BA), func=AF.Square),
                         r=[PSB[BA]], w=[B_sq[tsl]])
                    P.op("pe", lambda e: e.matmul(bank(BC), lhsT=ones_b, rhs=sq[tsl], start=True, stop=False),
                         r=[B_sq[tsl], B_const], w=[PSB[BC]])
                    P.op("pe", lambda e: e.matmul(bank(BC), lhsT=ones_b[H64, :], rhs=krsq[H64, ts_], start=False,
                                                  stop=True), r=[B_kr, B_const], w=[PSB[BC]])
                    rstd_from(BC, tsl, 1.0 / 192.0)
                    P.op("dve", lambda e: e.scalar_tensor_tensor(out=kn[sl][:, ts_], in0=bank(BA), scalar=gk_n[:, 0:1],
                                                                 in1=rs[tsl], op0=ALU.mult, op1=ALU.mult),
                         r=[PSB[BA], B_rs[tsl], B_g], w=[B_k[sl]])
                    P.op("dve", lambda e: e.tensor_tensor(out=kr[sl][H64, ts_], in0=Rk[H64, ts_], in1=rs[tsl][H64],
                                                          op=ALU.mult), r=[B_tab, B_rs[tsl]], w=[B_k[sl]])
                for tb in range(4):
                    body_tb(tb)
                for g4 in range(4):
                    for i in range(4):
                        kt = g4 * 4 + i
                        for k in range(4):
                            P.op("pe", lambda e, k=k, i=i, kt=kt: e.matmul(
                                bank(BD)[:, i * 128:(i + 1) * 128], lhsT=ckvT[:, k, kt * 128:(kt + 1) * 128],
                                rhs=wkv[sl][:, k, 128:256], start=(k == 0), stop=(k == 3)),
                                 r=[B_wkv[sl], B_lat[kt // 4]], w=[PSB[BD]])
                    P.op("act", lambda e, g4=g4: e.activation(
                        out=vh[sl][:, g4 * 4:(g4 + 1) * 4, :], in_=bank(BD).rearrange("p (a b) -> p a b", a=4),
                        func=AF.Copy), r=[PSB[BD]], w=[B_v[sl]])

            def attn_a(h, sl):
                for qb in range(4):
                    nj = 4 * qb + 4
                    qs_ = qb * 512
                    for j in range(nj):
                        lo = max(0, 128 * j - 512 * qb)
                        bs = BS[cnt["s"] % 2]
                        cnt["s"] += 1
                        pi = cnt["pt"] % NPT
                        cnt["pt"] += 1
                        ks = slice(j * 128, (j + 1) * 128)
                        qsl = slice(qs_ + lo, qs_ + 512)
                        P.op("pe", lambda e, bs=bs, lo=lo, ks=ks, qsl=qsl: e.matmul(
                            bank(bs)[:, lo:512], lhsT=kn[sl][:, ks], rhs=qn[sl][:, qsl], start=True, stop=False),
                             r=[B_k[sl], B_q[sl]], w=[PSB[bs]])
                        P.op("pe", lambda e, bs=bs, lo=lo, ks=ks, qsl=qsl: e.matmul(
                            bank(bs)[:, lo:512], lhsT=kr[sl][H64, ks], rhs=qr[sl][H64, qsl], start=False, stop=True),
                             r=[B_k[sl], B_q[sl]], w=[PSB[bs]])
                        P.op("act", lambda e, bs=bs, lo=lo, pi=pi: e.activation(
                            out=pT[pi][:, lo:512], in_=bank(bs)[:, lo:512], func=AF.Exp), r=[PSB[bs]], w=[B_pT[pi]])
                        if j >= 4 * qb:
                            jj = j - 4 * qb
                            m0 = 384 - 128 * jj + lo
                            P.op("pool", lambda e, lo=lo, pi=pi, m0=m0: e.tensor_tensor(
                                out=pT[pi][:, lo:512], in0=pT[pi][:, lo:512], in1=cm_b[:, m0:m0 + 512 - lo],
                                op=ALU.mult), r=[B_pT[pi], B_g], w=[B_pT[pi]])
                        P.op("pe", lambda e, lo=lo, pi=pi, j=j, nj=nj: e.matmul(
                            bank(BO)[:, lo:512], lhsT=vh[sl][:, j, :], rhs=pT[pi][:, lo:512], start=(j == 0),
                            stop=(j == nj - 1)), r=[B_v[sl], B_pT[pi]], w=[PSB[BO]])
                        P.op("pe", lambda e, lo=lo, pi=pi, j=j, nj=nj: e.matmul(
                            bank(BDEN)[:, lo:512], lhsT=ones_b, rhs=pT[pi][:, lo:512], start=(j == 0),
                            stop=(j == nj - 1)), r=[B_const, B_pT[pi]], w=[PSB[BDEN]])
                    fi = cnt["fin"] % 2
                    cnt["fin"] += 1
                    P.op("dve", lambda e, fi=fi: e.reciprocal(out=rden[fi], in_=bank(BDEN)), r=[PSB[BDEN]],
                         w=[B_rden[fi]])
                    P.op("pool", lambda e, fi=fi, qs_=qs_: e.tensor_tensor(
                        out=og[fi], in0=rden[fi], in1=gt[sl][:, qs_:qs_ + 512], op=ALU.mult),
                         r=[B_rden[fi], B_gt[sl]], w=[B_og[fi]])
                    P.op("dve", lambda e, fi=fi, qs_=qs_: e.tensor_tensor(
                        out=oh[sl][:, qs_:qs_ + 512], in0=bank(BO), in1=og[fi], op=ALU.mult),
                         r=[PSB[BO], B_og[fi]], w=[B_oh[sl]])
                P.op("sp", lambda e: e.dma_start(out=oaT[h * 128:(h + 1) * 128, :], in_=oh[sl]), r=[B_oh[sl]],
                     dma=True)

            import os as _os
            NH_A = int(_os.environ.get('NH_A', '16'))
            dump("Ct", Ct[H64], [B_tab], 64)
            dump("St", St[H64], [B_tab], 64)
            dump("Rk", Rk[H64], [B_tab], 64)
            dump("cqT", cqT, B_lat)
            dump("ckvT", ckvT, B_lat)
            for h in range(NH_A):
                prep(h, h % 2)
                if h == 0:
                    dump("qn", qn[0], [B_q[0]])
                    dump("qr", qr[0][H64], [B_q[0]], 64)
                    dump("kn", kn[0], [B_k[0]])
                    dump("kr", kr[0][H64], [B_k[0]], 64)
                    dump("vh", vh[0], [B_v[0]])
                attn_a(h, h % 2)
            P.barrier()


        if stop_after >= 3:
            a = Alloc(0)
            kiT = a([T], BF16)
            qiT = a([16, 512], BF16)
            wi = a([4, 32], F32)
            wabs = a([4, 32], F32)
            wsgn = a([4, 32], F32)
            acc = [a([T], F32) for _ in range(2)]
            tmpr = [a([1024], F32) for _ in range(2)]
            junk = a([T], BF16)
            mrow = [a([T], BF16) for _ in range(2)]
            maskT = a([16, 512], BF16)
            qbT = a([16, 512], BF16)
            kbT = a([T], BF16)
            vb = a([16, 128], BF16)
            BT = a([16, 256], F32)
            b31 = a([16], F32)
            gqb = a([1], F32)
            gkb = a([1], F32)
            gtb = a([16, 512], BF16)
            ob = a([16, 512], BF16)
            W0 = a([1], F32)
            Wtab = a([NITER], F32)
            mid = a([1], F32)
            cntv = a([1], F32)
            sgn = a([1], F32)
            thr = a([1], F32)
            pow2 = a([NITER], F32)
            negm = a([128], F32)
            t5s = a([16], F32)
            ohs = a([384], F32)
            F16 = a([384], F32)
            sqb = [a([512], BF16) for _ in range(2)]
            rs3 = [a([512], F32) for _ in range(2)]
            NPT3 = 4
            pT3v = [a([512], BF16) for _ in range(NPT3)]
            tnear = [a([256], F32) for _ in range(2)]
            rden3v = [a([512], F32) for _ in range(2)]
            og3v = [a([512], F32) for _ in range(2)]
            assert a.cur <= CB, a.cur
            B_ki = Buf("kiT"); B_qi = Buf("qiT"); B_wi = Buf("wi")
            B_acc = [Buf("acc0"), Buf("acc1")]
            B_tmpr = [Buf("tmpr0"), Buf("tmpr1")]
            B_junk = Buf("junk")
            B_mrow = [Buf("mrow0"), Buf("mrow1")]
            B_maskT = Buf("maskT")
            B_qb = Buf("qbT"); B_kb = Buf("kbT"); B_vb = Buf("vb"); B_BT = Buf("BT"); B_g3 = Buf("g3")
            B_gtb = Buf("gtb"); B_ob = Buf("ob"); B_bis = Buf("bis"); B_t5 = Buf("t5")
            B_sqb = [Buf("sqb0"), Buf("sqb1")]
            B_rs3 = [Buf("rs30"), Buf("rs31")]
            B_pT3 = [Buf(f"pT3{i}") for i in range(NPT3)]
            B_tn = [Buf("tn0"), Buf("tn1")]
            B_rden3 = [Buf("rden30"), Buf("rden31")]
            B_og3 = [Buf("og30"), Buf("og31")]
            H32 = slice(0, 32)
            H16 = slice(0, 16)
            c3 = {"tmp": 0, "lg": 0, "mr": 0, "tp": 0, "s": 0, "pt": 0, "tn": 0, "fin": 0, "acc": 0}

            P.op("sp", lambda e: e.dma_start(out=kiT, in_=kidxT), w=[B_ki], dma=True)
            P.op("sp", lambda e: e.dma_start(out=kbT, in_=projT[C_KB:C_KB + 128, :]), w=[B_kb], dma=True)
            P.op("sp", lambda e: e.dma_start(out=vb, in_=vb_s.rearrange("(j p) d -> p j d", p=128)), w=[B_vb], dma=True)
            P.op("sp", lambda e: e.dma_start(out=gqb, in_=g_qn_b_d.rearrange("(p o) -> p o", o=1)), w=[B_g3], dma=True)
            P.op("sp", lambda e: e.dma_start(out=gkb, in_=g_kn_b_d.rearrange("(p o) -> p o", o=1)), w=[B_g3], dma=True)
            P.op("sp", lambda e: e.dma_start(out=pow2, in_=c_pow2), w=[B_g3], dma=True)
            P.op("sp", lambda e: e.dma_start(out=negm, in_=c_negmask), w=[B_g3], dma=True)
            P.op("sp", lambda e: e.dma_start(out=t5s[H32], in_=t5_d), w=[B_t5], dma=True)
            P.op("sp", lambda e: e.dma_start(out=ohs[H32], in_=c_onehot), w=[B_t5], dma=True)
            P.op("sp", lambda e: e.dma_start(out=b31, in_=t5_d[31, :].partition_broadcast(128)), w=[B_g3], dma=True)
            P.op("dve", lambda e: e.tensor_scalar(out=gqb, in0=gqb, scalar1=128.0 ** -0.5, scalar2=None, op0=ALU.mult),
                 r=[B_g3], w=[B_g3])
            P.op("pe", lambda e: e.matmul(bank(7)[H16, 0:384], lhsT=t5s[H32], rhs=ohs[H32], start=True, stop=True),
                 r=[B_t5], w=[PSB[7]])
            P.op("dve", lambda e: e.tensor_copy(out=F16[H16], in_=bank(7)[H16, 0:384]), r=[PSB[7]], w=[B_t5])
            B_Xb = Buf("Xb")
            P.op("sp", lambda e: e.dma_start(out=Xb.rearrange("h (r m) -> h r m", m=384),
                                             in_=F16[H16].unsqueeze(1).broadcast_to([16, 129, 384])),
                 r=[B_t5], w=[B_Xb], dma=True)
            P.op("sp", lambda e: e.dma_start(out=BT, in_=bass.AP(Xb.tensor, 0, [[383, 128], [129 * 384, 16], [1, 256]])),
                 r=[B_Xb], w=[B_BT], dma=True)

            def rstd3(bankc, tsl, scale):
                P.op("act", lambda e: e.activation(out=rs3[tsl], in_=bank(bankc), func=AF.Sqrt, scale=scale,
                                                   bias=eps_t[:, 0:1]), r=[PSB[bankc], B_const], w=[B_rs3[tsl]])
                P.op("dve", lambda e: e.reciprocal(out=rs3[tsl], in_=rs3[tsl]), r=[B_rs3[tsl]], w=[B_rs3[tsl]])

            def norm_block(src_ap, gvec_ap, Bsrc):
                tsl = c3["tmp"] % 2
                c3["tmp"] += 1
                P.op("act", lambda e: e.activation(out=sqb[tsl], in_=src_ap, func=AF.Square), r=[Bsrc],
                     w=[B_sqb[tsl]])
                P.op("pe", lambda e: e.matmul(bank(6), lhsT=ones_b, rhs=sqb[tsl], start=True, stop=True),
                     r=[B_sqb[tsl], B_const], w=[PSB[6]])
                rstd3(6, tsl, 1.0 / 128.0)
                P.op("dve", lambda e: e.scalar_tensor_tensor(out=src_ap, in0=src_ap, scalar=gvec_ap, in1=rs3[tsl],
                                                             op0=ALU.mult, op1=ALU.mult),
                     r=[Bsrc, B_rs3[tsl], B_g3], w=[Bsrc])

            for tb in range(4):
                norm_block(kbT[:, tb * 512:(tb + 1) * 512], gkb[:, 0:1], B_kb)

            def indexer_tile(qb, i):
                qt = qb * 4 + i
                nk = (qt + 1) * 128
                asl = c3["acc"] % 2
                c3["acc"] += 1
                accv = acc[asl]
                Bacc = B_acc[asl]
                qsl = slice(i * 128, (i + 1) * 128)

                def head_half(h, k0, wk):
                    c = h // 2
                    hp = slice((h % 2) * 64, (h % 2) * 64 + 64)
                    lg = c3["lg"] % 2
                    c3["lg"] += 1
                    pb_ = bank(2 * lg, 2)
                    for off in range(0, wk, 512):
                        w_ = min(512, wk - off)
                        P.op("pe", lambda e, off=off, w_=w_: e.matmul(
                            pb_[:, off:off + w_], lhsT=qiT[hp, c, qsl], rhs=kiT[hp, k0 + off:k0 + off + w_],
                            start=True, stop=True), r=[B_qi, B_ki], w=[PSB[2 * lg + off // 512]])
                    rb = [PSB[2 * lg + o // 512] for o in range(0, wk, 512)]
                    P.op("act", lambda e: e.activation(out=tmpr[lg][:, 0:wk], in_=pb_[:, 0:wk], func=AF.Relu,
                                                       scale=wabs[:, i, h:h + 1]), r=rb + [B_wi], w=[B_tmpr[lg]])
                    if h == 0:
                        P.op("dve", lambda e: e.tensor_scalar(out=accv[:, k0:k0 + wk], in0=tmpr[lg][:, 0:wk],
                                                              scalar1=wsgn[:, i, h:h + 1], scalar2=None, op0=ALU.mult),
                             r=[B_tmpr[lg], B_wi], w=[Bacc])
                    else:
                        P.op("dve", lambda e: e.scalar_tensor_tensor(
                            out=accv[:, k0:k0 + wk], in0=tmpr[lg][:, 0:wk], scalar=wsgn[:, i, h:h + 1],
                            in1=accv[:, k0:k0 + wk], op0=ALU.mult, op1=ALU.add),
                             r=[B_tmpr[lg], B_wi, Bacc], w=[Bacc])

                for k0 in range(0, nk, 1024):
                    wk = min(1024, nk - k0)
                    for h in range(32):
                        head_half(h, k0, wk)
                P.op("dve", lambda e: e.tensor_reduce(out=W0[:, 0:1], in_=accv[:, 0:nk], axis=AX.X, op=ALU.max,
                                                      apply_absolute_value=True), r=[Bacc], w=[B_bis])
                P.op("dve", lambda e: e.tensor_scalar(out=W0, in0=W0, scalar1=1.001, scalar2=1e-6, op0=ALU.mult,
                                                      op1=ALU.add), r=[B_bis], w=[B_bis])
                P.op("dve", lambda e: e.tensor_scalar(out=Wtab, in0=pow2, scalar1=W0[:, 0:1], scalar2=None,
                                                      op0=ALU.mult), r=[B_bis, B_g3], w=[B_bis])
                P.op("dve", lambda e: e.tensor_tensor(out=accv[:, qt * 128:nk], in0=accv[:, qt * 128:nk], in1=negm,
                                                      op=ALU.add), r=[Bacc, B_g3], w=[Bacc])
                if qt >= 2:
                    P.op("dve", lambda e: e.memset(mid, 0.0), r=[B_bis], w=[B_bis])

                    def one_iter(it):
                        P.op("dve", lambda e: e.tensor_scalar(out=junk[:, 0:nk], in0=accv[:, 0:nk], scalar1=mid[:, 0:1],
                                                              scalar2=0.0, op0=ALU.is_ge, op1=ALU.add,
                                                              accum_out=cntv[:, 0:1]),
                             r=[Bacc, B_bis], w=[B_junk, B_bis])
                        P.op("dve", lambda e: e.tensor_scalar(out=sgn, in0=cntv, scalar1=256.0, scalar2=-0.5,
                                                              op0=ALU.is_ge, op1=ALU.add), r=[B_bis], w=[B_bis])
                        P.op("dve", lambda e: e.scalar_tensor_tensor(out=mid, in0=sgn, scalar=Wtab[:, it:it + 1],
                                                                     in1=mid, op0=ALU.mult, op1=ALU.add),
                             r=[B_bis], w=[B_bis])

                    for it in range(NITER):
                        one_iter(it)
                    P.op("dve", lambda e: e.scalar_tensor_tensor(out=thr, in0=Wtab[:, NITER - 1:NITER], scalar=-0.5,
                                                                 in1=mid, op0=ALU.mult, op1=ALU.add),
                         r=[B_bis], w=[B_bis])
                else:
                    P.op("dve", lambda e: e.memset(thr, -1e29), r=[B_bis], w=[B_bis])
                ms = c3["mr"] % 2
                c3["mr"] += 1
                P.op("dve", lambda e: e.tensor_scalar(out=mrow[ms][:, 0:nk], in0=accv[:, 0:nk], scalar1=thr[:, 0:1],
                                                      scalar2=None, op0=ALU.is_ge), r=[Bacc, B_bis], w=[B_mrow[ms]])
                for j0 in range(0, qt + 1, 8):
                    n = min(8, qt + 1 - j0)
                    bk = 4 + c3["tp"] % 2
                    c3["tp"] += 1
                    for jj in range(n):
                        j = j0 + jj
                        P.op("pe", lambda e, jj=jj, j=j, bk=bk: e.transpose(
                            out=bank16(bk)[:, jj * 128:(jj + 1) * 128], in_=mrow[ms][:, j * 128:(j + 1) * 128],
                            identity=ident_b), r=[B_mrow[ms], B_const], w=[PSB[bk]])
                    P.op("act", lambda e, j0=j0, n=n, bk=bk: e.activation(
                        out=maskT[:, j0:j0 + n, qsl], in_=bank16(bk)[:, 0:n * 128].rearrange("p (a b) -> p a b", a=n),
                        func=AF.Copy), r=[PSB[bk]], w=[B_maskT])

            def attn_b(qb, h):
                nj = 4 * qb + 4
                BS_ = (4, 5)
                BO_, BDEN_ = 6, 7

                def unit(j):
                    lo = max(0, 128 * j - 512 * qb)
                    c0 = 512 * qb + lo - 128 * j
                    nn_ = max(0, min(256 - c0, 512 - lo))
                    bs = BS_[c3["s"] % 2]
                    c3["s"] += 1
                    pi = c3["pt"] % NPT3
                    c3["pt"] += 1
                    P.op("pe", lambda e: e.matmul(bank(bs)[:, lo:512], lhsT=kbT[:, j * 128:(j + 1) * 128],
                                                  rhs=qbT[:, h, lo:512], start=True, stop=True),
                         r=[B_kb, B_qb], w=[PSB[bs]])
                    if nn_ > 0:
                        tn = c3["tn"] % 2
                        c3["tn"] += 1
                        P.op("dve", lambda e: e.tensor_tensor(out=tnear[tn][:, 0:nn_], in0=bank(bs)[:, lo:lo + nn_],
                                                              in1=BT[:, h, c0:c0 + nn_], op=ALU.add),
                             r=[PSB[bs], B_BT], w=[B_tn[tn]])
                        P.op("act", lambda e: e.activation(out=pT3v[pi][:, lo:lo + nn_], in_=tnear[tn][:, 0:nn_],
                                                           func=AF.Exp), r=[B_tn[tn]], w=[B_pT3[pi]])
                    if lo + nn_ < 512:
                        P.op("act", lambda e: e.activation(out=pT3v[pi][:, lo + nn_:512], in_=bank(bs)[:, lo + nn_:512],
                                                           func=AF.Exp, bias=b31[:, h:h + 1]),
                             r=[PSB[bs], B_g3], w=[B_pT3[pi]])
                    P.op("pool", lambda e: e.tensor_tensor(out=pT3v[pi][:, lo:512], in0=pT3v[pi][:, lo:512],
                                                           in1=maskT[:, j, lo:512], op=ALU.mult),
                         r=[B_pT3[pi], B_maskT], w=[B_pT3[pi]])
                    P.op("pe", lambda e: e.matmul(bank(BO_)[:, lo:512], lhsT=vb[:, j, :], rhs=pT3v[pi][:, lo:512],
                                                  start=(j == 0), stop=(j == nj - 1)), r=[B_vb, B_pT3[pi]],
                         w=[PSB[BO_]])
                    P.op("pe", lambda e: e.matmul(bank(BDEN_)[:, lo:512], lhsT=ones_b, rhs=pT3v[pi][:, lo:512],
                                                  start=(j == 0), stop=(j == nj - 1)), r=[B_const, B_pT3[pi]],
                         w=[PSB[BDEN_]])

                for j in range(nj):
                    unit(j)
                fi = c3["fin"] % 2
                c3["fin"] += 1
                P.op("dve", lambda e: e.reciprocal(out=rden3v[fi], in_=bank(BDEN_)), r=[PSB[BDEN_]], w=[B_rden3[fi]])
                P.op("pool", lambda e: e.tensor_tensor(out=og3v[fi], in0=rden3v[fi], in1=gtb[:, h, :], op=ALU.mult),
                     r=[B_rden3[fi], B_gtb], w=[B_og3[fi]])
                P.op("dve", lambda e: e.tensor_tensor(out=ob[:, h, :], in0=bank(BO_), in1=og3v[fi], op=ALU.mult),
                     r=[PSB[BO_], B_og3[fi]], w=[B_ob])

            def do_qb(qb):
                qcs = slice(qb * 512, (qb + 1) * 512)
                P.op("sp", lambda e: e.dma_start(out=qiT, in_=projT[C_QI:C_QI + 2048, qcs].rearrange(
                    "(c p) t -> p c t", p=128)), w=[B_qi], dma=True)
                P.op("sp", lambda e: e.dma_start(out=qbT, in_=projT[C_QB:C_QB + 2048, qcs].rearrange(
                    "(c p) t -> p c t", p=128)), w=[B_qb], dma=True)
                P.op("sp", lambda e: e.dma_start(out=gtb, in_=projT[C_GB:C_GB + 2048, qcs].rearrange(
                    "(c p) t -> p c t", p=128)), w=[B_gtb], dma=True)
                P.op("sp", lambda e: e.dma_start(out=wi, in_=widx_s[qcs, :].rearrange("(i p) h -> p i h", p=128)),
                     w=[B_wi], dma=True)
                P.op("act", lambda e: e.activation(out=wabs, in_=wi, func=AF.Abs, scale=32.0 ** -0.5), r=[B_wi],
                     w=[B_wi])
                P.op("act", lambda e: e.activation(out=wsgn, in_=wi, func=AF.Sign), r=[B_wi], w=[B_wi])
                for h in range(16):
                    norm_block(qbT[:, h, :], gqb[:, 0:1], B_qb)
                for i in range(4):
                    indexer_tile(qb, i)
                for h in range(16):
                    attn_b(qb, h)
                P.op("sp", lambda e: e.dma_start(out=obT[:, qcs].rearrange("(h p) t -> p h t", p=128), in_=ob),
                     r=[B_ob], dma=True)

            NQB = 4
            for qb in range(NQB):
                do_qb(qb)
            P.barrier()

        if stop_after >= 4:
            a = Alloc(0)
            oaH = a([16, 1024], BF16)
            obH = a([16, 1024], BF16)
            s2_base = a.cur
            mT = a([32, 1024], BF16)
            pa = [a([16, 128], BF16) for _ in range(2)]
            pb = [a([16, 128], BF16) for _ in range(2)]
            sA = [a([1024], BF16) for _ in range(2)]
            sB = [a([1024], BF16) for _ in range(2)]
            u1 = [a([512], F32) for _ in range(2)]
            u2 = [a([512], F32) for _ in range(2)]
            xin = [a([512], F32) for _ in range(2)]
            oout = [a([512], F32) for _ in range(2)]
            assert a.cur <= CB, a.cur
            a2 = Alloc(0)
            wo = [a2([32, 512], BF16) for _ in range(2)]
            assert a2.cur <= s2_base
            B_oaH = Buf("oaH"); B_obH = Buf("obH"); B_mT = [Buf(f"mT{i}") for i in range(32)]
            B_pa = [Buf("pa0"), Buf("pa1")]; B_pb = [Buf("pb0"), Buf("pb1")]
            B_sA = [Buf("sA0"), Buf("sA1")]; B_sB = [Buf("sB0"), Buf("sB1")]
            B_u1 = [Buf("u10"), Buf("u11")]; B_u2 = [Buf("u20"), Buf("u21")]
            B_xin = [Buf("xin0"), Buf("xin1")]; B_oout = [Buf("oo0"), Buf("oo1")]
            B_wo = [Buf("wo0"), Buf("wo1")]
            c4 = {"u": 0, "b": 0, "x": 0}

            def merged_block(th, fb):
                sl = fb % 2
                tcs = slice(th * 1024, (th + 1) * 1024)
                P.op("pool", lambda e: e.dma_start(out=pa[sl], in_=p_a_d[:, fb * 128:(fb + 1) * 128].rearrange(
                    "(k p) c -> p k c", p=128)), w=[B_pa[sl]], dma=True)
                P.op("pool", lambda e: e.dma_start(out=pb[sl], in_=p_b_d[:, fb * 128:(fb + 1) * 128].rearrange(
                    "(k p) c -> p k c", p=128)), w=[B_pb[sl]], dma=True)
                P.op("sp", lambda e: e.dma_start(out=sA[sl], in_=projT[C_MA + fb * 128:C_MA + (fb + 1) * 128, tcs]),
                     w=[B_sA[sl]], dma=True)
                P.op("sp", lambda e: e.dma_start(out=sB[sl], in_=projT[C_MB + fb * 128:C_MB + (fb + 1) * 128, tcs]),
                     w=[B_sB[sl]], dma=True)

                def sub(tb2):
                    bsl = c4["b"] % 2
                    c4["b"] += 1
                    us = c4["u"] % 2
                    c4["u"] += 1
                    ts_ = slice(tb2 * 512, (tb2 + 1) * 512)
                    for k in range(16):
                        P.op("pe", lambda e, k=k: e.matmul(bank(bsl), lhsT=pa[sl][:, k, :], rhs=oaH[:, k, ts_],
                                                           start=(k == 0), stop=(k == 15)), r=[B_pa[sl], B_oaH],
                             w=[PSB[bsl]])
                    for k in range(16):
                        P.op("pe", lambda e, k=k: e.matmul(bank(2 + bsl), lhsT=pb[sl][:, k, :], rhs=obH[:, k, ts_],
                                                           start=(k == 0), stop=(k == 15)), r=[B_pb[sl], B_obH],
                             w=[PSB[2 + bsl]])
                    P.op("dve", lambda e: e.tensor_tensor(out=u1[us], in0=bank(bsl), in1=sA[sl][:, ts_], op=ALU.mult),
                         r=[PSB[bsl], B_sA[sl]], w=[B_u1[us]])
                    P.op("dve", lambda e: e.tensor_tensor(out=u2[us], in0=bank(2 + bsl), in1=sB[sl][:, ts_],
                                                          op=ALU.mult), r=[PSB[2 + bsl], B_sB[sl]], w=[B_u2[us]])
                    P.op("pool", lambda e: e.tensor_tensor(out=mT[:, fb, ts_], in0=u1[us], in1=u2[us], op=ALU.add),
                         r=[B_u1[us], B_u2[us]], w=[B_mT[fb]])

                for tb2 in range(2):
                    sub(tb2)

            def final_block(th, cbk):
                sl = cbk % 2
                ccs = slice(cbk * 512, (cbk + 1) * 512)
                P.op("pool", lambda e: e.dma_start(out=wo[sl], in_=w_o_d[:, ccs].rearrange("(f p) c -> p f c", p=128)),
                     w=[B_wo[sl]], dma=True)

                def tok(tt):
                    bk = 4 + c4["b"] % 4
                    c4["b"] += 1
                    xs = c4["x"] % 2
                    c4["x"] += 1
                    r0 = th * 1024 + tt * 128
                    P.op("sp", lambda e: e.dma_start(out=xin[xs], in_=x_d[r0:r0 + 128, ccs]), w=[B_xin[xs]], dma=True)
                    for f in range(32):
                        P.op("pe", lambda e, f=f: e.matmul(bank(bk), lhsT=mT[:, f, tt * 128:(tt + 1) * 128],
                                                           rhs=wo[sl][:, f, :], start=(f == 0), stop=(f == 31)),
                             r=[B_wo[sl], B_mT[f]], w=[PSB[bk]])
                    P.op("dve", lambda e: e.tensor_tensor(out=oout[xs], in0=bank(bk), in1=xin[xs], op=ALU.add),
                         r=[PSB[bk], B_xin[xs]], w=[B_oout[xs]])
                    P.op("sp", lambda e: e.dma_start(out=out_d[r0:r0 + 128, ccs], in_=oout[xs]), r=[B_oout[xs]],
                         dma=True)

                for tt in range(8):
                    tok(tt)

            for th in range(2):
                tcs = slice(th * 1024, (th + 1) * 1024)
                P.op("sp", lambda e, tcs=tcs: e.dma_start(out=oaH, in_=oaT[:, tcs].rearrange("(k p) t -> p k t", p=128)),
                     w=[B_oaH], dma=True)
                P.op("sp", lambda e, tcs=tcs: e.dma_start(out=obH, in_=obT[:, tcs].rearrange("(k p) t -> p k t", p=128)),
                     w=[B_obH], dma=True)
                for fb in range(32):
                    merged_block(th, fb)
                P.barrier()
                for cbk in range(8):
                    final_block(th, cbk)
                P.barrier()

        P.finalize()
        keys = P.sem_keys()
        sems = {}
        for i, k in enumerate(keys):
            sems[k] = st.enter_context(nc.semaphore(f"s{i}"))
        block = st.enter_context(nc.Block())

        @block.tensor
        def _(e):
            P.emit_engine("pe", e, sems)

        @block.scalar
        def _(e):
            P.emit_engine("act", e, sems)

        @block.vector
        def _(e):
            P.emit_engine("dve", e, sems)

        @block.gpsimd
        def _(e):
            P.emit_engine("pool", e, sems)

        @block.sync
        def _(e):
            P.emit_engine("sp", e, sems, final_wait=True)
    return nc


def _in_maps(inputs):
    maps = []
    shared = {k: np.ascontiguousarray(np.asarray(v, dtype=np.float32)) for k, v in inputs.items()
              if k not in ("x", "positions")}
    for k, v in CONSTS.items():
        shared["c_" + k] = v
    x = np.asarray(inputs["x"], dtype=np.float32)
    pos = np.asarray(inputs["positions"], dtype=np.int32)
    for b in range(x.shape[0]):
        m = dict(shared)
        m["x"] = np.ascontiguousarray(x[b])
        m["positions"] = np.ascontiguousarray(pos[b])
        maps.append(m)
    return maps


_NC_CACHE = {}


def kernel(**inputs):
    if "nc" not in _NC_CACHE:
        _NC_CACHE["nc"] = build_nc()
    nc = _NC_CACHE["nc"]
    maps = _in_maps(inputs)
    res = run_bass_kernel_spmd(nc, maps, core_ids=list(range(len(maps))))
    return np.stack([r["out"] for r in res.results], axis=0).astype(np.float32)
```

```python
import math
import numpy as np
import concourse.bass as bass
import concourse.mybir as mybir
from concourse.bass_utils import run_bass_kernel_spmd

F32 = mybir.dt.float32
BF16 = mybir.dt.bfloat16
I32 = mybir.dt.int32
U8 = mybir.dt.uint8
AF = mybir.ActivationFunctionType
ALU = mybir.AluOpType
AX = mybir.AxisListType

D = 4096
T = 2048
NT = 16
IN_COLS = 18336
EPS = 1e-6
C_CQ, C_CKV, C_KR, C_QB, C_KB, C_VB, C_QI, C_KI, C_WI, C_GA, C_GB, C_MA, C_MB = (
    0, 1024, 1536, 1600, 3648, 3776, 3904, 5952, 6016, 6048, 8096, 10144, 14240)
NITER = 24
SZ = {F32: 4, BF16: 2, I32: 4, U8: 1}


class Buf:
    __slots__ = ("name", "w", "r")

    def __init__(self, name):
        self.name = name
        self.w = None
        self.r = {}


class Op:
    __slots__ = ("eng", "fn", "deps", "flag", "dma", "tok", "n")

    def __init__(self, eng, fn, dma):
        self.eng = eng
        self.fn = fn
        self.deps = set()
        self.flag = False
        self.dma = dma
        self.tok = None
        self.n = None


class Prog:
    ENG = ("pe", "act", "dve", "pool", "sp")
    NDSEM = 8
    CH = 30000

    def __init__(self):
        self.ops = []
        self.barrier_deps = set()
        self.last = {}
        self.dma_hist = {"sp": [], "pool": [], "act": []}

    def op(self, eng, fn, r=(), w=(), dma=False):
        idx = len(self.ops)
        o = Op(eng, fn, dma)
        deps = set(self.barrier_deps)
        for b in r:
            if b.w is not None:
                deps.add(b.w)
        for b in w:
            if b.w is not None:
                deps.add(b.w)
            deps.update(b.r.values())
        key = ("dma", eng, idx) if dma else eng
        for b in r:
            b.r[key if not dma else ("dma", idx)] = idx
        for b in w:
            b.w = idx
            b.r = {}
        if eng == "pe" and not dma:
            deps = {d for d in deps if not (self.ops[d].eng == "pe" and not self.ops[d].dma)}
        o.deps = deps
        self.ops.append(o)
        if dma:
            self.dma_hist[eng].append(idx)
        else:
            self.last[eng] = idx
        return idx

    def barrier(self):
        deps = set(self.last.values())
        for q, h in self.dma_hist.items():
            deps.update(h[-self.NDSEM:])
        self.barrier_deps = deps

    def finalize(self):
        for o in self.ops:
            for d in o.deps:
                self.ops[d].flag = True
        cnt = {e: 0 for e in self.ENG}
        dcnt = {"sp": 0, "pool": 0, "act": 0}
        for o in self.ops:
            if o.dma:
                o.n = dcnt[o.eng]
                dcnt[o.eng] += 1
                o.tok = (("d", o.eng, o.n % self.NDSEM), 16 * (o.n // self.NDSEM + 1))
            elif o.flag:
                k = cnt[o.eng]
                cnt[o.eng] += 1
                o.tok = (("c", o.eng, k // self.CH), k % self.CH + 1)
        self.cnt = cnt
        self.dcnt = dcnt

    def sem_keys(self):
        keys = []
        for e in self.ENG:
            for i in range(max(1, (self.cnt[e] + self.CH - 1) // self.CH)):
                keys.append(("c", e, i))
        for q in ("sp", "pool", "act"):
            if self.dcnt[q]:
                for i in range(self.NDSEM):
                    keys.append(("d", q, i))
        return keys

    def emit_engine(self, ename, eng, sems, final_wait=False):
        known = {}
        for o in self.ops:
            if o.eng != ename:
                continue
            waits = {}
            for d in o.deps:
                k, v = self.ops[d].tok
                if waits.get(k, 0) < v:
                    waits[k] = v
            if o.dma and o.n >= self.NDSEM:
                k = ("d", o.eng, o.n % self.NDSEM)
                v = 16 * (o.n // self.NDSEM)
                if waits.get(k, 0) < v:
                    waits[k] = v
            for k, v in waits.items():
                if known.get(k, 0) < v:
                    eng.wait_ge(sems[k], v)
                    known[k] = v
            ins = o.fn(eng)
            if o.dma:
                ins.then_inc(sems[o.tok[0]], 16)
            elif o.flag:
                ins.then_inc(sems[o.tok[0]], 1)
        if final_wait:
            for q in ("sp", "pool", "act"):
                n = self.dcnt[q]
                for i in range(min(n, self.NDSEM)):
                    tot = (n - i + self.NDSEM - 1) // self.NDSEM
                    k = ("d", q, i)
                    if known.get(k, 0) < 16 * tot:
                        eng.wait_ge(sems[k], 16 * tot)


def _t5_bucket_np(d):
    n = np.maximum(d, 0)
    nf = np.maximum(n, 1).astype(np.float32)
    large = 16 + (np.log(nf / np.float32(16)) / np.float32(math.log(128 / 16)) * np.float32(16)).astype(np.int32)
    large = np.minimum(large, 31)
    return np.where(n < 16, n, large)


def _consts():
    c = {}
    c["ident"] = np.eye(128, dtype=np.float32)
    m = np.arange(384)
    bucket = _t5_bucket_np(np.where(m < 256, m, 0))
    c["onehot"] = (bucket[None, :] == np.arange(32)[:, None]).astype(np.float32)
    half = 32
    inv = (np.float32(10000.0) ** (-np.arange(half, dtype=np.float32) / np.float32(half))).astype(np.float32)
    c["inv64"] = np.concatenate([inv, inv]).reshape(64, 1).astype(np.float32)
    P = np.zeros((64, 64), np.float32)
    for i in range(32):
        P[i, i + 32] = -1.0
        P[i + 32, i] = 1.0
    c["rotT"] = np.ascontiguousarray(P.T)
    s = np.arange(128)[:, None]
    cc = np.arange(896)[None, :]
    c["cmask"] = (cc - 384 >= s).astype(np.float32)
    c["negmask"] = np.where(np.arange(128)[None, :] <= np.arange(128)[:, None], 0.0, -1e30).astype(np.float32)
    c["pow2"] = np.tile((0.5 ** np.arange(NITER)).astype(np.float32)[None, :], (128, 1))
    return c


CONSTS = _consts()


def build_nc(stop_after=99, debug=False):
    nc = bass.Bass("TRN2", target_bir_lowering=False)
    P = Prog()

    def din(name, shape, dt=F32):
        return nc.dram_tensor(name, list(shape), dt, kind="ExternalInput").ap()

    x_d = din("x", [T, D])
    pos_d = din("positions", [T], I32)
    g_pre_d = din("g_pre", [D])
    w_in_d = din("w_in", [D, IN_COLS])
    g_q_lat_d = din("g_q_lat", [1024])
    g_kv_lat_d = din("g_kv_lat", [512])
    w_uq_d = din("w_uq", [1024, 3072])
    w_ukv_d = din("w_ukv", [512, 4096])
    g_qn_a_d = din("g_qn_a", [192])
    g_kn_a_d = din("g_kn_a", [192])
    g_qn_b_d = din("g_qn_b", [128])
    g_kn_b_d = din("g_kn_b", [128])
    t5_d = din("t5_bias", [32, 16])
    p_a_d = din("p_a", [2048, D])
    p_b_d = din("p_b", [2048, D])
    w_o_d = din("w_o", [D, D])
    c_ident = din("c_ident", [128, 128])
    c_onehot = din("c_onehot", [32, 384])
    c_inv64 = din("c_inv64", [64, 1])
    c_rotT = din("c_rotT", [64, 64])
    c_cmask = din("c_cmask", [128, 896])
    c_negmask = din("c_negmask", [128, 128])
    c_pow2 = din("c_pow2", [128, NITER])

    out_d = nc.dram_tensor("out", [T, D], F32, kind="ExternalOutput").ap()

    def dscr(name, shape, dt):
        kind = "ExternalOutput" if debug else "Internal"
        return nc.dram_tensor(name, list(shape), dt, kind=kind).ap()

    projT = dscr("projT", [IN_COLS, T], BF16)
    kidxT = dscr("kidxT", [128, T], BF16)
    vb_s = dscr("vb_s", [T, 128], BF16)
    widx_s = dscr("widx_s", [T, 32], F32)
    oaT = dscr("oaT", [2048, T], BF16)
    obT = dscr("obT", [2048, T], BF16)
    Xb = dscr("Xb", [16, 129 * 384], F32)

    import contextlib
    st = contextlib.ExitStack()
    with st:
        arena = st.enter_context(nc.sbuf_tensor("arena", [128, 200 * 1024], U8))
        psum = st.enter_context(nc.psum_tensor("psum", [128, 4096], F32))

        def sb(off, shape, dt, parts=128, p0=0):
            n = int(np.prod(shape)) * SZ[dt]
            assert off + n <= 200 * 1024, (off, n)
            ap = arena[p0:p0 + parts, off:off + n].bitcast(dt)
            if len(shape) == 2:
                ap = ap.rearrange("p (a b) -> p a b", a=shape[0])
            elif len(shape) == 3:
                ap = ap.rearrange("p (a b c) -> p a b c", a=shape[0], b=shape[1])
            return ap

        def bank(i, n=1):
            return psum[:, i * 512:(i + n) * 512]

        def bank16(i):
            return psum[:, i * 512:(i + 1) * 512].bitcast(BF16)

        PSB = [Buf(f"ps{i}") for i in range(8)]

        class Alloc:
            def __init__(self, base=0):
                self.cur = base

            def __call__(self, shape, dt, parts=128):
                n = int(np.prod(shape)) * SZ[dt]
                off = self.cur
                self.cur = (off + n + 63) // 64 * 64
                return sb(off, shape, dt, parts)

        def dump(name, ap, reads, parts=128):
            if not debug:
                return
            shp = [parts] + list(ap.shape[1:])
            dt_ = nc.dram_tensor("dbg_" + name, shp, ap.dtype, kind="ExternalOutput").ap()
            P.op("sp", lambda e: e.dma_start(out=dt_, in_=ap), r=reads, dma=True)

        CB = 192 * 1024
        ca = Alloc(CB)
        ident_f = ca([128], F32)
        ident_b = ca([128], BF16)
        ones_b = ca([128], BF16)
        eps_t = ca([4], F32)
        B_const = Buf("consts")
        P.op("sp", lambda e: e.dma_start(out=ident_f, in_=c_ident), w=[B_const], dma=True)
        P.op("dve", lambda e: e.tensor_copy(out=ident_b, in_=ident_f), r=[B_const], w=[B_const])
        P.op("dve", lambda e: e.memset(ones_b, 1.0), w=[B_const])
        P.op("dve", lambda e: e.memset(eps_t, EPS), w=[B_const])
        assert ca.cur <= 200 * 1024

        a = Alloc(0)
        hnT = a([32, T], BF16)
        B_hnT = [Buf(f"hnT{i}") for i in range(4)]
        p1_base = a.cur
        xt = [a([D], F32) for _ in range(2)]
        B_xt = [Buf("xt0"), Buf("xt1")]
        gbc = a([D], F32)
        B_gbc = Buf("gbc")
        hn_tm = a([D], BF16)
        B_hn = Buf("hn_tm")
        ss = a([NT], F32)
        rstd0 = a([NT], F32)
        B_ss = [Buf(f"ss{i}") for i in range(NT)]
        assert a.cur <= CB, a.cur

        P.op("sp", lambda e: e.dma_start(out=gbc, in_=g_pre_d.partition_broadcast(128)), w=[B_gbc], dma=True)
        for tt in range(NT):
            sl = tt % 2
            P.op("sp", lambda e, tt=tt, sl=sl: e.dma_start(out=xt[sl], in_=x_d[tt * 128:(tt + 1) * 128, :]),
                 w=[B_xt[sl]], dma=True)
            P.op("dve", lambda e, tt=tt, sl=sl: e.scalar_tensor_tensor(
                out=hn_tm, in0=xt[sl], scalar=1.0, in1=xt[sl], op0=ALU.mult, op1=ALU.mult,
                accum_out=ss[:, tt:tt + 1]), r=[B_xt[sl]], w=[B_hn, B_ss[tt]])
            P.op("act", lambda e, tt=tt: e.activation(out=rstd0[:, tt:tt + 1], in_=ss[:, tt:tt + 1], func=AF.Sqrt,
                                                      scale=1.0 / D, bias=eps_t[:, 0:1]),
                 r=[B_ss[tt], B_const], w=[B_ss[tt]])
            P.op("dve", lambda e, tt=tt: e.reciprocal(out=rstd0[:, tt:tt + 1], in_=rstd0[:, tt:tt + 1]),
                 r=[B_ss[tt]], w=[B_ss[tt]])
            P.op("dve", lambda e, tt=tt, sl=sl: e.scalar_tensor_tensor(
                out=hn_tm, in0=xt[sl], scalar=rstd0[:, tt:tt + 1], in1=gbc, op0=ALU.mult, op1=ALU.mult),
                 r=[B_xt[sl], B_ss[tt], B_gbc], w=[B_hn])
            for q in range(4):
                bk = (tt * 4 + q) % 8
                for j in range(8):
                    c = q * 8 + j
                    P.op("pe", lambda e, bk=bk, j=j, c=c: e.transpose(
                        out=bank16(bk)[:, j * 128:(j + 1) * 128], in_=hn_tm[:, c * 128:(c + 1) * 128],
                        identity=ident_b), r=[B_hn, B_const], w=[PSB[bk]])
                P.op("act", lambda e, bk=bk, q=q, tt=tt: e.activation(
                    out=hnT[:, q * 8:(q + 1) * 8, tt * 128:(tt + 1) * 128],
                    in_=bank16(bk).rearrange("p (a b) -> p a b", a=8), func=AF.Copy),
                     r=[PSB[bk]], w=[B_hnT[tt // 4]])
        P.barrier()

        a = Alloc(p1_base)
        WB = [a([32, 256], BF16) for _ in range(2)]
        B_WB = [Buf("WB0"), Buf("WB1")]
        SG = [a([T], BF16) for _ in range(2)]
        B_SG = [Buf("SG0"), Buf("SG1")]
        WT = a([32, 160], BF16)
        B_WT = Buf("WT")
        SGv = [a([128], BF16) for _ in range(2)]
        SGw = [a([32], F32) for _ in range(2)]
        B_SGv = [Buf("SGv0"), Buf("SGv1")]
        assert a.cur <= CB, a.cur

        groups = []

        def add_range(c0, c1, act):
            c = c0
            while c < c1:
                wdt = min(256, c1 - c)
                blks = []
                for o in range(0, wdt, 128):
                    blks.append((o, min(128, wdt - o), "proj", c + o, act))
                groups.append((c, wdt, blks))
                c += wdt

        add_range(C_CQ, C_KR, AF.Copy)
        groups.append((C_KR, 64, [(0, 64, "proj", C_KR, AF.Copy)]))
        add_range(C_QB, C_VB, AF.Copy)
        add_range(C_QI, C_KI, AF.Copy)
        groups.append((C_KI, 64, [(0, 128, "kidx", 0, AF.Copy)]))
        add_range(C_GA, C_MA, AF.Silu)
        add_range(C_MA, IN_COLS, AF.Sigmoid)

        def w_src(c0, wdt):
            return w_in_d[:, c0:c0 + wdt].rearrange("(k p) c -> p k c", p=128)

        def issue_load(gi):
            c0, wdt, blks = groups[gi]
            sl = gi % 2
            if blks[0][2] == "kidx":
                P.op("pool", lambda e: e.dma_start(out=WB[sl][:, :, 0:64], in_=w_src(c0, 64)), w=[B_WB[sl]], dma=True)
                P.op("pool", lambda e: e.dma_start(out=WB[sl][:, :, 64:128], in_=w_src(c0, 64)), w=[B_WB[sl]],
                     dma=True)
            else:
                P.op("pool", lambda e: e.dma_start(out=WB[sl][:, :, 0:wdt], in_=w_src(c0, wdt)), w=[B_WB[sl]],
                     dma=True)

        P.op("pool", lambda e: e.dma_start(out=WT[:, :, 0:128], in_=w_src(C_VB, 128)), w=[B_WT], dma=True)
        P.op("pool", lambda e: e.dma_start(out=WT[:, :, 128:160], in_=w_src(C_WI, 32)), w=[B_WT], dma=True)
        issue_load(0)
        issue_load(1)
        nblk = 0
        ngroups = len(groups) if stop_after >= 1 else 0
        for gi in range(ngroups):
            c0, wdt, blks = groups[gi]
            sl = gi % 2
            for (o, M, kind, row, act) in blks:
                bs = (nblk % 2) * 4
                sg = nblk % 2
                nblk += 1
                for k in range(32):
                    for tb in range(4):
                        P.op("pe", lambda e, k=k, tb=tb, bs=bs, o=o, M=M, sl=sl: e.matmul(
                            bank(bs + tb)[0:M, :], lhsT=WB[sl][:, k, o:o + M], rhs=hnT[:, k, tb * 512:(tb + 1) * 512],
                            start=(k == 0), stop=(k == 31)), r=[B_WB[sl], B_hnT[tb]], w=[PSB[bs + tb]])
                for tb in range(4):
                    P.op("act", lambda e, tb=tb, bs=bs, M=M, sg=sg, act=act: e.activation(
                        out=SG[sg][0:M, tb * 512:(tb + 1) * 512], in_=bank(bs + tb)[0:M, :], func=act),
                         r=[PSB[bs + tb]], w=[B_SG[sg]])
                if kind == "proj":
                    P.op("sp", lambda e, sg=sg, M=M, row=row: e.dma_start(out=projT[row:row + M, :], in_=SG[sg][0:M, :]),
                         r=[B_SG[sg]], dma=True)
                else:
                    P.op("sp", lambda e, sg=sg: e.dma_start(out=kidxT, in_=SG[sg]), r=[B_SG[sg]], dma=True)
            if gi + 2 < ngroups:
                issue_load(gi + 2)
        if stop_after >= 1:
            for tt in range(NT):
                bk = tt % 2
                sg = tt % 2
                for k in range(32):
                    P.op("pe", lambda e, k=k, tt=tt, bk=bk: e.matmul(
                        bank(bk)[:, 0:160], lhsT=hnT[:, k, tt * 128:(tt + 1) * 128], rhs=WT[:, k, :],
                        start=(k == 0), stop=(k == 31)), r=[B_WT, B_hnT[tt // 4]], w=[PSB[bk]])
                P.op("act", lambda e, bk=bk, sg=sg: e.activation(out=SGv[sg], in_=bank(bk)[:, 0:128], func=AF.Copy),
                     r=[PSB[bk]], w=[B_SGv[sg]])
                P.op("act", lambda e, bk=bk, sg=sg: e.activation(out=SGw[sg], in_=bank(bk)[:, 128:160], func=AF.Copy),
                     r=[PSB[bk]], w=[B_SGv[sg]])
                P.op("sp", lambda e, sg=sg, tt=tt: e.dma_start(out=vb_s[tt * 128:(tt + 1) * 128, :], in_=SGv[sg]),
                     r=[B_SGv[sg]], dma=True)
                P.op("sp", lambda e, sg=sg, tt=tt: e.dma_start(out=widx_s[tt * 128:(tt + 1) * 128, :], in_=SGw[sg]),
                     r=[B_SGv[sg]], dma=True)
        P.barrier()


        if stop_after >= 2:
            a = Alloc(0)
            cqT = a([8, T], BF16)
            ckvT = a([4, T], BF16)
            krb = a([T], BF16)
            Ct = a([T], F32)
            St = a([T], F32)
            Rk = a([T], F32)
            krsq = a([T], BF16)
            gql = a([8], F32)
            gkl = a([4], F32)
            gq_n = a([1], F32)
            gq_r = a([1], F32)
            gk_n = a([1], F32)
            gk_r = a([1], F32)
            inv64 = a([1], F32)
            rotT = a([64], F32)
            cm_f = a([896], F32)
            cm_b = a([896], BF16)
            B_lat = [Buf(f"lat{i}") for i in range(4)]
            B_kr = Buf("kr")
            B_g = Buf("g2")
            B_tab = Buf("tab")
            ph_base = a.cur
            posi = a([T], I32)
            posf = a([T], F32)
            ang = a([T], F32)
            tq = a([T], F32)
            nn = a([T], F32)
            B_tmp = Buf("tmp2")
            assert a.cur <= CB, a.cur
            H64 = slice(0, 64)

            def gvec(dst, src, n0, n1, eng="sp"):
                P.op(eng, lambda e: e.dma_start(out=dst, in_=src[n0:n1].rearrange("(p o) -> p o", o=1)), w=[B_g],
                     dma=True)

            P.op("sp", lambda e: e.dma_start(out=cqT, in_=projT[0:1024, :].rearrange("(k p) t -> p k t", p=128)),
                 w=B_lat, dma=True)
            P.op("sp", lambda e: e.dma_start(out=ckvT, in_=projT[1024:1536, :].rearrange("(k p) t -> p k t", p=128)),
                 w=B_lat, dma=True)
            P.op("sp", lambda e: e.dma_start(out=krb[H64], in_=projT[C_KR:C_KR + 64, :]), w=[B_kr], dma=True)
            P.op("sp", lambda e: e.dma_start(out=gql, in_=g_q_lat_d.rearrange("(k p) -> p k", p=128),
                                            allow_slow_non_contiguous=True), w=[B_g], dma=True)
            P.op("sp", lambda e: e.dma_start(out=gkl, in_=g_kv_lat_d.rearrange("(k p) -> p k", p=128),
                                            allow_slow_non_contiguous=True), w=[B_g], dma=True)
            gvec(gq_n, g_qn_a_d, 0, 128)
            gvec(gq_r[H64], g_qn_a_d, 128, 192)
            gvec(gk_n, g_kn_a_d, 0, 128)
            gvec(gk_r[H64], g_kn_a_d, 128, 192)
            P.op("sp", lambda e: e.dma_start(out=inv64[H64], in_=c_inv64), w=[B_g], dma=True)
            P.op("sp", lambda e: e.dma_start(out=rotT[H64], in_=c_rotT), w=[B_g], dma=True)
            P.op("sp", lambda e: e.dma_start(out=cm_f, in_=c_cmask), w=[B_g], dma=True)
            P.op("sp", lambda e: e.dma_start(out=posi[H64], in_=pos_d.partition_broadcast(64)), w=[B_tmp], dma=True)
            P.op("dve", lambda e: e.tensor_copy(out=cm_b, in_=cm_f), r=[B_g], w=[B_g])
            qs = 192.0 ** -0.5
            P.op("dve", lambda e: e.tensor_scalar(out=gq_n, in0=gq_n, scalar1=qs, scalar2=None, op0=ALU.mult),
                 r=[B_g], w=[B_g])
            P.op("dve", lambda e: e.tensor_scalar(out=gq_r[H64], in0=gq_r[H64], scalar1=qs, scalar2=None, op0=ALU.mult),
                 r=[B_g], w=[B_g])
            TWO_PI_INV = float(np.float32(1.0 / (2 * math.pi)))
            MAGIC = 12582912.0
            C1 = 6.28125
            C2 = float(2 * math.pi - 6.28125)
            PI_LO = 3.1415925
            P.op("dve", lambda e: e.tensor_copy(out=posf[H64], in_=posi[H64]), r=[B_tmp], w=[B_tmp])
            P.op("dve", lambda e: e.tensor_scalar(out=ang[H64], in0=posf[H64], scalar1=inv64[H64, 0:1], scalar2=None,
                                                  op0=ALU.mult), r=[B_tmp, B_g], w=[B_tmp])
            for which, dst in (("sin", St), ("cos", Ct)):
                if which == "cos":
                    P.op("dve", lambda e: e.tensor_scalar(out=ang[H64], in0=ang[H64], scalar1=float(math.pi / 2),
                                                          scalar2=None, op0=ALU.add), r=[B_tmp], w=[B_tmp])
                P.op("dve", lambda e: e.tensor_scalar(out=tq[H64], in0=ang[H64], scalar1=TWO_PI_INV, scalar2=MAGIC,
                                                      op0=ALU.mult, op1=ALU.add), r=[B_tmp], w=[B_tmp])
                P.op("dve", lambda e: e.tensor_scalar(out=nn[H64], in0=tq[H64], scalar1=-MAGIC, scalar2=None,
                                                      op0=ALU.add), r=[B_tmp], w=[B_tmp])
                P.op("dve", lambda e: e.scalar_tensor_tensor(out=tq[H64], in0=nn[H64], scalar=-C1, in1=ang[H64],
                                                             op0=ALU.mult, op1=ALU.add), r=[B_tmp], w=[B_tmp])
                P.op("dve", lambda e: e.scalar_tensor_tensor(out=tq[H64], in0=nn[H64], scalar=-C2, in1=tq[H64],
                                                             op0=ALU.mult, op1=ALU.add), r=[B_tmp], w=[B_tmp])
                P.op("dve", lambda e: e.tensor_scalar(out=tq[H64], in0=tq[H64], scalar1=-PI_LO, scalar2=PI_LO,
                                                      op0=ALU.max, op1=ALU.min), r=[B_tmp], w=[B_tmp])
                P.op("act", lambda e, dst=dst: e.activation(out=dst[H64], in_=tq[H64], func=AF.Sin),
                     r=[B_tmp], w=[B_tab])
            P.barrier()

            a = Alloc(ph_base)
            wq = [a([8, 192], BF16) for _ in range(2)]
            wkv = [a([4, 256], BF16) for _ in range(2)]
            qn = [a([T], BF16) for _ in range(2)]
            qr = [a([T], BF16) for _ in range(2)]
            kn = [a([T], BF16) for _ in range(2)]
            kr = [a([T], BF16) for _ in range(2)]
            vh = [a([16, 128], BF16) for _ in range(2)]
            gt = [a([T], BF16) for _ in range(2)]
            oh = [a([T], BF16) for _ in range(2)]
            B_wq = [Buf("wq0"), Buf("wq1")]
            B_wkv = [Buf("wkv0"), Buf("wkv1")]
            B_q = [Buf("q0"), Buf("q1")]
            B_k = [Buf("k0"), Buf("k1")]
            B_v = [Buf("v0"), Buf("v1")]
            B_gt = [Buf("gt0"), Buf("gt1")]
            B_oh = [Buf("oh0"), Buf("oh1")]
            NTMP = 2
            sq = [a([512], BF16) for _ in range(NTMP)]
            sqr = [a([512], BF16) for _ in range(NTMP)]
            rs = [a([512], F32) for _ in range(NTMP)]
            yv = [a([512], F32) for _ in range(NTMP)]
            t1 = [a([512], F32) for _ in range(NTMP)]
            t2 = [a([512], F32) for _ in range(NTMP)]
            B_sq = [Buf(f"sq{i}") for i in range(NTMP)]
            B_sqr = [Buf(f"sqr{i}") for i in range(NTMP)]
            B_rs = [Buf(f"rs{i}") for i in range(NTMP)]
            B_y = [Buf(f"y{i}") for i in range(NTMP)]
            B_t1 = [Buf(f"t1{i}") for i in range(NTMP)]
            B_t2 = [Buf(f"t2{i}") for i in range(NTMP)]
            NPT = 4
            pT = [a([512], BF16) for _ in range(NPT)]
            B_pT = [Buf(f"pT{i}") for i in range(NPT)]
            rden = [a([512], F32) for _ in range(2)]
            og = [a([512], F32) for _ in range(2)]
            B_rden = [Buf("rden0"), Buf("rden1")]
            B_og = [Buf("og0"), Buf("og1")]
            assert a.cur <= CB, a.cur
            BA, BB, BC, BD = 0, 1, 2, 3
            BS = (4, 5)
            BO, BDEN = 6, 7
            cnt = {"tmp": 0, "pt": 0, "s": 0, "fin": 0}

            def rstd_from(bankc, tsl, scale, parts=128):
                P.op("act", lambda e: e.activation(out=rs[tsl], in_=bank(bankc), func=AF.Sqrt, scale=scale,
                                                   bias=eps_t[:, 0:1]), r=[PSB[bankc], B_const], w=[B_rs[tsl]])
                P.op("dve", lambda e: e.reciprocal(out=rs[tsl], in_=rs[tsl]), r=[B_rs[tsl]], w=[B_rs[tsl]])

            for (latT, nk, gl, nfeat, li) in ((cqT, 8, gql, 1024.0, 0), (ckvT, 4, gkl, 512.0, 1)):
                for tb in range(4):
                    tsl = cnt["tmp"] % NTMP
                    cnt["tmp"] += 1
                    ts_ = slice(tb * 512, (tb + 1) * 512)
                    for k in range(nk):
                        eng = "dve" if k % 2 == 0 else "pool"
                        sl2 = k % NTMP
                        P.op(eng, lambda e, k=k, sl2=sl2, latT=latT, ts_=ts_: e.tensor_tensor(
                            out=sq[sl2], in0=latT[:, k, ts_], in1=latT[:, k, ts_], op=ALU.mult),
                             r=[B_lat[tb]], w=[B_sq[sl2]])
                        P.op("pe", lambda e, k=k, sl2=sl2, nk=nk: e.matmul(bank(BC), lhsT=ones_b, rhs=sq[sl2],
                                                                         start=(k == 0), stop=(k == nk - 1)),
                             r=[B_sq[sl2], B_const], w=[PSB[BC]])
                    rstd_from(BC, tsl, 1.0 / nfeat)
                    for k in range(nk):
                        P.op("dve", lambda e, k=k, latT=latT, ts_=ts_, gl=gl, tsl=tsl: e.scalar_tensor_tensor(
                            out=latT[:, k, ts_], in0=latT[:, k, ts_], scalar=gl[:, k:k + 1], in1=rs[tsl],
                            op0=ALU.mult, op1=ALU.mult), r=[B_rs[tsl], B_g, B_lat[tb]], w=[B_lat[tb]])

            P.op("pool", lambda e: e.tensor_tensor(out=krsq[H64], in0=krb[H64], in1=krb[H64], op=ALU.mult),
                 r=[B_kr], w=[B_kr])
            for tb in range(4):
                tsl = cnt["tmp"] % NTMP
                cnt["tmp"] += 1
                ts_ = slice(tb * 512, (tb + 1) * 512)
                P.op("dve", lambda e, ts_=ts_, tsl=tsl: e.tensor_scalar(out=yv[tsl][H64], in0=krb[H64, ts_],
                                                                        scalar1=gk_r[H64, 0:1], scalar2=None,
                                                                        op0=ALU.mult), r=[B_kr, B_g], w=[B_y[tsl]])
                P.op("pe", lambda e, tsl=tsl: e.matmul(bank(BD)[H64, :], lhsT=rotT[H64], rhs=yv[tsl][H64],
                                                       start=True, stop=True), r=[B_y[tsl], B_g], w=[PSB[BD]])
                P.op("pool", lambda e, ts_=ts_, tsl=tsl: e.tensor_tensor(out=t1[tsl][H64], in0=yv[tsl][H64],
                                                                         in1=Ct[H64, ts_], op=ALU.mult),
                     r=[B_y[tsl], B_tab], w=[B_t1[tsl]])
                P.op("dve", lambda e, ts_=ts_, tsl=tsl: e.tensor_tensor(out=t2[tsl][H64], in0=bank(BD)[H64, :],
                                                                        in1=St[H64, ts_], op=ALU.mult),
                     r=[PSB[BD], B_tab], w=[B_t2[tsl]])
                P.op("pool", lambda e, ts_=ts_, tsl=tsl: e.tensor_tensor(out=Rk[H64, ts_], in0=t1[tsl][H64],
                                                                         in1=t2[tsl][H64], op=ALU.add),
                     r=[B_t1[tsl], B_t2[tsl]], w=[B_tab])

            def q_stages(h, sl, tb):
                ts_ = slice(tb * 512, (tb + 1) * 512)
                tsl = cnt["tmp"] % NTMP
                cnt["tmp"] += 1

                def s1():
                    for k in range(8):
                        P.op("pe", lambda e, k=k: e.matmul(bank(BA), lhsT=wq[sl][:, k, 0:128], rhs=cqT[:, k, ts_],
                                                           start=(k == 0), stop=(k == 7)),
                             r=[B_wq[sl], B_lat[tb]], w=[PSB[BA]])
                    for k in range(8):
                        P.op("pe", lambda e, k=k: e.matmul(bank(BB)[H64, :], lhsT=wq[sl][:, k, 128:192],
                                                           rhs=cqT[:, k, ts_], start=(k == 0), stop=(k == 7)),
                             r=[B_wq[sl], B_lat[tb]], w=[PSB[BB]])

                def s2():
                    P.op("act", lambda e: e.activation(out=sq[tsl], in_=bank(BA), func=AF.Square),
                         r=[PSB[BA]], w=[B_sq[tsl]])
                    P.op("act", lambda e: e.activation(out=sqr[tsl][H64], in_=bank(BB)[H64, :], func=AF.Square),
                         r=[PSB[BB]], w=[B_sqr[tsl]])

                def s3():
                    P.op("pe", lambda e: e.matmul(bank(BC), lhsT=ones_b, rhs=sq[tsl], start=True, stop=False),
                         r=[B_sq[tsl], B_const], w=[PSB[BC]])
                    P.op("pe", lambda e: e.matmul(bank(BC), lhsT=ones_b[H64, :], rhs=sqr[tsl][H64], start=False,
                                                  stop=True), r=[B_sqr[tsl], B_const], w=[PSB[BC]])

                def s4():
                    rstd_from(BC, tsl, 1.0 / 192.0)
                    P.op("dve", lambda e: e.scalar_tensor_tensor(out=qn[sl][:, ts_], in0=bank(BA), scalar=gq_n[:, 0:1],
                                                                 in1=rs[tsl], op0=ALU.mult, op1=ALU.mult),
                         r=[PSB[BA], B_rs[tsl], B_g], w=[B_q[sl]])
                    P.op("dve", lambda e: e.tensor_scalar(out=yv[tsl][H64], in0=bank(BB)[H64, :], scalar1=gq_r[H64, 0:1],
                                                          scalar2=None, op0=ALU.mult), r=[PSB[BB], B_g], w=[B_y[tsl]])

                def s5():
                    P.op("pe", lambda e: e.matmul(bank(BD)[H64, :], lhsT=rotT[H64], rhs=yv[tsl][H64], start=True,
                                                  stop=True), r=[B_y[tsl], B_g], w=[PSB[BD]])

                def s6():
                    P.op("pool", lambda e: e.tensor_tensor(out=t1[tsl][H64], in0=yv[tsl][H64], in1=Ct[H64, ts_],
                                                           op=ALU.mult), r=[B_y[tsl], B_tab], w=[B_t1[tsl]])
                    P.op("dve", lambda e: e.tensor_tensor(out=t2[tsl][H64], in0=bank(BD)[H64, :], in1=St[H64, ts_],
                                                          op=ALU.mult), r=[PSB[BD], B_tab], w=[B_t2[tsl]])
                    P.op("pool", lambda e: e.tensor_tensor(out=t1[tsl][H64], in0=t1[tsl][H64], in1=t2[tsl][H64],
                                                           op=ALU.add), r=[B_t1[tsl], B_t2[tsl]], w=[B_t1[tsl]])
                    P.op("dve", lambda e: e.tensor_tensor(out=qr[sl][H64, ts_], in0=t1[tsl][H64], in1=rs[tsl][H64],
                                                          op=ALU.mult), r=[B_t1[tsl], B_rs[tsl]], w=[B_q[sl]])

                return [s1, s2, s3, s4, s5, s6]

            def k_stages(h, sl, tb):
                ts_ = slice(tb * 512, (tb + 1) * 512)
                tsl = cnt["tmp"] % NTMP
                cnt["tmp"] += 1

                def s1():
                    for k in range(4):
                        P.op("pe", lambda e, k=k: e.matmul(bank(BA), lhsT=wkv[sl][:, k, 0:128], rhs=ckvT[:, k, ts_],
                                                           start=(k == 0), stop=(k == 3)),
                             r=[B_wkv[sl], B_lat[tb]], w=[PSB[BA]])

                def s2():
                    P.op("act", lambda e: e.activation(out=sq[tsl], in_=bank(BA), func=AF.Square),
                         r=[PSB[BA]], w=[B_sq[tsl]])

                def s3():
                    P.op("pe", lambda e: e.matmul(bank(BC), lhsT=ones_b, rhs=sq[tsl], start=True, stop=False),
                         r=[B_sq[tsl], B_const], w=[PSB[BC]])
                    P.op("pe", lambda e: e.matmul(bank(BC), lhsT=ones_b[H64, :], rhs=krsq[H64, ts_], start=False,
                                                  stop=True), r=[B_kr, B_const], w=[PSB[BC]])

                def s4():
                    rstd_from(BC, tsl, 1.0 / 192.0)
                    P.op("dve", lambda e: e.scalar_tensor_tensor(out=kn[sl][:, ts_], in0=bank(BA), scalar=gk_n[:, 0:1],
                                                                 in1=rs[tsl], op0=ALU.mult, op1=ALU.mult),
                         r=[PSB[BA], B_rs[tsl], B_g], w=[B_k[sl]])
                    P.op("dve", lambda e: e.tensor_tensor(out=kr[sl][H64, ts_], in0=Rk[H64, ts_], in1=rs[tsl][H64],
                                                          op=ALU.mult), r=[B_tab, B_rs[tsl]], w=[B_k[sl]])

                return [s1, s2, s3, s4]

            def v_stages(sl, g4):
                def s1():
                    for i in range(4):
                        kt = g4 * 4 + i
                        for k in range(4):
                            P.op("pe", lambda e, k=k, i=i, kt=kt: e.matmul(
                                bank(BD)[:, i * 128:(i + 1) * 128], lhsT=ckvT[:, k, kt * 128:(kt + 1) * 128],
                                rhs=wkv[sl][:, k, 128:256], start=(k == 0), stop=(k == 3)),
                                 r=[B_wkv[sl], B_lat[kt // 4]], w=[PSB[BD]])

                def s2():
                    P.op("act", lambda e: e.activation(
                        out=vh[sl][:, g4 * 4:(g4 + 1) * 4, :], in_=bank(BD).rearrange("p (a b) -> p a b", a=4),
                        func=AF.Copy), r=[PSB[BD]], w=[B_v[sl]])

                return [s1, s2]

            def prep_stages(h, sl):
                def loads():
                    P.op("pool", lambda e: e.dma_start(out=wq[sl], in_=w_uq_d[:, h * 192:(h + 1) * 192].rearrange(
                        "(k p) c -> p k c", p=128)), w=[B_wq[sl]], dma=True)
                    P.op("pool", lambda e: e.dma_start(out=wkv[sl], in_=w_ukv_d[:, h * 256:(h + 1) * 256].rearrange(
                        "(k p) c -> p k c", p=128)), w=[B_wkv[sl]], dma=True)
                    P.op("sp", lambda e: e.dma_start(out=gt[sl], in_=projT[C_GA + h * 128:C_GA + (h + 1) * 128, :]),
                         w=[B_gt[sl]], dma=True)

                S = [loads]
                for tb in range(4):
                    S.extend(q_stages(h, sl, tb))
                    S.extend(k_stages(h, sl, tb))
                for g4 in range(4):
                    S.extend(v_stages(sl, g4))
                return S

            def attn_units_a(h, sl):
                U = []
                for qb in range(4):
                    nj = 4 * qb + 4
                    for j in range(nj):
                        U.append(make_unit_a(h, sl, qb, j, nj))
                return U

            def make_unit_a(h, sl, qb, j, nj):
                qs_ = qb * 512
                lo = max(0, 128 * j - 512 * qb)
                ks = slice(j * 128, (j + 1) * 128)
                qsl = slice(qs_ + lo, qs_ + 512)
                stt = {}

                def A():
                    bs = BS[cnt["s"] % 2]
                    cnt["s"] += 1
                    pi = cnt["pt"] % NPT
                    cnt["pt"] += 1
                    stt["pi"] = pi
                    P.op("pe", lambda e: e.matmul(bank(bs)[:, lo:512], lhsT=kn[sl][:, ks], rhs=qn[sl][:, qsl],
                                                  start=True, stop=False), r=[B_k[sl], B_q[sl]], w=[PSB[bs]])
                    P.op("pe", lambda e: e.matmul(bank(bs)[:, lo:512], lhsT=kr[sl][H64, ks], rhs=qr[sl][H64, qsl],
                                                  start=False, stop=True), r=[B_k[sl], B_q[sl]], w=[PSB[bs]])
                    P.op("act", lambda e: e.activation(out=pT[pi][:, lo:512], in_=bank(bs)[:, lo:512], func=AF.Exp),
                         r=[PSB[bs]], w=[B_pT[pi]])
                    if j >= 4 * qb:
                        jj = j - 4 * qb
                        m0 = 384 - 128 * jj + lo
                        P.op("pool", lambda e: e.tensor_tensor(out=pT[pi][:, lo:512], in0=pT[pi][:, lo:512],
                                                               in1=cm_b[:, m0:m0 + 512 - lo], op=ALU.mult),
                             r=[B_pT[pi], B_g], w=[B_pT[pi]])

                def Bf():
                    pi = stt["pi"]
                    P.op("pe", lambda e: e.matmul(bank(BO)[:, lo:512], lhsT=vh[sl][:, j, :], rhs=pT[pi][:, lo:512],
                                                  start=(j == 0), stop=(j == nj - 1)), r=[B_v[sl], B_pT[pi]],
                         w=[PSB[BO]])
                    P.op("pe", lambda e: e.matmul(bank(BDEN)[:, lo:512], lhsT=ones_b, rhs=pT[pi][:, lo:512],
                                                  start=(j == 0), stop=(j == nj - 1)), r=[B_const, B_pT[pi]],
                         w=[PSB[BDEN]])
                    if j == nj - 1:
                        fi = cnt["fin"] % 2
                        cnt["fin"] += 1
                        P.op("dve", lambda e: e.reciprocal(out=rden[fi], in_=bank(BDEN)), r=[PSB[BDEN]],
                             w=[B_rden[fi]])
                        P.op("pool", lambda e: e.tensor_tensor(out=og[fi], in0=rden[fi], in1=gt[sl][:, qs_:qs_ + 512],
                                                               op=ALU.mult), r=[B_rden[fi], B_gt[sl]], w=[B_og[fi]])
                        P.op("dve", lambda e: e.tensor_tensor(out=oh[sl][:, qs_:qs_ + 512], in0=bank(BO), in1=og[fi],
                                                              op=ALU.mult), r=[PSB[BO], B_og[fi]], w=[B_oh[sl]])
                        if qb == 3:
                            P.op("sp", lambda e: e.dma_start(out=oaT[h * 128:(h + 1) * 128, :], in_=oh[sl]),
                                 r=[B_oh[sl]], dma=True)

                return (A, Bf)

            import os as _os
            NH_A = int(_os.environ.get('NH_A', '16'))
            pend = [None]

            def run_unit(u):
                u[0]()
                if pend[0] is not None:
                    pend[0]()
                pend[0] = u[1]

            for st_ in prep_stages(0, 0):
                st_()
            for h in range(NH_A):
                U = attn_units_a(h, h % 2)
                Pn = prep_stages(h + 1, (h + 1) % 2) if h + 1 < NH_A else []
                kk = 0
                for idx, u in enumerate(U):
                    run_unit(u)
                    tgt = min(len(Pn), ((idx + 1) * len(Pn) + len(U) - 1) // len(U))
                    while kk < tgt:
                        Pn[kk]()
                        kk += 1
                while kk < len(Pn):
                    Pn[kk]()
                    kk += 1
            if pend[0] is not None:
                pend[0]()
            P.barrier()

        if stop_after >= 3:
            a = Alloc(0)
            kiT = a([T], BF16)
            qiT = a([16, 512], BF16)
            wi = a([4, 32], F32)
            wabs = a([4, 32], F32)
            wsgn = a([4, 32], F32)
            acc = [a([T], F32) for _ in range(2)]
            tmpr = [a([1024], F32) for _ in range(2)]
            junk = a([T], BF16)
            mrow = [a([T], BF16) for _ in range(2)]
            maskT = a([16, 512], BF16)
            qbT = a([16, 512], BF16)
            kbT = a([T], BF16)
            vb = a([16, 128], BF16)
            BT = a([16, 256], F32)
            b31 = a([16], F32)
            gqb = a([1], F32)
            gkb = a([1], F32)
            gtb = a([16, 512], BF16)
            ob = a([16, 512], BF16)
            W0 = a([1], F32)
            Wtab = a([NITER], F32)
            mid = a([1], F32)
            cntv = a([1], F32)
            sgn = a([1], F32)
            thr = a([1], F32)
            pow2 = a([NITER], F32)
            negm = a([128], F32)
            t5s = a([16], F32)
            ohs = a([384], F32)
            F16 = a([384], F32)
            sqb = [a([512], BF16) for _ in range(2)]
            rs3 = [a([512], F32) for _ in range(2)]
            NPT3 = 4
            pT3v = [a([512], BF16) for _ in range(NPT3)]
            tnear = [a([256], F32) for _ in range(2)]
            rden3v = [a([512], F32) for _ in range(2)]
            og3v = [a([512], F32) for _ in range(2)]
            assert a.cur <= CB, a.cur
            B_ki = Buf("kiT"); B_qi = Buf("qiT"); B_wi = Buf("wi")
            B_acc = [Buf("acc0"), Buf("acc1")]
            B_tmpr = [Buf("tmpr0"), Buf("tmpr1")]
            B_junk = Buf("junk")
            B_mrow = [Buf("mrow0"), Buf("mrow1")]
            B_maskT = Buf("maskT")
            B_qb = Buf("qbT"); B_kb = Buf("kbT"); B_vb = Buf("vb"); B_BT = Buf("BT"); B_g3 = Buf("g3")
            B_gtb = Buf("gtb"); B_ob = Buf("ob"); B_bis = Buf("bis"); B_t5 = Buf("t5")
            B_sqb = [Buf("sqb0"), Buf("sqb1")]
            B_rs3 = [Buf("rs30"), Buf("rs31")]
            B_pT3 = [Buf(f"pT3{i}") for i in range(NPT3)]
            B_tn = [Buf("tn0"), Buf("tn1")]
            B_rden3 = [Buf("rden30"), Buf("rden31")]
            B_og3 = [Buf("og30"), Buf("og31")]
            H32 = slice(0, 32)
            H16 = slice(0, 16)
            c3 = {"tmp": 0, "lg": 0, "mr": 0, "tp": 0, "s": 0, "pt": 0, "tn": 0, "fin": 0, "acc": 0}

            P.op("sp", lambda e: e.dma_start(out=kiT, in_=kidxT), w=[B_ki], dma=True)
            P.op("sp", lambda e: e.dma_start(out=kbT, in_=projT[C_KB:C_KB + 128, :]), w=[B_kb], dma=True)
            P.op("sp", lambda e: e.dma_start(out=vb, in_=vb_s.rearrange("(j p) d -> p j d", p=128)), w=[B_vb], dma=True)
            P.op("sp", lambda e: e.dma_start(out=gqb, in_=g_qn_b_d.rearrange("(p o) -> p o", o=1)), w=[B_g3], dma=True)
            P.op("sp", lambda e: e.dma_start(out=gkb, in_=g_kn_b_d.rearrange("(p o) -> p o", o=1)), w=[B_g3], dma=True)
            P.op("sp", lambda e: e.dma_start(out=pow2, in_=c_pow2), w=[B_g3], dma=True)
            P.op("sp", lambda e: e.dma_start(out=negm, in_=c_negmask), w=[B_g3], dma=True)
            P.op("sp", lambda e: e.dma_start(out=t5s[H32], in_=t5_d), w=[B_t5], dma=True)
            P.op("sp", lambda e: e.dma_start(out=ohs[H32], in_=c_onehot), w=[B_t5], dma=True)
            P.op("sp", lambda e: e.dma_start(out=b31, in_=t5_d[31, :].partition_broadcast(128)), w=[B_g3], dma=True)
            P.op("dve", lambda e: e.tensor_scalar(out=gqb, in0=gqb, scalar1=128.0 ** -0.5, scalar2=None, op0=ALU.mult),
                 r=[B_g3], w=[B_g3])
            P.op("pe", lambda e: e.matmul(bank(7)[H16, 0:384], lhsT=t5s[H32], rhs=ohs[H32], start=True, stop=True),
                 r=[B_t5], w=[PSB[7]])
            P.op("dve", lambda e: e.tensor_copy(out=F16[H16], in_=bank(7)[H16, 0:384]), r=[PSB[7]], w=[B_t5])
            B_Xb = Buf("Xb")
            P.op("sp", lambda e: e.dma_start(out=Xb.rearrange("h (r m) -> h r m", m=384),
                                             in_=F16[H16].unsqueeze(1).broadcast_to([16, 129, 384])),
                 r=[B_t5], w=[B_Xb], dma=True)
            P.op("sp", lambda e: e.dma_start(out=BT, in_=bass.AP(Xb.tensor, 0, [[383, 128], [129 * 384, 16], [1, 256]])),
                 r=[B_Xb], w=[B_BT], dma=True)

            def rstd3(bankc, tsl, scale):
                P.op("act", lambda e: e.activation(out=rs3[tsl], in_=bank(bankc), func=AF.Sqrt, scale=scale,
                                                   bias=eps_t[:, 0:1]), r=[PSB[bankc], B_const], w=[B_rs3[tsl]])
                P.op("dve", lambda e: e.reciprocal(out=rs3[tsl], in_=rs3[tsl]), r=[B_rs3[tsl]], w=[B_rs3[tsl]])

            def norm_block(src_ap, gvec_ap, Bsrc):
                tsl = c3["tmp"] % 2
                c3["tmp"] += 1
                P.op("act", lambda e: e.activation(out=sqb[tsl], in_=src_ap, func=AF.Square), r=[Bsrc],
                     w=[B_sqb[tsl]])
                P.op("pe", lambda e: e.matmul(bank(6), lhsT=ones_b, rhs=sqb[tsl], start=True, stop=True),
                     r=[B_sqb[tsl], B_const], w=[PSB[6]])
                rstd3(6, tsl, 1.0 / 128.0)
                P.op("dve", lambda e: e.scalar_tensor_tensor(out=src_ap, in0=src_ap, scalar=gvec_ap, in1=rs3[tsl],
                                                             op0=ALU.mult, op1=ALU.mult),
                     r=[Bsrc, B_rs3[tsl], B_g3], w=[Bsrc])

            for tb in range(4):
                norm_block(kbT[:, tb * 512:(tb + 1) * 512], gkb[:, 0:1], B_kb)

            def indexer_tile(qb, i):
                qt = qb * 4 + i
                nk = (qt + 1) * 128
                asl = c3["acc"] % 2
                c3["acc"] += 1
                accv = acc[asl]
                Bacc = B_acc[asl]
                qsl = slice(i * 128, (i + 1) * 128)

                def head_half(h, k0, wk):
                    c = h // 2
                    hp = slice((h % 2) * 64, (h % 2) * 64 + 64)
                    lg = c3["lg"] % 2
                    c3["lg"] += 1
                    pb_ = bank(2 * lg, 2)
                    for off in range(0, wk, 512):
                        w_ = min(512, wk - off)
                        P.op("pe", lambda e, off=off, w_=w_: e.matmul(
                            pb_[:, off:off + w_], lhsT=qiT[hp, c, qsl], rhs=kiT[hp, k0 + off:k0 + off + w_],
                            start=True, stop=True), r=[B_qi, B_ki], w=[PSB[2 * lg + off // 512]])
                    rb = [PSB[2 * lg + o // 512] for o in range(0, wk, 512)]
                    P.op("act", lambda e: e.activation(out=tmpr[lg][:, 0:wk], in_=pb_[:, 0:wk], func=AF.Relu,
                                                       scale=wabs[:, i, h:h + 1]), r=rb + [B_wi], w=[B_tmpr[lg]])
                    if h == 0:
                        P.op("dve", lambda e: e.tensor_scalar(out=accv[:, k0:k0 + wk], in0=tmpr[lg][:, 0:wk],
                                                              scalar1=wsgn[:, i, h:h + 1], scalar2=None, op0=ALU.mult),
                             r=[B_tmpr[lg], B_wi], w=[Bacc])
                    else:
                        P.op("dve", lambda e: e.scalar_tensor_tensor(
                            out=accv[:, k0:k0 + wk], in0=tmpr[lg][:, 0:wk], scalar=wsgn[:, i, h:h + 1],
                            in1=accv[:, k0:k0 + wk], op0=ALU.mult, op1=ALU.add),
                             r=[B_tmpr[lg], B_wi, Bacc], w=[Bacc])

                for k0 in range(0, nk, 1024):
                    wk = min(1024, nk - k0)
                    for h in range(32):
                        head_half(h, k0, wk)
                P.op("dve", lambda e: e.tensor_reduce(out=W0[:, 0:1], in_=accv[:, 0:nk], axis=AX.X, op=ALU.max,
                                                      apply_absolute_value=True), r=[Bacc], w=[B_bis])
                P.op("dve", lambda e: e.tensor_scalar(out=W0, in0=W0, scalar1=1.001, scalar2=1e-6, op0=ALU.mult,
                                                      op1=ALU.add), r=[B_bis], w=[B_bis])
                P.op("dve", lambda e: e.tensor_scalar(out=Wtab, in0=pow2, scalar1=W0[:, 0:1], scalar2=None,
                                                      op0=ALU.mult), r=[B_bis, B_g3], w=[B_bis])
                P.op("dve", lambda e: e.tensor_tensor(out=accv[:, qt * 128:nk], in0=accv[:, qt * 128:nk], in1=negm,
                                                      op=ALU.add), r=[Bacc, B_g3], w=[Bacc])
                if qt >= 2:
                    P.op("dve", lambda e: e.memset(mid, 0.0), r=[B_bis], w=[B_bis])

                    def one_iter(it):
                        P.op("dve", lambda e: e.tensor_scalar(out=junk[:, 0:nk], in0=accv[:, 0:nk], scalar1=mid[:, 0:1],
                                                              scalar2=0.0, op0=ALU.is_ge, op1=ALU.add,
                                                              accum_out=cntv[:, 0:1]),
                             r=[Bacc, B_bis], w=[B_junk, B_bis])
                        P.op("dve", lambda e: e.tensor_scalar(out=sgn, in0=cntv, scalar1=256.0, scalar2=-0.5,
                                                              op0=ALU.is_ge, op1=ALU.add), r=[B_bis], w=[B_bis])
                        P.op("dve", lambda e: e.scalar_tensor_tensor(out=mid, in0=sgn, scalar=Wtab[:, it:it + 1],
                                                                     in1=mid, op0=ALU.mult, op1=ALU.add),
                             r=[B_bis], w=[B_bis])

                    for it in range(NITER):
                        one_iter(it)
                    P.op("dve", lambda e: e.scalar_tensor_tensor(out=thr, in0=Wtab[:, NITER - 1:NITER], scalar=-0.5,
                                                                 in1=mid, op0=ALU.mult, op1=ALU.add),
                         r=[B_bis], w=[B_bis])
                else:
                    P.op("dve", lambda e: e.memset(thr, -1e29), r=[B_bis], w=[B_bis])
                ms = c3["mr"] % 2
                c3["mr"] += 1
                P.op("dve", lambda e: e.tensor_scalar(out=mrow[ms][:, 0:nk], in0=accv[:, 0:nk], scalar1=thr[:, 0:1],
                                                      scalar2=None, op0=ALU.is_ge), r=[Bacc, B_bis], w=[B_mrow[ms]])
                for j0 in range(0, qt + 1, 8):
                    n = min(8, qt + 1 - j0)
                    bk = 4 + c3["tp"] % 2
                    c3["tp"] += 1
                    for jj in range(n):
                        j = j0 + jj
                        P.op("pe", lambda e, jj=jj, j=j, bk=bk: e.transpose(
                            out=bank16(bk)[:, jj * 128:(jj + 1) * 128], in_=mrow[ms][:, j * 128:(j + 1) * 128],
                            identity=ident_b), r=[B_mrow[ms], B_const], w=[PSB[bk]])
                    P.op("act", lambda e, j0=j0, n=n, bk=bk: e.activation(
                        out=maskT[:, j0:j0 + n, qsl], in_=bank16(bk)[:, 0:n * 128].rearrange("p (a b) -> p a b", a=n),
                        func=AF.Copy), r=[PSB[bk]], w=[B_maskT])

            def make_unit_b(qb, h, j, nj):
                BS_ = (4, 5)
                BO_, BDEN_ = 6, 7
                lo = max(0, 128 * j - 512 * qb)
                c0 = 512 * qb + lo - 128 * j
                nn_ = max(0, min(256 - c0, 512 - lo))
                stt = {}

                def A():
                    bs = BS_[c3["s"] % 2]
                    c3["s"] += 1
                    pi = c3["pt"] % NPT3
                    c3["pt"] += 1
                    stt["pi"] = pi
                    P.op("pe", lambda e: e.matmul(bank(bs)[:, lo:512], lhsT=kbT[:, j * 128:(j + 1) * 128],
                                                  rhs=qbT[:, h, lo:512], start=True, stop=True),
                         r=[B_kb, B_qb], w=[PSB[bs]])
                    if nn_ > 0:
                        tn = c3["tn"] % 2
                        c3["tn"] += 1
                        P.op("dve", lambda e: e.tensor_tensor(out=tnear[tn][:, 0:nn_], in0=bank(bs)[:, lo:lo + nn_],
                                                              in1=BT[:, h, c0:c0 + nn_], op=ALU.add),
                             r=[PSB[bs], B_BT], w=[B_tn[tn]])
                        P.op("act", lambda e: e.activation(out=pT3v[pi][:, lo:lo + nn_], in_=tnear[tn][:, 0:nn_],
                                                           func=AF.Exp), r=[B_tn[tn]], w=[B_pT3[pi]])
                    if lo + nn_ < 512:
                        P.op("act", lambda e: e.activation(out=pT3v[pi][:, lo + nn_:512],
                                                           in_=bank(bs)[:, lo + nn_:512], func=AF.Exp,
                                                           bias=b31[:, h:h + 1]), r=[PSB[bs], B_g3], w=[B_pT3[pi]])
                    P.op("pool", lambda e: e.tensor_tensor(out=pT3v[pi][:, lo:512], in0=pT3v[pi][:, lo:512],
                                                           in1=maskT[:, j, lo:512], op=ALU.mult),
                         r=[B_pT3[pi], B_maskT], w=[B_pT3[pi]])

                def Bf():
                    pi = stt["pi"]
                    P.op("pe", lambda e: e.matmul(bank(BO_)[:, lo:512], lhsT=vb[:, j, :], rhs=pT3v[pi][:, lo:512],
                                                  start=(j == 0), stop=(j == nj - 1)), r=[B_vb, B_pT3[pi]],
                         w=[PSB[BO_]])
                    P.op("pe", lambda e: e.matmul(bank(BDEN_)[:, lo:512], lhsT=ones_b, rhs=pT3v[pi][:, lo:512],
                                                  start=(j == 0), stop=(j == nj - 1)), r=[B_const, B_pT3[pi]],
                         w=[PSB[BDEN_]])
                    if j == nj - 1:
                        fi = c3["fin"] % 2
                        c3["fin"] += 1
                        P.op("dve", lambda e: e.reciprocal(out=rden3v[fi], in_=bank(BDEN_)), r=[PSB[BDEN_]],
                             w=[B_rden3[fi]])
                        P.op("pool", lambda e: e.tensor_tensor(out=og3v[fi], in0=rden3v[fi], in1=gtb[:, h, :],
                                                               op=ALU.mult), r=[B_rden3[fi], B_gtb], w=[B_og3[fi]])
                        P.op("dve", lambda e: e.tensor_tensor(out=ob[:, h, :], in0=bank(BO_), in1=og3v[fi],
                                                              op=ALU.mult), r=[PSB[BO_], B_og3[fi]], w=[B_ob])

                return (A, Bf)

            pend3 = [None]

            def run_unit3(u):
                u[0]()
                if pend3[0] is not None:
                    pend3[0]()
                pend3[0] = u[1]

            def attn_b_all(qb):
                nj = 4 * qb + 4
                for h in range(16):
                    for j in range(nj):
                        run_unit3(make_unit_b(qb, h, j, nj))
                if pend3[0] is not None:
                    pend3[0]()
                    pend3[0] = None

            def do_qb(qb):
                qcs = slice(qb * 512, (qb + 1) * 512)
                P.op("sp", lambda e: e.dma_start(out=qiT, in_=projT[C_QI:C_QI + 2048, qcs].rearrange(
                    "(c p) t -> p c t", p=128)), w=[B_qi], dma=True)
                P.op("sp", lambda e: e.dma_start(out=qbT, in_=projT[C_QB:C_QB + 2048, qcs].rearrange(
                    "(c p) t -> p c t", p=128)), w=[B_qb], dma=True)
                P.op("sp", lambda e: e.dma_start(out=gtb, in_=projT[C_GB:C_GB + 2048, qcs].rearrange(
                    "(c p) t -> p c t", p=128)), w=[B_gtb], dma=True)
                P.op("sp", lambda e: e.dma_start(out=wi, in_=widx_s[qcs, :].rearrange("(i p) h -> p i h", p=128)),
                     w=[B_wi], dma=True)
                P.op("act", lambda e: e.activation(out=wabs, in_=wi, func=AF.Abs, scale=32.0 ** -0.5), r=[B_wi],
                     w=[B_wi])
                P.op("act", lambda e: e.activation(out=wsgn, in_=wi, func=AF.Sign), r=[B_wi], w=[B_wi])
                for h in range(16):
                    norm_block(qbT[:, h, :], gqb[:, 0:1], B_qb)
                for i in range(4):
                    indexer_tile(qb, i)
                attn_b_all(qb)
                P.op("sp", lambda e: e.dma_start(out=obT[:, qcs].rearrange("(h p) t -> p h t", p=128), in_=ob),
                     r=[B_ob], dma=True)

            NQB = 4
            for qb in range(NQB):
                do_qb(qb)
            P.barrier()

        if stop_after >= 4:
            a = Alloc(0)
            oaH = a([16, 1024], BF16)
            obH = a([16, 1024], BF16)
            s2_base = a.cur
            mT = a([32, 1024], BF16)
            pa = [a([16, 128], BF16) for _ in range(2)]
            pb = [a([16, 128], BF16) for _ in range(2)]
            sA = [a([1024], BF16) for _ in range(2)]
            sB = [a([1024], BF16) for _ in range(2)]
            u1 = [a([512], F32) for _ in range(2)]
            u2 = [a([512], F32) for _ in range(2)]
            xin = [a([512], F32) for _ in range(2)]
            oout = [a([512], F32) for _ in range(2)]
            assert a.cur <= CB, a.cur
            a2 = Alloc(0)
            wo = [a2([32, 512], BF16) for _ in range(2)]
            assert a2.cur <= s2_base
            B_oaH = Buf("oaH"); B_obH = Buf("obH"); B_mT = [Buf(f"mT{i}") for i in range(32)]
            B_pa = [Buf("pa0"), Buf("pa1")]; B_pb = [Buf("pb0"), Buf("pb1")]
            B_sA = [Buf("sA0"), Buf("sA1")]; B_sB = [Buf("sB0"), Buf("sB1")]
            B_u1 = [Buf("u10"), Buf("u11")]; B_u2 = [Buf("u20"), Buf("u21")]
            B_xin = [Buf("xin0"), Buf("xin1")]; B_oout = [Buf("oo0"), Buf("oo1")]
            B_wo = [Buf("wo0"), Buf("wo1")]
            c4 = {"u": 0, "b": 0, "x": 0}

            def merged_block(th, fb):
                sl = fb % 2
                tcs = slice(th * 1024, (th + 1) * 1024)
                P.op("pool", lambda e: e.dma_start(out=pa[sl], in_=p_a_d[:, fb * 128:(fb + 1) * 128].rearrange(
                    "(k p) c -> p k c", p=128)), w=[B_pa[sl]], dma=True)
                P.op("pool", lambda e: e.dma_start(out=pb[sl], in_=p_b_d[:, fb * 128:(fb + 1) * 128].rearrange(
                    "(k p) c -> p k c", p=128)), w=[B_pb[sl]], dma=True)
                P.op("sp", lambda e: e.dma_start(out=sA[sl], in_=projT[C_MA + fb * 128:C_MA + (fb + 1) * 128, tcs]),
                     w=[B_sA[sl]], dma=True)
                P.op("sp", lambda e: e.dma_start(out=sB[sl], in_=projT[C_MB + fb * 128:C_MB + (fb + 1) * 128, tcs]),
                     w=[B_sB[sl]], dma=True)

                def sub(tb2):
                    bsl = c4["b"] % 2
                    c4["b"] += 1
                    us = c4["u"] % 2
                    c4["u"] += 1
                    ts_ = slice(tb2 * 512, (tb2 + 1) * 512)
                    for k in range(16):
                        P.op("pe", lambda e, k=k: e.matmul(bank(bsl), lhsT=pa[sl][:, k, :], rhs=oaH[:, k, ts_],
                                                           start=(k == 0), stop=(k == 15)), r=[B_pa[sl], B_oaH],
                             w=[PSB[bsl]])
                    for k in range(16):
                        P.op("pe", lambda e, k=k: e.matmul(bank(2 + bsl), lhsT=pb[sl][:, k, :], rhs=obH[:, k, ts_],
                                                           start=(k == 0), stop=(k == 15)), r=[B_pb[sl], B_obH],
                             w=[PSB[2 + bsl]])
                    P.op("dve", lambda e: e.tensor_tensor(out=u1[us], in0=bank(bsl), in1=sA[sl][:, ts_], op=ALU.mult),
                         r=[PSB[bsl], B_sA[sl]], w=[B_u1[us]])
                    P.op("dve", lambda e: e.tensor_tensor(out=u2[us], in0=bank(2 + bsl), in1=sB[sl][:, ts_],
                                                          op=ALU.mult), r=[PSB[2 + bsl], B_sB[sl]], w=[B_u2[us]])
                    P.op("pool", lambda e: e.tensor_tensor(out=mT[:, fb, ts_], in0=u1[us], in1=u2[us], op=ALU.add),
                         r=[B_u1[us], B_u2[us]], w=[B_mT[fb]])

                for tb2 in range(2):
                    sub(tb2)

            def final_block(th, cbk):
                sl = cbk % 2
                ccs = slice(cbk * 512, (cbk + 1) * 512)
                P.op("pool", lambda e: e.dma_start(out=wo[sl], in_=w_o_d[:, ccs].rearrange("(f p) c -> p f c", p=128)),
                     w=[B_wo[sl]], dma=True)

                def tok(tt):
                    bk = 4 + c4["b"] % 4
                    c4["b"] += 1
                    xs = c4["x"] % 2
                    c4["x"] += 1
                    r0 = th * 1024 + tt * 128
                    P.op("sp", lambda e: e.dma_start(out=xin[xs], in_=x_d[r0:r0 + 128, ccs]), w=[B_xin[xs]], dma=True)
                    for f in range(32):
                        P.op("pe", lambda e, f=f: e.matmul(bank(bk), lhsT=mT[:, f, tt * 128:(tt + 1) * 128],
                                                           rhs=wo[sl][:, f, :], start=(f == 0), stop=(f == 31)),
                             r=[B_wo[sl], B_mT[f]], w=[PSB[bk]])
                    P.op("dve", lambda e: e.tensor_tensor(out=oout[xs], in0=bank(bk), in1=xin[xs], op=ALU.add),
                         r=[PSB[bk], B_xin[xs]], w=[B_oout[xs]])
                    P.op("sp", lambda e: e.dma_start(out=out_d[r0:r0 + 128, ccs], in_=oout[xs]), r=[B_oout[xs]],
                         dma=True)

                for tt in range(8):
                    tok(tt)

            for th in range(2):
                tcs = slice(th * 1024, (th + 1) * 1024)
                P.op("sp", lambda e, tcs=tcs: e.dma_start(out=oaH, in_=oaT[:, tcs].rearrange("(k p) t -> p k t", p=128)),
                     w=[B_oaH], dma=True)
                P.op("sp", lambda e, tcs=tcs: e.dma_start(out=obH, in_=obT[:, tcs].rearrange("(k p) t -> p k t", p=128)),
                     w=[B_obH], dma=True)
                for fb in range(32):
                    merged_block(th, fb)
                P.barrier()
                for cbk in range(8):
                    final_block(th, cbk)
                P.barrier()

        P.finalize()
        keys = P.sem_keys()
        sems = {}
        for i, k in enumerate(keys):
            sems[k] = st.enter_context(nc.semaphore(f"s{i}"))
        block = st.enter_context(nc.Block())

        @block.tensor
        def _(e):
            P.emit_engine("pe", e, sems)

        @block.scalar
        def _(e):
            P.emit_engine("act", e, sems)

        @block.vector
        def _(e):
            P.emit_engine("dve", e, sems)

        @block.gpsimd
        def _(e):
            P.emit_engine("pool", e, sems)

        @block.sync
        def _(e):
            P.emit_engine("sp", e, sems, final_wait=True)
    return nc


def _in_maps(inputs):
    maps = []
    shared = {k: np.ascontiguousarray(np.asarray(v, dtype=np.float32)) for k, v in inputs.items()
              if k not in ("x", "positions")}
    for k, v in CONSTS.items():
        shared["c_" + k] = v
    x = np.asarray(inputs["x"], dtype=np.float32)
    pos = np.asarray(inputs["positions"], dtype=np.int32)
    for b in range(x.shape[0]):
        m = dict(shared)
        m["x"] = np.ascontiguousarray(x[b])
        m["positions"] = np.ascontiguousarray(pos[b])
        maps.append(m)
    return maps


_NC_CACHE = {}


def kernel(**inputs):
    if "nc" not in _NC_CACHE:
        _NC_CACHE["nc"] = build_nc()
    nc = _NC_CACHE["nc"]
    maps = _in_maps(inputs)
    res = run_bass_kernel_spmd(nc, maps, core_ids=list(range(len(maps))))
    return np.stack([r["out"] for r in res.results], axis=0).astype(np.float32)
```

```python
import math
import numpy as np
import concourse.bass as bass
import concourse.mybir as mybir
from concourse.bass_utils import run_bass_kernel_spmd

F32 = mybir.dt.float32
BF16 = mybir.dt.bfloat16
I32 = mybir.dt.int32
U8 = mybir.dt.uint8
AF = mybir.ActivationFunctionType
ALU = mybir.AluOpType
AX = mybir.AxisListType

D = 4096
T = 2048
NT = 16
IN_COLS = 18336
EPS = 1e-6
C_CQ, C_CKV, C_KR, C_QB, C_KB, C_VB, C_QI, C_KI, C_WI, C_GA, C_GB, C_MA, C_MB = (
    0, 1024, 1536, 1600, 3648, 3776, 3904, 5952, 6016, 6048, 8096, 10144, 14240)
NITER = 20
SZ = {F32: 4, BF16: 2, I32: 4, U8: 1}


class Buf:
    __slots__ = ("name", "w", "r")

    def __init__(self, name):
        self.name = name
        self.w = None
        self.r = {}


class Op:
    __slots__ = ("eng", "fn", "deps", "flag", "dma", "tok", "n")

    def __init__(self, eng, fn, dma):
        self.eng = eng
        self.fn = fn
        self.deps = set()
        self.flag = False
        self.dma = dma
        self.tok = None
        self.n = None


class Prog:
    ENG = ("pe", "act", "dve", "pool", "sp")
    NDSEM = 8
    CH = 30000

    def __init__(self):
        self.ops = []
        self.barrier_deps = set()
        self.last = {}
        self.dma_hist = {"sp": [], "pool": [], "act": []}

    def op(self, eng, fn, r=(), w=(), dma=False):
        idx = len(self.ops)
        o = Op(eng, fn, dma)
        deps = set(self.barrier_deps)
        for b in r:
            if b.w is not None:
                deps.add(b.w)
        for b in w:
            if b.w is not None:
                deps.add(b.w)
            deps.update(b.r.values())
        key = ("dma", eng, idx) if dma else eng
        for b in r:
            b.r[key if not dma else ("dma", idx)] = idx
        for b in w:
            b.w = idx
            b.r = {}
        if eng == "pe" and not dma:
            deps = {d for d in deps if not (self.ops[d].eng == "pe" and not self.ops[d].dma)}
        o.deps = deps
        self.ops.append(o)
        if dma:
            self.dma_hist[eng].append(idx)
        else:
            self.last[eng] = idx
        return idx

    def barrier(self):
        deps = set(self.last.values())
        for q, h in self.dma_hist.items():
            deps.update(h[-self.NDSEM:])
        self.barrier_deps = deps

    def finalize(self):
        for o in self.ops:
            for d in o.deps:
                self.ops[d].flag = True
        cnt = {e: 0 for e in self.ENG}
        dcnt = {"sp": 0, "pool": 0, "act": 0}
        for o in self.ops:
            if o.dma:
                o.n = dcnt[o.eng]
                dcnt[o.eng] += 1
                o.tok = (("d", o.eng, o.n % self.NDSEM), 16 * (o.n // self.NDSEM + 1))
            elif o.flag:
                k = cnt[o.eng]
                cnt[o.eng] += 1
                o.tok = (("c", o.eng, k // self.CH), k % self.CH + 1)
        self.cnt = cnt
        self.dcnt = dcnt

    def sem_keys(self):
        keys = []
        for e in self.ENG:
            for i in range(max(1, (self.cnt[e] + self.CH - 1) // self.CH)):
                keys.append(("c", e, i))
        for q in ("sp", "pool", "act"):
            if self.dcnt[q]:
                for i in range(self.NDSEM):
                    keys.append(("d", q, i))
        return keys

    def emit_engine(self, ename, eng, sems, final_wait=False):
        known = {}
        for o in self.ops:
            if o.eng != ename:
                continue
            waits = {}
            for d in o.deps:
                k, v = self.ops[d].tok
                if waits.get(k, 0) < v:
                    waits[k] = v
            if o.dma and o.n >= self.NDSEM:
                k = ("d", o.eng, o.n % self.NDSEM)
                v = 16 * (o.n // self.NDSEM)
                if waits.get(k, 0) < v:
                    waits[k] = v
            for k, v in waits.items():
                if known.get(k, 0) < v:
                    eng.wait_ge(sems[k], v)
                    known[k] = v
            ins = o.fn(eng)
            if o.dma:
                ins.then_inc(sems[o.tok[0]], 16)
            elif o.flag:
                ins.then_inc(sems[o.tok[0]], 1)
        if final_wait:
            for q in ("sp", "pool", "act"):
                n = self.dcnt[q]
                for i in range(min(n, self.NDSEM)):
                    tot = (n - i + self.NDSEM - 1) // self.NDSEM
                    k = ("d", q, i)
                    if known.get(k, 0) < 16 * tot:
                        eng.wait_ge(sems[k], 16 * tot)


def _t5_bucket_np(d):
    n = np.maximum(d, 0)
    nf = np.maximum(n, 1).astype(np.float32)
    large = 16 + (np.log(nf / np.float32(16)) / np.float32(math.log(128 / 16)) * np.float32(16)).astype(np.int32)
    large = np.minimum(large, 31)
    return np.where(n < 16, n, large)


def _consts():
    c = {}
    c["ident"] = np.eye(128, dtype=np.float32)
    m = np.arange(384)
    bucket = _t5_bucket_np(np.where(m < 256, m, 0))
    c["onehot"] = (bucket[None, :] == np.arange(32)[:, None]).astype(np.float32)
    half = 32
    inv = (np.float32(10000.0) ** (-np.arange(half, dtype=np.float32) / np.float32(half))).astype(np.float32)
    c["inv64"] = np.concatenate([inv, inv]).reshape(64, 1).astype(np.float32)
    P = np.zeros((64, 64), np.float32)
    for i in range(32):
        P[i, i + 32] = -1.0
        P[i + 32, i] = 1.0
    c["rotT"] = np.ascontiguousarray(P.T)
    s = np.arange(128)[:, None]
    cc = np.arange(896)[None, :]
    c["cmask"] = (cc - 384 >= s).astype(np.float32)
    c["negmask"] = np.where(np.arange(128)[None, :] <= np.arange(128)[:, None], 0.0, -1e30).astype(np.float32)
    c["pow2"] = np.tile((0.5 ** np.arange(NITER)).astype(np.float32)[None, :], (128, 1))
    return c


CONSTS = _consts()


def build_nc(stop_after=99, debug=False):
    nc = bass.Bass("TRN2", target_bir_lowering=False)
    P = Prog()

    def din(name, shape, dt=F32):
        return nc.dram_tensor(name, list(shape), dt, kind="ExternalInput").ap()

    x_d = din("x", [T, D])
    pos_d = din("positions", [T], I32)
    g_pre_d = din("g_pre", [D])
    w_in_d = din("w_in", [D, IN_COLS])
    g_q_lat_d = din("g_q_lat", [1024])
    g_kv_lat_d = din("g_kv_lat", [512])
    w_uq_d = din("w_uq", [1024, 3072])
    w_ukv_d = din("w_ukv", [512, 4096])
    g_qn_a_d = din("g_qn_a", [192])
    g_kn_a_d = din("g_kn_a", [192])
    g_qn_b_d = din("g_qn_b", [128])
    g_kn_b_d = din("g_kn_b", [128])
    t5_d = din("t5_bias", [32, 16])
    p_a_d = din("p_a", [2048, D])
    p_b_d = din("p_b", [2048, D])
    w_o_d = din("w_o", [D, D])
    c_ident = din("c_ident", [128, 128])
    c_onehot = din("c_onehot", [32, 384])
    c_inv64 = din("c_inv64", [64, 1])
    c_rotT = din("c_rotT", [64, 64])
    c_cmask = din("c_cmask", [128, 896])
    c_negmask = din("c_negmask", [128, 128])
    c_pow2 = din("c_pow2", [128, NITER])

    out_d = nc.dram_tensor("out", [T, D], F32, kind="ExternalOutput").ap()

    def dscr(name, shape, dt):
        kind = "ExternalOutput" if debug else "Internal"
        return nc.dram_tensor(name, list(shape), dt, kind=kind).ap()

    projT = dscr("projT", [IN_COLS, T], BF16)
    kidxT = dscr("kidxT", [128, T], BF16)
    vb_s = dscr("vb_s", [T, 128], BF16)
    widx_s = dscr("widx_s", [T, 32], F32)
    oaT = dscr("oaT", [2048, T], BF16)
    obT = dscr("obT", [2048, T], BF16)
    Xb = dscr("Xb", [16, 129 * 384], F32)

    import contextlib
    st = contextlib.ExitStack()
    with st:
        arena = st.enter_context(nc.sbuf_tensor("arena", [128, 200 * 1024], U8))
        psum = st.enter_context(nc.psum_tensor("psum", [128, 4096], F32))

        def sb(off, shape, dt, parts=128, p0=0):
            n = int(np.prod(shape)) * SZ[dt]
            assert off + n <= 200 * 1024, (off, n)
            ap = arena[p0:p0 + parts, off:off + n].bitcast(dt)
            if len(shape) == 2:
                ap = ap.rearrange("p (a b) -> p a b", a=shape[0])
            elif len(shape) == 3:
                ap = ap.rearrange("p (a b c) -> p a b c", a=shape[0], b=shape[1])
            return ap

        def bank(i, n=1):
            return psum[:, i * 512:(i + n) * 512]

        def bank16(i):
            return psum[:, i * 512:(i + 1) * 512].bitcast(BF16)

        PSB = [Buf(f"ps{i}") for i in range(8)]

        class Alloc:
            def __init__(self, base=0):
                self.cur = base

            def __call__(self, shape, dt, parts=128):
                n = int(np.prod(shape)) * SZ[dt]
                off = self.cur
                self.cur = (off + n + 63) // 64 * 64
                return sb(off, shape, dt, parts)

        def dump(name, ap, reads, parts=128):
            if not debug:
                return
            shp = [parts] + list(ap.shape[1:])
            dt_ = nc.dram_tensor("dbg_" + name, shp, ap.dtype, kind="ExternalOutput").ap()
            P.op("sp", lambda e: e.dma_start(out=dt_, in_=ap), r=reads, dma=True)

        CB = 192 * 1024
        ca = Alloc(CB)
        ident_f = ca([128], F32)
        ident_b = ca([128], BF16)
        ones_b = ca([128], BF16)
        eps_t = ca([4], F32)
        negI_b = ca([128], BF16)
        B_const = Buf("consts")
        P.op("sp", lambda e: e.dma_start(out=ident_f, in_=c_ident), w=[B_const], dma=True)
        P.op("dve", lambda e: e.tensor_copy(out=ident_b, in_=ident_f), r=[B_const], w=[B_const])
        P.op("dve", lambda e: e.memset(ones_b, 1.0), w=[B_const])
        P.op("dve", lambda e: e.memset(eps_t, EPS), w=[B_const])
        P.op("dve", lambda e: e.tensor_scalar(out=negI_b, in0=ident_f, scalar1=-30000.0, scalar2=None, op0=ALU.mult),
             r=[B_const], w=[B_const])
        assert ca.cur <= 200 * 1024

        a = Alloc(0)
        hnT = a([32, T], BF16)
        B_hnT = [Buf(f"hnT{i}") for i in range(4)]
        p1_base = a.cur
        xt = [a([D], F32) for _ in range(2)]
        B_xt = [Buf("xt0"), Buf("xt1")]
        gbc = a([D], F32)
        B_gbc = Buf("gbc")
        hn_tm = a([D], BF16)
        B_hn = Buf("hn_tm")
        ss = a([NT], F32)
        rstd0 = a([NT], F32)
        B_ss = [Buf(f"ss{i}") for i in range(NT)]
        assert a.cur <= CB, a.cur

        P.op("sp", lambda e: e.dma_start(out=gbc, in_=g_pre_d.partition_broadcast(128)), w=[B_gbc], dma=True)
        for tt in range(NT):
            sl = tt % 2
            P.op("sp", lambda e, tt=tt, sl=sl: e.dma_start(out=xt[sl], in_=x_d[tt * 128:(tt + 1) * 128, :]),
                 w=[B_xt[sl]], dma=True)
            P.op("dve", lambda e, tt=tt, sl=sl: e.scalar_tensor_tensor(
                out=hn_tm, in0=xt[sl], scalar=1.0, in1=xt[sl], op0=ALU.mult, op1=ALU.mult,
                accum_out=ss[:, tt:tt + 1]), r=[B_xt[sl]], w=[B_hn, B_ss[tt]])
            P.op("act", lambda e, tt=tt: e.activation(out=rstd0[:, tt:tt + 1], in_=ss[:, tt:tt + 1], func=AF.Sqrt,
                                                      scale=1.0 / D, bias=eps_t[:, 0:1]),
                 r=[B_ss[tt], B_const], w=[B_ss[tt]])
            P.op("dve", lambda e, tt=tt: e.reciprocal(out=rstd0[:, tt:tt + 1], in_=rstd0[:, tt:tt + 1]),
                 r=[B_ss[tt]], w=[B_ss[tt]])
            P.op("dve", lambda e, tt=tt, sl=sl: e.scalar_tensor_tensor(
                out=hn_tm, in0=xt[sl], scalar=rstd0[:, tt:tt + 1], in1=gbc, op0=ALU.mult, op1=ALU.mult),
                 r=[B_xt[sl], B_ss[tt], B_gbc], w=[B_hn])
            for q in range(4):
                bk = (tt * 4 + q) % 8
                for j in range(8):
                    c = q * 8 + j
                    P.op("pe", lambda e, bk=bk, j=j, c=c: e.transpose(
                        out=bank16(bk)[:, j * 128:(j + 1) * 128], in_=hn_tm[:, c * 128:(c + 1) * 128],
                        identity=ident_b), r=[B_hn, B_const], w=[PSB[bk]])
                P.op("act", lambda e, bk=bk, q=q, tt=tt: e.activation(
                    out=hnT[:, q * 8:(q + 1) * 8, tt * 128:(tt + 1) * 128],
                    in_=bank16(bk).rearrange("p (a b) -> p a b", a=8), func=AF.Copy),
                     r=[PSB[bk]], w=[B_hnT[tt // 4]])
        P.barrier()

        a = Alloc(p1_base)
        WB = [a([32, 256], BF16) for _ in range(2)]
        B_WB = [Buf("WB0"), Buf("WB1")]
        SG = [a([T], BF16) for _ in range(2)]
        B_SG = [Buf("SG0"), Buf("SG1")]
        WT = a([32, 160], BF16)
        B_WT = Buf("WT")
        SGv = [a([128], BF16) for _ in range(2)]
        SGw = [a([32], F32) for _ in range(2)]
        B_SGv = [Buf("SGv0"), Buf("SGv1")]
        assert a.cur <= CB, a.cur

        groups = []

        def add_range(c0, c1, act):
            c = c0
            while c < c1:
                wdt = min(256, c1 - c)
                blks = []
                for o in range(0, wdt, 128):
                    blks.append((o, min(128, wdt - o), "proj", c + o, act))
                groups.append((c, wdt, blks))
                c += wdt

        add_range(C_CQ, C_KR, AF.Copy)
        groups.append((C_KR, 64, [(0, 64, "proj", C_KR, AF.Copy)]))
        add_range(C_QB, C_VB, AF.Copy)
        add_range(C_QI, C_KI, AF.Copy)
        groups.append((C_KI, 64, [(0, 128, "kidx", 0, AF.Copy)]))
        add_range(C_GA, C_MA, AF.Silu)
        add_range(C_MA, IN_COLS, AF.Sigmoid)

        def w_src(c0, wdt):
            return w_in_d[:, c0:c0 + wdt].rearrange("(k p) c -> p k c", p=128)

        def issue_load(gi):
            c0, wdt, blks = groups[gi]
            sl = gi % 2
            if blks[0][2] == "kidx":
                P.op("pool", lambda e: e.dma_start(out=WB[sl][:, :, 0:64], in_=w_src(c0, 64)), w=[B_WB[sl]], dma=True)
                P.op("pool", lambda e: e.dma_start(out=WB[sl][:, :, 64:128], in_=w_src(c0, 64)), w=[B_WB[sl]],
                     dma=True)
            else:
                P.op("pool", lambda e: e.dma_start(out=WB[sl][:, :, 0:wdt], in_=w_src(c0, wdt)), w=[B_WB[sl]],
                     dma=True)

        P.op("pool", lambda e: e.dma_start(out=WT[:, :, 0:128], in_=w_src(C_VB, 128)), w=[B_WT], dma=True)
        P.op("pool", lambda e: e.dma_start(out=WT[:, :, 128:160], in_=w_src(C_WI, 32)), w=[B_WT], dma=True)
        issue_load(0)
        issue_load(1)
        nblk = 0
        ngroups = len(groups) if stop_after >= 1 else 0
        for gi in range(ngroups):
            c0, wdt, blks = groups[gi]
            sl = gi % 2
            for (o, M, kind, row, act) in blks:
                bs = (nblk % 2) * 4
                sg = nblk % 2
                nblk += 1
                for k in range(32):
                    for tb in range(4):
                        P.op("pe", lambda e, k=k, tb=tb, bs=bs, o=o, M=M, sl=sl: e.matmul(
                            bank(bs + tb)[0:M, :], lhsT=WB[sl][:, k, o:o + M], rhs=hnT[:, k, tb * 512:(tb + 1) * 512],
                            start=(k == 0), stop=(k == 31)), r=[B_WB[sl], B_hnT[tb]], w=[PSB[bs + tb]])
                for tb in range(4):
                    P.op("act", lambda e, tb=tb, bs=bs, M=M, sg=sg, act=act: e.activation(
                        out=SG[sg][0:M, tb * 512:(tb + 1) * 512], in_=bank(bs + tb)[0:M, :], func=act),
                         r=[PSB[bs + tb]], w=[B_SG[sg]])
                if kind == "proj":
                    P.op("sp", lambda e, sg=sg, M=M, row=row: e.dma_start(out=projT[row:row + M, :], in_=SG[sg][0:M, :]),
                         r=[B_SG[sg]], dma=True)
                else:
                    P.op("sp", lambda e, sg=sg: e.dma_start(out=kidxT, in_=SG[sg]), r=[B_SG[sg]], dma=True)
            if gi + 2 < ngroups:
                issue_load(gi + 2)
        if stop_after >= 1:
            for tt in range(NT):
                bk = tt % 2
                sg = tt % 2
                for k in range(32):
                    P.op("pe", lambda e, k=k, tt=tt, bk=bk: e.matmul(
                        bank(bk)[:, 0:160], lhsT=hnT[:, k, tt * 128:(tt + 1) * 128], rhs=WT[:, k, :],
                        start=(k == 0), stop=(k == 31)), r=[B_WT, B_hnT[tt // 4]], w=[PSB[bk]])
                P.op("act", lambda e, bk=bk, sg=sg: e.activation(out=SGv[sg], in_=bank(bk)[:, 0:128], func=AF.Copy),
                     r=[PSB[bk]], w=[B_SGv[sg]])
                P.op("act", lambda e, bk=bk, sg=sg: e.activation(out=SGw[sg], in_=bank(bk)[:, 128:160], func=AF.Copy),
                     r=[PSB[bk]], w=[B_SGv[sg]])
                P.op("sp", lambda e, sg=sg, tt=tt: e.dma_start(out=vb_s[tt * 128:(tt + 1) * 128, :], in_=SGv[sg]),
                     r=[B_SGv[sg]], dma=True)
                P.op("sp", lambda e, sg=sg, tt=tt: e.dma_start(out=widx_s[tt * 128:(tt + 1) * 128, :], in_=SGw[sg]),
                     r=[B_SGv[sg]], dma=True)
        P.barrier()


        if stop_after >= 2:
            a = Alloc(0)
            cqT = a([8, T], BF16)
            ckvT = a([4, T], BF16)
            krb = a([T], BF16)
            Ct = a([T], F32)
            St = a([T], F32)
            Rk = a([T], F32)
            krsq = a([T], BF16)
            gql = a([8], F32)
            gkl = a([4], F32)
            gq_n = a([1], F32)
            gq_r = a([1], F32)
            gk_n = a([1], F32)
            gk_r = a([1], F32)
            inv64 = a([1], F32)
            rotT = a([64], F32)
            cm_f = a([896], F32)
            cm_b = a([896], BF16)
            B_lat = [Buf(f"lat{i}") for i in range(4)]
            B_kr = Buf("kr")
            B_g = Buf("g2")
            B_tab = Buf("tab")
            ph_base = a.cur
            posi = a([T], I32)
            posf = a([T], F32)
            ang = a([T], F32)
            tq = a([T], F32)
            nn = a([T], F32)
            B_tmp = Buf("tmp2")
            assert a.cur <= CB, a.cur
            H64 = slice(0, 64)

            def gvec(dst, src, n0, n1, eng="sp"):
                P.op(eng, lambda e: e.dma_start(out=dst, in_=src[n0:n1].rearrange("(p o) -> p o", o=1)), w=[B_g],
                     dma=True)

            P.op("sp", lambda e: e.dma_start(out=cqT, in_=projT[0:1024, :].rearrange("(k p) t -> p k t", p=128)),
                 w=B_lat, dma=True)
            P.op("sp", lambda e: e.dma_start(out=ckvT, in_=projT[1024:1536, :].rearrange("(k p) t -> p k t", p=128)),
                 w=B_lat, dma=True)
            P.op("sp", lambda e: e.dma_start(out=krb[H64], in_=projT[C_KR:C_KR + 64, :]), w=[B_kr], dma=True)
            P.op("sp", lambda e: e.dma_start(out=gql, in_=g_q_lat_d.rearrange("(k p) -> p k", p=128),
                                            allow_slow_non_contiguous=True), w=[B_g], dma=True)
            P.op("sp", lambda e: e.dma_start(out=gkl, in_=g_kv_lat_d.rearrange("(k p) -> p k", p=128),
                                            allow_slow_non_contiguous=True), w=[B_g], dma=True)
            gvec(gq_n, g_qn_a_d, 0, 128)
            gvec(gq_r[H64], g_qn_a_d, 128, 192)
            gvec(gk_n, g_kn_a_d, 0, 128)
            gvec(gk_r[H64], g_kn_a_d, 128, 192)
            P.op("sp", lambda e: e.dma_start(out=inv64[H64], in_=c_inv64), w=[B_g], dma=True)
            P.op("sp", lambda e: e.dma_start(out=rotT[H64], in_=c_rotT), w=[B_g], dma=True)
            P.op("sp", lambda e: e.dma_start(out=cm_f, in_=c_cmask), w=[B_g], dma=True)
            P.op("sp", lambda e: e.dma_start(out=posi[H64], in_=pos_d.partition_broadcast(64)), w=[B_tmp], dma=True)
            P.op("dve", lambda e: e.tensor_scalar(out=cm_b, in0=cm_f, scalar1=-1.0, scalar2=1.0, op0=ALU.mult,
                                                  op1=ALU.add), r=[B_g], w=[B_g])
            qs = 192.0 ** -0.5
            P.op("dve", lambda e: e.tensor_scalar(out=gq_n, in0=gq_n, scalar1=qs, scalar2=None, op0=ALU.mult),
                 r=[B_g], w=[B_g])
            P.op("dve", lambda e: e.tensor_scalar(out=gq_r[H64], in0=gq_r[H64], scalar1=qs, scalar2=None, op0=ALU.mult),
                 r=[B_g], w=[B_g])
            TWO_PI_INV = float(np.float32(1.0 / (2 * math.pi)))
            MAGIC = 12582912.0
            C1 = 6.28125
            C2 = float(2 * math.pi - 6.28125)
            PI_LO = 3.1415925
            P.op("dve", lambda e: e.tensor_copy(out=posf[H64], in_=posi[H64]), r=[B_tmp], w=[B_tmp])
            P.op("dve", lambda e: e.tensor_scalar(out=ang[H64], in0=posf[H64], scalar1=inv64[H64, 0:1], scalar2=None,
                                                  op0=ALU.mult), r=[B_tmp, B_g], w=[B_tmp])
            for which, dst in (("sin", St), ("cos", Ct)):
                if which == "cos":
                    P.op("dve", lambda e: e.tensor_scalar(out=ang[H64], in0=ang[H64], scalar1=float(math.pi / 2),
                                                          scalar2=None, op0=ALU.add), r=[B_tmp], w=[B_tmp])
                P.op("dve", lambda e: e.tensor_scalar(out=tq[H64], in0=ang[H64], scalar1=TWO_PI_INV, scalar2=MAGIC,
                                                      op0=ALU.mult, op1=ALU.add), r=[B_tmp], w=[B_tmp])
                P.op("dve", lambda e: e.tensor_scalar(out=nn[H64], in0=tq[H64], scalar1=-MAGIC, scalar2=None,
                                                      op0=ALU.add), r=[B_tmp], w=[B_tmp])
                P.op("dve", lambda e: e.scalar_tensor_tensor(out=tq[H64], in0=nn[H64], scalar=-C1, in1=ang[H64],
                                                             op0=ALU.mult, op1=ALU.add), r=[B_tmp], w=[B_tmp])
                P.op("dve", lambda e: e.scalar_tensor_tensor(out=tq[H64], in0=nn[H64], scalar=-C2, in1=tq[H64],
                                                             op0=ALU.mult, op1=ALU.add), r=[B_tmp], w=[B_tmp])
                P.op("dve", lambda e: e.tensor_scalar(out=tq[H64], in0=tq[H64], scalar1=-PI_LO, scalar2=PI_LO,
                                                      op0=ALU.max, op1=ALU.min), r=[B_tmp], w=[B_tmp])
                P.op("act", lambda e, dst=dst: e.activation(out=dst[H64], in_=tq[H64], func=AF.Sin),
                     r=[B_tmp], w=[B_tab])
            P.barrier()

            a = Alloc(ph_base)
            wq = [a([8, 192], BF16) for _ in range(2)]
            wkv = [a([4, 256], BF16) for _ in range(2)]
            qn = [a([T], BF16) for _ in range(2)]
            qr = [a([T], BF16) for _ in range(2)]
            kn = [a([T], BF16) for _ in range(2)]
            kr = [a([T], BF16) for _ in range(2)]
            vh = [a([16, 128], BF16) for _ in range(2)]
            gt = [a([T], BF16) for _ in range(2)]
            oh = [a([T], BF16) for _ in range(2)]
            B_wq = [Buf("wq0"), Buf("wq1")]
            B_wkv = [Buf("wkv0"), Buf("wkv1")]
            B_q = [Buf("q0"), Buf("q1")]
            B_k = [Buf("k0"), Buf("k1")]
            B_v = [Buf("v0"), Buf("v1")]
            B_gt = [Buf("gt0"), Buf("gt1")]
            B_oh = [Buf("oh0"), Buf("oh1")]
            NTMP = 2
            sq = [a([512], BF16) for _ in range(NTMP)]
            sqr = [a([512], BF16) for _ in range(NTMP)]
            rs = [a([512], F32) for _ in range(NTMP)]
            yv = [a([512], F32) for _ in range(NTMP)]
            t1 = [a([512], F32) for _ in range(NTMP)]
            t2 = [a([512], F32) for _ in range(NTMP)]
            B_sq = [Buf(f"sq{i}") for i in range(NTMP)]
            B_sqr = [Buf(f"sqr{i}") for i in range(NTMP)]
            B_rs = [Buf(f"rs{i}") for i in range(NTMP)]
            B_y = [Buf(f"y{i}") for i in range(NTMP)]
            B_t1 = [Buf(f"t1{i}") for i in range(NTMP)]
            B_t2 = [Buf(f"t2{i}") for i in range(NTMP)]
            NPT = 4
            pT = [a([512], BF16) for _ in range(NPT)]
            B_pT = [Buf(f"pT{i}") for i in range(NPT)]
            rden = [a([512], F32) for _ in range(2)]
            og = [a([512], F32) for _ in range(2)]
            B_rden = [Buf("rden0"), Buf("rden1")]
            B_og = [Buf("og0"), Buf("og1")]
            assert a.cur <= CB, a.cur
            BA, BB, BC, BD = 0, 1, 2, 3
            BS = (4, 5)
            BO, BDEN = 6, 7
            cnt = {"tmp": 0, "pt": 0, "s": 0, "fin": 0}

            def rstd_from(bankc, tsl, scale, parts=128):
                P.op("act", lambda e: e.activation(out=rs[tsl], in_=bank(bankc), func=AF.Ln, scale=scale,
                                                   bias=eps_t[:, 0:1]), r=[PSB[bankc], B_const], w=[B_rs[tsl]])
                P.op("act", lambda e: e.activation(out=rs[tsl], in_=rs[tsl], func=AF.Exp, scale=-0.5),
                     r=[B_rs[tsl]], w=[B_rs[tsl]])

            for (latT, nk, gl, nfeat, li) in ((cqT, 8, gql, 1024.0, 0), (ckvT, 4, gkl, 512.0, 1)):
                for tb in range(4):
                    tsl = cnt["tmp"] % NTMP
                    cnt["tmp"] += 1
                    ts_ = slice(tb * 512, (tb + 1) * 512)
                    for k in range(nk):
                        eng = "dve" if k % 2 == 0 else "pool"
                        sl2 = k % NTMP
                        P.op(eng, lambda e, k=k, sl2=sl2, latT=latT, ts_=ts_: e.tensor_tensor(
                            out=sq[sl2], in0=latT[:, k, ts_], in1=latT[:, k, ts_], op=ALU.mult),
                             r=[B_lat[tb]], w=[B_sq[sl2]])
                        P.op("pe", lambda e, k=k, sl2=sl2, nk=nk: e.matmul(bank(BC), lhsT=ones_b, rhs=sq[sl2],
                                                                         start=(k == 0), stop=(k == nk - 1)),
                             r=[B_sq[sl2], B_const], w=[PSB[BC]])
                    rstd_from(BC, tsl, 1.0 / nfeat)
                    for k in range(nk):
                        P.op("dve", lambda e, k=k, latT=latT, ts_=ts_, gl=gl, tsl=tsl: e.scalar_tensor_tensor(
                            out=latT[:, k, ts_], in0=latT[:, k, ts_], scalar=gl[:, k:k + 1], in1=rs[tsl],
                            op0=ALU.mult, op1=ALU.mult), r=[B_rs[tsl], B_g, B_lat[tb]], w=[B_lat[tb]])

            P.op("pool", lambda e: e.tensor_tensor(out=krsq[H64], in0=krb[H64], in1=krb[H64], op=ALU.mult),
                 r=[B_kr], w=[B_kr])
            for tb in range(4):
                tsl = cnt["tmp"] % NTMP
                cnt["tmp"] += 1
                ts_ = slice(tb * 512, (tb + 1) * 512)
                P.op("dve", lambda e, ts_=ts_, tsl=tsl: e.tensor_scalar(out=yv[tsl][H64], in0=krb[H64, ts_],
                                                                        scalar1=gk_r[H64, 0:1], scalar2=None,
                                                                        op0=ALU.mult), r=[B_kr, B_g], w=[B_y[tsl]])
                P.op("pe", lambda e, tsl=tsl: e.matmul(bank(BD)[H64, :], lhsT=rotT[H64], rhs=yv[tsl][H64],
                                                       start=True, stop=True), r=[B_y[tsl], B_g], w=[PSB[BD]])
                P.op("pool", lambda e, ts_=ts_, tsl=tsl: e.tensor_tensor(out=t1[tsl][H64], in0=yv[tsl][H64],
                                                                         in1=Ct[H64, ts_], op=ALU.mult),
                     r=[B_y[tsl], B_tab], w=[B_t1[tsl]])
                P.op("dve", lambda e, ts_=ts_, tsl=tsl: e.tensor_tensor(out=t2[tsl][H64], in0=bank(BD)[H64, :],
                                                                        in1=St[H64, ts_], op=ALU.mult),
                     r=[PSB[BD], B_tab], w=[B_t2[tsl]])
                P.op("pool", lambda e, ts_=ts_, tsl=tsl: e.tensor_tensor(out=Rk[H64, ts_], in0=t1[tsl][H64],
                                                                         in1=t2[tsl][H64], op=ALU.add),
                     r=[B_t1[tsl], B_t2[tsl]], w=[B_tab])

            def q_stages(h, sl, tb):
                ts_ = slice(tb * 512, (tb + 1) * 512)
                tsl = cnt["tmp"] % NTMP
                cnt["tmp"] += 1

                def s1():
                    for k in range(8):
                        P.op("pe", lambda e, k=k: e.matmul(bank(BA), lhsT=wq[sl][:, k, 0:128], rhs=cqT[:, k, ts_],
                                                           start=(k == 0), stop=(k == 7)),
                             r=[B_wq[sl], B_lat[tb]], w=[PSB[BA]])
                    for k in range(8):
                        P.op("pe", lambda e, k=k: e.matmul(bank(BB)[H64, :], lhsT=wq[sl][:, k, 128:192],
                                                           rhs=cqT[:, k, ts_], start=(k == 0), stop=(k == 7)),
                             r=[B_wq[sl], B_lat[tb]], w=[PSB[BB]])

                def s2():
                    P.op("act", lambda e: e.activation(out=sq[tsl], in_=bank(BA), func=AF.Square),
                         r=[PSB[BA]], w=[B_sq[tsl]])
                    P.op("act", lambda e: e.activation(out=sqr[tsl][H64], in_=bank(BB)[H64, :], func=AF.Square),
                         r=[PSB[BB]], w=[B_sqr[tsl]])

                def s3():
                    P.op("pe", lambda e: e.matmul(bank(BC), lhsT=ones_b, rhs=sq[tsl], start=True, stop=False),
                         r=[B_sq[tsl], B_const], w=[PSB[BC]])
                    P.op("pe", lambda e: e.matmul(bank(BC), lhsT=ones_b[H64, :], rhs=sqr[tsl][H64], start=False,
                                                  stop=True), r=[B_sqr[tsl], B_const], w=[PSB[BC]])

                def s4():
                    rstd_from(BC, tsl, 1.0 / 192.0)
                    P.op("dve", lambda e: e.scalar_tensor_tensor(out=qn[sl][:, ts_], in0=bank(BA), scalar=gq_n[:, 0:1],
                                                                 in1=rs[tsl], op0=ALU.mult, op1=ALU.mult),
                         r=[PSB[BA], B_rs[tsl], B_g], w=[B_q[sl]])
                    P.op("dve", lambda e: e.tensor_scalar(out=yv[tsl][H64], in0=bank(BB)[H64, :], scalar1=gq_r[H64, 0:1],
                                                          scalar2=None, op0=ALU.mult), r=[PSB[BB], B_g], w=[B_y[tsl]])

                def s5():
                    P.op("pe", lambda e: e.matmul(bank(BD)[H64, :], lhsT=rotT[H64], rhs=yv[tsl][H64], start=True,
                                                  stop=True), r=[B_y[tsl], B_g], w=[PSB[BD]])

                def s6():
                    P.op("pool", lambda e: e.tensor_tensor(out=t1[tsl][H64], in0=yv[tsl][H64], in1=Ct[H64, ts_],
                                                           op=ALU.mult), r=[B_y[tsl], B_tab], w=[B_t1[tsl]])
                    P.op("dve", lambda e: e.tensor_tensor(out=t2[tsl][H64], in0=bank(BD)[H64, :], in1=St[H64, ts_],
                                                          op=ALU.mult), r=[PSB[BD], B_tab], w=[B_t2[tsl]])
                    P.op("dve", lambda e: e.tensor_tensor(out=t1[tsl][H64], in0=t1[tsl][H64], in1=t2[tsl][H64],
                                                          op=ALU.add), r=[B_t1[tsl], B_t2[tsl]], w=[B_t1[tsl]])
                    P.op("dve", lambda e: e.tensor_tensor(out=qr[sl][H64, ts_], in0=t1[tsl][H64], in1=rs[tsl][H64],
                                                          op=ALU.mult), r=[B_t1[tsl], B_rs[tsl]], w=[B_q[sl]])

                return [s1, s2, s3, s4, s5, s6]

            def k_stages(h, sl, tb):
                ts_ = slice(tb * 512, (tb + 1) * 512)
                tsl = cnt["tmp"] % NTMP
                cnt["tmp"] += 1

                def s1():
                    for k in range(4):
                        P.op("pe", lambda e, k=k: e.matmul(bank(BA), lhsT=wkv[sl][:, k, 0:128], rhs=ckvT[:, k, ts_],
                                                           start=(k == 0), stop=(k == 3)),
                             r=[B_wkv[sl], B_lat[tb]], w=[PSB[BA]])

                def s2():
                    P.op("act", lambda e: e.activation(out=sq[tsl], in_=bank(BA), func=AF.Square),
                         r=[PSB[BA]], w=[B_sq[tsl]])

                def s3():
                    P.op("pe", lambda e: e.matmul(bank(BC), lhsT=ones_b, rhs=sq[tsl], start=True, stop=False),
                         r=[B_sq[tsl], B_const], w=[PSB[BC]])
                    P.op("pe", lambda e: e.matmul(bank(BC), lhsT=ones_b[H64, :], rhs=krsq[H64, ts_], start=False,
                                                  stop=True), r=[B_kr, B_const], w=[PSB[BC]])

                def s4():
                    rstd_from(BC, tsl, 1.0 / 192.0)
                    P.op("dve", lambda e: e.scalar_tensor_tensor(out=kn[sl][:, ts_], in0=bank(BA), scalar=gk_n[:, 0:1],
                                                                 in1=rs[tsl], op0=ALU.mult, op1=ALU.mult),
                         r=[PSB[BA], B_rs[tsl], B_g], w=[B_k[sl]])
                    P.op("dve", lambda e: e.tensor_tensor(out=kr[sl][H64, ts_], in0=Rk[H64, ts_], in1=rs[tsl][H64],
                                                          op=ALU.mult), r=[B_tab, B_rs[tsl]], w=[B_k[sl]])

                return [s1, s2, s3, s4]

            def v_stages(sl, g4):
                def s1():
                    for i in range(4):
                        kt = g4 * 4 + i
                        for k in range(4):
                            P.op("pe", lambda e, k=k, i=i, kt=kt: e.matmul(
                                bank(BD)[:, i * 128:(i + 1) * 128], lhsT=ckvT[:, k, kt * 128:(kt + 1) * 128],
                                rhs=wkv[sl][:, k, 128:256], start=(k == 0), stop=(k == 3)),
                                 r=[B_wkv[sl], B_lat[kt // 4]], w=[PSB[BD]])

                def s2():
                    P.op("act", lambda e: e.activation(
                        out=vh[sl][:, g4 * 4:(g4 + 1) * 4, :], in_=bank(BD).rearrange("p (a b) -> p a b", a=4),
                        func=AF.Copy), r=[PSB[BD]], w=[B_v[sl]])

                return [s1, s2]

            def prep_stages(h, sl):
                def loads():
                    P.op("pool", lambda e: e.dma_start(out=wq[sl], in_=w_uq_d[:, h * 192:(h + 1) * 192].rearrange(
                        "(k p) c -> p k c", p=128)), w=[B_wq[sl]], dma=True)
                    P.op("pool", lambda e: e.dma_start(out=wkv[sl], in_=w_ukv_d[:, h * 256:(h + 1) * 256].rearrange(
                        "(k p) c -> p k c", p=128)), w=[B_wkv[sl]], dma=True)
                    P.op("sp", lambda e: e.dma_start(out=gt[sl], in_=projT[C_GA + h * 128:C_GA + (h + 1) * 128, :]),
                         w=[B_gt[sl]], dma=True)

                S = [loads]
                for tb in range(4):
                    S.extend(q_stages(h, sl, tb))
                    S.extend(k_stages(h, sl, tb))
                for g4 in range(4):
                    S.extend(v_stages(sl, g4))
                return S

            def attn_units_a(h, sl):
                U = []
                for qb in range(4):
                    nj = 4 * qb + 4
                    for j in range(nj):
                        U.append(make_unit_a(h, sl, qb, j, nj))
                return U

            def make_unit_a(h, sl, qb, j, nj):
                qs_ = qb * 512
                lo = max(0, 128 * j - 512 * qb)
                ks = slice(j * 128, (j + 1) * 128)
                qsl = slice(qs_ + lo, qs_ + 512)
                stt = {}

                def A():
                    bs = BS[cnt["s"] % 2]
                    cnt["s"] += 1
                    pi = cnt["pt"] % NPT
                    cnt["pt"] += 1
                    stt["pi"] = pi
                    P.op("pe", lambda e: e.matmul(bank(bs)[:, lo:512], lhsT=kn[sl][:, ks], rhs=qn[sl][:, qsl],
                                                  start=True, stop=False), r=[B_k[sl], B_q[sl]], w=[PSB[bs]])
                    diag = j >= 4 * qb
                    P.op("pe", lambda e: e.matmul(bank(bs)[:, lo:512], lhsT=kr[sl][H64, ks], rhs=qr[sl][H64, qsl],
                                                  start=False, stop=not diag), r=[B_k[sl], B_q[sl]], w=[PSB[bs]])
                    if diag:
                        jj = j - 4 * qb
                        m0 = 384 - 128 * jj + lo
                        P.op("pe", lambda e: e.matmul(bank(bs)[:, lo:512], lhsT=negI_b, rhs=cm_b[:, m0:m0 + 512 - lo],
                                                      start=False, stop=True), r=[B_const, B_g], w=[PSB[bs]])
                    P.op("act", lambda e: e.activation(out=pT[pi][:, lo:512], in_=bank(bs)[:, lo:512], func=AF.Exp),
                         r=[PSB[bs]], w=[B_pT[pi]])

                def Bf():
                    pi = stt["pi"]
                    P.op("pe", lambda e: e.matmul(bank(BO)[:, lo:512], lhsT=vh[sl][:, j, :], rhs=pT[pi][:, lo:512],
                                                  start=(j == 0), stop=(j == nj - 1)), r=[B_v[sl], B_pT[pi]],
                         w=[PSB[BO]])
                    P.op("pe", lambda e: e.matmul(bank(BDEN)[:, lo:512], lhsT=ones_b, rhs=pT[pi][:, lo:512],
                                                  start=(j == 0), stop=(j == nj - 1)), r=[B_const, B_pT[pi]],
                         w=[PSB[BDEN]])
                    if j == nj - 1:
                        fi = cnt["fin"] % 2
                        cnt["fin"] += 1
                        P.op("act", lambda e: e.activation(out=rden[fi], in_=bank(BDEN), func=AF.Ln),
                             r=[PSB[BDEN]], w=[B_rden[fi]])
                        P.op("act", lambda e: e.activation(out=rden[fi], in_=rden[fi], func=AF.Exp, scale=-1.0),
                             r=[B_rden[fi]], w=[B_rden[fi]])
                        P.op("dve", lambda e: e.tensor_tensor(out=og[fi], in0=rden[fi], in1=gt[sl][:, qs_:qs_ + 512],
                                                              op=ALU.mult), r=[B_rden[fi], B_gt[sl]], w=[B_og[fi]])
                        P.op("dve", lambda e: e.tensor_tensor(out=oh[sl][:, qs_:qs_ + 512], in0=bank(BO), in1=og[fi],
                                                              op=ALU.mult), r=[PSB[BO], B_og[fi]], w=[B_oh[sl]])
                        if qb == 3:
                            P.op("sp", lambda e: e.dma_start(out=oaT[h * 128:(h + 1) * 128, :], in_=oh[sl]),
                                 r=[B_oh[sl]], dma=True)

                return (A, Bf)

            import os as _os
            NH_A = int(_os.environ.get('NH_A', '16'))
            pend = [None]

            def run_unit(u):
                u[0]()
                if pend[0] is not None:
                    pend[0]()
                pend[0] = u[1]

            for st_ in prep_stages(0, 0):
                st_()
            for h in range(NH_A):
                U = attn_units_a(h, h % 2)
                Pn = prep_stages(h + 1, (h + 1) % 2) if h + 1 < NH_A else []
                kk = 0
                for idx, u in enumerate(U):
                    run_unit(u)
                    tgt = min(len(Pn), ((idx + 1) * len(Pn) + len(U) - 1) // len(U))
                    while kk < tgt:
                        Pn[kk]()
                        kk += 1
                while kk < len(Pn):
                    Pn[kk]()
                    kk += 1
            if pend[0] is not None:
                pend[0]()
            P.barrier()

        if stop_after >= 3:
            a = Alloc(0)
            kiT = a([T], BF16)
            qiT = a([16, 512], BF16)
            wi = a([4, 32], F32)
            wabs = a([4, 32], F32)
            wsgn = a([4, 32], F32)
            acc = [a([T], F32) for _ in range(2)]
            tmpr = [a([1024], BF16) for _ in range(2)]
            dsg = [a([32, 128], BF16) for _ in range(2)]
            junk = a([T], BF16)
            mrow = [a([T], BF16) for _ in range(2)]
            maskT = a([16, 512], BF16)
            qbT = a([16, 512], BF16)
            kbT = a([T], BF16)
            vb = a([16, 128], BF16)
            BThi = a([16, 256], BF16)
            BTlo = a([16, 256], BF16)
            b31 = a([16], F32)
            gqb = a([1], F32)
            gkb = a([1], F32)
            gtb = a([16, 512], BF16)
            ob_off = a.cur
            ob = a([16, 512], BF16)
            BT = sb(ob_off, [16, 256], F32)
            W0 = a([1], F32)
            Wtab = a([NITER], F32)
            mid = a([1], F32)
            cntv = a([1], F32)
            sgn = a([1], F32)
            thr = a([1], F32)
            pow2 = a([NITER], F32)
            negm = a([128], F32)
            t5s = a([16], F32)
            ohs = a([384], F32)
            F16 = a([384], F32)
            sqb = [a([512], BF16) for _ in range(2)]
            rs3 = [a([512], F32) for _ in range(2)]
            NPT3 = 4
            pT3v = [a([512], BF16) for _ in range(NPT3)]
            rden3v = [a([512], F32) for _ in range(2)]
            og3v = [a([512], F32) for _ in range(2)]
            assert a.cur <= CB, a.cur
            B_ki = Buf("kiT"); B_qi = Buf("qiT"); B_wi = Buf("wi")
            B_acc = [Buf("acc0"), Buf("acc1")]
            B_tmpr = [Buf("tmpr0"), Buf("tmpr1")]
            B_junk = Buf("junk")
            B_mrow = [Buf("mrow0"), Buf("mrow1")]
            B_dsg = [Buf("dsg0"), Buf("dsg1")]
            B_maskT = Buf("maskT")
            B_qb = Buf("qbT"); B_kb = Buf("kbT"); B_vb = Buf("vb"); B_BT = Buf("BT"); B_g3 = Buf("g3")
            B_gtb = Buf("gtb"); B_ob = B_BT; B_bis = Buf("bis"); B_t5 = Buf("t5")
            B_sqb = [Buf("sqb0"), Buf("sqb1")]
            B_rs3 = [Buf("rs30"), Buf("rs31")]
            B_pT3 = [Buf(f"pT3{i}") for i in range(NPT3)]
            B_tn = [Buf("tn0"), Buf("tn1")]
            B_rden3 = [Buf("rden30"), Buf("rden31")]
            B_og3 = [Buf("og30"), Buf("og31")]
            H32 = slice(0, 32)
            H16 = slice(0, 16)
            c3 = {"ds": 0, "tmp": 0, "lg": 0, "mr": 0, "tp": 0, "s": 0, "pt": 0, "tn": 0, "fin": 0, "acc": 0}

            P.op("sp", lambda e: e.dma_start(out=kiT, in_=kidxT), w=[B_ki], dma=True)
            P.op("sp", lambda e: e.dma_start(out=kbT, in_=projT[C_KB:C_KB + 128, :]), w=[B_kb], dma=True)
            P.op("sp", lambda e: e.dma_start(out=vb, in_=vb_s.rearrange("(j p) d -> p j d", p=128)), w=[B_vb], dma=True)
            P.op("sp", lambda e: e.dma_start(out=gqb, in_=g_qn_b_d.rearrange("(p o) -> p o", o=1)), w=[B_g3], dma=True)
            P.op("sp", lambda e: e.dma_start(out=gkb, in_=g_kn_b_d.rearrange("(p o) -> p o", o=1)), w=[B_g3], dma=True)
            P.op("sp", lambda e: e.dma_start(out=pow2, in_=c_pow2), w=[B_g3], dma=True)
            P.op("sp", lambda e: e.dma_start(out=negm, in_=c_negmask), w=[B_g3], dma=True)
            P.op("sp", lambda e: e.dma_start(out=t5s[H32], in_=t5_d), w=[B_t5], dma=True)
            P.op("sp", lambda e: e.dma_start(out=ohs[H32], in_=c_onehot), w=[B_t5], dma=True)
            P.op("sp", lambda e: e.dma_start(out=b31, in_=t5_d[31, :].partition_broadcast(128)), w=[B_g3], dma=True)
            P.op("dve", lambda e: e.tensor_scalar(out=gqb, in0=gqb, scalar1=128.0 ** -0.5, scalar2=None, op0=ALU.mult),
                 r=[B_g3], w=[B_g3])
            P.op("pe", lambda e: e.matmul(bank(7)[H16, 0:384], lhsT=t5s[H32], rhs=ohs[H32], start=True, stop=True),
                 r=[B_t5], w=[PSB[7]])
            P.op("dve", lambda e: e.tensor_copy(out=F16[H16], in_=bank(7)[H16, 0:384]), r=[PSB[7]], w=[B_t5])
            B_Xb = Buf("Xb")
            P.op("sp", lambda e: e.dma_start(out=Xb.rearrange("h (r m) -> h r m", m=384),
                                             in_=F16[H16].unsqueeze(1).broadcast_to([16, 129, 384])),
                 r=[B_t5], w=[B_Xb], dma=True)
            P.op("sp", lambda e: e.dma_start(out=BT, in_=bass.AP(Xb.tensor, 0, [[383, 128], [129 * 384, 16], [1, 256]])),
                 r=[B_Xb], w=[B_BT], dma=True)
            B_BTs = Buf("BTs")

            def bt_fix(h):
                P.op("dve", lambda e: e.tensor_scalar(out=BT[:, h, :], in0=BT[:, h, :], scalar1=b31[:, h:h + 1],
                                                      scalar2=None, op0=ALU.subtract), r=[B_BT, B_g3], w=[B_BT])

            for h in range(16):
                bt_fix(h)
            P.op("dve", lambda e: e.tensor_copy(out=BThi, in_=BT), r=[B_BT], w=[B_BTs])
            P.op("dve", lambda e: e.tensor_tensor(out=BTlo, in0=BT, in1=BThi, op=ALU.subtract), r=[B_BT, B_BTs],
                 w=[B_BTs])

            def rstd3(bankc, tsl, scale):
                P.op("act", lambda e: e.activation(out=rs3[tsl], in_=bank(bankc), func=AF.Ln, scale=scale,
                                                   bias=eps_t[:, 0:1]), r=[PSB[bankc], B_const], w=[B_rs3[tsl]])
                P.op("act", lambda e: e.activation(out=rs3[tsl], in_=rs3[tsl], func=AF.Exp, scale=-0.5),
                     r=[B_rs3[tsl]], w=[B_rs3[tsl]])

            def norm_block(src_ap, gvec_ap, Bsrc):
                tsl = c3["tmp"] % 2
                c3["tmp"] += 1
                P.op("act", lambda e: e.activation(out=sqb[tsl], in_=src_ap, func=AF.Square), r=[Bsrc],
                     w=[B_sqb[tsl]])
                P.op("pe", lambda e: e.matmul(bank(6), lhsT=ones_b, rhs=sqb[tsl], start=True, stop=True),
                     r=[B_sqb[tsl], B_const], w=[PSB[6]])
                rstd3(6, tsl, 1.0 / 128.0)
                P.op("dve", lambda e: e.scalar_tensor_tensor(out=src_ap, in0=src_ap, scalar=gvec_ap, in1=rs3[tsl],
                                                             op0=ALU.mult, op1=ALU.mult),
                     r=[Bsrc, B_rs3[tsl], B_g3], w=[Bsrc])

            for tb in range(4):
                norm_block(kbT[:, tb * 512:(tb + 1) * 512], gkb[:, 0:1], B_kb)

            def idx_part(qb, i, g):
                qt = qb * 4 + i
                nk = (qt + 1) * 128
                accv = acc[g % 2]
                Bacc = B_acc[g % 2]
                qsl = slice(i * 128, (i + 1) * 128)
                dsl = g % 2
                P.op("pool", lambda e: e.tensor_tensor(
                    out=dsg[dsl], in0=ident_f.unsqueeze(1).broadcast_to([128, 32, 128]),
                    in1=wsgn[:, i, :].unsqueeze(2).broadcast_to([128, 32, 128]), op=ALU.mult),
                     r=[B_const, B_wi], w=[B_dsg[dsl]])

                def mk(h, k0, wk, sbk):
                    c = h // 2
                    hp = slice((h % 2) * 64, (h % 2) * 64 + 64)
                    stt = {}

                    def L():
                        lg = c3["lg"] % 2
                        c3["lg"] += 1
                        stt["lg"] = lg
                        pb_ = bank(2 * lg, 2)
                        for off in range(0, wk, 512):
                            w_ = min(512, wk - off)
                            P.op("pe", lambda e, off=off, w_=w_: e.matmul(
                                pb_[:, off:off + w_], lhsT=qiT[hp, c, qsl], rhs=kiT[hp, k0 + off:k0 + off + w_],
                                start=True, stop=True), r=[B_qi, B_ki], w=[PSB[2 * lg + off // 512]])
                        rb = [PSB[2 * lg + o // 512] for o in range(0, wk, 512)]
                        P.op("act", lambda e: e.activation(out=tmpr[lg][:, 0:wk], in_=pb_[:, 0:wk], func=AF.Relu,
                                                           scale=wabs[:, i, h:h + 1]), r=rb + [B_wi], w=[B_tmpr[lg]])

                    def A():
                        lg = stt["lg"]
                        for off in range(0, wk, 512):
                            w_ = min(512, wk - off)
                            P.op("pe", lambda e, off=off, w_=w_: e.matmul(
                                bank(sbk + off // 512)[:, 0:w_], lhsT=dsg[dsl][:, h, :], rhs=tmpr[lg][:, off:off + w_],
                                start=(h == 0), stop=(h == 31)), r=[B_dsg[dsl], B_tmpr[lg]],
                                 w=[PSB[sbk + off // 512]])

                    return (L, A)

                for k0 in range(0, nk, 1024):
                    wk = min(1024, nk - k0)
                    sbk = 4 + 2 * ((k0 // 1024) % 2)
                    prev = None
                    for h in range(32):
                        u = mk(h, k0, wk, sbk)
                        u[0]()
                        if prev is not None:
                            prev[1]()
                        prev = u
                    prev[1]()
                    for off in range(0, wk, 512):
                        w_ = min(512, wk - off)
                        P.op("act", lambda e, off=off, w_=w_, k0=k0, sbk=sbk: e.activation(
                            out=accv[:, k0 + off:k0 + off + w_], in_=bank(sbk + off // 512)[:, 0:w_], func=AF.Copy),
                             r=[PSB[sbk + off // 512]], w=[Bacc])

            def bis_part(qb, i, g):
                qt = qb * 4 + i
                nk = (qt + 1) * 128
                accv = acc[g % 2]
                Bacc = B_acc[g % 2]
                P.op("dve", lambda e: e.tensor_reduce(out=W0[:, 0:1], in_=accv[:, 0:nk], axis=AX.X, op=ALU.max,
                                                      apply_absolute_value=True), r=[Bacc], w=[B_bis])
                P.op("dve", lambda e: e.tensor_scalar(out=W0, in0=W0, scalar1=1.001, scalar2=1e-6, op0=ALU.mult,
                                                      op1=ALU.add), r=[B_bis], w=[B_bis])
                P.op("dve", lambda e: e.tensor_scalar(out=Wtab, in0=pow2, scalar1=W0[:, 0:1], scalar2=None,
                                                      op0=ALU.mult), r=[B_bis, B_g3], w=[B_bis])
                P.op("dve", lambda e: e.tensor_tensor(out=accv[:, qt * 128:nk], in0=accv[:, qt * 128:nk], in1=negm,
                                                      op=ALU.add), r=[Bacc, B_g3], w=[Bacc])
                if qt >= 2:
                    P.op("dve", lambda e: e.memset(mid, 0.0), r=[B_bis], w=[B_bis])

                    def one_iter(it):
                        P.op("dve", lambda e: e.tensor_scalar(out=junk[:, 0:nk], in0=accv[:, 0:nk], scalar1=mid[:, 0:1],
                                                              scalar2=0.0, op0=ALU.is_ge, op1=ALU.add,
                                                              accum_out=cntv[:, 0:1]),
                             r=[Bacc, B_bis], w=[B_junk, B_bis])
                        P.op("dve", lambda e: e.tensor_scalar(out=sgn, in0=cntv, scalar1=256.0, scalar2=-0.5,
                                                              op0=ALU.is_ge, op1=ALU.add), r=[B_bis], w=[B_bis])
                        P.op("dve", lambda e: e.scalar_tensor_tensor(out=mid, in0=sgn, scalar=Wtab[:, it:it + 1],
                                                                     in1=mid, op0=ALU.mult, op1=ALU.add),
                             r=[B_bis], w=[B_bis])

                    for it in range(NITER):
                        one_iter(it)
                    P.op("dve", lambda e: e.scalar_tensor_tensor(out=thr, in0=Wtab[:, NITER - 1:NITER], scalar=-0.5,
                                                                 in1=mid, op0=ALU.mult, op1=ALU.add),
                         r=[B_bis], w=[B_bis])
                else:
                    P.op("dve", lambda e: e.memset(thr, -1e29), r=[B_bis], w=[B_bis])
                ms = g % 2
                P.op("dve", lambda e: e.tensor_scalar(out=mrow[ms][:, 0:nk], in0=accv[:, 0:nk], scalar1=thr[:, 0:1],
                                                      scalar2=None, op0=ALU.is_lt), r=[Bacc, B_bis], w=[B_mrow[ms]])

            def tr_part(qb, i, g):
                qt = qb * 4 + i
                ms = g % 2
                qsl = slice(i * 128, (i + 1) * 128)
                for j0 in range(0, qt + 1, 8):
                    n = min(8, qt + 1 - j0)
                    bk = 4 + c3["tp"] % 2
                    c3["tp"] += 1
                    for jj in range(n):
                        j = j0 + jj
                        P.op("pe", lambda e, jj=jj, j=j, bk=bk: e.transpose(
                            out=bank16(bk)[:, jj * 128:(jj + 1) * 128], in_=mrow[ms][:, j * 128:(j + 1) * 128],
                            identity=ident_b), r=[B_mrow[ms], B_const], w=[PSB[bk]])
                    P.op("act", lambda e, j0=j0, n=n, bk=bk: e.activation(
                        out=maskT[:, j0:j0 + n, qsl], in_=bank16(bk)[:, 0:n * 128].rearrange("p (a b) -> p a b", a=n),
                        func=AF.Copy), r=[PSB[bk]], w=[B_maskT])

            def make_unit_b(qb, h, j, nj):
                BS_ = (4, 5)
                BO_, BDEN_ = 6, 7
                lo = max(0, 128 * j - 512 * qb)
                c0 = 512 * qb + lo - 128 * j
                nn_ = max(0, min(256 - c0, 512 - lo))
                stt = {}

                def A():
                    bs = BS_[c3["s"] % 2]
                    c3["s"] += 1
                    pi = c3["pt"] % NPT3
                    c3["pt"] += 1
                    stt["pi"] = pi
                    P.op("pe", lambda e: e.matmul(bank(bs)[:, lo:512], lhsT=kbT[:, j * 128:(j + 1) * 128],
                                                  rhs=qbT[:, h, lo:512], start=True, stop=False),
                         r=[B_kb, B_qb], w=[PSB[bs]])
                    if nn_ > 0:
                        P.op("pe", lambda e: e.matmul(bank(bs)[:, lo:lo + nn_], lhsT=ident_b,
                                                      rhs=BThi[:, h, c0:c0 + nn_], start=False, stop=False),
                             r=[B_const, B_BTs], w=[PSB[bs]])
                        P.op("pe", lambda e: e.matmul(bank(bs)[:, lo:lo + nn_], lhsT=ident_b,
                                                      rhs=BTlo[:, h, c0:c0 + nn_], start=False, stop=False),
                             r=[B_const, B_BTs], w=[PSB[bs]])
                    P.op("pe", lambda e: e.matmul(bank(bs)[:, lo:512], lhsT=negI_b, rhs=maskT[:, j, lo:512],
                                                  start=False, stop=True), r=[B_const, B_maskT], w=[PSB[bs]])
                    P.op("act", lambda e: e.activation(out=pT3v[pi][:, lo:512], in_=bank(bs)[:, lo:512], func=AF.Exp,
                                                       bias=b31[:, h:h + 1]), r=[PSB[bs], B_g3], w=[B_pT3[pi]])

                def Bf():
                    pi = stt["pi"]
                    P.op("pe", lambda e: e.matmul(bank(BO_)[:, lo:512], lhsT=vb[:, j, :], rhs=pT3v[pi][:, lo:512],
                                                  start=(j == 0), stop=(j == nj - 1)), r=[B_vb, B_pT3[pi]],
                         w=[PSB[BO_]])
                    P.op("pe", lambda e: e.matmul(bank(BDEN_)[:, lo:512], lhsT=ones_b, rhs=pT3v[pi][:, lo:512],
                                                  start=(j == 0), stop=(j == nj - 1)), r=[B_const, B_pT3[pi]],
                         w=[PSB[BDEN_]])
                    if j == nj - 1:
                        fi = c3["fin"] % 2
                        c3["fin"] += 1
                        P.op("act", lambda e: e.activation(out=rden3v[fi], in_=bank(BDEN_), func=AF.Ln),
                             r=[PSB[BDEN_]], w=[B_rden3[fi]])
                        P.op("act", lambda e: e.activation(out=rden3v[fi], in_=rden3v[fi], func=AF.Exp, scale=-1.0),
                             r=[B_rden3[fi]], w=[B_rden3[fi]])
                        P.op("dve", lambda e: e.tensor_tensor(out=og3v[fi], in0=rden3v[fi], in1=gtb[:, h, :],
                                                              op=ALU.mult), r=[B_rden3[fi], B_gtb], w=[B_og3[fi]])
                        P.op("dve", lambda e: e.tensor_tensor(out=ob[:, h, :], in0=bank(BO_), in1=og3v[fi],
                                                              op=ALU.mult), r=[PSB[BO_], B_og3[fi]], w=[B_ob])

                return (A, Bf)

            pend3 = [None]

            def run_unit3(u):
                u[0]()
                if pend3[0] is not None:
                    pend3[0]()
                pend3[0] = u[1]

            def attn_b_all(qb):
                nj = 4 * qb + 4
                for h in range(16):
                    for j in range(nj):
                        run_unit3(make_unit_b(qb, h, j, nj))
                if pend3[0] is not None:
                    pend3[0]()
                    pend3[0] = None

            def load_idx(qb):
                qcs = slice(qb * 512, (qb + 1) * 512)
                P.op("sp", lambda e: e.dma_start(out=qiT, in_=projT[C_QI:C_QI + 2048, qcs].rearrange(
                    "(c p) t -> p c t", p=128)), w=[B_qi], dma=True)
                P.op("sp", lambda e: e.dma_start(out=wi, in_=widx_s[qcs, :].rearrange("(i p) h -> p i h", p=128)),
                     w=[B_wi], dma=True)
                P.op("act", lambda e: e.activation(out=wabs, in_=wi, func=AF.Abs, scale=32.0 ** -0.5), r=[B_wi],
                     w=[B_wi])
                P.op("act", lambda e: e.activation(out=wsgn, in_=wi, func=AF.Sign), r=[B_wi], w=[B_wi])

            def load_att(qb):
                qcs = slice(qb * 512, (qb + 1) * 512)
                P.op("sp", lambda e: e.dma_start(out=qbT, in_=projT[C_QB:C_QB + 2048, qcs].rearrange(
                    "(c p) t -> p c t", p=128)), w=[B_qb], dma=True)
                P.op("sp", lambda e: e.dma_start(out=gtb, in_=projT[C_GB:C_GB + 2048, qcs].rearrange(
                    "(c p) t -> p c t", p=128)), w=[B_gtb], dma=True)
                for h in range(16):
                    norm_block(qbT[:, h, :], gqb[:, 0:1], B_qb)

            def att_and_store(qb):
                qcs = slice(qb * 512, (qb + 1) * 512)
                attn_b_all(qb)
                P.op("sp", lambda e: e.dma_start(out=obT[:, qcs].rearrange("(h p) t -> p h t", p=128), in_=ob),
                     r=[B_ob], dma=True)

            load_idx(0)
            load_att(0)
            idx_part(0, 0, 0)
            bis_part(0, 0, 0)
            for g in range(1, 16):
                qb = g // 4
                if g % 4 == 0:
                    load_idx(qb)
                idx_part(qb, g % 4, g)
                bis_part(qb, g % 4, g)
                tr_part((g - 1) // 4, (g - 1) % 4, g - 1)
                if g % 4 == 0:
                    att_and_store(qb - 1)
                    load_att(qb)
            tr_part(3, 3, 15)
            att_and_store(3)
            P.barrier()

        if stop_after >= 4:
            a = Alloc(0)
            oaH = a([16, 1024], BF16)
            obH = a([16, 1024], BF16)
            s2_base = a.cur
            mT = a([32, 1024], BF16)
            pa = [a([16, 128], BF16) for _ in range(2)]
            pb = [a([16, 128], BF16) for _ in range(2)]
            sA = [a([1024], BF16) for _ in range(2)]
            sB = [a([1024], BF16) for _ in range(2)]
            u1 = [a([512], F32) for _ in range(2)]
            u2 = [a([512], F32) for _ in range(2)]
            xin = [a([512], F32) for _ in range(2)]
            oout = [a([512], F32) for _ in range(2)]
            assert a.cur <= CB, a.cur
            a2 = Alloc(0)
            wo = [a2([32, 512], BF16) for _ in range(2)]
            assert a2.cur <= s2_base
            B_oaH = Buf("oaH"); B_obH = Buf("obH"); B_mT = [Buf(f"mT{i}") for i in range(32)]
            B_pa = [Buf("pa0"), Buf("pa1")]; B_pb = [Buf("pb0"), Buf("pb1")]
            B_sA = [Buf("sA0"), Buf("sA1")]; B_sB = [Buf("sB0"), Buf("sB1")]
            B_u1 = [Buf("u10"), Buf("u11")]; B_u2 = [Buf("u20"), Buf("u21")]
            B_xin = [Buf("xin0"), Buf("xin1")]; B_oout = [Buf("oo0"), Buf("oo1")]
            B_wo = [Buf("wo0"), Buf("wo1")]
            c4 = {"u": 0, "b": 0, "x": 0}

            def merged_block(th, fb):
                sl = fb % 2
                tcs = slice(th * 1024, (th + 1) * 1024)
                P.op("pool", lambda e: e.dma_start(out=pa[sl], in_=p_a_d[:, fb * 128:(fb + 1) * 128].rearrange(
                    "(k p) c -> p k c", p=128)), w=[B_pa[sl]], dma=True)
                P.op("pool", lambda e: e.dma_start(out=pb[sl], in_=p_b_d[:, fb * 128:(fb + 1) * 128].rearrange(
                    "(k p) c -> p k c", p=128)), w=[B_pb[sl]], dma=True)
                P.op("sp", lambda e: e.dma_start(out=sA[sl], in_=projT[C_MA + fb * 128:C_MA + (fb + 1) * 128, tcs]),
                     w=[B_sA[sl]], dma=True)
                P.op("sp", lambda e: e.dma_start(out=sB[sl], in_=projT[C_MB + fb * 128:C_MB + (fb + 1) * 128, tcs]),
                     w=[B_sB[sl]], dma=True)

                def sub(tb2):
                    bsl = c4["b"] % 2
                    c4["b"] += 1
                    us = c4["u"] % 2
                    c4["u"] += 1
                    ts_ = slice(tb2 * 512, (tb2 + 1) * 512)
                    for k in range(16):
                        P.op("pe", lambda e, k=k: e.matmul(bank(bsl), lhsT=pa[sl][:, k, :], rhs=oaH[:, k, ts_],
                                                           start=(k == 0), stop=(k == 15)), r=[B_pa[sl], B_oaH],
                             w=[PSB[bsl]])
                    for k in range(16):
                        P.op("pe", lambda e, k=k: e.matmul(bank(2 + bsl), lhsT=pb[sl][:, k, :], rhs=obH[:, k, ts_],
                                                           start=(k == 0), stop=(k == 15)), r=[B_pb[sl], B_obH],
                             w=[PSB[2 + bsl]])
                    P.op("dve", lambda e: e.tensor_tensor(out=u1[us], in0=bank(bsl), in1=sA[sl][:, ts_], op=ALU.mult),
                         r=[PSB[bsl], B_sA[sl]], w=[B_u1[us]])
                    P.op("dve", lambda e: e.tensor_tensor(out=u2[us], in0=bank(2 + bsl), in1=sB[sl][:, ts_],
                                                          op=ALU.mult), r=[PSB[2 + bsl], B_sB[sl]], w=[B_u2[us]])
                    P.op("pool", lambda e: e.tensor_tensor(out=mT[:, fb, ts_], in0=u1[us], in1=u2[us], op=ALU.add),
                         r=[B_u1[us], B_u2[us]], w=[B_mT[fb]])

                for tb2 in range(2):
                    sub(tb2)

            def final_block(th, cbk):
                sl = cbk % 2
                ccs = slice(cbk * 512, (cbk + 1) * 512)
                P.op("pool", lambda e: e.dma_start(out=wo[sl], in_=w_o_d[:, ccs].rearrange("(f p) c -> p f c", p=128)),
                     w=[B_wo[sl]], dma=True)

                def tok(tt):
                    bk = 4 + c4["b"] % 4
                    c4["b"] += 1
                    xs = c4["x"] % 2
                    c4["x"] += 1
                    r0 = th * 1024 + tt * 128
                    P.op("sp", lambda e: e.dma_start(out=xin[xs], in_=x_d[r0:r0 + 128, ccs]), w=[B_xin[xs]], dma=True)
                    for f in range(32):
                        P.op("pe", lambda e, f=f: e.matmul(bank(bk), lhsT=mT[:, f, tt * 128:(tt + 1) * 128],
                                                           rhs=wo[sl][:, f, :], start=(f == 0), stop=(f == 31)),
                             r=[B_wo[sl], B_mT[f]], w=[PSB[bk]])
                    P.op("dve", lambda e: e.tensor_tensor(out=oout[xs], in0=bank(bk), in1=xin[xs], op=ALU.add),
                         r=[PSB[bk], B_xin[xs]], w=[B_oout[xs]])
                    P.op("sp", lambda e: e.dma_start(out=out_d[r0:r0 + 128, ccs], in_=oout[xs]), r=[B_oout[xs]],
                         dma=True)

                for tt in range(8):
                    tok(tt)

            for th in range(2):
                tcs = slice(th * 1024, (th + 1) * 1024)
                P.op("sp", lambda e, tcs=tcs: e.dma_start(out=oaH, in_=oaT[:, tcs].rearrange("(k p) t -> p k t", p=128)),
                     w=[B_oaH], dma=True)
                P.op("sp", lambda e, tcs=tcs: e.dma_start(out=obH, in_=obT[:, tcs].rearrange("(k p) t -> p k t", p=128)),
                     w=[B_obH], dma=True)
                for fb in range(32):
                    merged_block(th, fb)
                P.barrier()
                for cbk in range(8):
                    final_block(th, cbk)
                P.barrier()

        P.finalize()
        keys = P.sem_keys()
        sems = {}
        for i, k in enumerate(keys):
            sems[k] = st.enter_context(nc.semaphore(f"s{i}"))
        block = st.enter_context(nc.Block())

        @block.tensor
        def _(e):
            P.emit_engine("pe", e, sems)

        @block.scalar
        def _(e):
            P.emit_engine("act", e, sems)

        @block.vector
        def _(e):
            P.emit_engine("dve", e, sems)

        @block.gpsimd
        def _(e):
            P.emit_engine("pool", e, sems)

        @block.sync
        def _(e):
            P.emit_engine("sp", e, sems, final_wait=True)
    return nc


def _in_maps(inputs):
    maps = []
    shared = {k: np.ascontiguousarray(np.asarray(v, dtype=np.float32)) for k, v in inputs.items()
              if k not in ("x", "positions")}
    for k, v in CONSTS.items():
        shared["c_" + k] = v
    x = np.asarray(inputs["x"], dtype=np.float32)
    pos = np.asarray(inputs["positions"], dtype=np.int32)
    for b in range(x.shape[0]):
        m = dict(shared)
        m["x"] = np.ascontiguousarray(x[b])
        m["positions"] = np.ascontiguousarray(pos[b])
        maps.append(m)
    return maps


_NC_CACHE = {}


def kernel(**inputs):
    if "nc" not in _NC_CACHE:
        _NC_CACHE["nc"] = build_nc()
    nc = _NC_CACHE["nc"]
    maps = _in_maps(inputs)
    res = run_bass_kernel_spmd(nc, maps, core_ids=list(range(len(maps))))
    return np.stack([r["out"] for r in res.results], axis=0).astype(np.float32)
```

```python
import math
import numpy as np
import concourse.bass as bass
import concourse.mybir as mybir
from concourse.bass_utils import run_bass_kernel_spmd

F32 = mybir.dt.float32
BF16 = mybir.dt.bfloat16
I32 = mybir.dt.int32
U8 = mybir.dt.uint8
AF = mybir.ActivationFunctionType
ALU = mybir.AluOpType
AX = mybir.AxisListType

D = 4096
T = 2048
NT = 16
IN_COLS = 18336
EPS = 1e-6
C_CQ, C_CKV, C_KR, C_QB, C_KB, C_VB, C_QI, C_KI, C_WI, C_GA, C_GB, C_MA, C_MB = (
    0, 1024, 1536, 1600, 3648, 3776, 3904, 5952, 6016, 6048, 8096, 10144, 14240)
NITER = 20
SZ = {F32: 4, BF16: 2, I32: 4, U8: 1}


class Buf:
    __slots__ = ("name", "w", "r")

    def __init__(self, name):
        self.name = name
        self.w = None
        self.r = {}


class Op:
    __slots__ = ("eng", "fn", "deps", "flag", "dma", "tok", "n")

    def __init__(self, eng, fn, dma):
        self.eng = eng
        self.fn = fn
        self.deps = set()
        self.flag = False
        self.dma = dma
        self.tok = None
        self.n = None


class Prog:
    ENG = ("pe", "act", "dve", "pool", "sp")
    NDSEM = 8
    CH = 30000

    def __init__(self):
        self.ops = []
        self.barrier_deps = set()
        self.last = {}
        self.dma_hist = {"sp": [], "pool": [], "act": []}

    def op(self, eng, fn, r=(), w=(), dma=False):
        idx = len(self.ops)
        o = Op(eng, fn, dma)
        deps = set(self.barrier_deps)
        for b in r:
            if b.w is not None:
                deps.add(b.w)
        for b in w:
            if b.w is not None:
                deps.add(b.w)
            deps.update(b.r.values())
        key = ("dma", eng, idx) if dma else eng
        for b in r:
            b.r[key if not dma else ("dma", idx)] = idx
        for b in w:
            b.w = idx
            b.r = {}
        if eng == "pe" and not dma:
            deps = {d for d in deps if not (self.ops[d].eng == "pe" and not self.ops[d].dma)}
        o.deps = deps
        self.ops.append(o)
        if dma:
            self.dma_hist[eng].append(idx)
        else:
            self.last[eng] = idx
        return idx

    def barrier(self):
        deps = set(self.last.values())
        for q, h in self.dma_hist.items():
            deps.update(h[-self.NDSEM:])
        self.barrier_deps = deps

    def finalize(self):
        for o in self.ops:
            for d in o.deps:
                self.ops[d].flag = True
        cnt = {e: 0 for e in self.ENG}
        dcnt = {"sp": 0, "pool": 0, "act": 0}
        for o in self.ops:
            if o.dma:
                o.n = dcnt[o.eng]
                dcnt[o.eng] += 1
                o.tok = (("d", o.eng, o.n % self.NDSEM), 16 * (o.n // self.NDSEM + 1))
            elif o.flag:
                k = cnt[o.eng]
                cnt[o.eng] += 1
                o.tok = (("c", o.eng, k // self.CH), k % self.CH + 1)
        self.cnt = cnt
        self.dcnt = dcnt

    def sem_keys(self):
        keys = []
        for e in self.ENG:
            for i in range(max(1, (self.cnt[e] + self.CH - 1) // self.CH)):
                keys.append(("c", e, i))
        for q in ("sp", "pool", "act"):
            if self.dcnt[q]:
                for i in range(self.NDSEM):
                    keys.append(("d", q, i))
        return keys

    def emit_engine(self, ename, eng, sems, final_wait=False):
        known = {}
        for o in self.ops:
            if o.eng != ename:
                continue
            waits = {}
            for d in o.deps:
                k, v = self.ops[d].tok
                if waits.get(k, 0) < v:
                    waits[k] = v
            if o.dma and o.n >= self.NDSEM:
                k = ("d", o.eng, o.n % self.NDSEM)
                v = 16 * (o.n // self.NDSEM)
                if waits.get(k, 0) < v:
                    waits[k] = v
            for k, v in waits.items():
                if known.get(k, 0) < v:
                    eng.wait_ge(sems[k], v)
                    known[k] = v
            ins = o.fn(eng)
            if o.dma:
                ins.then_inc(sems[o.tok[0]], 16)
            elif o.flag:
                ins.then_inc(sems[o.tok[0]], 1)
        if final_wait:
            for q in ("sp", "pool", "act"):
                n = self.dcnt[q]
                for i in range(min(n, self.NDSEM)):
                    tot = (n - i + self.NDSEM - 1) // self.NDSEM
                    k = ("d", q, i)
                    if known.get(k, 0) < 16 * tot:
                        eng.wait_ge(sems[k], 16 * tot)


def _t5_bucket_np(d):
    n = np.maximum(d, 0)
    nf = np.maximum(n, 1).astype(np.float32)
    large = 16 + (np.log(nf / np.float32(16)) / np.float32(math.log(128 / 16)) * np.float32(16)).astype(np.int32)
    large = np.minimum(large, 31)
    return np.where(n < 16, n, large)


def _consts():
    c = {}
    c["ident"] = np.eye(128, dtype=np.float32)
    m = np.arange(384)
    bucket = _t5_bucket_np(np.where(m < 256, m, 0))
    c["onehot"] = (bucket[None, :] == np.arange(32)[:, None]).astype(np.float32)
    half = 32
    inv = (np.float32(10000.0) ** (-np.arange(half, dtype=np.float32) / np.float32(half))).astype(np.float32)
    c["inv64"] = np.concatenate([inv, inv]).reshape(64, 1).astype(np.float32)
    P = np.zeros((64, 64), np.float32)
    for i in range(32):
        P[i, i + 32] = -1.0
        P[i + 32, i] = 1.0
    c["rotT"] = np.ascontiguousarray(P.T)
    s = np.arange(128)[:, None]
    cc = np.arange(896)[None, :]
    c["cmask"] = (cc - 384 >= s).astype(np.float32)
    c["negmask"] = np.where(np.arange(128)[None, :] <= np.arange(128)[:, None], 0.0, -1e30).astype(np.float32)
    c["pow2"] = np.tile((0.5 ** np.arange(NITER)).astype(np.float32)[None, :], (128, 1))
    return c


CONSTS = _consts()


def build_nc(stop_after=99, debug=False):
    nc = bass.Bass("TRN2", target_bir_lowering=False)
    P = Prog()

    def din(name, shape, dt=F32):
        return nc.dram_tensor(name, list(shape), dt, kind="ExternalInput").ap()

    x_d = din("x", [T, D])
    pos_d = din("positions", [T], I32)
    g_pre_d = din("g_pre", [D])
    w_in_d = din("w_in", [D, IN_COLS])
    g_q_lat_d = din("g_q_lat", [1024])
    g_kv_lat_d = din("g_kv_lat", [512])
    w_uq_d = din("w_uq", [1024, 3072])
    w_ukv_d = din("w_ukv", [512, 4096])
    g_qn_a_d = din("g_qn_a", [192])
    g_kn_a_d = din("g_kn_a", [192])
    g_qn_b_d = din("g_qn_b", [128])
    g_kn_b_d = din("g_kn_b", [128])
    t5_d = din("t5_bias", [32, 16])
    p_a_d = din("p_a", [2048, D])
    p_b_d = din("p_b", [2048, D])
    w_o_d = din("w_o", [D, D])
    c_ident = din("c_ident", [128, 128])
    c_onehot = din("c_onehot", [32, 384])
    c_inv64 = din("c_inv64", [64, 1])
    c_rotT = din("c_rotT", [64, 64])
    c_cmask = din("c_cmask", [128, 896])
    c_negmask = din("c_negmask", [128, 128])
    c_pow2 = din("c_pow2", [128, NITER])

    out_d = nc.dram_tensor("out", [T, D], F32, kind="ExternalOutput").ap()

    def dscr(name, shape, dt):
        kind = "ExternalOutput" if debug else "Internal"
        return nc.dram_tensor(name, list(shape), dt, kind=kind).ap()

    projT = dscr("projT", [IN_COLS, T], BF16)
    kidxT = dscr("kidxT", [128, T], BF16)
    vb_s = dscr("vb_s", [T, 128], BF16)
    widx_s = dscr("widx_s", [T, 32], F32)
    oaT = dscr("oaT", [2048, T], BF16)
    obT = dscr("obT", [2048, T], BF16)
    Xb = dscr("Xb", [16, 129 * 384], F32)

    import contextlib
    st = contextlib.ExitStack()
    with st:
        arena = st.enter_context(nc.sbuf_tensor("arena", [128, 200 * 1024], U8))
        psum = st.enter_context(nc.psum_tensor("psum", [128, 4096], F32))

        def sb(off, shape, dt, parts=128, p0=0):
            n = int(np.prod(shape)) * SZ[dt]
            assert off + n <= 200 * 1024, (off, n)
            ap = arena[p0:p0 + parts, off:off + n].bitcast(dt)
            if len(shape) == 2:
                ap = ap.rearrange("p (a b) -> p a b", a=shape[0])
            elif len(shape) == 3:
                ap = ap.rearrange("p (a b c) -> p a b c", a=shape[0], b=shape[1])
            return ap

        def bank(i, n=1):
            return psum[:, i * 512:(i + n) * 512]

        def bank16(i):
            return psum[:, i * 512:(i + 1) * 512].bitcast(BF16)

        PSB = [Buf(f"ps{i}") for i in range(8)]

        class Alloc:
            def __init__(self, base=0):
                self.cur = base

            def __call__(self, shape, dt, parts=128):
                n = int(np.prod(shape)) * SZ[dt]
                off = self.cur
                self.cur = (off + n + 63) // 64 * 64
                return sb(off, shape, dt, parts)

        def dump(name, ap, reads, parts=128):
            if not debug:
                return
            shp = [parts] + list(ap.shape[1:])
            dt_ = nc.dram_tensor("dbg_" + name, shp, ap.dtype, kind="ExternalOutput").ap()
            P.op("sp", lambda e: e.dma_start(out=dt_, in_=ap), r=reads, dma=True)

        CB = 192 * 1024
        ca = Alloc(CB)
        ident_f = ca([128], F32)
        ident_b = ca([128], BF16)
        ones_b = ca([128], BF16)
        eps_t = ca([4], F32)
        negI_b = ca([128], BF16)
        B_const = Buf("consts")
        P.op("sp", lambda e: e.dma_start(out=ident_f, in_=c_ident), w=[B_const], dma=True)
        P.op("dve", lambda e: e.tensor_copy(out=ident_b, in_=ident_f), r=[B_const], w=[B_const])
        P.op("dve", lambda e: e.memset(ones_b, 1.0), w=[B_const])
        P.op("dve", lambda e: e.memset(eps_t, EPS), w=[B_const])
        P.op("dve", lambda e: e.tensor_scalar(out=negI_b, in0=ident_f, scalar1=-30000.0, scalar2=None, op0=ALU.mult),
             r=[B_const], w=[B_const])
        assert ca.cur <= 200 * 1024

        a = Alloc(0)
        hnT = a([32, T], BF16)
        B_hnT = [Buf(f"hnT{i}") for i in range(4)]
        p1_base = a.cur
        xt = [a([D], F32) for _ in range(2)]
        B_xt = [Buf("xt0"), Buf("xt1")]
        gbc = a([D], F32)
        B_gbc = Buf("gbc")
        hn_tm = a([D], BF16)
        B_hn = Buf("hn_tm")
        ss = a([NT], F32)
        rstd0 = a([NT], F32)
        B_ss = [Buf(f"ss{i}") for i in range(NT)]
        assert a.cur <= CB, a.cur

        P.op("sp", lambda e: e.dma_start(out=gbc, in_=g_pre_d.partition_broadcast(128)), w=[B_gbc], dma=True)
        for tt in range(NT):
            sl = tt % 2
            P.op("sp", lambda e, tt=tt, sl=sl: e.dma_start(out=xt[sl], in_=x_d[tt * 128:(tt + 1) * 128, :]),
                 w=[B_xt[sl]], dma=True)
            P.op("dve", lambda e, tt=tt, sl=sl: e.scalar_tensor_tensor(
                out=hn_tm, in0=xt[sl], scalar=1.0, in1=xt[sl], op0=ALU.mult, op1=ALU.mult,
                accum_out=ss[:, tt:tt + 1]), r=[B_xt[sl]], w=[B_hn, B_ss[tt]])
            P.op("act", lambda e, tt=tt: e.activation(out=rstd0[:, tt:tt + 1], in_=ss[:, tt:tt + 1], func=AF.Sqrt,
                                                      scale=1.0 / D, bias=eps_t[:, 0:1]),
                 r=[B_ss[tt], B_const], w=[B_ss[tt]])
            P.op("dve", lambda e, tt=tt: e.reciprocal(out=rstd0[:, tt:tt + 1], in_=rstd0[:, tt:tt + 1]),
                 r=[B_ss[tt]], w=[B_ss[tt]])
            P.op("dve", lambda e, tt=tt, sl=sl: e.scalar_tensor_tensor(
                out=hn_tm, in0=xt[sl], scalar=rstd0[:, tt:tt + 1], in1=gbc, op0=ALU.mult, op1=ALU.mult),
                 r=[B_xt[sl], B_ss[tt], B_gbc], w=[B_hn])
            for q in range(4):
                bk = (tt * 4 + q) % 8
                for j in range(8):
                    c = q * 8 + j
                    P.op("pe", lambda e, bk=bk, j=j, c=c: e.transpose(
                        out=bank16(bk)[:, j * 128:(j + 1) * 128], in_=hn_tm[:, c * 128:(c + 1) * 128],
                        identity=ident_b), r=[B_hn, B_const], w=[PSB[bk]])
                P.op("act", lambda e, bk=bk, q=q, tt=tt: e.activation(
                    out=hnT[:, q * 8:(q + 1) * 8, tt * 128:(tt + 1) * 128],
                    in_=bank16(bk).rearrange("p (a b) -> p a b", a=8), func=AF.Copy),
                     r=[PSB[bk]], w=[B_hnT[tt // 4]])
        P.barrier()

        a = Alloc(p1_base)
        WB = [a([32, 256], BF16) for _ in range(2)]
        B_WB = [Buf("WB0"), Buf("WB1")]
        SG = [a([T], BF16) for _ in range(2)]
        B_SG = [Buf("SG0"), Buf("SG1")]
        WT = a([32, 160], BF16)
        B_WT = Buf("WT")
        SGv = [a([128], BF16) for _ in range(2)]
        SGw = [a([32], F32) for _ in range(2)]
        B_SGv = [Buf("SGv0"), Buf("SGv1")]
        assert a.cur <= CB, a.cur

        groups = []

        def add_range(c0, c1, act):
            c = c0
            while c < c1:
                wdt = min(256, c1 - c)
                blks = []
                for o in range(0, wdt, 128):
                    blks.append((o, min(128, wdt - o), "proj", c + o, act))
                groups.append((c, wdt, blks))
                c += wdt

        add_range(C_CQ, C_KR, AF.Copy)
        groups.append((C_KR, 64, [(0, 64, "proj", C_KR, AF.Copy)]))
        add_range(C_QB, C_VB, AF.Copy)
        add_range(C_QI, C_KI, AF.Copy)
        groups.append((C_KI, 64, [(0, 128, "kidx", 0, AF.Copy)]))
        add_range(C_GA, C_MA, AF.Silu)
        add_range(C_MA, IN_COLS, AF.Sigmoid)

        def w_src(c0, wdt):
            return w_in_d[:, c0:c0 + wdt].rearrange("(k p) c -> p k c", p=128)

        def issue_load(gi):
            c0, wdt, blks = groups[gi]
            sl = gi % 2
            if blks[0][2] == "kidx":
                P.op("pool", lambda e: e.dma_start(out=WB[sl][:, :, 0:64], in_=w_src(c0, 64)), w=[B_WB[sl]], dma=True)
                P.op("pool", lambda e: e.dma_start(out=WB[sl][:, :, 64:128], in_=w_src(c0, 64)), w=[B_WB[sl]],
                     dma=True)
            else:
                P.op("pool", lambda e: e.dma_start(out=WB[sl][:, :, 0:wdt], in_=w_src(c0, wdt)), w=[B_WB[sl]],
                     dma=True)

        P.op("pool", lambda e: e.dma_start(out=WT[:, :, 0:128], in_=w_src(C_VB, 128)), w=[B_WT], dma=True)
        P.op("pool", lambda e: e.dma_start(out=WT[:, :, 128:160], in_=w_src(C_WI, 32)), w=[B_WT], dma=True)
        issue_load(0)
        issue_load(1)
        nblk = 0
        ngroups = len(groups) if stop_after >= 1 else 0
        for gi in range(ngroups):
            c0, wdt, blks = groups[gi]
            sl = gi % 2
            for (o, M, kind, row, act) in blks:
                bs = (nblk % 2) * 4
                sg = nblk % 2
                nblk += 1
                for k in range(32):
                    for tb in range(4):
                        P.op("pe", lambda e, k=k, tb=tb, bs=bs, o=o, M=M, sl=sl: e.matmul(
                            bank(bs + tb)[0:M, :], lhsT=WB[sl][:, k, o:o + M], rhs=hnT[:, k, tb * 512:(tb + 1) * 512],
                            start=(k == 0), stop=(k == 31)), r=[B_WB[sl], B_hnT[tb]], w=[PSB[bs + tb]])
                for tb in range(4):
                    P.op("act", lambda e, tb=tb, bs=bs, M=M, sg=sg, act=act: e.activation(
                        out=SG[sg][0:M, tb * 512:(tb + 1) * 512], in_=bank(bs + tb)[0:M, :], func=act),
                         r=[PSB[bs + tb]], w=[B_SG[sg]])
                if kind == "proj":
                    P.op("sp", lambda e, sg=sg, M=M, row=row: e.dma_start(out=projT[row:row + M, :], in_=SG[sg][0:M, :]),
                         r=[B_SG[sg]], dma=True)
                else:
                    P.op("sp", lambda e, sg=sg: e.dma_start(out=kidxT, in_=SG[sg]), r=[B_SG[sg]], dma=True)
            if gi + 2 < ngroups:
                issue_load(gi + 2)
        if stop_after >= 1:
            for tt in range(NT):
                bk = tt % 2
                sg = tt % 2
                for k in range(32):
                    P.op("pe", lambda e, k=k, tt=tt, bk=bk: e.matmul(
                        bank(bk)[:, 0:160], lhsT=hnT[:, k, tt * 128:(tt + 1) * 128], rhs=WT[:, k, :],
                        start=(k == 0), stop=(k == 31)), r=[B_WT, B_hnT[tt // 4]], w=[PSB[bk]])
                P.op("act", lambda e, bk=bk, sg=sg: e.activation(out=SGv[sg], in_=bank(bk)[:, 0:128], func=AF.Copy),
                     r=[PSB[bk]], w=[B_SGv[sg]])
                P.op("act", lambda e, bk=bk, sg=sg: e.activation(out=SGw[sg], in_=bank(bk)[:, 128:160], func=AF.Copy),
                     r=[PSB[bk]], w=[B_SGv[sg]])
                P.op("sp", lambda e, sg=sg, tt=tt: e.dma_start(out=vb_s[tt * 128:(tt + 1) * 128, :], in_=SGv[sg]),
                     r=[B_SGv[sg]], dma=True)
                P.op("sp", lambda e, sg=sg, tt=tt: e.dma_start(out=widx_s[tt * 128:(tt + 1) * 128, :], in_=SGw[sg]),
                     r=[B_SGv[sg]], dma=True)
        P.barrier()


        if stop_after >= 2:
            a = Alloc(0)
            cqT = a([8, T], BF16)
            ckvT = a([4, T], BF16)
            krb = a([T], BF16)
            Ct = a([T], F32)
            St = a([T], F32)
            Rk = a([T], F32)
            krsq = a([T], BF16)
            gql = a([8], F32)
            gkl = a([4], F32)
            gq_n = a([1], F32)
            gq_r = a([1], F32)
            gk_n = a([1], F32)
            gk_r = a([1], F32)
            inv64 = a([1], F32)
            rotT = a([64], F32)
            cm_f = a([896], F32)
            cm_b = a([896], BF16)
            B_lat = [Buf(f"lat{i}") for i in range(4)]
            B_kr = Buf("kr")
            B_g = Buf("g2")
            B_tab = Buf("tab")
            ph_base = a.cur
            posi = a([T], I32)
            posf = a([T], F32)
            ang = a([T], F32)
            tq = a([T], F32)
            nn = a([T], F32)
            B_tmp = Buf("tmp2")
            assert a.cur <= CB, a.cur
            H64 = slice(0, 64)

            def gvec(dst, src, n0, n1, eng="sp"):
                P.op(eng, lambda e: e.dma_start(out=dst, in_=src[n0:n1].rearrange("(p o) -> p o", o=1)), w=[B_g],
                     dma=True)

            P.op("sp", lambda e: e.dma_start(out=cqT, in_=projT[0:1024, :].rearrange("(k p) t -> p k t", p=128)),
                 w=B_lat, dma=True)
            P.op("sp", lambda e: e.dma_start(out=ckvT, in_=projT[1024:1536, :].rearrange("(k p) t -> p k t", p=128)),
                 w=B_lat, dma=True)
            P.op("sp", lambda e: e.dma_start(out=krb[H64], in_=projT[C_KR:C_KR + 64, :]), w=[B_kr], dma=True)
            P.op("sp", lambda e: e.dma_start(out=gql, in_=g_q_lat_d.rearrange("(k p) -> p k", p=128),
                                            allow_slow_non_contiguous=True), w=[B_g], dma=True)
            P.op("sp", lambda e: e.dma_start(out=gkl, in_=g_kv_lat_d.rearrange("(k p) -> p k", p=128),
                                            allow_slow_non_contiguous=True), w=[B_g], dma=True)
            gvec(gq_n, g_qn_a_d, 0, 128)
            gvec(gq_r[H64], g_qn_a_d, 128, 192)
            gvec(gk_n, g_kn_a_d, 0, 128)
            gvec(gk_r[H64], g_kn_a_d, 128, 192)
            P.op("sp", lambda e: e.dma_start(out=inv64[H64], in_=c_inv64), w=[B_g], dma=True)
            P.op("sp", lambda e: e.dma_start(out=rotT[H64], in_=c_rotT), w=[B_g], dma=True)
            P.op("sp", lambda e: e.dma_start(out=cm_f, in_=c_cmask), w=[B_g], dma=True)
            P.op("sp", lambda e: e.dma_start(out=posi[H64], in_=pos_d.partition_broadcast(64)), w=[B_tmp], dma=True)
            P.op("dve", lambda e: e.tensor_scalar(out=cm_b, in0=cm_f, scalar1=-1.0, scalar2=1.0, op0=ALU.mult,
                                                  op1=ALU.add), r=[B_g], w=[B_g])
            qs = 192.0 ** -0.5
            P.op("dve", lambda e: e.tensor_scalar(out=gq_n, in0=gq_n, scalar1=qs, scalar2=None, op0=ALU.mult),
                 r=[B_g], w=[B_g])
            P.op("dve", lambda e: e.tensor_scalar(out=gq_r[H64], in0=gq_r[H64], scalar1=qs, scalar2=None, op0=ALU.mult),
                 r=[B_g], w=[B_g])
            TWO_PI_INV = float(np.float32(1.0 / (2 * math.pi)))
            MAGIC = 12582912.0
            C1 = 6.28125
            C2 = float(2 * math.pi - 6.28125)
            PI_LO = 3.1415925
            P.op("dve", lambda e: e.tensor_copy(out=posf[H64], in_=posi[H64]), r=[B_tmp], w=[B_tmp])
            P.op("dve", lambda e: e.tensor_scalar(out=ang[H64], in0=posf[H64], scalar1=inv64[H64, 0:1], scalar2=None,
                                                  op0=ALU.mult), r=[B_tmp, B_g], w=[B_tmp])
            for which, dst in (("sin", St), ("cos", Ct)):
                if which == "cos":
                    P.op("dve", lambda e: e.tensor_scalar(out=ang[H64], in0=ang[H64], scalar1=float(math.pi / 2),
                                                          scalar2=None, op0=ALU.add), r=[B_tmp], w=[B_tmp])
                P.op("dve", lambda e: e.tensor_scalar(out=tq[H64], in0=ang[H64], scalar1=TWO_PI_INV, scalar2=MAGIC,
                                                      op0=ALU.mult, op1=ALU.add), r=[B_tmp], w=[B_tmp])
                P.op("dve", lambda e: e.tensor_scalar(out=nn[H64], in0=tq[H64], scalar1=-MAGIC, scalar2=None,
                                                      op0=ALU.add), r=[B_tmp], w=[B_tmp])
                P.op("dve", lambda e: e.scalar_tensor_tensor(out=tq[H64], in0=nn[H64], scalar=-C1, in1=ang[H64],
                                                             op0=ALU.mult, op1=ALU.add), r=[B_tmp], w=[B_tmp])
                P.op("dve", lambda e: e.scalar_tensor_tensor(out=tq[H64], in0=nn[H64], scalar=-C2, in1=tq[H64],
                                                             op0=ALU.mult, op1=ALU.add), r=[B_tmp], w=[B_tmp])
                P.op("dve", lambda e: e.tensor_scalar(out=tq[H64], in0=tq[H64], scalar1=-PI_LO, scalar2=PI_LO,
                                                      op0=ALU.max, op1=ALU.min), r=[B_tmp], w=[B_tmp])
                P.op("act", lambda e, dst=dst: e.activation(out=dst[H64], in_=tq[H64], func=AF.Sin),
                     r=[B_tmp], w=[B_tab])
            P.barrier()

            a = Alloc(ph_base)
            wq = [a([8, 192], BF16) for _ in range(2)]
            wkv = [a([4, 256], BF16) for _ in range(2)]
            qn = [a([T], BF16) for _ in range(2)]
            qr = [a([T], BF16) for _ in range(2)]
            kn = [a([T], BF16) for _ in range(2)]
            kr = [a([T], BF16) for _ in range(2)]
            vh = [a([16, 128], BF16) for _ in range(2)]
            gt = [a([T], BF16) for _ in range(2)]
            oh = [a([T], BF16) for _ in range(2)]
            B_wq = [Buf("wq0"), Buf("wq1")]
            B_wkv = [Buf("wkv0"), Buf("wkv1")]
            B_q = [Buf("q0"), Buf("q1")]
            B_k = [Buf("k0"), Buf("k1")]
            B_v = [Buf("v0"), Buf("v1")]
            B_gt = [Buf("gt0"), Buf("gt1")]
            B_oh = [Buf("oh0"), Buf("oh1")]
            NTMP = 2
            sq = [a([512], BF16) for _ in range(NTMP)]
            sqr = [a([512], BF16) for _ in range(NTMP)]
            rs = [a([512], F32) for _ in range(NTMP)]
            yv = [a([512], F32) for _ in range(NTMP)]
            t1 = [a([512], F32) for _ in range(NTMP)]
            t2 = [a([512], F32) for _ in range(NTMP)]
            B_sq = [Buf(f"sq{i}") for i in range(NTMP)]
            B_sqr = [Buf(f"sqr{i}") for i in range(NTMP)]
            B_rs = [Buf(f"rs{i}") for i in range(NTMP)]
            B_y = [Buf(f"y{i}") for i in range(NTMP)]
            B_t1 = [Buf(f"t1{i}") for i in range(NTMP)]
            B_t2 = [Buf(f"t2{i}") for i in range(NTMP)]
            NPT = 4
            pT = [a([512], BF16) for _ in range(NPT)]
            B_pT = [Buf(f"pT{i}") for i in range(NPT)]
            rden = [a([512], F32) for _ in range(2)]
            og = [a([512], F32) for _ in range(2)]
            B_rden = [Buf("rden0"), Buf("rden1")]
            B_og = [Buf("og0"), Buf("og1")]
            assert a.cur <= CB, a.cur
            BA, BB, BC, BD = 0, 1, 2, 3
            BS = (4, 5)
            BO, BDEN = 6, 7
            cnt = {"tmp": 0, "pt": 0, "s": 0, "fin": 0}

            def rstd_from(bankc, tsl, scale, parts=128):
                P.op("act", lambda e: e.activation(out=rs[tsl], in_=bank(bankc), func=AF.Ln, scale=scale,
                                                   bias=eps_t[:, 0:1]), r=[PSB[bankc], B_const], w=[B_rs[tsl]])
                P.op("act", lambda e: e.activation(out=rs[tsl], in_=rs[tsl], func=AF.Exp, scale=-0.5),
                     r=[B_rs[tsl]], w=[B_rs[tsl]])

            for (latT, nk, gl, nfeat, li) in ((cqT, 8, gql, 1024.0, 0), (ckvT, 4, gkl, 512.0, 1)):
                for tb in range(4):
                    tsl = cnt["tmp"] % NTMP
                    cnt["tmp"] += 1
                    ts_ = slice(tb * 512, (tb + 1) * 512)
                    for k in range(nk):
                        eng = "dve" if k % 2 == 0 else "pool"
                        sl2 = k % NTMP
                        P.op(eng, lambda e, k=k, sl2=sl2, latT=latT, ts_=ts_: e.tensor_tensor(
                            out=sq[sl2], in0=latT[:, k, ts_], in1=latT[:, k, ts_], op=ALU.mult),
                             r=[B_lat[tb]], w=[B_sq[sl2]])
                        P.op("pe", lambda e, k=k, sl2=sl2, nk=nk: e.matmul(bank(BC), lhsT=ones_b, rhs=sq[sl2],
                                                                         start=(k == 0), stop=(k == nk - 1)),
                             r=[B_sq[sl2], B_const], w=[PSB[BC]])
                    rstd_from(BC, tsl, 1.0 / nfeat)
                    for k in range(nk):
                        P.op("dve", lambda e, k=k, latT=latT, ts_=ts_, gl=gl, tsl=tsl: e.scalar_tensor_tensor(
                            out=latT[:, k, ts_], in0=latT[:, k, ts_], scalar=gl[:, k:k + 1], in1=rs[tsl],
                            op0=ALU.mult, op1=ALU.mult), r=[B_rs[tsl], B_g, B_lat[tb]], w=[B_lat[tb]])

            P.op("pool", lambda e: e.tensor_tensor(out=krsq[H64], in0=krb[H64], in1=krb[H64], op=ALU.mult),
                 r=[B_kr], w=[B_kr])
            for tb in range(4):
                tsl = cnt["tmp"] % NTMP
                cnt["tmp"] += 1
                ts_ = slice(tb * 512, (tb + 1) * 512)
                P.op("dve", lambda e, ts_=ts_, tsl=tsl: e.tensor_scalar(out=yv[tsl][H64], in0=krb[H64, ts_],
                                                                        scalar1=gk_r[H64, 0:1], scalar2=None,
                                                                        op0=ALU.mult), r=[B_kr, B_g], w=[B_y[tsl]])
                P.op("pe", lambda e, tsl=tsl: e.matmul(bank(BD)[H64, :], lhsT=rotT[H64], rhs=yv[tsl][H64],
                                                       start=True, stop=True), r=[B_y[tsl], B_g], w=[PSB[BD]])
                P.op("pool", lambda e, ts_=ts_, tsl=tsl: e.tensor_tensor(out=t1[tsl][H64], in0=yv[tsl][H64],
                                                                         in1=Ct[H64, ts_], op=ALU.mult),
                     r=[B_y[tsl], B_tab], w=[B_t1[tsl]])
                P.op("dve", lambda e, ts_=ts_, tsl=tsl: e.tensor_tensor(out=t2[tsl][H64], in0=bank(BD)[H64, :],
                                                                        in1=St[H64, ts_], op=ALU.mult),
                     r=[PSB[BD], B_tab], w=[B_t2[tsl]])
                P.op("pool", lambda e, ts_=ts_, tsl=tsl: e.tensor_tensor(out=Rk[H64, ts_], in0=t1[tsl][H64],
                                                                         in1=t2[tsl][H64], op=ALU.add),
                     r=[B_t1[tsl], B_t2[tsl]], w=[B_tab])

            def q_stages(h, sl, tb):
                ts_ = slice(tb * 512, (tb + 1) * 512)
                tsl = cnt["tmp"] % NTMP
                cnt["tmp"] += 1

                def s1():
                    for k in range(8):
                        P.op("pe", lambda e, k=k: e.matmul(bank(BA), lhsT=wq[sl][:, k, 0:128], rhs=cqT[:, k, ts_],
                                                           start=(k == 0), stop=(k == 7)),
                             r=[B_wq[sl], B_lat[tb]], w=[PSB[BA]])
                    for k in range(8):
                        P.op("pe", lambda e, k=k: e.matmul(bank(BB)[H64, :], lhsT=wq[sl][:, k, 128:192],
                                                           rhs=cqT[:, k, ts_], start=(k == 0), stop=(k == 7)),
                             r=[B_wq[sl], B_lat[tb]], w=[PSB[BB]])

                def s2():
                    P.op("act", lambda e: e.activation(out=sq[tsl], in_=bank(BA), func=AF.Square),
                         r=[PSB[BA]], w=[B_sq[tsl]])
                    P.op("act", lambda e: e.activation(out=sqr[tsl][H64], in_=bank(BB)[H64, :], func=AF.Square),
                         r=[PSB[BB]], w=[B_sqr[tsl]])

                def s3():
                    P.op("pe", lambda e: e.matmul(bank(BC), lhsT=ones_b, rhs=sq[tsl], start=True, stop=False),
                         r=[B_sq[tsl], B_const], w=[PSB[BC]])
                    P.op("pe", lambda e: e.matmul(bank(BC), lhsT=ones_b[H64, :], rhs=sqr[tsl][H64], start=False,
                                                  stop=True), r=[B_sqr[tsl], B_const], w=[PSB[BC]])

                def s4():
                    rstd_from(BC, tsl, 1.0 / 192.0)
                    P.op("dve", lambda e: e.scalar_tensor_tensor(out=qn[sl][:, ts_], in0=bank(BA), scalar=gq_n[:, 0:1],
                                                                 in1=rs[tsl], op0=ALU.mult, op1=ALU.mult),
                         r=[PSB[BA], B_rs[tsl], B_g], w=[B_q[sl]])
                    P.op("dve", lambda e: e.tensor_scalar(out=yv[tsl][H64], in0=bank(BB)[H64, :], scalar1=gq_r[H64, 0:1],
                                                          scalar2=None, op0=ALU.mult), r=[PSB[BB], B_g], w=[B_y[tsl]])

                def s5():
                    P.op("pe", lambda e: e.matmul(bank(BD)[H64, :], lhsT=rotT[H64], rhs=yv[tsl][H64], start=True,
                                                  stop=True), r=[B_y[tsl], B_g], w=[PSB[BD]])

                def s6():
                    P.op("pool", lambda e: e.tensor_tensor(out=t1[tsl][H64], in0=yv[tsl][H64], in1=Ct[H64, ts_],
                                                           op=ALU.mult), r=[B_y[tsl], B_tab], w=[B_t1[tsl]])
                    P.op("dve", lambda e: e.tensor_tensor(out=t2[tsl][H64], in0=bank(BD)[H64, :], in1=St[H64, ts_],
                                                          op=ALU.mult), r=[PSB[BD], B_tab], w=[B_t2[tsl]])
                    P.op("dve", lambda e: e.tensor_tensor(out=t1[tsl][H64], in0=t1[tsl][H64], in1=t2[tsl][H64],
                                                          op=ALU.add), r=[B_t1[tsl], B_t2[tsl]], w=[B_t1[tsl]])
                    P.op("dve", lambda e: e.tensor_tensor(out=qr[sl][H64, ts_], in0=t1[tsl][H64], in1=rs[tsl][H64],
                                                          op=ALU.mult), r=[B_t1[tsl], B_rs[tsl]], w=[B_q[sl]])

                return [s1, s2, s3, s4, s5, s6]

            def k_stages(h, sl, tb):
                ts_ = slice(tb * 512, (tb + 1) * 512)
                tsl = cnt["tmp"] % NTMP
                cnt["tmp"] += 1

                def s1():
                    for k in range(4):
                        P.op("pe", lambda e, k=k: e.matmul(bank(BA), lhsT=wkv[sl][:, k, 0:128], rhs=ckvT[:, k, ts_],
                                                           start=(k == 0), stop=(k == 3)),
                             r=[B_wkv[sl], B_lat[tb]], w=[PSB[BA]])

                def s2():
                    P.op("act", lambda e: e.activation(out=sq[tsl], in_=bank(BA), func=AF.Square),
                         r=[PSB[BA]], w=[B_sq[tsl]])

                def s3():
                    P.op("pe", lambda e: e.matmul(bank(BC), lhsT=ones_b, rhs=sq[tsl], start=True, stop=False),
                         r=[B_sq[tsl], B_const], w=[PSB[BC]])
                    P.op("pe", lambda e: e.matmul(bank(BC), lhsT=ones_b[H64, :], rhs=krsq[H64, ts_], start=False,
                                                  stop=True), r=[B_kr, B_const], w=[PSB[BC]])

                def s4():
                    rstd_from(BC, tsl, 1.0 / 192.0)
                    P.op("dve", lambda e: e.scalar_tensor_tensor(out=kn[sl][:, ts_], in0=bank(BA), scalar=gk_n[:, 0:1],
                                                                 in1=rs[tsl], op0=ALU.mult, op1=ALU.mult),
                         r=[PSB[BA], B_rs[tsl], B_g], w=[B_k[sl]])
                    P.op("dve", lambda e: e.tensor_tensor(out=kr[sl][H64, ts_], in0=Rk[H64, ts_], in1=rs[tsl][H64],
                                                          op=ALU.mult), r=[B_tab, B_rs[tsl]], w=[B_k[sl]])

                return [s1, s2, s3, s4]

            def v_stages(sl, g4):
                def s1():
                    for i in range(4):
                        kt = g4 * 4 + i
                        for k in range(4):
                            P.op("pe", lambda e, k=k, i=i, kt=kt: e.matmul(
                                bank(BD)[:, i * 128:(i + 1) * 128], lhsT=ckvT[:, k, kt * 128:(kt + 1) * 128],
                                rhs=wkv[sl][:, k, 128:256], start=(k == 0), stop=(k == 3)),
                                 r=[B_wkv[sl], B_lat[kt // 4]], w=[PSB[BD]])

                def s2():
                    P.op("act", lambda e: e.activation(
                        out=vh[sl][:, g4 * 4:(g4 + 1) * 4, :], in_=bank(BD).rearrange("p (a b) -> p a b", a=4),
                        func=AF.Copy), r=[PSB[BD]], w=[B_v[sl]])

                return [s1, s2]

            def prep_stages(h, sl):
                def loads():
                    P.op("pool", lambda e: e.dma_start(out=wq[sl], in_=w_uq_d[:, h * 192:(h + 1) * 192].rearrange(
                        "(k p) c -> p k c", p=128)), w=[B_wq[sl]], dma=True)
                    P.op("pool", lambda e: e.dma_start(out=wkv[sl], in_=w_ukv_d[:, h * 256:(h + 1) * 256].rearrange(
                        "(k p) c -> p k c", p=128)), w=[B_wkv[sl]], dma=True)
                    P.op("sp", lambda e: e.dma_start(out=gt[sl], in_=projT[C_GA + h * 128:C_GA + (h + 1) * 128, :]),
                         w=[B_gt[sl]], dma=True)

                S = [loads]
                for tb in range(4):
                    S.extend(q_stages(h, sl, tb))
                    S.extend(k_stages(h, sl, tb))
                for g4 in range(4):
                    S.extend(v_stages(sl, g4))
                return S

            def attn_units_a(h, sl):
                U = []
                for qb in range(4):
                    nj = 4 * qb + 4
                    for j in range(nj):
                        U.append(make_unit_a(h, sl, qb, j, nj))
                return U

            def make_unit_a(h, sl, qb, j, nj):
                qs_ = qb * 512
                lo = max(0, 128 * j - 512 * qb)
                ks = slice(j * 128, (j + 1) * 128)
                qsl = slice(qs_ + lo, qs_ + 512)
                stt = {}

                def A():
                    bs = BS[cnt["s"] % 2]
                    cnt["s"] += 1
                    pi = cnt["pt"] % NPT
                    cnt["pt"] += 1
                    stt["pi"] = pi
                    P.op("pe", lambda e: e.matmul(bank(bs)[:, lo:512], lhsT=kn[sl][:, ks], rhs=qn[sl][:, qsl],
                                                  start=True, stop=False), r=[B_k[sl], B_q[sl]], w=[PSB[bs]])
                    diag = j >= 4 * qb
                    P.op("pe", lambda e: e.matmul(bank(bs)[:, lo:512], lhsT=kr[sl][H64, ks], rhs=qr[sl][H64, qsl],
                                                  start=False, stop=not diag), r=[B_k[sl], B_q[sl]], w=[PSB[bs]])
                    if diag:
                        jj = j - 4 * qb
                        m0 = 384 - 128 * jj + lo
                        P.op("pe", lambda e: e.matmul(bank(bs)[:, lo:512], lhsT=negI_b, rhs=cm_b[:, m0:m0 + 512 - lo],
                                                      start=False, stop=True), r=[B_const, B_g], w=[PSB[bs]])
                    P.op("act", lambda e: e.activation(out=pT[pi][:, lo:512], in_=bank(bs)[:, lo:512], func=AF.Exp),
                         r=[PSB[bs]], w=[B_pT[pi]])

                def Bf():
                    pi = stt["pi"]
                    P.op("pe", lambda e: e.matmul(bank(BO)[:, lo:512], lhsT=vh[sl][:, j, :], rhs=pT[pi][:, lo:512],
                                                  start=(j == 0), stop=(j == nj - 1)), r=[B_v[sl], B_pT[pi]],
                         w=[PSB[BO]])
                    P.op("pe", lambda e: e.matmul(bank(BDEN)[:, lo:512], lhsT=ones_b, rhs=pT[pi][:, lo:512],
                                                  start=(j == 0), stop=(j == nj - 1)), r=[B_const, B_pT[pi]],
                         w=[PSB[BDEN]])
                    if j == nj - 1:
                        fi = cnt["fin"] % 2
                        cnt["fin"] += 1
                        P.op("act", lambda e: e.activation(out=rden[fi], in_=bank(BDEN), func=AF.Ln),
                             r=[PSB[BDEN]], w=[B_rden[fi]])
                        P.op("act", lambda e: e.activation(out=rden[fi], in_=rden[fi], func=AF.Exp, scale=-1.0),
                             r=[B_rden[fi]], w=[B_rden[fi]])
                        P.op("dve", lambda e: e.tensor_tensor(out=og[fi], in0=rden[fi], in1=gt[sl][:, qs_:qs_ + 512],
                                                              op=ALU.mult), r=[B_rden[fi], B_gt[sl]], w=[B_og[fi]])
                        P.op("dve", lambda e: e.tensor_tensor(out=oh[sl][:, qs_:qs_ + 512], in0=bank(BO), in1=og[fi],
                                                              op=ALU.mult), r=[PSB[BO], B_og[fi]], w=[B_oh[sl]])
                        if qb == 3:
                            P.op("sp", lambda e: e.dma_start(out=oaT[h * 128:(h + 1) * 128, :], in_=oh[sl]),
                                 r=[B_oh[sl]], dma=True)

                return (A, Bf)

            import os as _os
            NH_A = int(_os.environ.get('NH_A', '16'))
            pend = [None]

            def run_unit(u):
                u[0]()
                if pend[0] is not None:
                    pend[0]()
                pend[0] = u[1]

            for st_ in prep_stages(0, 0):
                st_()
            for h in range(NH_A):
                U = attn_units_a(h, h % 2)
                Pn = prep_stages(h + 1, (h + 1) % 2) if h + 1 < NH_A else []
                kk = 0
                for idx, u in enumerate(U):
                    run_unit(u)
                    tgt = min(len(Pn), ((idx + 1) * len(Pn) + len(U) - 1) // len(U))
                    while kk < tgt:
                        Pn[kk]()
                        kk += 1
                while kk < len(Pn):
                    Pn[kk]()
                    kk += 1
            if pend[0] is not None:
                pend[0]()
            P.barrier()

        if stop_after >= 3:
            a = Alloc(0)
            kiT = a([T], BF16)
            qiT = a([16, 512], BF16)
            wi = a([4, 32], F32)
            wabs = a([4, 32], F32)
            wsgn = a([4, 32], F32)
            acc = [a([T], F32) for _ in range(2)]
            tmpr = [a([1024], BF16) for _ in range(2)]
            dsg = [a([32, 128], BF16) for _ in range(2)]
            junk = a([T], BF16)
            mrow = [a([T], BF16) for _ in range(2)]
            maskT = a([16, 512], BF16)
            qbT = a([16, 512], BF16)
            kbT = a([T], BF16)
            vb = a([16, 128], BF16)
            BThi = a([16, 256], BF16)
            BTlo = a([16, 256], BF16)
            b31 = a([16], F32)
            gqb = a([1], F32)
            gkb = a([1], F32)
            gtb = a([16, 512], BF16)
            ob_off = a.cur
            ob = a([16, 512], BF16)
            BT = sb(ob_off, [16, 256], F32)
            W0 = a([1], F32)
            Wtab = a([NITER], F32)
            mid = a([1], F32)
            cntv = a([1], F32)
            sgn = a([1], F32)
            thr = a([1], F32)
            pow2 = a([NITER], F32)
            negm = a([128], F32)
            t5s = a([16], F32)
            ohs = a([384], F32)
            F16 = a([384], F32)
            sqb = [a([512], BF16) for _ in range(2)]
            rs3 = [a([512], F32) for _ in range(2)]
            NPT3 = 4
            pT3v = [a([512], BF16) for _ in range(NPT3)]
            rden3v = [a([512], F32) for _ in range(2)]
            og3v = [a([512], F32) for _ in range(2)]
            assert a.cur <= CB, a.cur
            B_ki = Buf("kiT"); B_qi = Buf("qiT"); B_wi = Buf("wi")
            B_acc = [Buf("acc0"), Buf("acc1")]
            B_tmpr = [Buf("tmpr0"), Buf("tmpr1")]
            B_junk = Buf("junk")
            B_mrow = [Buf("mrow0"), Buf("mrow1")]
            B_dsg = [Buf("dsg0"), Buf("dsg1")]
            B_maskT = Buf("maskT")
            B_qb = Buf("qbT"); B_kb = Buf("kbT"); B_vb = Buf("vb"); B_BT = Buf("BT"); B_g3 = Buf("g3")
            B_gtb = Buf("gtb"); B_ob = B_BT; B_bis = Buf("bis"); B_t5 = Buf("t5")
            B_sqb = [Buf("sqb0"), Buf("sqb1")]
            B_rs3 = [Buf("rs30"), Buf("rs31")]
            B_pT3 = [Buf(f"pT3{i}") for i in range(NPT3)]
            B_tn = [Buf("tn0"), Buf("tn1")]
            B_rden3 = [Buf("rden30"), Buf("rden31")]
            B_og3 = [Buf("og30"), Buf("og31")]
            H32 = slice(0, 32)
            H16 = slice(0, 16)
            c3 = {"ds": 0, "tmp": 0, "lg": 0, "mr": 0, "tp": 0, "s": 0, "pt": 0, "tn": 0, "fin": 0, "acc": 0}

            P.op("sp", lambda e: e.dma_start(out=kiT, in_=kidxT), w=[B_ki], dma=True)
            P.op("sp", lambda e: e.dma_start(out=kbT, in_=projT[C_KB:C_KB + 128, :]), w=[B_kb], dma=True)
            P.op("sp", lambda e: e.dma_start(out=vb, in_=vb_s.rearrange("(j p) d -> p j d", p=128)), w=[B_vb], dma=True)
            P.op("sp", lambda e: e.dma_start(out=gqb, in_=g_qn_b_d.rearrange("(p o) -> p o", o=1)), w=[B_g3], dma=True)
            P.op("sp", lambda e: e.dma_start(out=gkb, in_=g_kn_b_d.rearrange("(p o) -> p o", o=1)), w=[B_g3], dma=True)
            P.op("sp", lambda e: e.dma_start(out=pow2, in_=c_pow2), w=[B_g3], dma=True)
            P.op("sp", lambda e: e.dma_start(out=negm, in_=c_negmask), w=[B_g3], dma=True)
            P.op("sp", lambda e: e.dma_start(out=t5s[H32], in_=t5_d), w=[B_t5], dma=True)
            P.op("sp", lambda e: e.dma_start(out=ohs[H32], in_=c_onehot), w=[B_t5], dma=True)
            P.op("sp", lambda e: e.dma_start(out=b31, in_=t5_d[31, :].partition_broadcast(128)), w=[B_g3], dma=True)
            P.op("dve", lambda e: e.tensor_scalar(out=gqb, in0=gqb, scalar1=128.0 ** -0.5, scalar2=None, op0=ALU.mult),
                 r=[B_g3], w=[B_g3])
            P.op("pe", lambda e: e.matmul(bank(7)[H16, 0:384], lhsT=t5s[H32], rhs=ohs[H32], start=True, stop=True),
                 r=[B_t5], w=[PSB[7]])
            P.op("dve", lambda e: e.tensor_copy(out=F16[H16], in_=bank(7)[H16, 0:384]), r=[PSB[7]], w=[B_t5])
            B_Xb = Buf("Xb")
            P.op("sp", lambda e: e.dma_start(out=Xb.rearrange("h (r m) -> h r m", m=384),
                                             in_=F16[H16].unsqueeze(1).broadcast_to([16, 129, 384])),
                 r=[B_t5], w=[B_Xb], dma=True)
            P.op("sp", lambda e: e.dma_start(out=BT, in_=bass.AP(Xb.tensor, 0, [[383, 128], [129 * 384, 16], [1, 256]])),
                 r=[B_Xb], w=[B_BT], dma=True)
            B_BTs = Buf("BTs")

            def bt_fix(h):
                P.op("dve", lambda e: e.tensor_scalar(out=BT[:, h, :], in0=BT[:, h, :], scalar1=b31[:, h:h + 1],
                                                      scalar2=None, op0=ALU.subtract), r=[B_BT, B_g3], w=[B_BT])

            for h in range(16):
                bt_fix(h)
            P.op("dve", lambda e: e.tensor_copy(out=BThi, in_=BT), r=[B_BT], w=[B_BTs])
            P.op("dve", lambda e: e.tensor_tensor(out=BTlo, in0=BT, in1=BThi, op=ALU.subtract), r=[B_BT, B_BTs],
                 w=[B_BTs])

            def rstd3(bankc, tsl, scale):
                P.op("act", lambda e: e.activation(out=rs3[tsl], in_=bank(bankc), func=AF.Ln, scale=scale,
                                                   bias=eps_t[:, 0:1]), r=[PSB[bankc], B_const], w=[B_rs3[tsl]])
                P.op("act", lambda e: e.activation(out=rs3[tsl], in_=rs3[tsl], func=AF.Exp, scale=-0.5),
                     r=[B_rs3[tsl]], w=[B_rs3[tsl]])

            def norm_block(src_ap, gvec_ap, Bsrc):
                tsl = c3["tmp"] % 2
                c3["tmp"] += 1
                P.op("act", lambda e: e.activation(out=sqb[tsl], in_=src_ap, func=AF.Square), r=[Bsrc],
                     w=[B_sqb[tsl]])
                P.op("pe", lambda e: e.matmul(bank(6), lhsT=ones_b, rhs=sqb[tsl], start=True, stop=True),
                     r=[B_sqb[tsl], B_const], w=[PSB[6]])
                rstd3(6, tsl, 1.0 / 128.0)
                P.op("dve", lambda e: e.scalar_tensor_tensor(out=src_ap, in0=src_ap, scalar=gvec_ap, in1=rs3[tsl],
                                                             op0=ALU.mult, op1=ALU.mult),
                     r=[Bsrc, B_rs3[tsl], B_g3], w=[Bsrc])

            for tb in range(4):
                norm_block(kbT[:, tb * 512:(tb + 1) * 512], gkb[:, 0:1], B_kb)

            def idx_part(qb, i, g):
                qt = qb * 4 + i
                nk = (qt + 1) * 128
                accv = acc[g % 2]
                Bacc = B_acc[g % 2]
                qsl = slice(i * 128, (i + 1) * 128)
                dsl = g % 2
                P.op("pool", lambda e: e.tensor_tensor(
                    out=dsg[dsl], in0=ident_f.unsqueeze(1).broadcast_to([128, 32, 128]),
                    in1=wsgn[:, i, :].unsqueeze(2).broadcast_to([128, 32, 128]), op=ALU.mult),
                     r=[B_const, B_wi], w=[B_dsg[dsl]])

                def mk(h, k0, wk, sbk):
                    c = h // 2
                    hp = slice((h % 2) * 64, (h % 2) * 64 + 64)
                    stt = {}

                    def L():
                        lg = c3["lg"] % 2
                        c3["lg"] += 1
                        stt["lg"] = lg
                        pb_ = bank(2 * lg, 2)
                        for off in range(0, wk, 512):
                            w_ = min(512, wk - off)
                            P.op("pe", lambda e, off=off, w_=w_: e.matmul(
                                pb_[:, off:off + w_], lhsT=qiT[hp, c, qsl], rhs=kiT[hp, k0 + off:k0 + off + w_],
                                start=True, stop=True), r=[B_qi, B_ki], w=[PSB[2 * lg + off // 512]])
                        rb = [PSB[2 * lg + o // 512] for o in range(0, wk, 512)]
                        P.op("act", lambda e: e.activation(out=tmpr[lg][:, 0:wk], in_=pb_[:, 0:wk], func=AF.Relu,
                                                           scale=wabs[:, i, h:h + 1]), r=rb + [B_wi], w=[B_tmpr[lg]])

                    def A():
                        lg = stt["lg"]
                        for off in range(0, wk, 512):
                            w_ = min(512, wk - off)
                            P.op("pe", lambda e, off=off, w_=w_: e.matmul(
                                bank(sbk + off // 512)[:, 0:w_], lhsT=dsg[dsl][:, h, :], rhs=tmpr[lg][:, off:off + w_],
                                start=(h == 0), stop=(h == 31)), r=[B_dsg[dsl], B_tmpr[lg]],
                                 w=[PSB[sbk + off // 512]])

                    return (L, A)

                for k0 in range(0, nk, 1024):
                    wk = min(1024, nk - k0)
                    sbk = 4 + 2 * ((k0 // 1024) % 2)
                    prev = None
                    for h in range(32):
                        u = mk(h, k0, wk, sbk)
                        u[0]()
                        if prev is not None:
                            prev[1]()
                        prev = u
                    prev[1]()
                    for off in range(0, wk, 512):
                        w_ = min(512, wk - off)
                        P.op("act", lambda e, off=off, w_=w_, k0=k0, sbk=sbk: e.activation(
                            out=accv[:, k0 + off:k0 + off + w_], in_=bank(sbk + off // 512)[:, 0:w_], func=AF.Copy),
                             r=[PSB[sbk + off // 512]], w=[Bacc])

            def bis_part(qb, i, g):
                qt = qb * 4 + i
                nk = (qt + 1) * 128
                accv = acc[g % 2]
                Bacc = B_acc[g % 2]
                P.op("dve", lambda e: e.tensor_reduce(out=W0[:, 0:1], in_=accv[:, 0:nk], axis=AX.X, op=ALU.max,
                                                      apply_absolute_value=True), r=[Bacc], w=[B_bis])
                P.op("dve", lambda e: e.tensor_scalar(out=W0, in0=W0, scalar1=1.001, scalar2=1e-6, op0=ALU.mult,
                                                      op1=ALU.add), r=[B_bis], w=[B_bis])
                P.op("dve", lambda e: e.tensor_scalar(out=Wtab, in0=pow2, scalar1=W0[:, 0:1], scalar2=None,
                                                      op0=ALU.mult), r=[B_bis, B_g3], w=[B_bis])
                P.op("dve", lambda e: e.tensor_tensor(out=accv[:, qt * 128:nk], in0=accv[:, qt * 128:nk], in1=negm,
                                                      op=ALU.add), r=[Bacc, B_g3], w=[Bacc])
                if qt >= 2:
                    P.op("dve", lambda e: e.memset(mid, 0.0), r=[B_bis], w=[B_bis])

                    def one_iter(it):
                        P.op("dve", lambda e: e.tensor_scalar(out=junk[:, 0:nk], in0=accv[:, 0:nk], scalar1=mid[:, 0:1],
                                                              scalar2=0.0, op0=ALU.is_ge, op1=ALU.add,
                                                              accum_out=cntv[:, 0:1]),
                             r=[Bacc, B_bis], w=[B_junk, B_bis])
                        P.op("dve", lambda e: e.tensor_scalar(out=sgn, in0=cntv, scalar1=256.0, scalar2=-0.5,
                                                              op0=ALU.is_ge, op1=ALU.add), r=[B_bis], w=[B_bis])
                        P.op("dve", lambda e: e.scalar_tensor_tensor(out=mid, in0=sgn, scalar=Wtab[:, it:it + 1],
                                                                     in1=mid, op0=ALU.mult, op1=ALU.add),
                             r=[B_bis], w=[B_bis])

                    for it in range(NITER):
                        one_iter(it)
                    P.op("dve", lambda e: e.scalar_tensor_tensor(out=thr, in0=Wtab[:, NITER - 1:NITER], scalar=-0.5,
                                                                 in1=mid, op0=ALU.mult, op1=ALU.add),
                         r=[B_bis], w=[B_bis])
                else:
                    P.op("dve", lambda e: e.memset(thr, -1e29), r=[B_bis], w=[B_bis])
                ms = g % 2
                P.op("dve", lambda e: e.tensor_scalar(out=mrow[ms][:, 0:nk], in0=accv[:, 0:nk], scalar1=thr[:, 0:1],
                                                      scalar2=None, op0=ALU.is_lt), r=[Bacc, B_bis], w=[B_mrow[ms]])

            def tr_part(qb, i, g):
                qt = qb * 4 + i
                ms = g % 2
                qsl = slice(i * 128, (i + 1) * 128)
                for j0 in range(0, qt + 1, 8):
                    n = min(8, qt + 1 - j0)
                    bk = 4 + c3["tp"] % 2
                    c3["tp"] += 1
                    for jj in range(n):
                        j = j0 + jj
                        P.op("pe", lambda e, jj=jj, j=j, bk=bk: e.transpose(
                            out=bank16(bk)[:, jj * 128:(jj + 1) * 128], in_=mrow[ms][:, j * 128:(j + 1) * 128],
                            identity=ident_b), r=[B_mrow[ms], B_const], w=[PSB[bk]])
                    P.op("act", lambda e, j0=j0, n=n, bk=bk: e.activation(
                        out=maskT[:, j0:j0 + n, qsl], in_=bank16(bk)[:, 0:n * 128].rearrange("p (a b) -> p a b", a=n),
                        func=AF.Copy), r=[PSB[bk]], w=[B_maskT])

            def make_unit_b(qb, h, j, nj):
                BS_ = (4, 5)
                BO_, BDEN_ = 6, 7
                lo = max(0, 128 * j - 512 * qb)
                c0 = 512 * qb + lo - 128 * j
                nn_ = max(0, min(256 - c0, 512 - lo))
                stt = {}

                def A():
                    bs = BS_[c3["s"] % 2]
                    c3["s"] += 1
                    pi = c3["pt"] % NPT3
                    c3["pt"] += 1
                    stt["pi"] = pi
                    P.op("pe", lambda e: e.matmul(bank(bs)[:, lo:512], lhsT=kbT[:, j * 128:(j + 1) * 128],
                                                  rhs=qbT[:, h, lo:512], start=True, stop=False),
                         r=[B_kb, B_qb], w=[PSB[bs]])
                    if nn_ > 0:
                        P.op("pe", lambda e: e.matmul(bank(bs)[:, lo:lo + nn_], lhsT=ident_b,
                                                      rhs=BThi[:, h, c0:c0 + nn_], start=False, stop=False),
                             r=[B_const, B_BTs], w=[PSB[bs]])
                        P.op("pe", lambda e: e.matmul(bank(bs)[:, lo:lo + nn_], lhsT=ident_b,
                                                      rhs=BTlo[:, h, c0:c0 + nn_], start=False, stop=False),
                             r=[B_const, B_BTs], w=[PSB[bs]])
                    P.op("pe", lambda e: e.matmul(bank(bs)[:, lo:512], lhsT=negI_b, rhs=maskT[:, j, lo:512],
                                                  start=False, stop=True), r=[B_const, B_maskT], w=[PSB[bs]])
                    P.op("act", lambda e: e.activation(out=pT3v[pi][:, lo:512], in_=bank(bs)[:, lo:512], func=AF.Exp,
                                                       bias=b31[:, h:h + 1]), r=[PSB[bs], B_g3], w=[B_pT3[pi]])

                def Bf():
                    pi = stt["pi"]
                    P.op("pe", lambda e: e.matmul(bank(BO_)[:, lo:512], lhsT=vb[:, j, :], rhs=pT3v[pi][:, lo:512],
                                                  start=(j == 0), stop=(j == nj - 1)), r=[B_vb, B_pT3[pi]],
                         w=[PSB[BO_]])
                    P.op("pe", lambda e: e.matmul(bank(BDEN_)[:, lo:512], lhsT=ones_b, rhs=pT3v[pi][:, lo:512],
                                                  start=(j == 0), stop=(j == nj - 1)), r=[B_const, B_pT3[pi]],
                         w=[PSB[BDEN_]])
                    if j == nj - 1:
                        fi = c3["fin"] % 2
                        c3["fin"] += 1
                        P.op("act", lambda e: e.activation(out=rden3v[fi], in_=bank(BDEN_), func=AF.Ln),
                             r=[PSB[BDEN_]], w=[B_rden3[fi]])
                        P.op("act", lambda e: e.activation(out=rden3v[fi], in_=rden3v[fi], func=AF.Exp, scale=-1.0),
                             r=[B_rden3[fi]], w=[B_rden3[fi]])
                        P.op("dve", lambda e: e.tensor_tensor(out=og3v[fi], in0=rden3v[fi], in1=gtb[:, h, :],
                                                              op=ALU.mult), r=[B_rden3[fi], B_gtb], w=[B_og3[fi]])
                        P.op("dve", lambda e: e.tensor_tensor(out=ob[:, h, :], in0=bank(BO_), in1=og3v[fi],
                                                              op=ALU.mult), r=[PSB[BO_], B_og3[fi]], w=[B_ob])

                return (A, Bf)

            pend3 = [None]

            def run_unit3(u):
                u[0]()
                if pend3[0] is not None:
                    pend3[0]()
                pend3[0] = u[1]

            def attn_b_all(qb):
                nj = 4 * qb + 4
                for h in range(16):
                    for j in range(nj):
                        run_unit3(make_unit_b(qb, h, j, nj))
                if pend3[0] is not None:
                    pend3[0]()
                    pend3[0] = None

            def load_idx(qb):
                qcs = slice(qb * 512, (qb + 1) * 512)
                P.op("sp", lambda e: e.dma_start(out=qiT, in_=projT[C_QI:C_QI + 2048, qcs].rearrange(
                    "(c p) t -> p c t", p=128)), w=[B_qi], dma=True)
                P.op("sp", lambda e: e.dma_start(out=wi, in_=widx_s[qcs, :].rearrange("(i p) h -> p i h", p=128)),
                     w=[B_wi], dma=True)
                P.op("act", lambda e: e.activation(out=wabs, in_=wi, func=AF.Abs, scale=32.0 ** -0.5), r=[B_wi],
                     w=[B_wi])
                P.op("act", lambda e: e.activation(out=wsgn, in_=wi, func=AF.Sign), r=[B_wi], w=[B_wi])

            def load_att(qb):
                qcs = slice(qb * 512, (qb + 1) * 512)
                P.op("sp", lambda e: e.dma_start(out=qbT, in_=projT[C_QB:C_QB + 2048, qcs].rearrange(
                    "(c p) t -> p c t", p=128)), w=[B_qb], dma=True)
                P.op("sp", lambda e: e.dma_start(out=gtb, in_=projT[C_GB:C_GB + 2048, qcs].rearrange(
                    "(c p) t -> p c t", p=128)), w=[B_gtb], dma=True)
                for h in range(16):
                    norm_block(qbT[:, h, :], gqb[:, 0:1], B_qb)

            def att_and_store(qb):
                qcs = slice(qb * 512, (qb + 1) * 512)
                attn_b_all(qb)
                P.op("sp", lambda e: e.dma_start(out=obT[:, qcs].rearrange("(h p) t -> p h t", p=128), in_=ob),
                     r=[B_ob], dma=True)

            load_idx(0)
            load_att(0)
            idx_part(0, 0, 0)
            bis_part(0, 0, 0)
            for g in range(1, 16):
                qb = g // 4
                if g % 4 == 0:
                    load_idx(qb)
                idx_part(qb, g % 4, g)
                bis_part(qb, g % 4, g)
                tr_part((g - 1) // 4, (g - 1) % 4, g - 1)
                if g % 4 == 0:
                    att_and_store(qb - 1)
                    load_att(qb)
            tr_part(3, 3, 15)
            att_and_store(3)
            P.barrier()

        if stop_after >= 4:
            a = Alloc(0)
            oaH = a([16, 1024], BF16)
            obH = a([16, 1024], BF16)
            s2_base = a.cur
            mT = a([32, 1024], BF16)
            s2b_base = a.cur
            pa = [a([16, 256], BF16) for _ in range(2)]
            pb = [a([16, 256], BF16) for _ in range(2)]
            sA = [a([2, 1024], BF16) for _ in range(2)]
            sB = [a([2, 1024], BF16) for _ in range(2)]
            u1 = [a([512], F32) for _ in range(2)]
            u2 = [a([512], F32) for _ in range(2)]
            assert a.cur <= CB, a.cur
            a2 = Alloc(0)
            wo = [a2([32, 512], BF16) for _ in range(2)]
            assert a2.cur <= s2_base
            a3 = Alloc(s2b_base)
            xin = [a3([512], F32) for _ in range(2)]
            oout = [a3([512], F32) for _ in range(2)]
            B_oaH = Buf("oaH"); B_obH = Buf("obH"); B_mT = [Buf(f"mT{i}") for i in range(32)]
            B_pa = [Buf("pa0"), Buf("pa1")]; B_pb = [Buf("pb0"), Buf("pb1")]
            B_sA = [Buf("sA0"), Buf("sA1")]; B_sB = [Buf("sB0"), Buf("sB1")]
            B_u1 = [Buf("u10"), Buf("u11")]; B_u2 = [Buf("u20"), Buf("u21")]
            B_xin = [Buf("xin0"), Buf("xin1")]; B_oout = [Buf("oo0"), Buf("oo1")]
            B_wo = [Buf("wo0"), Buf("wo1")]
            c4 = {"u": 0, "b": 0, "x": 0}

            def merged_loads(th, fbp):
                sl = fbp % 2
                tcs = slice(th * 1024, (th + 1) * 1024)
                cs = slice(fbp * 256, (fbp + 1) * 256)
                P.op("pool", lambda e: e.dma_start(out=pa[sl], in_=p_a_d[:, cs].rearrange("(k p) c -> p k c", p=128)),
                     w=[B_pa[sl]], dma=True)
                P.op("pool", lambda e: e.dma_start(out=pb[sl], in_=p_b_d[:, cs].rearrange("(k p) c -> p k c", p=128)),
                     w=[B_pb[sl]], dma=True)
                P.op("sp", lambda e: e.dma_start(out=sA[sl], in_=projT[C_MA + fbp * 256:C_MA + (fbp + 1) * 256, tcs]
                                                 .rearrange("(f p) t -> p f t", p=128)), w=[B_sA[sl]], dma=True)
                P.op("sp", lambda e: e.dma_start(out=sB[sl], in_=projT[C_MB + fbp * 256:C_MB + (fbp + 1) * 256, tcs]
                                                 .rearrange("(f p) t -> p f t", p=128)), w=[B_sB[sl]], dma=True)

            def merged_compute(th, fbp):
                sl = fbp % 2

                def sub(f2, tb2):
                    fb = fbp * 2 + f2
                    bsl = c4["b"] % 2
                    c4["b"] += 1
                    us = c4["u"] % 2
                    c4["u"] += 1
                    ts_ = slice(tb2 * 512, (tb2 + 1) * 512)
                    fc = slice(f2 * 128, (f2 + 1) * 128)
                    for k in range(16):
                        P.op("pe", lambda e, k=k: e.matmul(bank(bsl), lhsT=pa[sl][:, k, fc], rhs=oaH[:, k, ts_],
                                                           start=(k == 0), stop=(k == 15)), r=[B_pa[sl], B_oaH],
                             w=[PSB[bsl]])
                    for k in range(16):
                        P.op("pe", lambda e, k=k: e.matmul(bank(2 + bsl), lhsT=pb[sl][:, k, fc], rhs=obH[:, k, ts_],
                                                           start=(k == 0), stop=(k == 15)), r=[B_pb[sl], B_obH],
                             w=[PSB[2 + bsl]])
                    P.op("dve", lambda e: e.tensor_tensor(out=u1[us], in0=bank(bsl), in1=sA[sl][:, f2, ts_],
                                                          op=ALU.mult), r=[PSB[bsl], B_sA[sl]], w=[B_u1[us]])
                    P.op("dve", lambda e: e.tensor_tensor(out=u2[us], in0=bank(2 + bsl), in1=sB[sl][:, f2, ts_],
                                                          op=ALU.mult), r=[PSB[2 + bsl], B_sB[sl]], w=[B_u2[us]])
                    P.op("dve", lambda e: e.tensor_tensor(out=mT[:, fb, ts_], in0=u1[us], in1=u2[us], op=ALU.add),
                         r=[B_u1[us], B_u2[us]], w=[B_mT[fb]])

                for f2 in range(2):
                    for tb2 in range(2):
                        sub(f2, tb2)

            def final_block(th, cbk):
                sl = cbk % 2
                ccs = slice(cbk * 512, (cbk + 1) * 512)
                P.op("pool", lambda e: e.dma_start(out=wo[sl], in_=w_o_d[:, ccs].rearrange("(f p) c -> p f c", p=128)),
                     w=[B_wo[sl]], dma=True)

                def tok(tt):
                    bk = 4 + c4["b"] % 4
                    c4["b"] += 1
                    xs = c4["x"] % 2
                    c4["x"] += 1
                    r0 = th * 1024 + tt * 128
                    P.op("sp", lambda e: e.dma_start(out=xin[xs], in_=x_d[r0:r0 + 128, ccs]), w=[B_xin[xs]], dma=True)
                    for f in range(32):
                        P.op("pe", lambda e, f=f: e.matmul(bank(bk), lhsT=mT[:, f, tt * 128:(tt + 1) * 128],
                                                           rhs=wo[sl][:, f, :], start=(f == 0), stop=(f == 31)),
                             r=[B_wo[sl], B_mT[f]], w=[PSB[bk]])
                    P.op("dve", lambda e: e.tensor_tensor(out=oout[xs], in0=bank(bk), in1=xin[xs], op=ALU.add),
                         r=[PSB[bk], B_xin[xs]], w=[B_oout[xs]])
                    P.op("sp", lambda e: e.dma_start(out=out_d[r0:r0 + 128, ccs], in_=oout[xs]), r=[B_oout[xs]],
                         dma=True)

                for tt in range(8):
                    tok(tt)

            for th in range(2):
                tcs = slice(th * 1024, (th + 1) * 1024)
                P.op("sp", lambda e, tcs=tcs: e.dma_start(out=oaH, in_=oaT[:, tcs].rearrange("(k p) t -> p k t", p=128)),
                     w=[B_oaH], dma=True)
                P.op("sp", lambda e, tcs=tcs: e.dma_start(out=obH, in_=obT[:, tcs].rearrange("(k p) t -> p k t", p=128)),
                     w=[B_obH], dma=True)
                merged_loads(th, 0)
                for fbp in range(16):
                    if fbp + 1 < 16:
                        merged_loads(th, fbp + 1)
                    merged_compute(th, fbp)
                P.barrier()
                for cbk in range(8):
                    final_block(th, cbk)
                P.barrier()

        P.finalize()
        keys = P.sem_keys()
        sems = {}
        for i, k in enumerate(keys):
            sems[k] = st.enter_context(nc.semaphore(f"s{i}"))
        block = st.enter_context(nc.Block())

        @block.tensor
        def _(e):
            P.emit_engine("pe", e, sems)

        @block.scalar
        def _(e):
            P.emit_engine("act", e, sems)

        @block.vector
        def _(e):
            P.emit_engine("dve", e, sems)

        @block.gpsimd
        def _(e):
            P.emit_engine("pool", e, sems)

        @block.sync
        def _(e):
            P.emit_engine("sp", e, sems, final_wait=True)
    return nc


def _in_maps(inputs):
    maps = []
    shared = {k: np.ascontiguousarray(np.asarray(v, dtype=np.float32)) for k, v in inputs.items()
              if k not in ("x", "positions")}
    for k, v in CONSTS.items():
        shared["c_" + k] = v
    x = np.asarray(inputs["x"], dtype=np.float32)
    pos = np.asarray(inputs["positions"], dtype=np.int32)
    for b in range(x.shape[0]):
        m = dict(shared)
        m["x"] = np.ascontiguousarray(x[b])
        m["positions"] = np.ascontiguousarray(pos[b])
        maps.append(m)
    return maps


_NC_CACHE = {}


def kernel(**inputs):
    if "nc" not in _NC_CACHE:
        _NC_CACHE["nc"] = build_nc()
    nc = _NC_CACHE["nc"]
    maps = _in_maps(inputs)
    res = run_bass_kernel_spmd(nc, maps, core_ids=list(range(len(maps))))
    return np.stack([r["out"] for r in res.results], axis=0).astype(np.float32)
```

```python
import math
import numpy as np
import concourse.bass as bass
import concourse.mybir as mybir
from concourse.bass_utils import run_bass_kernel_spmd

F32 = mybir.dt.float32
BF16 = mybir.dt.bfloat16
I32 = mybir.dt.int32
U8 = mybir.dt.uint8
AF = mybir.ActivationFunctionType
ALU = mybir.AluOpType
AX = mybir.AxisListType

D = 4096
T = 2048
NT = 16
IN_COLS = 18336
EPS = 1e-6
C_CQ, C_CKV, C_KR, C_QB, C_KB, C_VB, C_QI, C_KI, C_WI, C_GA, C_GB, C_MA, C_MB = (
    0, 1024, 1536, 1600, 3648, 3776, 3904, 5952, 6016, 6048, 8096, 10144, 14240)
NITER = 16
SZ = {F32: 4, BF16: 2, I32: 4, U8: 1}


class Buf:
    __slots__ = ("name", "w", "r")

    def __init__(self, name):
        self.name = name
        self.w = None
        self.r = {}


class Op:
    __slots__ = ("eng", "fn", "deps", "flag", "dma", "tok", "n")

    def __init__(self, eng, fn, dma):
        self.eng = eng
        self.fn = fn
        self.deps = set()
        self.flag = False
        self.dma = dma
        self.tok = None
        self.n = None


class Prog:
    ENG = ("pe", "act", "dve", "pool", "sp")
    NDSEM = 8
    CH = 30000

    def __init__(self):
        self.ops = []
        self.barrier_deps = set()
        self.last = {}
        self.dma_hist = {"sp": [], "pool": [], "act": []}

    def op(self, eng, fn, r=(), w=(), dma=False):
        idx = len(self.ops)
        o = Op(eng, fn, dma)
        deps = set(self.barrier_deps)
        for b in r:
            if b.w is not None:
                deps.add(b.w)
        for b in w:
            if b.w is not None:
                deps.add(b.w)
            deps.update(b.r.values())
        key = ("dma", eng, idx) if dma else eng
        for b in r:
            b.r[key if not dma else ("dma", idx)] = idx
        for b in w:
            b.w = idx
            b.r = {}
        if eng == "pe" and not dma:
            deps = {d for d in deps if not (self.ops[d].eng == "pe" and not self.ops[d].dma)}
        o.deps = deps
        self.ops.append(o)
        if dma:
            self.dma_hist[eng].append(idx)
        else:
            self.last[eng] = idx
        return idx

    def barrier(self):
        deps = set(self.last.values())
        for q, h in self.dma_hist.items():
            deps.update(h[-self.NDSEM:])
        self.barrier_deps = deps

    def finalize(self):
        for o in self.ops:
            for d in o.deps:
                self.ops[d].flag = True
        cnt = {e: 0 for e in self.ENG}
        dcnt = {"sp": 0, "pool": 0, "act": 0}
        for o in self.ops:
            if o.dma:
                o.n = dcnt[o.eng]
                dcnt[o.eng] += 1
                o.tok = (("d", o.eng, o.n % self.NDSEM), 16 * (o.n // self.NDSEM + 1))
            elif o.flag:
                k = cnt[o.eng]
                cnt[o.eng] += 1
                o.tok = (("c", o.eng, k // self.CH), k % self.CH + 1)
        self.cnt = cnt
        self.dcnt = dcnt

    def sem_keys(self):
        keys = []
        for e in self.ENG:
            for i in range(max(1, (self.cnt[e] + self.CH - 1) // self.CH)):
                keys.append(("c", e, i))
        for q in ("sp", "pool", "act"):
            if self.dcnt[q]:
                for i in range(self.NDSEM):
                    keys.append(("d", q, i))
        return keys

    def emit_engine(self, ename, eng, sems, final_wait=False):
        known = {}
        for o in self.ops:
            if o.eng != ename:
                continue
            waits = {}
            for d in o.deps:
                k, v = self.ops[d].tok
                if waits.get(k, 0) < v:
                    waits[k] = v
            if o.dma and o.n >= self.NDSEM:
                k = ("d", o.eng, o.n % self.NDSEM)
                v = 16 * (o.n // self.NDSEM)
                if waits.get(k, 0) < v:
                    waits[k] = v
            for k, v in waits.items():
                if known.get(k, 0) < v:
                    eng.wait_ge(sems[k], v)
                    known[k] = v
            ins = o.fn(eng)
            if o.dma:
                ins.then_inc(sems[o.tok[0]], 16)
            elif o.flag:
                ins.then_inc(sems[o.tok[0]], 1)
        if final_wait:
            for q in ("sp", "pool", "act"):
                n = self.dcnt[q]
                for i in range(min(n, self.NDSEM)):
                    tot = (n - i + self.NDSEM - 1) // self.NDSEM
                    k = ("d", q, i)
                    if known.get(k, 0) < 16 * tot:
                        eng.wait_ge(sems[k], 16 * tot)


def _t5_bucket_np(d):
    n = np.maximum(d, 0)
    nf = np.maximum(n, 1).astype(np.float32)
    large = 16 + (np.log(nf / np.float32(16)) / np.float32(math.log(128 / 16)) * np.float32(16)).astype(np.int32)
    large = np.minimum(large, 31)
    return np.where(n < 16, n, large)


def _consts():
    c = {}
    c["ident"] = np.eye(128, dtype=np.float32)
    m = np.arange(384)
    bucket = _t5_bucket_np(np.where(m < 256, m, 0))
    c["onehot"] = (bucket[None, :] == np.arange(32)[:, None]).astype(np.float32)
    half = 32
    inv = (np.float32(10000.0) ** (-np.arange(half, dtype=np.float32) / np.float32(half))).astype(np.float32)
    c["inv64"] = np.concatenate([inv, inv]).reshape(64, 1).astype(np.float32)
    P = np.zeros((64, 64), np.float32)
    for i in range(32):
        P[i, i + 32] = -1.0
        P[i + 32, i] = 1.0
    c["rotT"] = np.ascontiguousarray(P.T)
    s = np.arange(128)[:, None]
    cc = np.arange(896)[None, :]
    c["cmask"] = (cc - 384 >= s).astype(np.float32)
    c["negmask"] = np.where(np.arange(128)[None, :] <= np.arange(128)[:, None], 0.0, -1e30).astype(np.float32)
    c["pow2"] = np.tile((0.5 ** np.arange(NITER)).astype(np.float32)[None, :], (128, 1))
    return c


CONSTS = _consts()


def build_nc(stop_after=99, debug=False):
    nc = bass.Bass("TRN2", target_bir_lowering=False)
    P = Prog()

    def din(name, shape, dt=F32):
        return nc.dram_tensor(name, list(shape), dt, kind="ExternalInput").ap()

    x_d = din("x", [T, D])
    pos_d = din("positions", [T], I32)
    g_pre_d = din("g_pre", [D])
    w_in_d = din("w_in", [D, IN_COLS])
    g_q_lat_d = din("g_q_lat", [1024])
    g_kv_lat_d = din("g_kv_lat", [512])
    w_uq_d = din("w_uq", [1024, 3072])
    w_ukv_d = din("w_ukv", [512, 4096])
    g_qn_a_d = din("g_qn_a", [192])
    g_kn_a_d = din("g_kn_a", [192])
    g_qn_b_d = din("g_qn_b", [128])
    g_kn_b_d = din("g_kn_b", [128])
    t5_d = din("t5_bias", [32, 16])
    p_a_d = din("p_a", [2048, D])
    p_b_d = din("p_b", [2048, D])
    w_o_d = din("w_o", [D, D])
    c_ident = din("c_ident", [128, 128])
    c_onehot = din("c_onehot", [32, 384])
    c_inv64 = din("c_inv64", [64, 1])
    c_rotT = din("c_rotT", [64, 64])
    c_cmask = din("c_cmask", [128, 896])
    c_negmask = din("c_negmask", [128, 128])
    c_pow2 = din("c_pow2", [128, NITER])

    out_d = nc.dram_tensor("out", [T, D], F32, kind="ExternalOutput").ap()

    def dscr(name, shape, dt):
        kind = "ExternalOutput" if debug else "Internal"
        return nc.dram_tensor(name, list(shape), dt, kind=kind).ap()

    projT = dscr("projT", [IN_COLS, T], BF16)
    kidxT = dscr("kidxT", [128, T], BF16)
    vb_s = dscr("vb_s", [T, 128], BF16)
    widx_s = dscr("widx_s", [T, 32], F32)
    oaT = dscr("oaT", [2048, T], BF16)
    obT = dscr("obT", [2048, T], BF16)
    Xb = dscr("Xb", [16, 129 * 384], F32)

    import contextlib
    st = contextlib.ExitStack()
    with st:
        arena = st.enter_context(nc.sbuf_tensor("arena", [128, 200 * 1024], U8))
        psum = st.enter_context(nc.psum_tensor("psum", [128, 4096], F32))

        def sb(off, shape, dt, parts=128, p0=0):
            n = int(np.prod(shape)) * SZ[dt]
            assert off + n <= 200 * 1024, (off, n)
            ap = arena[p0:p0 + parts, off:off + n].bitcast(dt)
            if len(shape) == 2:
                ap = ap.rearrange("p (a b) -> p a b", a=shape[0])
            elif len(shape) == 3:
                ap = ap.rearrange("p (a b c) -> p a b c", a=shape[0], b=shape[1])
            return ap

        def bank(i, n=1):
            return psum[:, i * 512:(i + n) * 512]

        def bank16(i):
            return psum[:, i * 512:(i + 1) * 512].bitcast(BF16)

        PSB = [Buf(f"ps{i}") for i in range(8)]

        class Alloc:
            def __init__(self, base=0):
                self.cur = base

            def __call__(self, shape, dt, parts=128):
                n = int(np.prod(shape)) * SZ[dt]
                off = self.cur
                self.cur = (off + n + 63) // 64 * 64
                return sb(off, shape, dt, parts)

        def dump(name, ap, reads, parts=128):
            if not debug:
                return
            shp = [parts] + list(ap.shape[1:])
            dt_ = nc.dram_tensor("dbg_" + name, shp, ap.dtype, kind="ExternalOutput").ap()
            P.op("sp", lambda e: e.dma_start(out=dt_, in_=ap), r=reads, dma=True)

        CB = 192 * 1024
        ca = Alloc(CB)
        ident_f = ca([128], F32)
        ident_b = ca([128], BF16)
        ones_b = ca([128], BF16)
        eps_t = ca([4], F32)
        negI_b = ca([128], BF16)
        B_const = Buf("consts")
        P.op("sp", lambda e: e.dma_start(out=ident_f, in_=c_ident), w=[B_const], dma=True)
        P.op("dve", lambda e: e.tensor_copy(out=ident_b, in_=ident_f), r=[B_const], w=[B_const])
        P.op("dve", lambda e: e.memset(ones_b, 1.0), w=[B_const])
        P.op("dve", lambda e: e.memset(eps_t, EPS), w=[B_const])
        P.op("dve", lambda e: e.tensor_scalar(out=negI_b, in0=ident_f, scalar1=-30000.0, scalar2=None, op0=ALU.mult),
             r=[B_const], w=[B_const])
        assert ca.cur <= 200 * 1024

        a = Alloc(0)
        hnT = a([32, T], BF16)
        B_hnT = [Buf(f"hnT{i}") for i in range(4)]
        p1_base = a.cur
        xt = [a([D], F32) for _ in range(2)]
        B_xt = [Buf("xt0"), Buf("xt1")]
        gbc = a([D], F32)
        B_gbc = Buf("gbc")
        hn_tm = a([D], BF16)
        B_hn = Buf("hn_tm")
        ss = a([NT], F32)
        rstd0 = a([NT], F32)
        B_ss = [Buf(f"ss{i}") for i in range(NT)]
        assert a.cur <= CB, a.cur

        P.op("sp", lambda e: e.dma_start(out=gbc, in_=g_pre_d.partition_broadcast(128)), w=[B_gbc], dma=True)
        for tt in range(NT):
            sl = tt % 2
            P.op("sp", lambda e, tt=tt, sl=sl: e.dma_start(out=xt[sl], in_=x_d[tt * 128:(tt + 1) * 128, :]),
                 w=[B_xt[sl]], dma=True)
            P.op("dve", lambda e, tt=tt, sl=sl: e.scalar_tensor_tensor(
                out=hn_tm, in0=xt[sl], scalar=1.0, in1=xt[sl], op0=ALU.mult, op1=ALU.mult,
                accum_out=ss[:, tt:tt + 1]), r=[B_xt[sl]], w=[B_hn, B_ss[tt]])
            P.op("act", lambda e, tt=tt: e.activation(out=rstd0[:, tt:tt + 1], in_=ss[:, tt:tt + 1], func=AF.Sqrt,
                                                      scale=1.0 / D, bias=eps_t[:, 0:1]),
                 r=[B_ss[tt], B_const], w=[B_ss[tt]])
            P.op("dve", lambda e, tt=tt: e.reciprocal(out=rstd0[:, tt:tt + 1], in_=rstd0[:, tt:tt + 1]),
                 r=[B_ss[tt]], w=[B_ss[tt]])
            P.op("dve", lambda e, tt=tt, sl=sl: e.scalar_tensor_tensor(
                out=hn_tm, in0=xt[sl], scalar=rstd0[:, tt:tt + 1], in1=gbc, op0=ALU.mult, op1=ALU.mult),
                 r=[B_xt[sl], B_ss[tt], B_gbc], w=[B_hn])
            for q in range(4):
                bk = (tt * 4 + q) % 8
                for j in range(8):
                    c = q * 8 + j
                    P.op("pe", lambda e, bk=bk, j=j, c=c: e.transpose(
                        out=bank16(bk)[:, j * 128:(j + 1) * 128], in_=hn_tm[:, c * 128:(c + 1) * 128],
                        identity=ident_b), r=[B_hn, B_const], w=[PSB[bk]])
                P.op("act", lambda e, bk=bk, q=q, tt=tt: e.activation(
                    out=hnT[:, q * 8:(q + 1) * 8, tt * 128:(tt + 1) * 128],
                    in_=bank16(bk).rearrange("p (a b) -> p a b", a=8), func=AF.Copy),
                     r=[PSB[bk]], w=[B_hnT[tt // 4]])
        P.barrier()

        a = Alloc(p1_base)
        WB = [a([32, 256], BF16) for _ in range(2)]
        B_WB = [Buf("WB0"), Buf("WB1")]
        SG = [a([T], BF16) for _ in range(2)]
        B_SG = [Buf("SG0"), Buf("SG1")]
        WT = a([32, 160], BF16)
        B_WT = Buf("WT")
        SGv = [a([128], BF16) for _ in range(2)]
        SGw = [a([32], F32) for _ in range(2)]
        B_SGv = [Buf("SGv0"), Buf("SGv1")]
        assert a.cur <= CB, a.cur

        groups = []

        def add_range(c0, c1, act):
            c = c0
            while c < c1:
                wdt = min(256, c1 - c)
                blks = []
                for o in range(0, wdt, 128):
                    blks.append((o, min(128, wdt - o), "proj", c + o, act))
                groups.append((c, wdt, blks))
                c += wdt

        add_range(C_CQ, C_KR, AF.Copy)
        groups.append((C_KR, 64, [(0, 64, "proj", C_KR, AF.Copy)]))
        add_range(C_QB, C_VB, AF.Copy)
        add_range(C_QI, C_KI, AF.Copy)
        groups.append((C_KI, 64, [(0, 128, "kidx", 0, AF.Copy)]))
        add_range(C_GA, C_MA, AF.Silu)
        add_range(C_MA, IN_COLS, AF.Sigmoid)

        def w_src(c0, wdt):
            return w_in_d[:, c0:c0 + wdt].rearrange("(k p) c -> p k c", p=128)

        def issue_load(gi):
            c0, wdt, blks = groups[gi]
            sl = gi % 2
            if blks[0][2] == "kidx":
                P.op("pool", lambda e: e.dma_start(out=WB[sl][:, :, 0:64], in_=w_src(c0, 64)), w=[B_WB[sl]], dma=True)
                P.op("pool", lambda e: e.dma_start(out=WB[sl][:, :, 64:128], in_=w_src(c0, 64)), w=[B_WB[sl]],
                     dma=True)
            else:
                P.op("pool", lambda e: e.dma_start(out=WB[sl][:, :, 0:wdt], in_=w_src(c0, wdt)), w=[B_WB[sl]],
                     dma=True)

        P.op("pool", lambda e: e.dma_start(out=WT[:, :, 0:128], in_=w_src(C_VB, 128)), w=[B_WT], dma=True)
        P.op("pool", lambda e: e.dma_start(out=WT[:, :, 128:160], in_=w_src(C_WI, 32)), w=[B_WT], dma=True)
        issue_load(0)
        issue_load(1)
        nblk = 0
        ngroups = len(groups) if stop_after >= 1 else 0
        for gi in range(ngroups):
            c0, wdt, blks = groups[gi]
            sl = gi % 2
            for (o, M, kind, row, act) in blks:
                bs = (nblk % 2) * 4
                sg = nblk % 2
                nblk += 1
                for k in range(32):
                    for tb in range(4):
                        P.op("pe", lambda e, k=k, tb=tb, bs=bs, o=o, M=M, sl=sl: e.matmul(
                            bank(bs + tb)[0:M, :], lhsT=WB[sl][:, k, o:o + M], rhs=hnT[:, k, tb * 512:(tb + 1) * 512],
                            start=(k == 0), stop=(k == 31)), r=[B_WB[sl], B_hnT[tb]], w=[PSB[bs + tb]])
                for tb in range(4):
                    P.op("act", lambda e, tb=tb, bs=bs, M=M, sg=sg, act=act: e.activation(
                        out=SG[sg][0:M, tb * 512:(tb + 1) * 512], in_=bank(bs + tb)[0:M, :], func=act),
                         r=[PSB[bs + tb]], w=[B_SG[sg]])
                if kind == "proj":
                    P.op("sp", lambda e, sg=sg, M=M, row=row: e.dma_start(out=projT[row:row + M, :], in_=SG[sg][0:M, :]),
                         r=[B_SG[sg]], dma=True)
                else:
                    P.op("sp", lambda e, sg=sg: e.dma_start(out=kidxT, in_=SG[sg]), r=[B_SG[sg]], dma=True)
            if gi + 2 < ngroups:
                issue_load(gi + 2)
        if stop_after >= 1:
            for tt in range(NT):
                bk = tt % 2
                sg = tt % 2
                for k in range(32):
                    P.op("pe", lambda e, k=k, tt=tt, bk=bk: e.matmul(
                        bank(bk)[:, 0:160], lhsT=hnT[:, k, tt * 128:(tt + 1) * 128], rhs=WT[:, k, :],
                        start=(k == 0), stop=(k == 31)), r=[B_WT, B_hnT[tt // 4]], w=[PSB[bk]])
                P.op("act", lambda e, bk=bk, sg=sg: e.activation(out=SGv[sg], in_=bank(bk)[:, 0:128], func=AF.Copy),
                     r=[PSB[bk]], w=[B_SGv[sg]])
                P.op("act", lambda e, bk=bk, sg=sg: e.activation(out=SGw[sg], in_=bank(bk)[:, 128:160], func=AF.Copy),
                     r=[PSB[bk]], w=[B_SGv[sg]])
                P.op("sp", lambda e, sg=sg, tt=tt: e.dma_start(out=vb_s[tt * 128:(tt + 1) * 128, :], in_=SGv[sg]),
                     r=[B_SGv[sg]], dma=True)
                P.op("sp", lambda e, sg=sg, tt=tt: e.dma_start(out=widx_s[tt * 128:(tt + 1) * 128, :], in_=SGw[sg]),
                     r=[B_SGv[sg]], dma=True)
        P.barrier()


        if stop_after >= 2:
            a = Alloc(0)
            cqT = a([8, T], BF16)
            ckvT = a([4, T], BF16)
            krb = a([T], BF16)
            Ct = a([T], F32)
            St = a([T], F32)
            Rk = a([T], F32)
            krsq = a([T], BF16)
            gql = a([8], F32)
            gkl = a([4], F32)
            gq_n = a([1], F32)
            gq_r = a([1], F32)
            gk_n = a([1], F32)
            gk_r = a([1], F32)
            inv64 = a([1], F32)
            rotT = a([64], F32)
            cm_f = a([896], F32)
            cm_b = a([896], BF16)
            B_lat = [Buf(f"lat{i}") for i in range(4)]
            B_kr = Buf("kr")
            B_g = Buf("g2")
            B_tab = Buf("tab")
            ph_base = a.cur
            posi = a([T], I32)
            posf = a([T], F32)
            ang = a([T], F32)
            tq = a([T], F32)
            nn = a([T], F32)
            B_tmp = Buf("tmp2")
            assert a.cur <= CB, a.cur
            H64 = slice(0, 64)

            def gvec(dst, src, n0, n1, eng="sp"):
                P.op(eng, lambda e: e.dma_start(out=dst, in_=src[n0:n1].rearrange("(p o) -> p o", o=1)), w=[B_g],
                     dma=True)

            P.op("sp", lambda e: e.dma_start(out=cqT, in_=projT[0:1024, :].rearrange("(k p) t -> p k t", p=128)),
                 w=B_lat, dma=True)
            P.op("sp", lambda e: e.dma_start(out=ckvT, in_=projT[1024:1536, :].rearrange("(k p) t -> p k t", p=128)),
                 w=B_lat, dma=True)
            P.op("sp", lambda e: e.dma_start(out=krb[H64], in_=projT[C_KR:C_KR + 64, :]), w=[B_kr], dma=True)
            P.op("sp", lambda e: e.dma_start(out=gql, in_=g_q_lat_d.rearrange("(k p) -> p k", p=128),
                                            allow_slow_non_contiguous=True), w=[B_g], dma=True)
            P.op("sp", lambda e: e.dma_start(out=gkl, in_=g_kv_lat_d.rearrange("(k p) -> p k", p=128),
                                            allow_slow_non_contiguous=True), w=[B_g], dma=True)
            gvec(gq_n, g_qn_a_d, 0, 128)
            gvec(gq_r[H64], g_qn_a_d, 128, 192)
            gvec(gk_n, g_kn_a_d, 0, 128)
            gvec(gk_r[H64], g_kn_a_d, 128, 192)
            P.op("sp", lambda e: e.dma_start(out=inv64[H64], in_=c_inv64), w=[B_g], dma=True)
            P.op("sp", lambda e: e.dma_start(out=rotT[H64], in_=c_rotT), w=[B_g], dma=True)
            P.op("sp", lambda e: e.dma_start(out=cm_f, in_=c_cmask), w=[B_g], dma=True)
            P.op("sp", lambda e: e.dma_start(out=posi[H64], in_=pos_d.partition_broadcast(64)), w=[B_tmp], dma=True)
            P.op("dve", lambda e: e.tensor_scalar(out=cm_b, in0=cm_f, scalar1=-1.0, scalar2=1.0, op0=ALU.mult,
                                                  op1=ALU.add), r=[B_g], w=[B_g])
            qs = 192.0 ** -0.5
            P.op("dve", lambda e: e.tensor_scalar(out=gq_n, in0=gq_n, scalar1=qs, scalar2=None, op0=ALU.mult),
                 r=[B_g], w=[B_g])
            P.op("dve", lambda e: e.tensor_scalar(out=gq_r[H64], in0=gq_r[H64], scalar1=qs, scalar2=None, op0=ALU.mult),
                 r=[B_g], w=[B_g])
            TWO_PI_INV = float(np.float32(1.0 / (2 * math.pi)))
            MAGIC = 12582912.0
            C1 = 6.28125
            C2 = float(2 * math.pi - 6.28125)
            PI_LO = 3.1415925
            P.op("dve", lambda e: e.tensor_copy(out=posf[H64], in_=posi[H64]), r=[B_tmp], w=[B_tmp])
            P.op("dve", lambda e: e.tensor_scalar(out=ang[H64], in0=posf[H64], scalar1=inv64[H64, 0:1], scalar2=None,
                                                  op0=ALU.mult), r=[B_tmp, B_g], w=[B_tmp])
            for which, dst in (("sin", St), ("cos", Ct)):
                if which == "cos":
                    P.op("dve", lambda e: e.tensor_scalar(out=ang[H64], in0=ang[H64], scalar1=float(math.pi / 2),
                                                          scalar2=None, op0=ALU.add), r=[B_tmp], w=[B_tmp])
                P.op("dve", lambda e: e.tensor_scalar(out=tq[H64], in0=ang[H64], scalar1=TWO_PI_INV, scalar2=MAGIC,
                                                      op0=ALU.mult, op1=ALU.add), r=[B_tmp], w=[B_tmp])
                P.op("dve", lambda e: e.tensor_scalar(out=nn[H64], in0=tq[H64], scalar1=-MAGIC, scalar2=None,
                                                      op0=ALU.add), r=[B_tmp], w=[B_tmp])
                P.op("dve", lambda e: e.scalar_tensor_tensor(out=tq[H64], in0=nn[H64], scalar=-C1, in1=ang[H64],
                                                             op0=ALU.mult, op1=ALU.add), r=[B_tmp], w=[B_tmp])
                P.op("dve", lambda e: e.scalar_tensor_tensor(out=tq[H64], in0=nn[H64], scalar=-C2, in1=tq[H64],
                                                             op0=ALU.mult, op1=ALU.add), r=[B_tmp], w=[B_tmp])
                P.op("dve", lambda e: e.tensor_scalar(out=tq[H64], in0=tq[H64], scalar1=-PI_LO, scalar2=PI_LO,
                                                      op0=ALU.max, op1=ALU.min), r=[B_tmp], w=[B_tmp])
                P.op("act", lambda e, dst=dst: e.activation(out=dst[H64], in_=tq[H64], func=AF.Sin),
                     r=[B_tmp], w=[B_tab])
            P.barrier()

            a = Alloc(ph_base)
            wq = [a([8, 192], BF16) for _ in range(2)]
            wkv = [a([4, 256], BF16) for _ in range(2)]
            qn = [a([T], BF16) for _ in range(2)]
            qr = [a([T], BF16) for _ in range(2)]
            kn = [a([T], BF16) for _ in range(2)]
            kr = [a([T], BF16) for _ in range(2)]
            vh = [a([16, 128], BF16) for _ in range(2)]
            gt = [a([T], BF16) for _ in range(2)]
            oh = [a([T], BF16) for _ in range(2)]
            B_wq = [Buf("wq0"), Buf("wq1")]
            B_wkv = [Buf("wkv0"), Buf("wkv1")]
            B_q = [Buf("q0"), Buf("q1")]
            B_k = [Buf("k0"), Buf("k1")]
            B_v = [Buf("v0"), Buf("v1")]
            B_gt = [Buf("gt0"), Buf("gt1")]
            B_oh = [Buf("oh0"), Buf("oh1")]
            NTMP = 2
            sq = [a([512], BF16) for _ in range(NTMP)]
            sqr = [a([512], BF16) for _ in range(NTMP)]
            rs = [a([512], F32) for _ in range(NTMP)]
            yv = [a([512], F32) for _ in range(NTMP)]
            t1 = [a([512], F32) for _ in range(NTMP)]
            t2 = [a([512], F32) for _ in range(NTMP)]
            B_sq = [Buf(f"sq{i}") for i in range(NTMP)]
            B_sqr = [Buf(f"sqr{i}") for i in range(NTMP)]
            B_rs = [Buf(f"rs{i}") for i in range(NTMP)]
            B_y = [Buf(f"y{i}") for i in range(NTMP)]
            B_t1 = [Buf(f"t1{i}") for i in range(NTMP)]
            B_t2 = [Buf(f"t2{i}") for i in range(NTMP)]
            NPT = 4
            pT = [a([512], BF16) for _ in range(NPT)]
            B_pT = [Buf(f"pT{i}") for i in range(NPT)]
            rden = [a([512], F32) for _ in range(2)]
            og = [a([512], F32) for _ in range(2)]
            B_rden = [Buf("rden0"), Buf("rden1")]
            B_og = [Buf("og0"), Buf("og1")]
            ocp = [a([512], F32) for _ in range(2)]
            B_ocp = [Buf("ocp0"), Buf("ocp1")]
            assert a.cur <= CB, a.cur
            for _sl in range(2):
                P.op("pool", lambda e, _sl=_sl: e.memset(qr[_sl][64:128, :], 0.0), w=[B_q[_sl]])
                P.op("pool", lambda e, _sl=_sl: e.memset(kr[_sl][64:128, :], 0.0), w=[B_k[_sl]])
            BA, BB, BC, BD = 0, 1, 2, 3
            BS = (4, 5)
            BO, BDEN = 6, 7
            cnt = {"tmp": 0, "pt": 0, "s": 0, "fin": 0}

            def rstd_from(bankc, tsl, scale, parts=128):
                P.op("act", lambda e: e.activation(out=rs[tsl], in_=bank(bankc), func=AF.Ln, scale=scale,
                                                   bias=eps_t[:, 0:1]), r=[PSB[bankc], B_const], w=[B_rs[tsl]])
                P.op("act", lambda e: e.activation(out=rs[tsl], in_=rs[tsl], func=AF.Exp, scale=-0.5),
                     r=[B_rs[tsl]], w=[B_rs[tsl]])

            for (latT, nk, gl, nfeat, li) in ((cqT, 8, gql, 1024.0, 0), (ckvT, 4, gkl, 512.0, 1)):
                for tb in range(4):
                    tsl = cnt["tmp"] % NTMP
                    cnt["tmp"] += 1
                    ts_ = slice(tb * 512, (tb + 1) * 512)
                    for k in range(nk):
                        sl2 = k % NTMP
                        P.op("act", lambda e, k=k, sl2=sl2, latT=latT, ts_=ts_: e.activation(
                            out=sq[sl2], in_=latT[:, k, ts_], func=AF.Square),
                             r=[B_lat[tb]], w=[B_sq[sl2]])
                        P.op("pe", lambda e, k=k, sl2=sl2, nk=nk: e.matmul(bank(BC), lhsT=ones_b, rhs=sq[sl2],
                                                                         start=(k == 0), stop=(k == nk - 1)),
                             r=[B_sq[sl2], B_const], w=[PSB[BC]])
                    rstd_from(BC, tsl, 1.0 / nfeat)
                    for k in range(nk):
                        P.op("dve", lambda e, k=k, latT=latT, ts_=ts_, gl=gl, tsl=tsl: e.scalar_tensor_tensor(
                            out=latT[:, k, ts_], in0=latT[:, k, ts_], scalar=gl[:, k:k + 1], in1=rs[tsl],
                            op0=ALU.mult, op1=ALU.mult), r=[B_rs[tsl], B_g, B_lat[tb]], w=[B_lat[tb]])

            P.op("pool", lambda e: e.tensor_tensor(out=krsq[H64], in0=krb[H64], in1=krb[H64], op=ALU.mult),
                 r=[B_kr], w=[B_kr])
            for tb in range(4):
                tsl = cnt["tmp"] % NTMP
                cnt["tmp"] += 1
                ts_ = slice(tb * 512, (tb + 1) * 512)
                P.op("dve", lambda e, ts_=ts_, tsl=tsl: e.tensor_scalar(out=yv[tsl][H64], in0=krb[H64, ts_],
                                                                        scalar1=gk_r[H64, 0:1], scalar2=None,
                                                                        op0=ALU.mult), r=[B_kr, B_g], w=[B_y[tsl]])
                P.op("pe", lambda e, tsl=tsl: e.matmul(bank(BD)[H64, :], lhsT=rotT[H64], rhs=yv[tsl][H64],
                                                       start=True, stop=True), r=[B_y[tsl], B_g], w=[PSB[BD]])
                P.op("pool", lambda e, ts_=ts_, tsl=tsl: e.tensor_tensor(out=t1[tsl][H64], in0=yv[tsl][H64],
                                                                         in1=Ct[H64, ts_], op=ALU.mult),
                     r=[B_y[tsl], B_tab], w=[B_t1[tsl]])
                P.op("dve", lambda e, ts_=ts_, tsl=tsl: e.tensor_tensor(out=t2[tsl][H64], in0=bank(BD)[H64, :],
                                                                        in1=St[H64, ts_], op=ALU.mult),
                     r=[PSB[BD], B_tab], w=[B_t2[tsl]])
                P.op("pool", lambda e, ts_=ts_, tsl=tsl: e.tensor_tensor(out=Rk[H64, ts_], in0=t1[tsl][H64],
                                                                         in1=t2[tsl][H64], op=ALU.add),
                     r=[B_t1[tsl], B_t2[tsl]], w=[B_tab])

            def q_stages(h, sl, tb):
                ts_ = slice(tb * 512, (tb + 1) * 512)
                tsl = cnt["tmp"] % NTMP
                cnt["tmp"] += 1

                def s1():
                    for k in range(8):
                        P.op("pe", lambda e, k=k: e.matmul(bank(BA), lhsT=wq[sl][:, k, 0:128], rhs=cqT[:, k, ts_],
                                                           start=(k == 0), stop=(k == 7)),
                             r=[B_wq[sl], B_lat[tb]], w=[PSB[BA]])
                    for k in range(8):
                        P.op("pe", lambda e, k=k: e.matmul(bank(BB)[H64, :], lhsT=wq[sl][:, k, 128:192],
                                                           rhs=cqT[:, k, ts_], start=(k == 0), stop=(k == 7)),
                             r=[B_wq[sl], B_lat[tb]], w=[PSB[BB]])

                def s2():
                    P.op("act", lambda e: e.activation(out=sq[tsl], in_=bank(BA), func=AF.Square),
                         r=[PSB[BA]], w=[B_sq[tsl]])
                    P.op("act", lambda e: e.activation(out=sqr[tsl][H64], in_=bank(BB)[H64, :], func=AF.Square),
                         r=[PSB[BB]], w=[B_sqr[tsl]])

                def s3():
                    P.op("pe", lambda e: e.matmul(bank(BC), lhsT=ones_b, rhs=sq[tsl], start=True, stop=False),
                         r=[B_sq[tsl], B_const], w=[PSB[BC]])
                    P.op("pe", lambda e: e.matmul(bank(BC), lhsT=ones_b[H64, :], rhs=sqr[tsl][H64], start=False,
                                                  stop=True), r=[B_sqr[tsl], B_const], w=[PSB[BC]])

                def s4():
                    rstd_from(BC, tsl, 1.0 / 192.0)
                    P.op("dve", lambda e: e.scalar_tensor_tensor(out=qn[sl][:, ts_], in0=bank(BA), scalar=gq_n[:, 0:1],
                                                                 in1=rs[tsl], op0=ALU.mult, op1=ALU.mult),
                         r=[PSB[BA], B_rs[tsl], B_g], w=[B_q[sl]])
                    P.op("dve", lambda e: e.tensor_scalar(out=yv[tsl][H64], in0=bank(BB)[H64, :], scalar1=gq_r[H64, 0:1],
                                                          scalar2=None, op0=ALU.mult), r=[PSB[BB], B_g], w=[B_y[tsl]])

                def s5():
                    P.op("pe", lambda e: e.matmul(bank(BD)[H64, :], lhsT=rotT[H64], rhs=yv[tsl][H64], start=True,
                                                  stop=True), r=[B_y[tsl], B_g], w=[PSB[BD]])

                def s6():
                    P.op("pool", lambda e: e.tensor_tensor(out=t1[tsl][H64], in0=yv[tsl][H64], in1=Ct[H64, ts_],
                                                           op=ALU.mult), r=[B_y[tsl], B_tab], w=[B_t1[tsl]])
                    P.op("dve", lambda e: e.tensor_tensor(out=t2[tsl][H64], in0=bank(BD)[H64, :], in1=St[H64, ts_],
                                                          op=ALU.mult), r=[PSB[BD], B_tab], w=[B_t2[tsl]])
                    P.op("dve", lambda e: e.tensor_tensor(out=t1[tsl][H64], in0=t1[tsl][H64], in1=t2[tsl][H64],
                                                          op=ALU.add), r=[B_t1[tsl], B_t2[tsl]], w=[B_t1[tsl]])
                    P.op("dve", lambda e: e.tensor_tensor(out=qr[sl][H64, ts_], in0=t1[tsl][H64], in1=rs[tsl][H64],
                                                          op=ALU.mult), r=[B_t1[tsl], B_rs[tsl]], w=[B_q[sl]])

                return [s1, s2, s3, s4, s5, s6]

            def k_stages(h, sl, tb):
                ts_ = slice(tb * 512, (tb + 1) * 512)
                tsl = cnt["tmp"] % NTMP
                cnt["tmp"] += 1

                def s1():
                    for k in range(4):
                        P.op("pe", lambda e, k=k: e.matmul(bank(BA), lhsT=wkv[sl][:, k, 0:128], rhs=ckvT[:, k, ts_],
                                                           start=(k == 0), stop=(k == 3)),
                             r=[B_wkv[sl], B_lat[tb]], w=[PSB[BA]])

                def s2():
                    P.op("act", lambda e: e.activation(out=sq[tsl], in_=bank(BA), func=AF.Square),
                         r=[PSB[BA]], w=[B_sq[tsl]])

                def s3():
                    P.op("pe", lambda e: e.matmul(bank(BC), lhsT=ones_b, rhs=sq[tsl], start=True, stop=False),
                         r=[B_sq[tsl], B_const], w=[PSB[BC]])
                    P.op("pe", lambda e: e.matmul(bank(BC), lhsT=ones_b[H64, :], rhs=krsq[H64, ts_], start=False,
                                                  stop=True), r=[B_kr, B_const], w=[PSB[BC]])

                def s4():
                    rstd_from(BC, tsl, 1.0 / 192.0)
                    P.op("dve", lambda e: e.scalar_tensor_tensor(out=kn[sl][:, ts_], in0=bank(BA), scalar=gk_n[:, 0:1],
                                                                 in1=rs[tsl], op0=ALU.mult, op1=ALU.mult),
                         r=[PSB[BA], B_rs[tsl], B_g], w=[B_k[sl]])
                    P.op("dve", lambda e: e.tensor_tensor(out=kr[sl][H64, ts_], in0=Rk[H64, ts_], in1=rs[tsl][H64],
                                                          op=ALU.mult), r=[B_tab, B_rs[tsl]], w=[B_k[sl]])

                return [s1, s2, s3, s4]

            def v_stages(sl, g4):
                def s1():
                    for i in range(4):
                        kt = g4 * 4 + i
                        for k in range(4):
                            P.op("pe", lambda e, k=k, i=i, kt=kt: e.matmul(
                                bank(BD)[:, i * 128:(i + 1) * 128], lhsT=ckvT[:, k, kt * 128:(kt + 1) * 128],
                                rhs=wkv[sl][:, k, 128:256], start=(k == 0), stop=(k == 3)),
                                 r=[B_wkv[sl], B_lat[kt // 4]], w=[PSB[BD]])

                def s2():
                    P.op("act", lambda e: e.activation(
                        out=vh[sl][:, g4 * 4:(g4 + 1) * 4, :], in_=bank(BD).rearrange("p (a b) -> p a b", a=4),
                        func=AF.Copy), r=[PSB[BD]], w=[B_v[sl]])

                return [s1, s2]

            def prep_stages(h, sl):
                def loads():
                    P.op("pool", lambda e: e.dma_start(out=wq[sl], in_=w_uq_d[:, h * 192:(h + 1) * 192].rearrange(
                        "(k p) c -> p k c", p=128)), w=[B_wq[sl]], dma=True)
                    P.op("pool", lambda e: e.dma_start(out=wkv[sl], in_=w_ukv_d[:, h * 256:(h + 1) * 256].rearrange(
                        "(k p) c -> p k c", p=128)), w=[B_wkv[sl]], dma=True)
                    P.op("sp", lambda e: e.dma_start(out=gt[sl], in_=projT[C_GA + h * 128:C_GA + (h + 1) * 128, :]),
                         w=[B_gt[sl]], dma=True)

                S = [loads]
                for tb in range(4):
                    S.extend(q_stages(h, sl, tb))
                    S.extend(k_stages(h, sl, tb))
                for g4 in range(4):
                    S.extend(v_stages(sl, g4))
                return S

            def attn_units_a(h, sl):
                U = []
                for qb in range(4):
                    nj = 4 * qb + 4
                    for j in range(nj):
                        U.append(make_unit_a(h, sl, qb, j, nj))
                return U

            def make_unit_a(h, sl, qb, j, nj):
                qs_ = qb * 512
                lo = max(0, 128 * j - 512 * qb)
                ks = slice(j * 128, (j + 1) * 128)
                qsl = slice(qs_ + lo, qs_ + 512)
                stt = {}

                def A():
                    bs = BS[cnt["s"] % 2]
                    cnt["s"] += 1
                    pi = cnt["pt"] % NPT
                    cnt["pt"] += 1
                    stt["pi"] = pi
                    P.op("pe", lambda e: e.matmul(bank(bs)[:, lo:512], lhsT=kn[sl][:, ks], rhs=qn[sl][:, qsl],
                                                  start=True, stop=False), r=[B_k[sl], B_q[sl]], w=[PSB[bs]])
                    diag = j >= 4 * qb
                    P.op("pe", lambda e: e.matmul(bank(bs)[:, lo:512], lhsT=kr[sl][:, ks], rhs=qr[sl][:, qsl],
                                                  start=False, stop=not diag), r=[B_k[sl], B_q[sl]], w=[PSB[bs]])
                    if diag:
                        jj = j - 4 * qb
                        m0 = 384 - 128 * jj + lo
                        P.op("pe", lambda e: e.matmul(bank(bs)[:, lo:512], lhsT=negI_b, rhs=cm_b[:, m0:m0 + 512 - lo],
                                                      start=False, stop=True), r=[B_const, B_g], w=[PSB[bs]])
                    P.op("act", lambda e: e.activation(out=pT[pi][:, lo:512], in_=bank(bs)[:, lo:512], func=AF.Exp),
                         r=[PSB[bs]], w=[B_pT[pi]])

                def Bf():
                    pi = stt["pi"]
                    P.op("pe", lambda e: e.matmul(bank(BO)[:, lo:512], lhsT=vh[sl][:, j, :], rhs=pT[pi][:, lo:512],
                                                  start=(j == 0), stop=(j == nj - 1)), r=[B_v[sl], B_pT[pi]],
                         w=[PSB[BO]])
                    P.op("pe", lambda e: e.matmul(bank(BDEN)[:, lo:512], lhsT=ones_b, rhs=pT[pi][:, lo:512],
                                                  start=(j == 0), stop=(j == nj - 1)), r=[B_const, B_pT[pi]],
                         w=[PSB[BDEN]])
                    if j == nj - 1:
                        fi = cnt["fin"] % 2
                        cnt["fin"] += 1
                        P.op("act", lambda e: e.activation(out=rden[fi], in_=bank(BDEN), func=AF.Ln),
                             r=[PSB[BDEN]], w=[B_rden[fi]])
                        P.op("dve", lambda e: e.tensor_copy(out=ocp[fi], in_=bank(BO)), r=[PSB[BO]], w=[B_ocp[fi]])
                        P.op("act", lambda e: e.activation(out=rden[fi], in_=rden[fi], func=AF.Exp, scale=-1.0),
                             r=[B_rden[fi]], w=[B_rden[fi]])
                        P.op("dve", lambda e: e.tensor_tensor(out=og[fi], in0=rden[fi], in1=gt[sl][:, qs_:qs_ + 512],
                                                              op=ALU.mult), r=[B_rden[fi], B_gt[sl]], w=[B_og[fi]])
                        P.op("dve", lambda e: e.tensor_tensor(out=oh[sl][:, qs_:qs_ + 512], in0=ocp[fi], in1=og[fi],
                                                              op=ALU.mult), r=[B_ocp[fi], B_og[fi]], w=[B_oh[sl]])
                        if qb == 3:
                            P.op("sp", lambda e: e.dma_start(out=oaT[h * 128:(h + 1) * 128, :], in_=oh[sl]),
                                 r=[B_oh[sl]], dma=True)

                return (A, Bf)

            import os as _os
            NH_A = int(_os.environ.get('NH_A', '16'))
            pend = [None]

            def run_unit(u):
                u[0]()
                if pend[0] is not None:
                    pend[0]()
                pend[0] = u[1]

            for st_ in prep_stages(0, 0):
                st_()
            for h in range(NH_A):
                U = attn_units_a(h, h % 2)
                Pn = prep_stages(h + 1, (h + 1) % 2) if h + 1 < NH_A else []
                kk = 0
                for idx, u in enumerate(U):
                    run_unit(u)
                    tgt = min(len(Pn), ((idx + 1) * len(Pn) + len(U) - 1) // len(U))
                    while kk < tgt:
                        Pn[kk]()
                        kk += 1
                while kk < len(Pn):
                    Pn[kk]()
                    kk += 1
            if pend[0] is not None:
                pend[0]()
            P.barrier()

        if stop_after >= 3:
            a = Alloc(0)
            kiT = a([T], BF16)
            qiT = a([16, 512], BF16)
            wi = a([4, 32], F32)
            wabs = a([4, 32], F32)
            wsgn = a([4, 32], F32)
            acc = [a([T], F32) for _ in range(2)]
            tmpr = [a([1024], BF16) for _ in range(2)]
            dsg = [a([32, 128], BF16) for _ in range(2)]
            junk = a([T], BF16)
            mrow = [a([T], BF16) for _ in range(2)]
            maskT = a([16, 512], BF16)
            qbT = a([16, 512], BF16)
            kbT = a([T], BF16)
            vb = a([16, 128], BF16)
            BThi = a([16, 256], BF16)
            BTlo = a([16, 256], BF16)
            b31 = a([16], F32)
            gqb = a([1], F32)
            gkb = a([1], F32)
            gtb = a([16, 512], BF16)
            ob_off = a.cur
            ob = a([16, 512], BF16)
            BT = sb(ob_off, [16, 256], F32)
            W0 = a([1], F32)
            Wtab = a([NITER], F32)
            mid = a([1], F32)
            cntv = a([1], F32)
            sgn = a([1], F32)
            thr = a([1], F32)
            pow2 = a([NITER], F32)
            negm = a([128], F32)
            t5s = a([16], F32)
            ohs = a([384], F32)
            F16 = a([384], F32)
            sqb = [a([512], BF16) for _ in range(2)]
            rs3 = [a([512], F32) for _ in range(2)]
            NPT3 = 4
            pT3v = [a([512], BF16) for _ in range(NPT3)]
            rden3v = [a([512], F32) for _ in range(2)]
            og3v = [a([512], F32) for _ in range(2)]
            ocp3 = [a([512], F32) for _ in range(2)]
            B_ocp3 = [Buf("ocp30"), Buf("ocp31")]
            kiT_o = a([T], BF16)
            assert a.cur <= CB, a.cur
            B_ki = Buf("kiT"); B_qi = Buf("qiT"); B_wi = Buf("wi")
            B_acc = [Buf("acc0"), Buf("acc1")]
            B_tmpr = [Buf("tmpr0"), Buf("tmpr1")]
            B_junk = Buf("junk")
            B_mrow = [Buf("mrow0"), Buf("mrow1")]
            B_dsg = [Buf("dsg0"), Buf("dsg1")]
            B_maskT = Buf("maskT")
            B_qb = Buf("qbT"); B_kb = Buf("kbT"); B_vb = Buf("vb"); B_BT = Buf("BT"); B_g3 = Buf("g3")
            B_gtb = Buf("gtb"); B_ob = B_BT; B_bis = Buf("bis"); B_t5 = Buf("t5")
            B_sqb = [Buf("sqb0"), Buf("sqb1")]
            B_rs3 = [Buf("rs30"), Buf("rs31")]
            B_pT3 = [Buf(f"pT3{i}") for i in range(NPT3)]
            B_tn = [Buf("tn0"), Buf("tn1")]
            B_rden3 = [Buf("rden30"), Buf("rden31")]
            B_og3 = [Buf("og30"), Buf("og31")]
            H32 = slice(0, 32)
            H16 = slice(0, 16)
            c3 = {"ds": 0, "tmp": 0, "lg": 0, "mr": 0, "tp": 0, "s": 0, "pt": 0, "tn": 0, "fin": 0, "acc": 0}

            P.op("sp", lambda e: e.dma_start(out=kiT[0:64, :], in_=kidxT[0:64, :]), w=[B_ki], dma=True)
            P.op("sp", lambda e: e.dma_start(out=kiT_o[64:128, :], in_=kidxT[64:128, :]), w=[B_ki], dma=True)
            P.op("pool", lambda e: e.memset(kiT[64:128, :], 0.0), w=[B_ki])
            P.op("pool", lambda e: e.memset(kiT_o[0:64, :], 0.0), w=[B_ki])
            P.op("sp", lambda e: e.dma_start(out=kbT, in_=projT[C_KB:C_KB + 128, :]), w=[B_kb], dma=True)
            P.op("sp", lambda e: e.dma_start(out=vb, in_=vb_s.rearrange("(j p) d -> p j d", p=128)), w=[B_vb], dma=True)
            P.op("sp", lambda e: e.dma_start(out=gqb, in_=g_qn_b_d.rearrange("(p o) -> p o", o=1)), w=[B_g3], dma=True)
            P.op("sp", lambda e: e.dma_start(out=gkb, in_=g_kn_b_d.rearrange("(p o) -> p o", o=1)), w=[B_g3], dma=True)
            P.op("sp", lambda e: e.dma_start(out=pow2, in_=c_pow2), w=[B_g3], dma=True)
            P.op("sp", lambda e: e.dma_start(out=negm, in_=c_negmask), w=[B_g3], dma=True)
            P.op("sp", lambda e: e.dma_start(out=t5s[H32], in_=t5_d), w=[B_t5], dma=True)
            P.op("sp", lambda e: e.dma_start(out=ohs[H32], in_=c_onehot), w=[B_t5], dma=True)
            P.op("sp", lambda e: e.dma_start(out=b31, in_=t5_d[31, :].partition_broadcast(128)), w=[B_g3], dma=True)
            P.op("dve", lambda e: e.tensor_scalar(out=gqb, in0=gqb, scalar1=128.0 ** -0.5, scalar2=None, op0=ALU.mult),
                 r=[B_g3], w=[B_g3])
            P.op("pe", lambda e: e.matmul(bank(7)[H16, 0:384], lhsT=t5s[H32], rhs=ohs[H32], start=True, stop=True),
                 r=[B_t5], w=[PSB[7]])
            P.op("dve", lambda e: e.tensor_copy(out=F16[H16], in_=bank(7)[H16, 0:384]), r=[PSB[7]], w=[B_t5])
            B_Xb = Buf("Xb")
            P.op("sp", lambda e: e.dma_start(out=Xb.rearrange("h (r m) -> h r m", m=384),
                                             in_=F16[H16].unsqueeze(1).broadcast_to([16, 129, 384])),
                 r=[B_t5], w=[B_Xb], dma=True)
            P.op("sp", lambda e: e.dma_start(out=BT, in_=bass.AP(Xb.tensor, 0, [[383, 128], [129 * 384, 16], [1, 256]])),
                 r=[B_Xb], w=[B_BT], dma=True)
            B_BTs = Buf("BTs")

            def bt_fix(h):
                P.op("dve", lambda e: e.tensor_scalar(out=BT[:, h, :], in0=BT[:, h, :], scalar1=b31[:, h:h + 1],
                                                      scalar2=None, op0=ALU.subtract), r=[B_BT, B_g3], w=[B_BT])

            for h in range(16):
                bt_fix(h)
            P.op("dve", lambda e: e.tensor_copy(out=BThi, in_=BT), r=[B_BT], w=[B_BTs])
            P.op("dve", lambda e: e.tensor_tensor(out=BTlo, in0=BT, in1=BThi, op=ALU.subtract), r=[B_BT, B_BTs],
                 w=[B_BTs])

            def rstd3(bankc, tsl, scale):
                P.op("act", lambda e: e.activation(out=rs3[tsl], in_=bank(bankc), func=AF.Ln, scale=scale,
                                                   bias=eps_t[:, 0:1]), r=[PSB[bankc], B_const], w=[B_rs3[tsl]])
                P.op("act", lambda e: e.activation(out=rs3[tsl], in_=rs3[tsl], func=AF.Exp, scale=-0.5),
                     r=[B_rs3[tsl]], w=[B_rs3[tsl]])

            def norm_block(src_ap, gvec_ap, Bsrc):
                tsl = c3["tmp"] % 2
                c3["tmp"] += 1
                P.op("act", lambda e: e.activation(out=sqb[tsl], in_=src_ap, func=AF.Square), r=[Bsrc],
                     w=[B_sqb[tsl]])
                P.op("pe", lambda e: e.matmul(bank(6), lhsT=ones_b, rhs=sqb[tsl], start=True, stop=True),
                     r=[B_sqb[tsl], B_const], w=[PSB[6]])
                rstd3(6, tsl, 1.0 / 128.0)
                P.op("dve", lambda e: e.scalar_tensor_tensor(out=src_ap, in0=src_ap, scalar=gvec_ap, in1=rs3[tsl],
                                                             op0=ALU.mult, op1=ALU.mult),
                     r=[Bsrc, B_rs3[tsl], B_g3], w=[Bsrc])

            for tb in range(4):
                norm_block(kbT[:, tb * 512:(tb + 1) * 512], gkb[:, 0:1], B_kb)

            def idx_part(qb, i, g):
                qt = qb * 4 + i
                nk = (qt + 1) * 128
                accv = acc[g % 2]
                Bacc = B_acc[g % 2]
                qsl = slice(i * 128, (i + 1) * 128)
                dsl = g % 2
                P.op("pool", lambda e: e.tensor_tensor(
                    out=dsg[dsl], in0=ident_f.unsqueeze(1).broadcast_to([128, 32, 128]),
                    in1=wsgn[:, i, :].unsqueeze(2).broadcast_to([128, 32, 128]), op=ALU.mult),
                     r=[B_const, B_wi], w=[B_dsg[dsl]])

                def mk(h, k0, wk, sbk):
                    c = h // 2
                    hp = slice((h % 2) * 64, (h % 2) * 64 + 64)
                    stt = {}

                    def L():
                        lg = c3["lg"] % 2
                        c3["lg"] += 1
                        stt["lg"] = lg
                        pb_ = bank(2 * lg, 2)
                        for off in range(0, wk, 512):
                            w_ = min(512, wk - off)
                            P.op("pe", lambda e, off=off, w_=w_: e.matmul(
                                pb_[:, off:off + w_], lhsT=qiT[:, c, qsl],
                                rhs=(kiT if h % 2 == 0 else kiT_o)[:, k0 + off:k0 + off + w_],
                                start=True, stop=True), r=[B_qi, B_ki], w=[PSB[2 * lg + off // 512]])
                        rb = [PSB[2 * lg + o // 512] for o in range(0, wk, 512)]
                        P.op("act", lambda e: e.activation(out=tmpr[lg][:, 0:wk], in_=pb_[:, 0:wk], func=AF.Relu,
                                                           scale=wabs[:, i, h:h + 1]), r=rb + [B_wi], w=[B_tmpr[lg]])

                    def A():
                        lg = stt["lg"]
                        for off in range(0, wk, 512):
                            w_ = min(512, wk - off)
                            P.op("pe", lambda e, off=off, w_=w_: e.matmul(
                                bank(sbk + off // 512)[:, 0:w_], lhsT=dsg[dsl][:, h, :], rhs=tmpr[lg][:, off:off + w_],
                                start=(h == 0), stop=(h == 31)), r=[B_dsg[dsl], B_tmpr[lg]],
                                 w=[PSB[sbk + off // 512]])

                    return (L, A)

                for k0 in range(0, nk, 1024):
                    wk = min(1024, nk - k0)
                    sbk = 4 + 2 * ((k0 // 1024) % 2)
                    prev = None
                    for h in range(32):
                        u = mk(h, k0, wk, sbk)
                        u[0]()
                        if prev is not None:
                            prev[1]()
                        prev = u
                    prev[1]()
                    for off in range(0, wk, 512):
                        w_ = min(512, wk - off)
                        P.op("act", lambda e, off=off, w_=w_, k0=k0, sbk=sbk: e.activation(
                            out=accv[:, k0 + off:k0 + off + w_], in_=bank(sbk + off // 512)[:, 0:w_], func=AF.Copy),
                             r=[PSB[sbk + off // 512]], w=[Bacc])

            def bis_part(qb, i, g):
                qt = qb * 4 + i
                nk = (qt + 1) * 128
                accv = acc[g % 2]
                Bacc = B_acc[g % 2]
                P.op("dve", lambda e: e.tensor_reduce(out=W0[:, 0:1], in_=accv[:, 0:nk], axis=AX.X, op=ALU.max,
                                                      apply_absolute_value=True), r=[Bacc], w=[B_bis])
                P.op("dve", lambda e: e.tensor_scalar(out=W0, in0=W0, scalar1=1.001, scalar2=1e-6, op0=ALU.mult,
                                                      op1=ALU.add), r=[B_bis], w=[B_bis])
                P.op("dve", lambda e: e.tensor_scalar(out=Wtab, in0=pow2, scalar1=W0[:, 0:1], scalar2=None,
                                                      op0=ALU.mult), r=[B_bis, B_g3], w=[B_bis])
                P.op("dve", lambda e: e.tensor_tensor(out=accv[:, qt * 128:nk], in0=accv[:, qt * 128:nk], in1=negm,
                                                      op=ALU.add), r=[Bacc, B_g3], w=[Bacc])
                if qt >= 2:
                    P.op("dve", lambda e: e.memset(mid, 0.0), r=[B_bis], w=[B_bis])

                    def one_iter(it):
                        P.op("dve", lambda e: e.tensor_scalar(out=junk[:, 0:nk], in0=accv[:, 0:nk], scalar1=mid[:, 0:1],
                                                              scalar2=0.0, op0=ALU.is_ge, op1=ALU.add,
                                                              accum_out=cntv[:, 0:1]),
                             r=[Bacc, B_bis], w=[B_junk, B_bis])
                        P.op("dve", lambda e: e.tensor_scalar(out=sgn, in0=cntv, scalar1=256.0, scalar2=-0.5,
                                                              op0=ALU.is_ge, op1=ALU.add), r=[B_bis], w=[B_bis])
                        P.op("dve", lambda e: e.scalar_tensor_tensor(out=mid, in0=sgn, scalar=Wtab[:, it:it + 1],
                                                                     in1=mid, op0=ALU.mult, op1=ALU.add),
                             r=[B_bis], w=[B_bis])

                    for it in range(NITER):
                        one_iter(it)
                    P.op("dve", lambda e: e.scalar_tensor_tensor(out=thr, in0=Wtab[:, NITER - 1:NITER], scalar=-0.5,
                                                                 in1=mid, op0=ALU.mult, op1=ALU.add),
                         r=[B_bis], w=[B_bis])
                else:
                    P.op("dve", lambda e: e.memset(thr, -1e29), r=[B_bis], w=[B_bis])
                ms = g % 2
                P.op("dve", lambda e: e.tensor_scalar(out=mrow[ms][:, 0:nk], in0=accv[:, 0:nk], scalar1=thr[:, 0:1],
                                                      scalar2=None, op0=ALU.is_lt), r=[Bacc, B_bis], w=[B_mrow[ms]])

            def tr_part(qb, i, g):
                qt = qb * 4 + i
                ms = g % 2
                qsl = slice(i * 128, (i + 1) * 128)
                for j0 in range(0, qt + 1, 8):
                    n = min(8, qt + 1 - j0)
                    bk = 4 + c3["tp"] % 2
                    c3["tp"] += 1
                    for jj in range(n):
                        j = j0 + jj
                        P.op("pe", lambda e, jj=jj, j=j, bk=bk: e.transpose(
                            out=bank16(bk)[:, jj * 128:(jj + 1) * 128], in_=mrow[ms][:, j * 128:(j + 1) * 128],
                            identity=ident_b), r=[B_mrow[ms], B_const], w=[PSB[bk]])
                    P.op("act", lambda e, j0=j0, n=n, bk=bk: e.activation(
                        out=maskT[:, j0:j0 + n, qsl], in_=bank16(bk)[:, 0:n * 128].rearrange("p (a b) -> p a b", a=n),
                        func=AF.Copy), r=[PSB[bk]], w=[B_maskT])

            def make_unit_b(qb, h, j, nj):
                BS_ = (4, 5)
                BO_, BDEN_ = (6, 7) if h % 2 == 0 else (2, 3)
                lo = max(0, 128 * j - 512 * qb)
                c0 = 512 * qb + lo - 128 * j
                nn_ = max(0, min(256 - c0, 512 - lo))
                stt = {}

                def A():
                    bs = BS_[c3["s"] % 2]
                    c3["s"] += 1
                    pi = c3["pt"] % NPT3
                    c3["pt"] += 1
                    stt["pi"] = pi
                    P.op("pe", lambda e: e.matmul(bank(bs)[:, lo:512], lhsT=kbT[:, j * 128:(j + 1) * 128],
                                                  rhs=qbT[:, h, lo:512], start=True, stop=False),
                         r=[B_kb, B_qb], w=[PSB[bs]])
                    if nn_ > 0:
                        P.op("pe", lambda e: e.matmul(bank(bs)[:, lo:lo + nn_], lhsT=ident_b,
                                                      rhs=BThi[:, h, c0:c0 + nn_], start=False, stop=False),
                             r=[B_const, B_BTs], w=[PSB[bs]])
                        P.op("pe", lambda e: e.matmul(bank(bs)[:, lo:lo + nn_], lhsT=ident_b,
                                                      rhs=BTlo[:, h, c0:c0 + nn_], start=False, stop=False),
                             r=[B_const, B_BTs], w=[PSB[bs]])
                    P.op("pe", lambda e: e.matmul(bank(bs)[:, lo:512], lhsT=negI_b, rhs=maskT[:, j, lo:512],
                                                  start=False, stop=True), r=[B_const, B_maskT], w=[PSB[bs]])
                    P.op("act", lambda e: e.activation(out=pT3v[pi][:, lo:512], in_=bank(bs)[:, lo:512], func=AF.Exp,
                                                       bias=b31[:, h:h + 1]), r=[PSB[bs], B_g3], w=[B_pT3[pi]])

                def Bf():
                    pi = stt["pi"]
                    P.op("pe", lambda e: e.matmul(bank(BO_)[:, lo:512], lhsT=vb[:, j, :], rhs=pT3v[pi][:, lo:512],
                                                  start=(j == 0), stop=(j == nj - 1)), r=[B_vb, B_pT3[pi]],
                         w=[PSB[BO_]])
                    P.op("pe", lambda e: e.matmul(bank(BDEN_)[:, lo:512], lhsT=ones_b, rhs=pT3v[pi][:, lo:512],
                                                  start=(j == 0), stop=(j == nj - 1)), r=[B_const, B_pT3[pi]],
                         w=[PSB[BDEN_]])
                    if j == nj - 1:
                        fi = c3["fin"] % 2
                        c3["fin"] += 1
                        P.op("act", lambda e: e.activation(out=rden3v[fi], in_=bank(BDEN_), func=AF.Ln),
                             r=[PSB[BDEN_]], w=[B_rden3[fi]])
                        P.op("dve", lambda e: e.tensor_copy(out=ocp3[fi], in_=bank(BO_)), r=[PSB[BO_]],
                             w=[B_ocp3[fi]])
                        P.op("act", lambda e: e.activation(out=rden3v[fi], in_=rden3v[fi], func=AF.Exp, scale=-1.0),
                             r=[B_rden3[fi]], w=[B_rden3[fi]])
                        P.op("dve", lambda e: e.tensor_tensor(out=og3v[fi], in0=rden3v[fi], in1=gtb[:, h, :],
                                                              op=ALU.mult), r=[B_rden3[fi], B_gtb], w=[B_og3[fi]])
                        P.op("dve", lambda e: e.tensor_tensor(out=ob[:, h, :], in0=ocp3[fi], in1=og3v[fi],
                                                              op=ALU.mult), r=[B_ocp3[fi], B_og3[fi]], w=[B_ob])

                return (A, Bf)

            pend3 = [None]

            def run_unit3(u):
                u[0]()
                if pend3[0] is not None:
                    pend3[0]()
                pend3[0] = u[1]

            def attn_b_all(qb):
                nj = 4 * qb + 4
                for h in range(16):
                    for j in range(nj):
                        run_unit3(make_unit_b(qb, h, j, nj))
                if pend3[0] is not None:
                    pend3[0]()
                    pend3[0] = None

            def load_idx(qb):
                qcs = slice(qb * 512, (qb + 1) * 512)
                P.op("sp", lambda e: e.dma_start(out=qiT, in_=projT[C_QI:C_QI + 2048, qcs].rearrange(
                    "(c p) t -> p c t", p=128)), w=[B_qi], dma=True)
                P.op("sp", lambda e: e.dma_start(out=wi, in_=widx_s[qcs, :].rearrange("(i p) h -> p i h", p=128)),
                     w=[B_wi], dma=True)
                P.op("act", lambda e: e.activation(out=wabs, in_=wi, func=AF.Abs, scale=32.0 ** -0.5), r=[B_wi],
                     w=[B_wi])
                P.op("act", lambda e: e.activation(out=wsgn, in_=wi, func=AF.Sign), r=[B_wi], w=[B_wi])

            def load_att(qb):
                qcs = slice(qb * 512, (qb + 1) * 512)
                P.op("sp", lambda e: e.dma_start(out=qbT, in_=projT[C_QB:C_QB + 2048, qcs].rearrange(
                    "(c p) t -> p c t", p=128)), w=[B_qb], dma=True)
                P.op("sp", lambda e: e.dma_start(out=gtb, in_=projT[C_GB:C_GB + 2048, qcs].rearrange(
                    "(c p) t -> p c t", p=128)), w=[B_gtb], dma=True)
                for h in range(16):
                    norm_block(qbT[:, h, :], gqb[:, 0:1], B_qb)

            def att_and_store(qb):
                qcs = slice(qb * 512, (qb + 1) * 512)
                attn_b_all(qb)
                P.op("sp", lambda e: e.dma_start(out=obT[:, qcs].rearrange("(h p) t -> p h t", p=128), in_=ob),
                     r=[B_ob], dma=True)

            load_idx(0)
            load_att(0)
            idx_part(0, 0, 0)
            bis_part(0, 0, 0)
            for g in range(1, 16):
                qb = g // 4
                if g % 4 == 0:
                    load_idx(qb)
                idx_part(qb, g % 4, g)
                bis_part(qb, g % 4, g)
                tr_part((g - 1) // 4, (g - 1) % 4, g - 1)
                if g % 4 == 0:
                    att_and_store(qb - 1)
                    load_att(qb)
            tr_part(3, 3, 15)
            att_and_store(3)
            P.barrier()

        if stop_after >= 4:
            a = Alloc(0)
            oaH = a([16, 1024], BF16)
            obH = a([16, 1024], BF16)
            s2_base = a.cur
            mT = a([32, 1024], BF16)
            s2b_base = a.cur
            pa = [a([16, 256], BF16) for _ in range(2)]
            pb = [a([16, 256], BF16) for _ in range(2)]
            sA = [a([2, 1024], BF16) for _ in range(2)]
            sB = [a([2, 1024], BF16) for _ in range(2)]
            u1 = [a([512], F32) for _ in range(2)]
            u2 = [a([512], F32) for _ in range(2)]
            assert a.cur <= CB, a.cur
            a2 = Alloc(0)
            wo = [a2([32, 512], BF16) for _ in range(2)]
            assert a2.cur <= s2_base
            a3 = Alloc(s2b_base)
            xin = [a3([512], F32) for _ in range(2)]
            oout = [a3([512], F32) for _ in range(2)]
            B_oaH = Buf("oaH"); B_obH = Buf("obH"); B_mT = [Buf(f"mT{i}") for i in range(32)]
            B_pa = [Buf("pa0"), Buf("pa1")]; B_pb = [Buf("pb0"), Buf("pb1")]
            B_sA = [Buf("sA0"), Buf("sA1")]; B_sB = [Buf("sB0"), Buf("sB1")]
            B_u1 = [Buf("u10"), Buf("u11")]; B_u2 = [Buf("u20"), Buf("u21")]
            B_xin = [Buf("xin0"), Buf("xin1")]; B_oout = [Buf("oo0"), Buf("oo1")]
            B_wo = [Buf("wo0"), Buf("wo1")]
            c4 = {"u": 0, "b": 0, "x": 0}

            def merged_loads(th, fbp):
                sl = fbp % 2
                tcs = slice(th * 1024, (th + 1) * 1024)
                cs = slice(fbp * 256, (fbp + 1) * 256)
                P.op("pool", lambda e: e.dma_start(out=pa[sl], in_=p_a_d[:, cs].rearrange("(k p) c -> p k c", p=128)),
                     w=[B_pa[sl]], dma=True)
                P.op("pool", lambda e: e.dma_start(out=pb[sl], in_=p_b_d[:, cs].rearrange("(k p) c -> p k c", p=128)),
                     w=[B_pb[sl]], dma=True)
                P.op("sp", lambda e: e.dma_start(out=sA[sl], in_=projT[C_MA + fbp * 256:C_MA + (fbp + 1) * 256, tcs]
                                                 .rearrange("(f p) t -> p f t", p=128)), w=[B_sA[sl]], dma=True)
                P.op("sp", lambda e: e.dma_start(out=sB[sl], in_=projT[C_MB + fbp * 256:C_MB + (fbp + 1) * 256, tcs]
                                                 .rearrange("(f p) t -> p f t", p=128)), w=[B_sB[sl]], dma=True)

            def merged_compute(th, fbp):
                sl = fbp % 2

                def sub(f2, tb2):
                    fb = fbp * 2 + f2
                    bsl = c4["b"] % 2
                    c4["b"] += 1
                    us = c4["u"] % 2
                    c4["u"] += 1
                    ts_ = slice(tb2 * 512, (tb2 + 1) * 512)
                    fc = slice(f2 * 128, (f2 + 1) * 128)
                    for k in range(16):
                        P.op("pe", lambda e, k=k: e.matmul(bank(bsl), lhsT=pa[sl][:, k, fc], rhs=oaH[:, k, ts_],
                                                           start=(k == 0), stop=(k == 15)), r=[B_pa[sl], B_oaH],
                             w=[PSB[bsl]])
                    for k in range(16):
                        P.op("pe", lambda e, k=k: e.matmul(bank(2 + bsl), lhsT=pb[sl][:, k, fc], rhs=obH[:, k, ts_],
                                                           start=(k == 0), stop=(k == 15)), r=[B_pb[sl], B_obH],
                             w=[PSB[2 + bsl]])
                    P.op("dve", lambda e: e.tensor_tensor(out=u1[us], in0=bank(bsl), in1=sA[sl][:, f2, ts_],
                                                          op=ALU.mult), r=[PSB[bsl], B_sA[sl]], w=[B_u1[us]])
                    P.op("dve", lambda e: e.tensor_tensor(out=u2[us], in0=bank(2 + bsl), in1=sB[sl][:, f2, ts_],
                                                          op=ALU.mult), r=[PSB[2 + bsl], B_sB[sl]], w=[B_u2[us]])
                    P.op("dve", lambda e: e.tensor_tensor(out=mT[:, fb, ts_], in0=u1[us], in1=u2[us], op=ALU.add),
                         r=[B_u1[us], B_u2[us]], w=[B_mT[fb]])

                for f2 in range(2):
                    for tb2 in range(2):
                        sub(f2, tb2)

            def final_block(th, cbk):
                sl = cbk % 2
                ccs = slice(cbk * 512, (cbk + 1) * 512)
                P.op("pool", lambda e: e.dma_start(out=wo[sl], in_=w_o_d[:, ccs].rearrange("(f p) c -> p f c", p=128)),
                     w=[B_wo[sl]], dma=True)

                def tok(tt):
                    bk = 4 + c4["b"] % 4
                    c4["b"] += 1
                    xs = c4["x"] % 2
                    c4["x"] += 1
                    r0 = th * 1024 + tt * 128
                    P.op("sp", lambda e: e.dma_start(out=xin[xs], in_=x_d[r0:r0 + 128, ccs]), w=[B_xin[xs]], dma=True)
                    for f in range(32):
                        P.op("pe", lambda e, f=f: e.matmul(bank(bk), lhsT=mT[:, f, tt * 128:(tt + 1) * 128],
                                                           rhs=wo[sl][:, f, :], start=(f == 0), stop=(f == 31)),
                             r=[B_wo[sl], B_mT[f]], w=[PSB[bk]])
                    P.op("dve", lambda e: e.tensor_tensor(out=oout[xs], in0=bank(bk), in1=xin[xs], op=ALU.add),
                         r=[PSB[bk], B_xin[xs]], w=[B_oout[xs]])
                    P.op("sp", lambda e: e.dma_start(out=out_d[r0:r0 + 128, ccs], in_=oout[xs]), r=[B_oout[xs]],
                         dma=True)

                for tt in range(8):
                    tok(tt)

            for th in range(2):
                tcs = slice(th * 1024, (th + 1) * 1024)
                P.op("sp", lambda e, tcs=tcs: e.dma_start(out=oaH, in_=oaT[:, tcs].rearrange("(k p) t -> p k t", p=128)),
                     w=[B_oaH], dma=True)
                P.op("sp", lambda e, tcs=tcs: e.dma_start(out=obH, in_=obT[:, tcs].rearrange("(k p) t -> p k t", p=128)),
                     w=[B_obH], dma=True)
                merged_loads(th, 0)
                for fbp in range(16):
                    if fbp + 1 < 16:
                        merged_loads(th, fbp + 1)
                    merged_compute(th, fbp)
                P.barrier()
                for cbk in range(8):
                    final_block(th, cbk)
                P.barrier()

        P.finalize()
        keys = P.sem_keys()
        sems = {}
        for i, k in enumerate(keys):
            sems[k] = st.enter_context(nc.semaphore(f"s{i}"))
        block = st.enter_context(nc.Block())

        @block.tensor
        def _(e):
            P.emit_engine("pe", e, sems)

        @block.scalar
        def _(e):
            P.emit_engine("act", e, sems)

        @block.vector
        def _(e):
            P.emit_engine("dve", e, sems)

        @block.gpsimd
        def _(e):
            P.emit_engine("pool", e, sems)

        @block.sync
        def _(e):
            P.emit_engine("sp", e, sems, final_wait=True)
    return nc


def _in_maps(inputs):
    maps = []
    shared = {k: np.ascontiguousarray(np.asarray(v, dtype=np.float32)) for k, v in inputs.items()
              if k not in ("x", "positions")}
    for k, v in CONSTS.items():
        shared["c_" + k] = v
    x = np.asarray(inputs["x"], dtype=np.float32)
    pos = np.asarray(inputs["positions"], dtype=np.int32)
    for b in range(x.shape[0]):
        m = dict(shared)
        m["x"] = np.ascontiguousarray(x[b])
        m["positions"] = np.ascontiguousarray(pos[b])
        maps.append(m)
    return maps


_NC_CACHE = {}


def kernel(**inputs):
    if "nc" not in _NC_CACHE:
        _NC_CACHE["nc"] = build_nc()
    nc = _NC_CACHE["nc"]
    maps = _in_maps(inputs)
    res = run_bass_kernel_spmd(nc, maps, core_ids=list(range(len(maps))))
    return np.stack([r["out"] for r in res.results], axis=0).astype(np.float32)
```

```python
import math
import numpy as np
import concourse.bass as bass
import concourse.mybir as mybir
from concourse.bass_utils import run_bass_kernel_spmd

F32 = mybir.dt.float32
BF16 = mybir.dt.bfloat16
I32 = mybir.dt.int32
U8 = mybir.dt.uint8
AF = mybir.ActivationFunctionType
ALU = mybir.AluOpType
AX = mybir.AxisListType

D = 4096
T = 2048
NT = 16
IN_COLS = 18336
EPS = 1e-6
C_CQ, C_CKV, C_KR, C_QB, C_KB, C_VB, C_QI, C_KI, C_WI, C_GA, C_GB, C_MA, C_MB = (
    0, 1024, 1536, 1600, 3648, 3776, 3904, 5952, 6016, 6048, 8096, 10144, 14240)
NITER = 16
SZ = {F32: 4, BF16: 2, I32: 4, U8: 1}


class Buf:
    __slots__ = ("name", "w", "r")

    def __init__(self, name):
        self.name = name
        self.w = None
        self.r = {}


class Op:
    __slots__ = ("eng", "fn", "deps", "flag", "dma", "tok", "n")

    def __init__(self, eng, fn, dma):
        self.eng = eng
        self.fn = fn
        self.deps = set()
        self.flag = False
        self.dma = dma
        self.tok = None
        self.n = None


class Prog:
    ENG = ("pe", "act", "dve", "pool", "sp")
    NDSEM = 8
    CH = 30000

    def __init__(self):
        self.ops = []
        self.barrier_deps = set()
        self.last = {}
        self.dma_hist = {"sp": [], "pool": [], "act": []}

    def op(self, eng, fn, r=(), w=(), dma=False):
        idx = len(self.ops)
        o = Op(eng, fn, dma)
        deps = set(self.barrier_deps)
        for b in r:
            if b.w is not None:
                deps.add(b.w)
        for b in w:
            if b.w is not None:
                deps.add(b.w)
            deps.update(b.r.values())
        key = ("dma", eng, idx) if dma else eng
        for b in r:
            b.r[key if not dma else ("dma", idx)] = idx
        for b in w:
            b.w = idx
            b.r = {}
        if eng == "pe" and not dma:
            deps = {d for d in deps if not (self.ops[d].eng == "pe" and not self.ops[d].dma)}
        o.deps = deps
        self.ops.append(o)
        if dma:
            self.dma_hist[eng].append(idx)
        else:
            self.last[eng] = idx
        return idx

    def barrier(self):
        deps = set(self.last.values())
        for q, h in self.dma_hist.items():
            deps.update(h[-self.NDSEM:])
        self.barrier_deps = deps

    def finalize(self):
        for o in self.ops:
            for d in o.deps:
                self.ops[d].flag = True
        cnt = {e: 0 for e in self.ENG}
        dcnt = {"sp": 0, "pool": 0, "act": 0}
        for o in self.ops:
            if o.dma:
                o.n = dcnt[o.eng]
                dcnt[o.eng] += 1
                o.tok = (("d", o.eng, o.n % self.NDSEM), 16 * (o.n // self.NDSEM + 1))
            elif o.flag:
                k = cnt[o.eng]
                cnt[o.eng] += 1
                o.tok = (("c", o.eng, k // self.CH), k % self.CH + 1)
        self.cnt = cnt
        self.dcnt = dcnt

    def sem_keys(self):
        keys = []
        for e in self.ENG:
            for i in range(max(1, (self.cnt[e] + self.CH - 1) // self.CH)):
                keys.append(("c", e, i))
        for q in ("sp", "pool", "act"):
            if self.dcnt[q]:
                for i in range(self.NDSEM):
                    keys.append(("d", q, i))
        return keys

    def emit_engine(self, ename, eng, sems, final_wait=False):
        known = {}
        for o in self.ops:
            if o.eng != ename:
                continue
            waits = {}
            for d in o.deps:
                k, v = self.ops[d].tok
                if waits.get(k, 0) < v:
                    waits[k] = v
            if o.dma and o.n >= self.NDSEM:
                k = ("d", o.eng, o.n % self.NDSEM)
                v = 16 * (o.n // self.NDSEM)
                if waits.get(k, 0) < v:
                    waits[k] = v
            for k, v in waits.items():
                if known.get(k, 0) < v:
                    eng.wait_ge(sems[k], v)
                    known[k] = v
            ins = o.fn(eng)
            if o.dma:
                ins.then_inc(sems[o.tok[0]], 16)
            elif o.flag:
                ins.then_inc(sems[o.tok[0]], 1)
        if final_wait:
            for q in ("sp", "pool", "act"):
                n = self.dcnt[q]
                for i in range(min(n, self.NDSEM)):
                    tot = (n - i + self.NDSEM - 1) // self.NDSEM
                    k = ("d", q, i)
                    if known.get(k, 0) < 16 * tot:
                        eng.wait_ge(sems[k], 16 * tot)


def _t5_bucket_np(d):
    n = np.maximum(d, 0)
    nf = np.maximum(n, 1).astype(np.float32)
    large = 16 + (np.log(nf / np.float32(16)) / np.float32(math.log(128 / 16)) * np.float32(16)).astype(np.int32)
    large = np.minimum(large, 31)
    return np.where(n < 16, n, large)


def _consts():
    c = {}
    c["ident"] = np.eye(128, dtype=np.float32)
    m = np.arange(384)
    bucket = _t5_bucket_np(np.where(m < 256, m, 0))
    c["onehot"] = (bucket[None, :] == np.arange(32)[:, None]).astype(np.float32)
    half = 32
    inv = (np.float32(10000.0) ** (-np.arange(half, dtype=np.float32) / np.float32(half))).astype(np.float32)
    c["inv64"] = np.concatenate([inv, inv]).reshape(64, 1).astype(np.float32)
    P = np.zeros((64, 64), np.float32)
    for i in range(32):
        P[i, i + 32] = -1.0
        P[i + 32, i] = 1.0
    c["rotT"] = np.ascontiguousarray(P.T)
    s = np.arange(128)[:, None]
    cc = np.arange(896)[None, :]
    c["cmask"] = (cc - 384 >= s).astype(np.float32)
    c["negmask"] = np.where(np.arange(128)[None, :] <= np.arange(128)[:, None], 0.0, -1e30).astype(np.float32)
    c["pow2"] = np.tile((0.5 ** np.arange(NITER)).astype(np.float32)[None, :], (128, 1))
    return c


CONSTS = _consts()


def build_nc(stop_after=99, debug=False):
    nc = bass.Bass("TRN2", target_bir_lowering=False)
    P = Prog()

    def din(name, shape, dt=F32):
        return nc.dram_tensor(name, list(shape), dt, kind="ExternalInput").ap()

    x_d = din("x", [T, D])
    pos_d = din("positions", [T], I32)
    g_pre_d = din("g_pre", [D])
    w_in_d = din("w_in", [D, IN_COLS])
    g_q_lat_d = din("g_q_lat", [1024])
    g_kv_lat_d = din("g_kv_lat", [512])
    w_uq_d = din("w_uq", [1024, 3072])
    w_ukv_d = din("w_ukv", [512, 4096])
    g_qn_a_d = din("g_qn_a", [192])
    g_kn_a_d = din("g_kn_a", [192])
    g_qn_b_d = din("g_qn_b", [128])
    g_kn_b_d = din("g_kn_b", [128])
    t5_d = din("t5_bias", [32, 16])
    p_a_d = din("p_a", [2048, D])
    p_b_d = din("p_b", [2048, D])
    w_o_d = din("w_o", [D, D])
    c_ident = din("c_ident", [128, 128])
    c_onehot = din("c_onehot", [32, 384])
    c_inv64 = din("c_inv64", [64, 1])
    c_rotT = din("c_rotT", [64, 64])
    c_cmask = din("c_cmask", [128, 896])
    c_negmask = din("c_negmask", [128, 128])
    c_pow2 = din("c_pow2", [128, NITER])

    out_d = nc.dram_tensor("out", [T, D], F32, kind="ExternalOutput").ap()

    def dscr(name, shape, dt):
        kind = "ExternalOutput" if debug else "Internal"
        return nc.dram_tensor(name, list(shape), dt, kind=kind).ap()

    projT = dscr("projT", [IN_COLS, T], BF16)
    kidxT = dscr("kidxT", [128, T], BF16)
    vb_s = dscr("vb_s", [T, 128], BF16)
    widx_s = dscr("widx_s", [T, 32], F32)
    oaT = dscr("oaT", [2048, T], BF16)
    obT = dscr("obT", [2048, T], BF16)
    Xb = dscr("Xb", [16, 129 * 384], F32)

    import contextlib
    st = contextlib.ExitStack()
    with st:
        arena = st.enter_context(nc.sbuf_tensor("arena", [128, 200 * 1024], U8))
        psum = st.enter_context(nc.psum_tensor("psum", [128, 4096], F32))

        def sb(off, shape, dt, parts=128, p0=0):
            n = int(np.prod(shape)) * SZ[dt]
            assert off + n <= 200 * 1024, (off, n)
            ap = arena[p0:p0 + parts, off:off + n].bitcast(dt)
            if len(shape) == 2:
                ap = ap.rearrange("p (a b) -> p a b", a=shape[0])
            elif len(shape) == 3:
                ap = ap.rearrange("p (a b c) -> p a b c", a=shape[0], b=shape[1])
            return ap

        def bank(i, n=1):
            return psum[:, i * 512:(i + n) * 512]

        def bank16(i):
            return psum[:, i * 512:(i + 1) * 512].bitcast(BF16)

        PSB = [Buf(f"ps{i}") for i in range(8)]

        class Alloc:
            def __init__(self, base=0):
                self.cur = base

            def __call__(self, shape, dt, parts=128):
                n = int(np.prod(shape)) * SZ[dt]
                off = self.cur
                self.cur = (off + n + 63) // 64 * 64
                return sb(off, shape, dt, parts)

        def dump(name, ap, reads, parts=128):
            if not debug:
                return
            shp = [parts] + list(ap.shape[1:])
            dt_ = nc.dram_tensor("dbg_" + name, shp, ap.dtype, kind="ExternalOutput").ap()
            P.op("sp", lambda e: e.dma_start(out=dt_, in_=ap), r=reads, dma=True)

        CB = 192 * 1024
        ca = Alloc(CB)
        ident_f = ca([128], F32)
        ident_b = ca([128], BF16)
        ones_b = ca([128], BF16)
        eps_t = ca([4], F32)
        negI_b = ca([128], BF16)
        B_const = Buf("consts")
        P.op("sp", lambda e: e.dma_start(out=ident_f, in_=c_ident), w=[B_const], dma=True)
        P.op("dve", lambda e: e.tensor_copy(out=ident_b, in_=ident_f), r=[B_const], w=[B_const])
        P.op("dve", lambda e: e.memset(ones_b, 1.0), w=[B_const])
        P.op("dve", lambda e: e.memset(eps_t, EPS), w=[B_const])
        P.op("dve", lambda e: e.tensor_scalar(out=negI_b, in0=ident_f, scalar1=-30000.0, scalar2=None, op0=ALU.mult),
             r=[B_const], w=[B_const])
        assert ca.cur <= 200 * 1024

        a = Alloc(0)
        hnT = a([32, T], BF16)
        B_hnT = [Buf(f"hnT{i}") for i in range(4)]
        p1_base = a.cur
        xt = [a([D], F32) for _ in range(2)]
        B_xt = [Buf("xt0"), Buf("xt1")]
        gbc = a([D], F32)
        B_gbc = Buf("gbc")
        hn_tm = a([D], BF16)
        B_hn = Buf("hn_tm")
        ss = a([NT], F32)
        rstd0 = a([NT], F32)
        B_ss = [Buf(f"ss{i}") for i in range(NT)]
        assert a.cur <= CB, a.cur

        P.op("sp", lambda e: e.dma_start(out=gbc, in_=g_pre_d.partition_broadcast(128)), w=[B_gbc], dma=True)
        for tt in range(NT):
            sl = tt % 2
            P.op("sp", lambda e, tt=tt, sl=sl: e.dma_start(out=xt[sl], in_=x_d[tt * 128:(tt + 1) * 128, :]),
                 w=[B_xt[sl]], dma=True)
            P.op("dve", lambda e, tt=tt, sl=sl: e.scalar_tensor_tensor(
                out=hn_tm, in0=xt[sl], scalar=1.0, in1=xt[sl], op0=ALU.mult, op1=ALU.mult,
                accum_out=ss[:, tt:tt + 1]), r=[B_xt[sl]], w=[B_hn, B_ss[tt]])
            P.op("act", lambda e, tt=tt: e.activation(out=rstd0[:, tt:tt + 1], in_=ss[:, tt:tt + 1], func=AF.Sqrt,
                                                      scale=1.0 / D, bias=eps_t[:, 0:1]),
                 r=[B_ss[tt], B_const], w=[B_ss[tt]])
            P.op("dve", lambda e, tt=tt: e.reciprocal(out=rstd0[:, tt:tt + 1], in_=rstd0[:, tt:tt + 1]),
                 r=[B_ss[tt]], w=[B_ss[tt]])
            P.op("dve", lambda e, tt=tt, sl=sl: e.scalar_tensor_tensor(
                out=hn_tm, in0=xt[sl], scalar=rstd0[:, tt:tt + 1], in1=gbc, op0=ALU.mult, op1=ALU.mult),
                 r=[B_xt[sl], B_ss[tt], B_gbc], w=[B_hn])
            for q in range(4):
                bk = (tt * 4 + q) % 8
                for j in range(8):
                    c = q * 8 + j
                    P.op("pe", lambda e, bk=bk, j=j, c=c: e.transpose(
                        out=bank16(bk)[:, j * 128:(j + 1) * 128], in_=hn_tm[:, c * 128:(c + 1) * 128],
                        identity=ident_b), r=[B_hn, B_const], w=[PSB[bk]])
                P.op("act", lambda e, bk=bk, q=q, tt=tt: e.activation(
                    out=hnT[:, q * 8:(q + 1) * 8, tt * 128:(tt + 1) * 128],
                    in_=bank16(bk).rearrange("p (a b) -> p a b", a=8), func=AF.Copy),
                     r=[PSB[bk]], w=[B_hnT[tt // 4]])
        P.barrier()

        a = Alloc(p1_base)
        WB = [a([32, 256], BF16) for _ in range(2)]
        B_WB = [Buf("WB0"), Buf("WB1")]
        SG = [a([T], BF16) for _ in range(2)]
        B_SG = [Buf("SG0"), Buf("SG1")]
        WT = a([32, 160], BF16)
        B_WT = Buf("WT")
        SGv = [a([128], BF16) for _ in range(2)]
        SGw = [a([32], F32) for _ in range(2)]
        B_SGv = [Buf("SGv0"), Buf("SGv1")]
        assert a.cur <= CB, a.cur

        groups = []

        def add_range(c0, c1, act):
            c = c0
            while c < c1:
                wdt = min(256, c1 - c)
                blks = []
                for o in range(0, wdt, 128):
                    blks.append((o, min(128, wdt - o), "proj", c + o, act))
                groups.append((c, wdt, blks))
                c += wdt

        add_range(C_CQ, C_KR, AF.Copy)
        groups.append((C_KR, 64, [(0, 64, "proj", C_KR, AF.Copy)]))
        add_range(C_QB, C_VB, AF.Copy)
        add_range(C_QI, C_KI, AF.Copy)
        groups.append((C_KI, 64, [(0, 128, "kidx", 0, AF.Copy)]))
        add_range(C_GA, C_MA, AF.Silu)
        add_range(C_MA, IN_COLS, AF.Sigmoid)

        def w_src(c0, wdt):
            return w_in_d[:, c0:c0 + wdt].rearrange("(k p) c -> p k c", p=128)

        def issue_load(gi):
            c0, wdt, blks = groups[gi]
            sl = gi % 2
            if blks[0][2] == "kidx":
                P.op("pool", lambda e: e.dma_start(out=WB[sl][:, :, 0:64], in_=w_src(c0, 64)), w=[B_WB[sl]], dma=True)
                P.op("pool", lambda e: e.dma_start(out=WB[sl][:, :, 64:128], in_=w_src(c0, 64)), w=[B_WB[sl]],
                     dma=True)
            else:
                P.op("pool", lambda e: e.dma_start(out=WB[sl][:, :, 0:wdt], in_=w_src(c0, wdt)), w=[B_WB[sl]],
                     dma=True)

        P.op("pool", lambda e: e.dma_start(out=WT[:, :, 0:128], in_=w_src(C_VB, 128)), w=[B_WT], dma=True)
        P.op("pool", lambda e: e.dma_start(out=WT[:, :, 128:160], in_=w_src(C_WI, 32)), w=[B_WT], dma=True)
        issue_load(0)
        issue_load(1)
        nblk = 0
        ngroups = len(groups) if stop_after >= 1 else 0
        for gi in range(ngroups):
            c0, wdt, blks = groups[gi]
            sl = gi % 2
            for (o, M, kind, row, act) in blks:
                bs = (nblk % 2) * 4
                sg = nblk % 2
                nblk += 1
                for k in range(32):
                    for tb in range(4):
                        P.op("pe", lambda e, k=k, tb=tb, bs=bs, o=o, M=M, sl=sl: e.matmul(
                            bank(bs + tb)[0:M, :], lhsT=WB[sl][:, k, o:o + M], rhs=hnT[:, k, tb * 512:(tb + 1) * 512],
                            start=(k == 0), stop=(k == 31)), r=[B_WB[sl], B_hnT[tb]], w=[PSB[bs + tb]])
                for tb in range(4):
                    P.op("act", lambda e, tb=tb, bs=bs, M=M, sg=sg, act=act: e.activation(
                        out=SG[sg][0:M, tb * 512:(tb + 1) * 512], in_=bank(bs + tb)[0:M, :], func=act),
                         r=[PSB[bs + tb]], w=[B_SG[sg]])
                if kind == "proj":
                    P.op("sp", lambda e, sg=sg, M=M, row=row: e.dma_start(out=projT[row:row + M, :], in_=SG[sg][0:M, :]),
                         r=[B_SG[sg]], dma=True)
                else:
                    P.op("sp", lambda e, sg=sg: e.dma_start(out=kidxT, in_=SG[sg]), r=[B_SG[sg]], dma=True)
            if gi + 2 < ngroups:
                issue_load(gi + 2)
        if stop_after >= 1:
            for tt in range(NT):
                bk = tt % 2
                sg = tt % 2
                for k in range(32):
                    P.op("pe", lambda e, k=k, tt=tt, bk=bk: e.matmul(
                        bank(bk)[:, 0:160], lhsT=hnT[:, k, tt * 128:(tt + 1) * 128], rhs=WT[:, k, :],
                        start=(k == 0), stop=(k == 31)), r=[B_WT, B_hnT[tt // 4]], w=[PSB[bk]])
                P.op("act", lambda e, bk=bk, sg=sg: e.activation(out=SGv[sg], in_=bank(bk)[:, 0:128], func=AF.Copy),
                     r=[PSB[bk]], w=[B_SGv[sg]])
                P.op("act", lambda e, bk=bk, sg=sg: e.activation(out=SGw[sg], in_=bank(bk)[:, 128:160], func=AF.Copy),
                     r=[PSB[bk]], w=[B_SGv[sg]])
                P.op("sp", lambda e, sg=sg, tt=tt: e.dma_start(out=vb_s[tt * 128:(tt + 1) * 128, :], in_=SGv[sg]),
                     r=[B_SGv[sg]], dma=True)
                P.op("sp", lambda e, sg=sg, tt=tt: e.dma_start(out=widx_s[tt * 128:(tt + 1) * 128, :], in_=SGw[sg]),
                     r=[B_SGv[sg]], dma=True)
        P.barrier()


        if stop_after >= 2:
            a = Alloc(0)
            cqT = a([8, T], BF16)
            ckvT = a([4, T], BF16)
            krb = a([T], BF16)
            Ct = a([T], F32)
            St = a([T], F32)
            Rk = a([T], F32)
            krsq = a([T], BF16)
            gql = a([8], F32)
            gkl = a([4], F32)
            gq_n = a([1], F32)
            gq_r = a([1], F32)
            gk_n = a([1], F32)
            gk_r = a([1], F32)
            inv64 = a([1], F32)
            rotT = a([64], F32)
            cm_f = a([896], F32)
            cm_b = a([896], BF16)
            B_lat = [Buf(f"lat{i}") for i in range(4)]
            B_kr = Buf("kr")
            B_g = Buf("g2")
            B_tab = Buf("tab")
            ph_base = a.cur
            posi = a([T], I32)
            posf = a([T], F32)
            ang = a([T], F32)
            tq = a([T], F32)
            nn = a([T], F32)
            B_tmp = Buf("tmp2")
            assert a.cur <= CB, a.cur
            H64 = slice(0, 64)

            def gvec(dst, src, n0, n1, eng="sp"):
                P.op(eng, lambda e: e.dma_start(out=dst, in_=src[n0:n1].rearrange("(p o) -> p o", o=1)), w=[B_g],
                     dma=True)

            P.op("sp", lambda e: e.dma_start(out=cqT, in_=projT[0:1024, :].rearrange("(k p) t -> p k t", p=128)),
                 w=B_lat, dma=True)
            P.op("sp", lambda e: e.dma_start(out=ckvT, in_=projT[1024:1536, :].rearrange("(k p) t -> p k t", p=128)),
                 w=B_lat, dma=True)
            P.op("sp", lambda e: e.dma_start(out=krb[H64], in_=projT[C_KR:C_KR + 64, :]), w=[B_kr], dma=True)
            P.op("sp", lambda e: e.dma_start(out=gql, in_=g_q_lat_d.rearrange("(k p) -> p k", p=128),
                                            allow_slow_non_contiguous=True), w=[B_g], dma=True)
            P.op("sp", lambda e: e.dma_start(out=gkl, in_=g_kv_lat_d.rearrange("(k p) -> p k", p=128),
                                            allow_slow_non_contiguous=True), w=[B_g], dma=True)
            gvec(gq_n, g_qn_a_d, 0, 128)
            gvec(gq_r[H64], g_qn_a_d, 128, 192)
            gvec(gk_n, g_kn_a_d, 0, 128)
            gvec(gk_r[H64], g_kn_a_d, 128, 192)
            P.op("sp", lambda e: e.dma_start(out=inv64[H64], in_=c_inv64), w=[B_g], dma=True)
            P.op("sp", lambda e: e.dma_start(out=rotT[H64], in_=c_rotT), w=[B_g], dma=True)
            P.op("sp", lambda e: e.dma_start(out=cm_f, in_=c_cmask), w=[B_g], dma=True)
            P.op("sp", lambda e: e.dma_start(out=posi[H64], in_=pos_d.partition_broadcast(64)), w=[B_tmp], dma=True)
            P.op("dve", lambda e: e.tensor_scalar(out=cm_b, in0=cm_f, scalar1=-1.0, scalar2=1.0, op0=ALU.mult,
                                                  op1=ALU.add), r=[B_g], w=[B_g])
            qs = 192.0 ** -0.5
            P.op("dve", lambda e: e.tensor_scalar(out=gq_n, in0=gq_n, scalar1=qs, scalar2=None, op0=ALU.mult),
                 r=[B_g], w=[B_g])
            P.op("dve", lambda e: e.tensor_scalar(out=gq_r[H64], in0=gq_r[H64], scalar1=qs, scalar2=None, op0=ALU.mult),
                 r=[B_g], w=[B_g])
            TWO_PI_INV = float(np.float32(1.0 / (2 * math.pi)))
            MAGIC = 12582912.0
            C1 = 6.28125
            C2 = float(2 * math.pi - 6.28125)
            PI_LO = 3.1415925
            P.op("dve", lambda e: e.tensor_copy(out=posf[H64], in_=posi[H64]), r=[B_tmp], w=[B_tmp])
            P.op("dve", lambda e: e.tensor_scalar(out=ang[H64], in0=posf[H64], scalar1=inv64[H64, 0:1], scalar2=None,
                                                  op0=ALU.mult), r=[B_tmp, B_g], w=[B_tmp])
            for which, dst in (("sin", St), ("cos", Ct)):
                if which == "cos":
                    P.op("dve", lambda e: e.tensor_scalar(out=ang[H64], in0=ang[H64], scalar1=float(math.pi / 2),
                                                          scalar2=None, op0=ALU.add), r=[B_tmp], w=[B_tmp])
                P.op("dve", lambda e: e.tensor_scalar(out=tq[H64], in0=ang[H64], scalar1=TWO_PI_INV, scalar2=MAGIC,
                                                      op0=ALU.mult, op1=ALU.add), r=[B_tmp], w=[B_tmp])
                P.op("dve", lambda e: e.tensor_scalar(out=nn[H64], in0=tq[H64], scalar1=-MAGIC, scalar2=None,
                                                      op0=ALU.add), r=[B_tmp], w=[B_tmp])
                P.op("dve", lambda e: e.scalar_tensor_tensor(out=tq[H64], in0=nn[H64], scalar=-C1, in1=ang[H64],
                                                             op0=ALU.mult, op1=ALU.add), r=[B_tmp], w=[B_tmp])
                P.op("dve", lambda e: e.scalar_tensor_tensor(out=tq[H64], in0=nn[H64], scalar=-C2, in1=tq[H64],
                                                             op0=ALU.mult, op1=ALU.add), r=[B_tmp], w=[B_tmp])
                P.op("dve", lambda e: e.tensor_scalar(out=tq[H64], in0=tq[H64], scalar1=-PI_LO, scalar2=PI_LO,
                                                      op0=ALU.max, op1=ALU.min), r=[B_tmp], w=[B_tmp])
                P.op("act", lambda e, dst=dst: e.activation(out=dst[H64], in_=tq[H64], func=AF.Sin),
                     r=[B_tmp], w=[B_tab])
            P.barrier()

            a = Alloc(ph_base)
            wq = [a([8, 192], BF16) for _ in range(2)]
            wkv = [a([4, 256], BF16) for _ in range(2)]
            qn = [a([T], BF16) for _ in range(2)]
            qr = [a([T], BF16) for _ in range(2)]
            kn = [a([T], BF16) for _ in range(2)]
            kr = [a([T], BF16) for _ in range(2)]
            vh = [a([16, 128], BF16) for _ in range(2)]
            gt = [a([T], BF16) for _ in range(2)]
            oh = [a([T], BF16) for _ in range(2)]
            B_wq = [Buf("wq0"), Buf("wq1")]
            B_wkv = [Buf("wkv0"), Buf("wkv1")]
            B_q = [Buf("q0"), Buf("q1")]
            B_k = [Buf("k0"), Buf("k1")]
            B_v = [Buf("v0"), Buf("v1")]
            B_gt = [Buf("gt0"), Buf("gt1")]
            B_oh = [Buf("oh0"), Buf("oh1")]
            NTMP = 2
            sq = [a([512], BF16) for _ in range(NTMP)]
            sqr = [a([512], BF16) for _ in range(NTMP)]
            rs = [a([512], F32) for _ in range(NTMP)]
            yv = [a([512], F32) for _ in range(NTMP)]
            t1 = [a([512], F32) for _ in range(NTMP)]
            t2 = [a([512], F32) for _ in range(NTMP)]
            B_sq = [Buf(f"sq{i}") for i in range(NTMP)]
            B_sqr = [Buf(f"sqr{i}") for i in range(NTMP)]
            B_rs = [Buf(f"rs{i}") for i in range(NTMP)]
            B_y = [Buf(f"y{i}") for i in range(NTMP)]
            B_t1 = [Buf(f"t1{i}") for i in range(NTMP)]
            B_t2 = [Buf(f"t2{i}") for i in range(NTMP)]
            NPT = 4
            pT = [a([512], BF16) for _ in range(NPT)]
            B_pT = [Buf(f"pT{i}") for i in range(NPT)]
            rden = [a([512], F32) for _ in range(2)]
            og = [a([512], F32) for _ in range(2)]
            B_rden = [Buf("rden0"), Buf("rden1")]
            B_og = [Buf("og0"), Buf("og1")]
            ocp = [a([512], F32) for _ in range(2)]
            B_ocp = [Buf("ocp0"), Buf("ocp1")]
            assert a.cur <= CB, a.cur
            for _sl in range(2):
                P.op("pool", lambda e, _sl=_sl: e.memset(qr[_sl][64:128, :], 0.0), w=[B_q[_sl]])
                P.op("pool", lambda e, _sl=_sl: e.memset(kr[_sl][64:128, :], 0.0), w=[B_k[_sl]])
            BA, BB, BC, BD = 0, 1, 2, 3
            BS = (4, 5)
            BO, BDEN = 6, 7
            cnt = {"tmp": 0, "pt": 0, "s": 0, "fin": 0}

            def rstd_from(bankc, tsl, scale, parts=128):
                P.op("act", lambda e: e.activation(out=rs[tsl], in_=bank(bankc), func=AF.Ln, scale=scale,
                                                   bias=eps_t[:, 0:1]), r=[PSB[bankc], B_const], w=[B_rs[tsl]])
                P.op("act", lambda e: e.activation(out=rs[tsl], in_=rs[tsl], func=AF.Exp, scale=-0.5),
                     r=[B_rs[tsl]], w=[B_rs[tsl]])

            for (latT, nk, gl, nfeat, li) in ((cqT, 8, gql, 1024.0, 0), (ckvT, 4, gkl, 512.0, 1)):
                for tb in range(4):
                    tsl = cnt["tmp"] % NTMP
                    cnt["tmp"] += 1
                    ts_ = slice(tb * 512, (tb + 1) * 512)
                    for k in range(nk):
                        sl2 = k % NTMP
                        P.op("act", lambda e, k=k, sl2=sl2, latT=latT, ts_=ts_: e.activation(
                            out=sq[sl2], in_=latT[:, k, ts_], func=AF.Square),
                             r=[B_lat[tb]], w=[B_sq[sl2]])
                        P.op("pe", lambda e, k=k, sl2=sl2, nk=nk: e.matmul(bank(BC), lhsT=ones_b, rhs=sq[sl2],
                                                                         start=(k == 0), stop=(k == nk - 1)),
                             r=[B_sq[sl2], B_const], w=[PSB[BC]])
                    rstd_from(BC, tsl, 1.0 / nfeat)
                    for k in range(nk):
                        P.op("dve", lambda e, k=k, latT=latT, ts_=ts_, gl=gl, tsl=tsl: e.scalar_tensor_tensor(
                            out=latT[:, k, ts_], in0=latT[:, k, ts_], scalar=gl[:, k:k + 1], in1=rs[tsl],
                            op0=ALU.mult, op1=ALU.mult), r=[B_rs[tsl], B_g, B_lat[tb]], w=[B_lat[tb]])

            P.op("pool", lambda e: e.tensor_tensor(out=krsq[H64], in0=krb[H64], in1=krb[H64], op=ALU.mult),
                 r=[B_kr], w=[B_kr])
            for tb in range(4):
                tsl = cnt["tmp"] % NTMP
                cnt["tmp"] += 1
                ts_ = slice(tb * 512, (tb + 1) * 512)
                P.op("dve", lambda e, ts_=ts_, tsl=tsl: e.tensor_scalar(out=yv[tsl][H64], in0=krb[H64, ts_],
                                                                        scalar1=gk_r[H64, 0:1], scalar2=None,
                                                                        op0=ALU.mult), r=[B_kr, B_g], w=[B_y[tsl]])
                P.op("pe", lambda e, tsl=tsl: e.matmul(bank(BD)[H64, :], lhsT=rotT[H64], rhs=yv[tsl][H64],
                                                       start=True, stop=True), r=[B_y[tsl], B_g], w=[PSB[BD]])
                P.op("pool", lambda e, ts_=ts_, tsl=tsl: e.tensor_tensor(out=t1[tsl][H64], in0=yv[tsl][H64],
                                                                         in1=Ct[H64, ts_], op=ALU.mult),
                     r=[B_y[tsl], B_tab], w=[B_t1[tsl]])
                P.op("dve", lambda e, ts_=ts_, tsl=tsl: e.tensor_tensor(out=t2[tsl][H64], in0=bank(BD)[H64, :],
                                                                        in1=St[H64, ts_], op=ALU.mult),
                     r=[PSB[BD], B_tab], w=[B_t2[tsl]])
                P.op("pool", lambda e, ts_=ts_, tsl=tsl: e.tensor_tensor(out=Rk[H64, ts_], in0=t1[tsl][H64],
                                                                         in1=t2[tsl][H64], op=ALU.add),
                     r=[B_t1[tsl], B_t2[tsl]], w=[B_tab])

            def q_stages(h, sl, tb):
                ts_ = slice(tb * 512, (tb + 1) * 512)
                tsl = cnt["tmp"] % NTMP
                cnt["tmp"] += 1

                def s1():
                    for k in range(8):
                        P.op("pe", lambda e, k=k: e.matmul(bank(BA), lhsT=wq[sl][:, k, 0:128], rhs=cqT[:, k, ts_],
                                                           start=(k == 0), stop=(k == 7)),
                             r=[B_wq[sl], B_lat[tb]], w=[PSB[BA]])
                    for k in range(8):
                        P.op("pe", lambda e, k=k: e.matmul(bank(BB)[H64, :], lhsT=wq[sl][:, k, 128:192],
                                                           rhs=cqT[:, k, ts_], start=(k == 0), stop=(k == 7)),
                             r=[B_wq[sl], B_lat[tb]], w=[PSB[BB]])

                def s2():
                    P.op("act", lambda e: e.activation(out=sq[tsl], in_=bank(BA), func=AF.Square),
                         r=[PSB[BA]], w=[B_sq[tsl]])
                    P.op("act", lambda e: e.activation(out=sqr[tsl][H64], in_=bank(BB)[H64, :], func=AF.Square),
                         r=[PSB[BB]], w=[B_sqr[tsl]])

                def s3():
                    P.op("pe", lambda e: e.matmul(bank(BC), lhsT=ones_b, rhs=sq[tsl], start=True, stop=False),
                         r=[B_sq[tsl], B_const], w=[PSB[BC]])
                    P.op("pe", lambda e: e.matmul(bank(BC), lhsT=ones_b[H64, :], rhs=sqr[tsl][H64], start=False,
                                                  stop=True), r=[B_sqr[tsl], B_const], w=[PSB[BC]])

                def s4():
                    rstd_from(BC, tsl, 1.0 / 192.0)
                    P.op("dve", lambda e: e.scalar_tensor_tensor(out=qn[sl][:, ts_], in0=bank(BA), scalar=gq_n[:, 0:1],
                                                                 in1=rs[tsl], op0=ALU.mult, op1=ALU.mult),
                         r=[PSB[BA], B_rs[tsl], B_g], w=[B_q[sl]])
                    P.op("dve", lambda e: e.tensor_scalar(out=yv[tsl][H64], in0=bank(BB)[H64, :], scalar1=gq_r[H64, 0:1],
                                                          scalar2=None, op0=ALU.mult), r=[PSB[BB], B_g], w=[B_y[tsl]])

                def s5():
                    P.op("pe", lambda e: e.matmul(bank(BD)[H64, :], lhsT=rotT[H64], rhs=yv[tsl][H64], start=True,
                                                  stop=True), r=[B_y[tsl], B_g], w=[PSB[BD]])

                def s6():
                    P.op("pool", lambda e: e.tensor_tensor(out=t1[tsl][H64], in0=yv[tsl][H64], in1=Ct[H64, ts_],
                                                           op=ALU.mult), r=[B_y[tsl], B_tab], w=[B_t1[tsl]])
                    P.op("dve", lambda e: e.tensor_tensor(out=t2[tsl][H64], in0=bank(BD)[H64, :], in1=St[H64, ts_],
                                                          op=ALU.mult), r=[PSB[BD], B_tab], w=[B_t2[tsl]])
                    P.op("dve", lambda e: e.tensor_tensor(out=t1[tsl][H64], in0=t1[tsl][H64], in1=t2[tsl][H64],
                                                          op=ALU.add), r=[B_t1[tsl], B_t2[tsl]], w=[B_t1[tsl]])
                    P.op("dve", lambda e: e.tensor_tensor(out=qr[sl][H64, ts_], in0=t1[tsl][H64], in1=rs[tsl][H64],
                                                          op=ALU.mult), r=[B_t1[tsl], B_rs[tsl]], w=[B_q[sl]])

                return [s1, s2, s3, s4, s5, s6]

            def k_stages(h, sl, tb):
                ts_ = slice(tb * 512, (tb + 1) * 512)
                tsl = cnt["tmp"] % NTMP
                cnt["tmp"] += 1

                def s1():
                    for k in range(4):
                        P.op("pe", lambda e, k=k: e.matmul(bank(BA), lhsT=wkv[sl][:, k, 0:128], rhs=ckvT[:, k, ts_],
                                                           start=(k == 0), stop=(k == 3)),
                             r=[B_wkv[sl], B_lat[tb]], w=[PSB[BA]])

                def s2():
                    P.op("act", lambda e: e.activation(out=sq[tsl], in_=bank(BA), func=AF.Square),
                         r=[PSB[BA]], w=[B_sq[tsl]])

                def s3():
                    P.op("pe", lambda e: e.matmul(bank(BC), lhsT=ones_b, rhs=sq[tsl], start=True, stop=False),
                         r=[B_sq[tsl], B_const], w=[PSB[BC]])
                    P.op("pe", lambda e: e.matmul(bank(BC), lhsT=ones_b[H64, :], rhs=krsq[H64, ts_], start=False,
                                                  stop=True), r=[B_kr, B_const], w=[PSB[BC]])

                def s4():
                    rstd_from(BC, tsl, 1.0 / 192.0)
                    P.op("dve", lambda e: e.scalar_tensor_tensor(out=kn[sl][:, ts_], in0=bank(BA), scalar=gk_n[:, 0:1],
                                                                 in1=rs[tsl], op0=ALU.mult, op1=ALU.mult),
                         r=[PSB[BA], B_rs[tsl], B_g], w=[B_k[sl]])
                    P.op("dve", lambda e: e.tensor_tensor(out=kr[sl][H64, ts_], in0=Rk[H64, ts_], in1=rs[tsl][H64],
                                                          op=ALU.mult), r=[B_tab, B_rs[tsl]], w=[B_k[sl]])

                return [s1, s2, s3, s4]

            def v_stages(sl, g4):
                def s1():
                    for i in range(4):
                        kt = g4 * 4 + i
                        for k in range(4):
                            P.op("pe", lambda e, k=k, i=i, kt=kt: e.matmul(
                                bank(BD)[:, i * 128:(i + 1) * 128], lhsT=ckvT[:, k, kt * 128:(kt + 1) * 128],
                                rhs=wkv[sl][:, k, 128:256], start=(k == 0), stop=(k == 3)),
                                 r=[B_wkv[sl], B_lat[kt // 4]], w=[PSB[BD]])

                def s2():
                    P.op("act", lambda e: e.activation(
                        out=vh[sl][:, g4 * 4:(g4 + 1) * 4, :], in_=bank(BD).rearrange("p (a b) -> p a b", a=4),
                        func=AF.Copy), r=[PSB[BD]], w=[B_v[sl]])

                return [s1, s2]

            def prep_stages(h, sl):
                def loads():
                    P.op("pool", lambda e: e.dma_start(out=wq[sl], in_=w_uq_d[:, h * 192:(h + 1) * 192].rearrange(
                        "(k p) c -> p k c", p=128)), w=[B_wq[sl]], dma=True)
                    P.op("pool", lambda e: e.dma_start(out=wkv[sl], in_=w_ukv_d[:, h * 256:(h + 1) * 256].rearrange(
                        "(k p) c -> p k c", p=128)), w=[B_wkv[sl]], dma=True)
                    P.op("sp", lambda e: e.dma_start(out=gt[sl], in_=projT[C_GA + h * 128:C_GA + (h + 1) * 128, :]),
                         w=[B_gt[sl]], dma=True)

                S = [loads]
                for tb in range(4):
                    S.extend(q_stages(h, sl, tb))
                    S.extend(k_stages(h, sl, tb))
                for g4 in range(4):
                    S.extend(v_stages(sl, g4))
                return S

            def attn_units_a(h, sl):
                U = []
                for qb in range(4):
                    nj = 4 * qb + 4
                    for j in range(nj):
                        U.append(make_unit_a(h, sl, qb, j, nj))
                return U

            def make_unit_a(h, sl, qb, j, nj):
                qs_ = qb * 512
                lo = max(0, 128 * j - 512 * qb)
                ks = slice(j * 128, (j + 1) * 128)
                qsl = slice(qs_ + lo, qs_ + 512)
                stt = {}

                def A():
                    bs = BS[cnt["s"] % 2]
                    cnt["s"] += 1
                    pi = cnt["pt"] % NPT
                    cnt["pt"] += 1
                    stt["pi"] = pi
                    P.op("pe", lambda e: e.matmul(bank(bs)[:, lo:512], lhsT=kn[sl][:, ks], rhs=qn[sl][:, qsl],
                                                  start=True, stop=False), r=[B_k[sl], B_q[sl]], w=[PSB[bs]])
                    diag = j >= 4 * qb
                    P.op("pe", lambda e: e.matmul(bank(bs)[:, lo:512], lhsT=kr[sl][:, ks], rhs=qr[sl][:, qsl],
                                                  start=False, stop=not diag), r=[B_k[sl], B_q[sl]], w=[PSB[bs]])
                    if diag:
                        jj = j - 4 * qb
                        m0 = 384 - 128 * jj + lo
                        P.op("pe", lambda e: e.matmul(bank(bs)[:, lo:512], lhsT=negI_b, rhs=cm_b[:, m0:m0 + 512 - lo],
                                                      start=False, stop=True), r=[B_const, B_g], w=[PSB[bs]])
                    P.op("act", lambda e: e.activation(out=pT[pi][:, lo:512], in_=bank(bs)[:, lo:512], func=AF.Exp),
                         r=[PSB[bs]], w=[B_pT[pi]])

                def Bf():
                    pi = stt["pi"]
                    P.op("pe", lambda e: e.matmul(bank(BO)[:, lo:512], lhsT=vh[sl][:, j, :], rhs=pT[pi][:, lo:512],
                                                  start=(j == 0), stop=(j == nj - 1)), r=[B_v[sl], B_pT[pi]],
                         w=[PSB[BO]])
                    P.op("pe", lambda e: e.matmul(bank(BDEN)[:, lo:512], lhsT=ones_b, rhs=pT[pi][:, lo:512],
                                                  start=(j == 0), stop=(j == nj - 1)), r=[B_const, B_pT[pi]],
                         w=[PSB[BDEN]])
                    if j == nj - 1:
                        fi = cnt["fin"] % 2
                        cnt["fin"] += 1
                        P.op("act", lambda e: e.activation(out=rden[fi], in_=bank(BDEN), func=AF.Ln),
                             r=[PSB[BDEN]], w=[B_rden[fi]])
                        P.op("dve", lambda e: e.tensor_copy(out=ocp[fi], in_=bank(BO)), r=[PSB[BO]], w=[B_ocp[fi]])
                        P.op("act", lambda e: e.activation(out=rden[fi], in_=rden[fi], func=AF.Exp, scale=-1.0),
                             r=[B_rden[fi]], w=[B_rden[fi]])
                        P.op("dve", lambda e: e.tensor_tensor(out=og[fi], in0=rden[fi], in1=gt[sl][:, qs_:qs_ + 512],
                                                              op=ALU.mult), r=[B_rden[fi], B_gt[sl]], w=[B_og[fi]])
                        P.op("dve", lambda e: e.tensor_tensor(out=oh[sl][:, qs_:qs_ + 512], in0=ocp[fi], in1=og[fi],
                                                              op=ALU.mult), r=[B_ocp[fi], B_og[fi]], w=[B_oh[sl]])
                        if qb == 3:
                            P.op("sp", lambda e: e.dma_start(out=oaT[h * 128:(h + 1) * 128, :], in_=oh[sl]),
                                 r=[B_oh[sl]], dma=True)

                return (A, Bf)

            import os as _os
            NH_A = int(_os.environ.get('NH_A', '16'))
            pend = [None]

            def run_unit(u):
                u[0]()
                if pend[0] is not None:
                    pend[0]()
                pend[0] = u[1]

            for st_ in prep_stages(0, 0):
                st_()
            for h in range(NH_A):
                U = attn_units_a(h, h % 2)
                Pn = prep_stages(h + 1, (h + 1) % 2) if h + 1 < NH_A else []
                kk = 0
                for idx, u in enumerate(U):
                    run_unit(u)
                    tgt = min(len(Pn), ((idx + 1) * len(Pn) + len(U) - 1) // len(U))
                    while kk < tgt:
                        Pn[kk]()
                        kk += 1
                while kk < len(Pn):
                    Pn[kk]()
                    kk += 1
            if pend[0] is not None:
                pend[0]()
            P.barrier()

        if stop_after >= 3:
            a = Alloc(0)
            kiT = a([T], BF16)
            qiT = a([16, 512], BF16)
            wi = a([4, 32], F32)
            wabs = a([4, 32], F32)
            wsgn = a([4, 32], F32)
            acc = [a([T], F32) for _ in range(2)]
            tmpr = [a([1024], BF16) for _ in range(2)]
            dsg = [a([32, 128], BF16) for _ in range(2)]
            junk = a([T], BF16)
            mrow = [a([T], BF16) for _ in range(2)]
            maskT = a([16, 512], BF16)
            qbT = a([16, 512], BF16)
            kbT = a([T], BF16)
            vb = a([16, 128], BF16)
            BThi = a([16, 256], BF16)
            BTlo = a([16, 256], BF16)
            b31 = a([16], F32)
            gqb = a([1], F32)
            gkb = a([1], F32)
            gtb = a([16, 512], BF16)
            ob_off = a.cur
            ob = a([16, 512], BF16)
            BT = sb(ob_off, [16, 256], F32)
            W0 = a([1], F32)
            Wtab = a([NITER], F32)
            mid = a([1], F32)
            cntv = a([1], F32)
            sgn = a([1], F32)
            thr = a([1], F32)
            pow2 = a([NITER], F32)
            negm = a([128], F32)
            t5s = a([16], F32)
            ohs = a([384], F32)
            F16 = a([384], F32)
            sqb = [a([512], BF16) for _ in range(2)]
            rs3 = [a([512], F32) for _ in range(2)]
            NPT3 = 4
            pT3v = [a([512], BF16) for _ in range(NPT3)]
            rden3v = [a([512], F32) for _ in range(2)]
            og3v = [a([512], F32) for _ in range(2)]
            ocp3 = [a([512], F32) for _ in range(2)]
            B_ocp3 = [Buf("ocp30"), Buf("ocp31")]
            kiT_o = a([T], BF16)
            assert a.cur <= CB, a.cur
            B_ki = Buf("kiT"); B_qi = Buf("qiT"); B_wi = Buf("wi")
            B_acc = [Buf("acc0"), Buf("acc1")]
            B_tmpr = [Buf("tmpr0"), Buf("tmpr1")]
            B_junk = Buf("junk")
            B_mrow = [Buf("mrow0"), Buf("mrow1")]
            B_dsg = [Buf("dsg0"), Buf("dsg1")]
            B_maskT = Buf("maskT")
            B_qb = Buf("qbT"); B_kb = Buf("kbT"); B_vb = Buf("vb"); B_BT = Buf("BT"); B_g3 = Buf("g3")
            B_gtb = Buf("gtb"); B_ob = B_BT; B_bis = Buf("bis"); B_t5 = Buf("t5")
            B_sqb = [Buf("sqb0"), Buf("sqb1")]
            B_rs3 = [Buf("rs30"), Buf("rs31")]
            B_pT3 = [Buf(f"pT3{i}") for i in range(NPT3)]
            B_tn = [Buf("tn0"), Buf("tn1")]
            B_rden3 = [Buf("rden30"), Buf("rden31")]
            B_og3 = [Buf("og30"), Buf("og31")]
            H32 = slice(0, 32)
            H16 = slice(0, 16)
            c3 = {"ds": 0, "tmp": 0, "lg": 0, "mr": 0, "tp": 0, "s": 0, "pt": 0, "tn": 0, "fin": 0, "acc": 0}

            P.op("sp", lambda e: e.dma_start(out=kiT[0:64, :], in_=kidxT[0:64, :]), w=[B_ki], dma=True)
            P.op("sp", lambda e: e.dma_start(out=kiT_o[64:128, :], in_=kidxT[64:128, :]), w=[B_ki], dma=True)
            P.op("pool", lambda e: e.memset(kiT[64:128, :], 0.0), w=[B_ki])
            P.op("pool", lambda e: e.memset(kiT_o[0:64, :], 0.0), w=[B_ki])
            P.op("sp", lambda e: e.dma_start(out=kbT, in_=projT[C_KB:C_KB + 128, :]), w=[B_kb], dma=True)
            P.op("sp", lambda e: e.dma_start(out=vb, in_=vb_s.rearrange("(j p) d -> p j d", p=128)), w=[B_vb], dma=True)
            P.op("sp", lambda e: e.dma_start(out=gqb, in_=g_qn_b_d.rearrange("(p o) -> p o", o=1)), w=[B_g3], dma=True)
            P.op("sp", lambda e: e.dma_start(out=gkb, in_=g_kn_b_d.rearrange("(p o) -> p o", o=1)), w=[B_g3], dma=True)
            P.op("sp", lambda e: e.dma_start(out=pow2, in_=c_pow2), w=[B_g3], dma=True)
            P.op("sp", lambda e: e.dma_start(out=negm, in_=c_negmask), w=[B_g3], dma=True)
            P.op("sp", lambda e: e.dma_start(out=t5s[H32], in_=t5_d), w=[B_t5], dma=True)
            P.op("sp", lambda e: e.dma_start(out=ohs[H32], in_=c_onehot), w=[B_t5], dma=True)
            P.op("sp", lambda e: e.dma_start(out=b31, in_=t5_d[31, :].partition_broadcast(128)), w=[B_g3], dma=True)
            P.op("dve", lambda e: e.tensor_scalar(out=gqb, in0=gqb, scalar1=128.0 ** -0.5, scalar2=None, op0=ALU.mult),
                 r=[B_g3], w=[B_g3])
            P.op("pe", lambda e: e.matmul(bank(7)[H16, 0:384], lhsT=t5s[H32], rhs=ohs[H32], start=True, stop=True),
                 r=[B_t5], w=[PSB[7]])
            P.op("dve", lambda e: e.tensor_copy(out=F16[H16], in_=bank(7)[H16, 0:384]), r=[PSB[7]], w=[B_t5])
            B_Xb = Buf("Xb")
            P.op("sp", lambda e: e.dma_start(out=Xb.rearrange("h (r m) -> h r m", m=384),
                                             in_=F16[H16].unsqueeze(1).broadcast_to([16, 129, 384])),
                 r=[B_t5], w=[B_Xb], dma=True)
            P.op("sp", lambda e: e.dma_start(out=BT, in_=bass.AP(Xb.tensor, 0, [[383, 128], [129 * 384, 16], [1, 256]])),
                 r=[B_Xb], w=[B_BT], dma=True)
            B_BTs = Buf("BTs")

            def bt_fix(h):
                P.op("dve", lambda e: e.tensor_scalar(out=BT[:, h, :], in0=BT[:, h, :], scalar1=b31[:, h:h + 1],
                                                      scalar2=None, op0=ALU.subtract), r=[B_BT, B_g3], w=[B_BT])

            for h in range(16):
                bt_fix(h)
            P.op("dve", lambda e: e.tensor_copy(out=BThi, in_=BT), r=[B_BT], w=[B_BTs])
            P.op("dve", lambda e: e.tensor_tensor(out=BTlo, in0=BT, in1=BThi, op=ALU.subtract), r=[B_BT, B_BTs],
                 w=[B_BTs])

            def rstd3(bankc, tsl, scale):
                P.op("act", lambda e: e.activation(out=rs3[tsl], in_=bank(bankc), func=AF.Ln, scale=scale,
                                                   bias=eps_t[:, 0:1]), r=[PSB[bankc], B_const], w=[B_rs3[tsl]])
                P.op("act", lambda e: e.activation(out=rs3[tsl], in_=rs3[tsl], func=AF.Exp, scale=-0.5),
                     r=[B_rs3[tsl]], w=[B_rs3[tsl]])

            def norm_block(src_ap, gvec_ap, Bsrc):
                tsl = c3["tmp"] % 2
                c3["tmp"] += 1
                P.op("act", lambda e: e.activation(out=sqb[tsl], in_=src_ap, func=AF.Square), r=[Bsrc],
                     w=[B_sqb[tsl]])
                P.op("pe", lambda e: e.matmul(bank(6), lhsT=ones_b, rhs=sqb[tsl], start=True, stop=True),
                     r=[B_sqb[tsl], B_const], w=[PSB[6]])
                rstd3(6, tsl, 1.0 / 128.0)
                P.op("dve", lambda e: e.scalar_tensor_tensor(out=src_ap, in0=src_ap, scalar=gvec_ap, in1=rs3[tsl],
                                                             op0=ALU.mult, op1=ALU.mult),
                     r=[Bsrc, B_rs3[tsl], B_g3], w=[Bsrc])

            for tb in range(4):
                norm_block(kbT[:, tb * 512:(tb + 1) * 512], gkb[:, 0:1], B_kb)

            def norm_closures(src_ap, gvec_ap, Bsrc, bk):
                stt = {}

                def c_sq():
                    tsl = c3["tmp"] % 2
                    c3["tmp"] += 1
                    stt["t"] = tsl
                    P.op("act", lambda e: e.activation(out=sqb[tsl], in_=src_ap, func=AF.Square), r=[Bsrc],
                         w=[B_sqb[tsl]])

                def c_rest():
                    tsl = stt["t"]
                    P.op("pe", lambda e: e.matmul(bank(bk), lhsT=ones_b, rhs=sqb[tsl], start=True, stop=True),
                         r=[B_sqb[tsl], B_const], w=[PSB[bk]])
                    rstd3(bk, tsl, 1.0 / 128.0)
                    P.op("dve", lambda e: e.scalar_tensor_tensor(out=src_ap, in0=src_ap, scalar=gvec_ap, in1=rs3[tsl],
                                                                 op0=ALU.mult, op1=ALU.mult),
                         r=[Bsrc, B_rs3[tsl], B_g3], w=[Bsrc])

                return [c_sq, c_rest]

            def idx_part(qb, i, g, side=None):
                side = list(side) if side else []
                qt = qb * 4 + i
                nk = (qt + 1) * 128
                accv = acc[g % 2]
                Bacc = B_acc[g % 2]
                qsl = slice(i * 128, (i + 1) * 128)
                dsl = g % 2
                P.op("pool", lambda e: e.tensor_tensor(
                    out=dsg[dsl], in0=ident_f.unsqueeze(1).broadcast_to([128, 32, 128]),
                    in1=wsgn[:, i, :].unsqueeze(2).broadcast_to([128, 32, 128]), op=ALU.mult),
                     r=[B_const, B_wi], w=[B_dsg[dsl]])

                def mk(h, k0, wk, sbk):
                    c = h // 2
                    hp = slice((h % 2) * 64, (h % 2) * 64 + 64)
                    stt = {}

                    def L():
                        lg = c3["lg"] % 2
                        c3["lg"] += 1
                        stt["lg"] = lg
                        pb_ = bank(2 * lg, 2)
                        for off in range(0, wk, 512):
                            w_ = min(512, wk - off)
                            P.op("pe", lambda e, off=off, w_=w_: e.matmul(
                                pb_[:, off:off + w_], lhsT=qiT[:, c, qsl],
                                rhs=(kiT if h % 2 == 0 else kiT_o)[:, k0 + off:k0 + off + w_],
                                start=True, stop=True), r=[B_qi, B_ki], w=[PSB[2 * lg + off // 512]])
                        rb = [PSB[2 * lg + o // 512] for o in range(0, wk, 512)]
                        P.op("act", lambda e: e.activation(out=tmpr[lg][:, 0:wk], in_=pb_[:, 0:wk], func=AF.Relu,
                                                           scale=wabs[:, i, h:h + 1]), r=rb + [B_wi], w=[B_tmpr[lg]])

                    def A():
                        lg = stt["lg"]
                        for off in range(0, wk, 512):
                            w_ = min(512, wk - off)
                            P.op("pe", lambda e, off=off, w_=w_: e.matmul(
                                bank(sbk + off // 512)[:, 0:w_], lhsT=dsg[dsl][:, h, :], rhs=tmpr[lg][:, off:off + w_],
                                start=(h == 0), stop=(h == 31)), r=[B_dsg[dsl], B_tmpr[lg]],
                                 w=[PSB[sbk + off // 512]])

                    return (L, A)

                for k0 in range(0, nk, 1024):
                    wk = min(1024, nk - k0)
                    sbk = 4 + 2 * ((k0 // 1024) % 2)
                    prev = None
                    for h in range(32):
                        u = mk(h, k0, wk, sbk)
                        u[0]()
                        if prev is not None:
                            prev[1]()
                        prev = u
                        if k0 == 0 and side:
                            side.pop(0)()
                    prev[1]()
                    while k0 == 0 and side:
                        side.pop(0)()
                    for off in range(0, wk, 512):
                        w_ = min(512, wk - off)
                        P.op("act", lambda e, off=off, w_=w_, k0=k0, sbk=sbk: e.activation(
                            out=accv[:, k0 + off:k0 + off + w_], in_=bank(sbk + off // 512)[:, 0:w_], func=AF.Copy),
                             r=[PSB[sbk + off // 512]], w=[Bacc])

            def bis_part(qb, i, g):
                qt = qb * 4 + i
                nk = (qt + 1) * 128
                accv = acc[g % 2]
                Bacc = B_acc[g % 2]
                P.op("dve", lambda e: e.tensor_reduce(out=W0[:, 0:1], in_=accv[:, 0:nk], axis=AX.X, op=ALU.max,
                                                      apply_absolute_value=True), r=[Bacc], w=[B_bis])
                P.op("dve", lambda e: e.tensor_scalar(out=W0, in0=W0, scalar1=1.001, scalar2=1e-6, op0=ALU.mult,
                                                      op1=ALU.add), r=[B_bis], w=[B_bis])
                P.op("dve", lambda e: e.tensor_scalar(out=Wtab, in0=pow2, scalar1=W0[:, 0:1], scalar2=None,
                                                      op0=ALU.mult), r=[B_bis, B_g3], w=[B_bis])
                P.op("dve", lambda e: e.tensor_tensor(out=accv[:, qt * 128:nk], in0=accv[:, qt * 128:nk], in1=negm,
                                                      op=ALU.add), r=[Bacc, B_g3], w=[Bacc])
                if qt >= 2:
                    P.op("dve", lambda e: e.memset(mid, 0.0), r=[B_bis], w=[B_bis])

                    def one_iter(it):
                        P.op("dve", lambda e: e.tensor_scalar(out=junk[:, 0:nk], in0=accv[:, 0:nk], scalar1=mid[:, 0:1],
                                                              scalar2=0.0, op0=ALU.is_ge, op1=ALU.add,
                                                              accum_out=cntv[:, 0:1]),
                             r=[Bacc, B_bis], w=[B_junk, B_bis])
                        P.op("dve", lambda e: e.tensor_scalar(out=sgn, in0=cntv, scalar1=256.0, scalar2=-0.5,
                                                              op0=ALU.is_ge, op1=ALU.add), r=[B_bis], w=[B_bis])
                        P.op("dve", lambda e: e.scalar_tensor_tensor(out=mid, in0=sgn, scalar=Wtab[:, it:it + 1],
                                                                     in1=mid, op0=ALU.mult, op1=ALU.add),
                             r=[B_bis], w=[B_bis])

                    for it in range(NITER):
                        one_iter(it)
                    P.op("dve", lambda e: e.scalar_tensor_tensor(out=thr, in0=Wtab[:, NITER - 1:NITER], scalar=-0.5,
                                                                 in1=mid, op0=ALU.mult, op1=ALU.add),
                         r=[B_bis], w=[B_bis])
                else:
                    P.op("dve", lambda e: e.memset(thr, -1e29), r=[B_bis], w=[B_bis])
                ms = g % 2
                P.op("dve", lambda e: e.tensor_scalar(out=mrow[ms][:, 0:nk], in0=accv[:, 0:nk], scalar1=thr[:, 0:1],
                                                      scalar2=None, op0=ALU.is_lt), r=[Bacc, B_bis], w=[B_mrow[ms]])

            def tr_part(qb, i, g):
                qt = qb * 4 + i
                ms = g % 2
                qsl = slice(i * 128, (i + 1) * 128)
                for j0 in range(0, qt + 1, 8):
                    n = min(8, qt + 1 - j0)
                    bk = 4 + c3["tp"] % 2
                    c3["tp"] += 1
                    for jj in range(n):
                        j = j0 + jj
                        P.op("pe", lambda e, jj=jj, j=j, bk=bk: e.transpose(
                            out=bank16(bk)[:, jj * 128:(jj + 1) * 128], in_=mrow[ms][:, j * 128:(j + 1) * 128],
                            identity=ident_b), r=[B_mrow[ms], B_const], w=[PSB[bk]])
                    P.op("act", lambda e, j0=j0, n=n, bk=bk: e.activation(
                        out=maskT[:, j0:j0 + n, qsl], in_=bank16(bk)[:, 0:n * 128].rearrange("p (a b) -> p a b", a=n),
                        func=AF.Copy), r=[PSB[bk]], w=[B_maskT])

            def make_unit_b(qb, h, j, nj):
                BS_ = (4, 5)
                BO_, BDEN_ = (6, 7) if h % 2 == 0 else (2, 3)
                lo = max(0, 128 * j - 512 * qb)
                c0 = 512 * qb + lo - 128 * j
                nn_ = max(0, min(256 - c0, 512 - lo))
                stt = {}

                def A():
                    bs = BS_[c3["s"] % 2]
                    c3["s"] += 1
                    pi = c3["pt"] % NPT3
                    c3["pt"] += 1
                    stt["pi"] = pi
                    P.op("pe", lambda e: e.matmul(bank(bs)[:, lo:512], lhsT=kbT[:, j * 128:(j + 1) * 128],
                                                  rhs=qbT[:, h, lo:512], start=True, stop=False),
                         r=[B_kb, B_qb], w=[PSB[bs]])
                    if nn_ > 0:
                        P.op("pe", lambda e: e.matmul(bank(bs)[:, lo:lo + nn_], lhsT=ident_b,
                                                      rhs=BThi[:, h, c0:c0 + nn_], start=False, stop=False),
                             r=[B_const, B_BTs], w=[PSB[bs]])
                        P.op("pe", lambda e: e.matmul(bank(bs)[:, lo:lo + nn_], lhsT=ident_b,
                                                      rhs=BTlo[:, h, c0:c0 + nn_], start=False, stop=False),
                             r=[B_const, B_BTs], w=[PSB[bs]])
                    P.op("pe", lambda e: e.matmul(bank(bs)[:, lo:512], lhsT=negI_b, rhs=maskT[:, j, lo:512],
                                                  start=False, stop=True), r=[B_const, B_maskT], w=[PSB[bs]])
                    P.op("act", lambda e: e.activation(out=pT3v[pi][:, lo:512], in_=bank(bs)[:, lo:512], func=AF.Exp,
                                                       bias=b31[:, h:h + 1]), r=[PSB[bs], B_g3], w=[B_pT3[pi]])

                def Bf():
                    pi = stt["pi"]
                    P.op("pe", lambda e: e.matmul(bank(BO_)[:, lo:512], lhsT=vb[:, j, :], rhs=pT3v[pi][:, lo:512],
                                                  start=(j == 0), stop=(j == nj - 1)), r=[B_vb, B_pT3[pi]],
                         w=[PSB[BO_]])
                    P.op("pe", lambda e: e.matmul(bank(BDEN_)[:, lo:512], lhsT=ones_b, rhs=pT3v[pi][:, lo:512],
                                                  start=(j == 0), stop=(j == nj - 1)), r=[B_const, B_pT3[pi]],
                         w=[PSB[BDEN_]])
                    if j == nj - 1:
                        fi = c3["fin"] % 2
                        c3["fin"] += 1
                        P.op("act", lambda e: e.activation(out=rden3v[fi], in_=bank(BDEN_), func=AF.Ln),
                             r=[PSB[BDEN_]], w=[B_rden3[fi]])
                        P.op("dve", lambda e: e.tensor_copy(out=ocp3[fi], in_=bank(BO_)), r=[PSB[BO_]],
                             w=[B_ocp3[fi]])
                        P.op("act", lambda e: e.activation(out=rden3v[fi], in_=rden3v[fi], func=AF.Exp, scale=-1.0),
                             r=[B_rden3[fi]], w=[B_rden3[fi]])
                        P.op("dve", lambda e: e.tensor_tensor(out=og3v[fi], in0=rden3v[fi], in1=gtb[:, h, :],
                                                              op=ALU.mult), r=[B_rden3[fi], B_gtb], w=[B_og3[fi]])
                        P.op("dve", lambda e: e.tensor_tensor(out=ob[:, h, :], in0=ocp3[fi], in1=og3v[fi],
                                                              op=ALU.mult), r=[B_ocp3[fi], B_og3[fi]], w=[B_ob])

                return (A, Bf)

            pend3 = [None]

            def run_unit3(u):
                u[0]()
                if pend3[0] is not None:
                    pend3[0]()
                pend3[0] = u[1]

            def attn_b_all(qb):
                nj = 4 * qb + 4
                for h in range(16):
                    for j in range(nj):
                        run_unit3(make_unit_b(qb, h, j, nj))
                if pend3[0] is not None:
                    pend3[0]()
                    pend3[0] = None

            def load_idx(qb):
                qcs = slice(qb * 512, (qb + 1) * 512)
                P.op("sp", lambda e: e.dma_start(out=qiT, in_=projT[C_QI:C_QI + 2048, qcs].rearrange(
                    "(c p) t -> p c t", p=128)), w=[B_qi], dma=True)
                P.op("sp", lambda e: e.dma_start(out=wi, in_=widx_s[qcs, :].rearrange("(i p) h -> p i h", p=128)),
                     w=[B_wi], dma=True)
                P.op("act", lambda e: e.activation(out=wabs, in_=wi, func=AF.Abs, scale=32.0 ** -0.5), r=[B_wi],
                     w=[B_wi])
                P.op("act", lambda e: e.activation(out=wsgn, in_=wi, func=AF.Sign), r=[B_wi], w=[B_wi])

            def load_att_closures(qb):
                qcs = slice(qb * 512, (qb + 1) * 512)

                def c_loads():
                    P.op("sp", lambda e: e.dma_start(out=qbT, in_=projT[C_QB:C_QB + 2048, qcs].rearrange(
                        "(c p) t -> p c t", p=128)), w=[B_qb], dma=True)
                    P.op("sp", lambda e: e.dma_start(out=gtb, in_=projT[C_GB:C_GB + 2048, qcs].rearrange(
                        "(c p) t -> p c t", p=128)), w=[B_gtb], dma=True)

                L = [c_loads]
                for h in range(16):
                    L.extend(norm_closures(qbT[:, h, :], gqb[:, 0:1], B_qb, 6 + h % 2))
                return L

            def att_and_store(qb):
                qcs = slice(qb * 512, (qb + 1) * 512)
                attn_b_all(qb)
                P.op("sp", lambda e: e.dma_start(out=obT[:, qcs].rearrange("(h p) t -> p h t", p=128), in_=ob),
                     r=[B_ob], dma=True)

            load_idx(0)
            idx_part(0, 0, 0, side=load_att_closures(0))
            bis_part(0, 0, 0)
            for g in range(1, 16):
                qb = g // 4
                if g % 4 == 0:
                    load_idx(qb)
                idx_part(qb, g % 4, g, side=(load_att_closures(qb) if (g % 4 == 1 and qb > 0) else None))
                if g % 4 != 0:
                    bis_part(qb, g % 4, g)
                tr_part((g - 1) // 4, (g - 1) % 4, g - 1)
                if g % 4 == 0:
                    att_and_store(qb - 1)
                    bis_part(qb, g % 4, g)
            tr_part(3, 3, 15)
            att_and_store(3)
            P.barrier()

        if stop_after >= 4:
            a = Alloc(0)
            oaH = a([16, 1024], BF16)
            obH = a([16, 1024], BF16)
            s2_base = a.cur
            mT = a([32, 1024], BF16)
            s2b_base = a.cur
            pa = [a([16, 256], BF16) for _ in range(2)]
            pb = [a([16, 256], BF16) for _ in range(2)]
            sA = [a([2, 1024], BF16) for _ in range(2)]
            sB = [a([2, 1024], BF16) for _ in range(2)]
            u1 = [a([512], F32) for _ in range(2)]
            u2 = [a([512], F32) for _ in range(2)]
            assert a.cur <= CB, a.cur
            a2 = Alloc(0)
            wo = [a2([32, 512], BF16) for _ in range(2)]
            assert a2.cur <= s2_base
            a3 = Alloc(s2b_base)
            xin = [a3([512], F32) for _ in range(2)]
            oout = [a3([512], F32) for _ in range(2)]
            B_oaH = Buf("oaH"); B_obH = Buf("obH"); B_mT = [Buf(f"mT{i}") for i in range(32)]
            B_pa = [Buf("pa0"), Buf("pa1")]; B_pb = [Buf("pb0"), Buf("pb1")]
            B_sA = [Buf("sA0"), Buf("sA1")]; B_sB = [Buf("sB0"), Buf("sB1")]
            B_u1 = [Buf("u10"), Buf("u11")]; B_u2 = [Buf("u20"), Buf("u21")]
            B_xin = [Buf("xin0"), Buf("xin1")]; B_oout = [Buf("oo0"), Buf("oo1")]
            B_wo = [Buf("wo0"), Buf("wo1")]
            c4 = {"u": 0, "b": 0, "x": 0}

            def merged_loads(th, fbp):
                sl = fbp % 2
                tcs = slice(th * 1024, (th + 1) * 1024)
                cs = slice(fbp * 256, (fbp + 1) * 256)
                P.op("pool", lambda e: e.dma_start(out=pa[sl], in_=p_a_d[:, cs].rearrange("(k p) c -> p k c", p=128)),
                     w=[B_pa[sl]], dma=True)
                P.op("pool", lambda e: e.dma_start(out=pb[sl], in_=p_b_d[:, cs].rearrange("(k p) c -> p k c", p=128)),
                     w=[B_pb[sl]], dma=True)
                P.op("sp", lambda e: e.dma_start(out=sA[sl], in_=projT[C_MA + fbp * 256:C_MA + (fbp + 1) * 256, tcs]
                                                 .rearrange("(f p) t -> p f t", p=128)), w=[B_sA[sl]], dma=True)
                P.op("sp", lambda e: e.dma_start(out=sB[sl], in_=projT[C_MB + fbp * 256:C_MB + (fbp + 1) * 256, tcs]
                                                 .rearrange("(f p) t -> p f t", p=128)), w=[B_sB[sl]], dma=True)

            def merged_compute(th, fbp):
                sl = fbp % 2

                def sub(f2, tb2):
                    fb = fbp * 2 + f2
                    bsl = c4["b"] % 2
                    c4["b"] += 1
                    us = c4["u"] % 2
                    c4["u"] += 1
                    ts_ = slice(tb2 * 512, (tb2 + 1) * 512)
                    fc = slice(f2 * 128, (f2 + 1) * 128)
                    for k in range(16):
                        P.op("pe", lambda e, k=k: e.matmul(bank(bsl), lhsT=pa[sl][:, k, fc], rhs=oaH[:, k, ts_],
                                                           start=(k == 0), stop=(k == 15)), r=[B_pa[sl], B_oaH],
                             w=[PSB[bsl]])
                    for k in range(16):
                        P.op("pe", lambda e, k=k: e.matmul(bank(2 + bsl), lhsT=pb[sl][:, k, fc], rhs=obH[:, k, ts_],
                                                           start=(k == 0), stop=(k == 15)), r=[B_pb[sl], B_obH],
                             w=[PSB[2 + bsl]])
                    P.op("dve", lambda e: e.tensor_tensor(out=u1[us], in0=bank(bsl), in1=sA[sl][:, f2, ts_],
                                                          op=ALU.mult), r=[PSB[bsl], B_sA[sl]], w=[B_u1[us]])
                    P.op("dve", lambda e: e.tensor_tensor(out=u2[us], in0=bank(2 + bsl), in1=sB[sl][:, f2, ts_],
                                                          op=ALU.mult), r=[PSB[2 + bsl], B_sB[sl]], w=[B_u2[us]])
                    P.op("dve", lambda e: e.tensor_tensor(out=mT[:, fb, ts_], in0=u1[us], in1=u2[us], op=ALU.add),
                         r=[B_u1[us], B_u2[us]], w=[B_mT[fb]])

                for f2 in range(2):
                    for tb2 in range(2):
                        sub(f2, tb2)

            def final_block(th, cbk):
                sl = cbk % 2
                ccs = slice(cbk * 512, (cbk + 1) * 512)
                P.op("pool", lambda e: e.dma_start(out=wo[sl], in_=w_o_d[:, ccs].rearrange("(f p) c -> p f c", p=128)),
                     w=[B_wo[sl]], dma=True)

                def tok(tt):
                    bk = 4 + c4["b"] % 4
                    c4["b"] += 1
                    xs = c4["x"] % 2
                    c4["x"] += 1
                    r0 = th * 1024 + tt * 128
                    P.op("sp", lambda e: e.dma_start(out=xin[xs], in_=x_d[r0:r0 + 128, ccs]), w=[B_xin[xs]], dma=True)
                    for f in range(32):
                        P.op("pe", lambda e, f=f: e.matmul(bank(bk), lhsT=mT[:, f, tt * 128:(tt + 1) * 128],
                                                           rhs=wo[sl][:, f, :], start=(f == 0), stop=(f == 31)),
                             r=[B_wo[sl], B_mT[f]], w=[PSB[bk]])
                    P.op("dve", lambda e: e.tensor_tensor(out=oout[xs], in0=bank(bk), in1=xin[xs], op=ALU.add),
                         r=[PSB[bk], B_xin[xs]], w=[B_oout[xs]])
                    P.op("sp", lambda e: e.dma_start(out=out_d[r0:r0 + 128, ccs], in_=oout[xs]), r=[B_oout[xs]],
                         dma=True)

                for tt in range(8):
                    tok(tt)

            for th in range(2):
                tcs = slice(th * 1024, (th + 1) * 1024)
                P.op("sp", lambda e, tcs=tcs: e.dma_start(out=oaH, in_=oaT[:, tcs].rearrange("(k p) t -> p k t", p=128)),
                     w=[B_oaH], dma=True)
                P.op("sp", lambda e, tcs=tcs: e.dma_start(out=obH, in_=obT[:, tcs].rearrange("(k p) t -> p k t", p=128)),
                     w=[B_obH], dma=True)
                merged_loads(th, 0)
                for fbp in range(16):
                    if fbp + 1 < 16:
                        merged_loads(th, fbp + 1)
                    merged_compute(th, fbp)
                P.barrier()
                for cbk in range(8):
                    final_block(th, cbk)
                P.barrier()

        P.finalize()
        keys = P.sem_keys()
        sems = {}
        for i, k in enumerate(keys):
            sems[k] = st.enter_context(nc.semaphore(f"s{i}"))
        block = st.enter_context(nc.Block())

        @block.tensor
        def _(e):
            P.emit_engine("pe", e, sems)

        @block.scalar
        def _(e):
            P.emit_engine("act", e, sems)

        @block.vector
        def _(e):
            P.emit_engine("dve", e, sems)

        @block.gpsimd
        def _(e):
            P.emit_engine("pool", e, sems)

        @block.sync
        def _(e):
            P.emit_engine("sp", e, sems, final_wait=True)
    return nc


def _in_maps(inputs):
    maps = []
    shared = {k: np.ascontiguousarray(np.asarray(v, dtype=np.float32)) for k, v in inputs.items()
              if k not in ("x", "positions")}
    for k, v in CONSTS.items():
        shared["c_" + k] = v
    x = np.asarray(inputs["x"], dtype=np.float32)
    pos = np.asarray(inputs["positions"], dtype=np.int32)
    for b in range(x.shape[0]):
        m = dict(shared)
        m["x"] = np.ascontiguousarray(x[b])
        m["positions"] = np.ascontiguousarray(pos[b])
        maps.append(m)
    return maps


_NC_CACHE = {}


def kernel(**inputs):
    if "nc" not in _NC_CACHE:
        _NC_CACHE["nc"] = build_nc()
    nc = _NC_CACHE["nc"]
    maps = _in_maps(inputs)
    res = run_bass_kernel_spmd(nc, maps, core_ids=list(range(len(maps))))
    return np.stack([r["out"] for r in res.results], axis=0).astype(np.float32)
```
